# Optimizing a Trainium2 kernel written in Bass

```python
import math
import jax, jax.numpy as jnp
from jax import lax
import numpy as np

D_MODEL = 1024
BATCH = 8
SEQ = 2048
DEPTH = 4

MEM_LEN = 256
EXPAND = 2
MIX_WIDTH = EXPAND * D_MODEL
A_WIDTH = MIX_WIDTH // 2
A_HEAD_DIM = 64
A_HEADS = A_WIDTH // A_HEAD_DIM
A_PATTERNS = ((128, 1), (512, 4), (2048, 16))
A_BLOCK = 128
B_WIDTH = MIX_WIDTH - A_WIDTH
POOL_WINDOWS = (2, 4, 8, 16)
B_GROUP = B_WIDTH // len(POOL_WINDOWS)
C_WIDTH = D_MODEL
C_CHUNK = 128
C_GROUPS = 4
C_GROUP_DIM = C_WIDTH // C_GROUPS
D_WIDTH = D_MODEL // 2
S5_GROUP_DIM = 16
S5_GROUPS = D_WIDTH // S5_GROUP_DIM
S5_STATE = 64
X_HEADS = 4
X_HEAD_DIM = D_MODEL // X_HEADS
N_EVEN = (DEPTH + 1) // 2
N_ODD = DEPTH // 2
EPS = 1e-6
NEG = -1e30

kernel_name = 'hybrid_dilated_pool_sgu_s5_trunk'


def rms_norm(x, g):
    xf = x.astype(jnp.float32)
    y = xf * lax.rsqrt(jnp.mean(xf * xf, axis=-1, keepdims=True) + EPS)
    return (y * g.astype(jnp.float32)).astype(x.dtype)


def _dilated_pattern(q, k, v, window, dilation):
    b, s, h, dh = q.shape
    d = dilation
    L = s // d
    w = window // d
    nb = -(-L // A_BLOCK)
    lp = nb * A_BLOCK
    n = b * d

    def to_dilated(t):
        t = t.reshape(b, L, d, h, dh).transpose(0, 2, 1, 3, 4).reshape(n, L, h, dh)
        return jnp.pad(t, ((0, 0), (0, lp - L), (0, 0), (0, 0)))

    def band(t):
        tp = jnp.pad(t, ((0, 0), (A_BLOCK, 0), (0, 0), (0, 0))).reshape(n, nb + 1, A_BLOCK, h, dh)
        return jnp.concatenate([tp[:, :-1], tp[:, 1:]], axis=2)

    qb = to_dilated(q).reshape(n, nb, A_BLOCK, h, dh)
    kb = band(to_dilated(k))
    vb = band(to_dilated(v))
    i = jnp.arange(A_BLOCK)[:, None]
    j = jnp.arange(2 * A_BLOCK)[None, :]
    dist = i + A_BLOCK - j
    blk = jnp.arange(nb)[:, None, None]
    valid = (dist >= 0) & (dist <= w) & ((j >= A_BLOCK) | (blk > 0))
    sc = jnp.einsum('nbihd,nbjhd->nbhij', qb, kb, preferred_element_type=jnp.float32)
    sc = jnp.where(valid[None, :, None], sc, NEG)
    m = jnp.max(sc, axis=-1, keepdims=True)
    p = jnp.exp(sc - m)
    den = jnp.sum(p, axis=-1, keepdims=True)
    o = jnp.einsum('nbhij,nbjhd->nbihd', (p / den).astype(v.dtype), vb)
    lse = (m + jnp.log(den))[..., 0].transpose(0, 1, 3, 2)

    def from_dilated(t):
        rest = t.shape[3:]
        t = t.reshape((n, lp) + rest)[:, :L]
        return t.reshape((b, d, L) + rest).swapaxes(1, 2).reshape((b, s) + rest)

    return from_dilated(o), from_dilated(lse)


def dilated_attention(q, k, v):
    outs, lses = zip(*[_dilated_pattern(q, k, v, w, d) for (w, d) in A_PATTERNS])
    wts = jax.nn.softmax(jnp.stack(lses, axis=0), axis=0)
    o = jnp.sum(jnp.stack(outs, axis=0).astype(jnp.float32) * wts[..., None], axis=0)
    return o.astype(q.dtype)


def multiscale_pool(v, pool_w, pool_scale):
    b, s, _ = v.shape
    vf = v.astype(jnp.float32)
    c0 = jnp.pad(jnp.cumsum(vf, axis=1), ((0, 0), (1, 0), (0, 0)))
    pos = jnp.arange(1, s + 1, dtype=jnp.float32)[None, :, None]
    groups = []
    for g, w in enumerate(POOL_WINDOWS):
        sl = slice(g * B_GROUP, (g + 1) * B_GROUP)
        cg = c0[..., sl]
        lower = jnp.pad(cg, ((0, 0), (w - 1, 0), (0, 0)))[:, :s]
        mean = (cg[:, 1:] - lower) / jnp.minimum(pos, float(w))
        groups.append(mean - vf[..., sl])
    pooled = jnp.stack(groups, axis=2).astype(v.dtype)
    mixed = jnp.einsum('bsgc,gcd->bsgd', pooled, pool_w).reshape(b, s, B_WIDTH)
    return mixed * pool_scale


def spatial_gating(u, v, ln_g, ln_b, w_s, b_s):
    b, s, _ = u.shape
    vf = v.astype(jnp.float32)
    mu = jnp.mean(vf, axis=-1, keepdims=True)
    var = jnp.mean(jnp.square(vf - mu), axis=-1, keepdims=True)
    vn = ((vf - mu) * lax.rsqrt(var + EPS) * ln_g.astype(jnp.float32) + ln_b.astype(jnp.float32)).astype(v.dtype)
    nc = s // C_CHUNK
    vc = vn.reshape(b, nc, C_CHUNK, C_GROUPS, C_GROUP_DIM)
    mask = jnp.tril(jnp.ones((C_CHUNK, C_CHUNK), dtype=bool))
    w = jnp.where(mask[None], w_s, jnp.zeros_like(w_s))
    mixed = jnp.einsum('gij,bnjgc->bnigc', w, vc) + b_s.T[None, None, :, :, None]
    return u * mixed.reshape(b, s, C_WIDTH)


def _ssm_combine(e1, e2):
    a1r, a1i, b1r, b1i = e1
    a2r, a2i, b2r, b2i = e2
    return (a2r * a1r - a2i * a1i,
            a2r * a1i + a2i * a1r,
            a2r * b1r - a2i * b1i + b2r,
            a2r * b1i + a2i * b1r + b2i)


def s5_ssm(u, a_re, a_im, log_dt, b_re, b_im, c_re, c_im, d_skip, w1, w2):
    bsz, s, _ = u.shape
    f32 = jnp.float32
    uf = u.astype(f32).reshape(bsz, s, S5_GROUPS, S5_GROUP_DIM)
    ar, ai = a_re.astype(f32), a_im.astype(f32)
    dt = jnp.exp(log_dt.astype(f32))[:, None]
    mag = jnp.exp(dt * ar)
    abar_re = mag * jnp.cos(dt * ai)
    abar_im = mag * jnp.sin(dt * ai)
    nr, ni = abar_re - 1.0, abar_im
    inv = 1.0 / (ar * ar + ai * ai)
    coef_re = (nr * ar + ni * ai) * inv
    coef_im = (ni * ar - nr * ai) * inv
    br, bi = b_re.astype(f32), b_im.astype(f32)
    bbar_re = coef_re[..., None] * br - coef_im[..., None] * bi
    bbar_im = coef_re[..., None] * bi + coef_im[..., None] * br
    bu_re = jnp.einsum('bsgh,gph->bsgp', uf, bbar_re)
    bu_im = jnp.einsum('bsgh,gph->bsgp', uf, bbar_im)
    shape_a = (1, s, S5_GROUPS, S5_STATE)
    a_seq_re = jnp.broadcast_to(abar_re[None, None], shape_a)
    a_seq_im = jnp.broadcast_to(abar_im[None, None], shape_a)
    _, _, h_re, h_im = lax.associative_scan(_ssm_combine, (a_seq_re, a_seq_im, bu_re, bu_im), axis=1)
    y = (jnp.einsum('bsgp,ghp->bsgh', h_re, c_re.astype(f32))
         - jnp.einsum('bsgp,ghp->bsgh', h_im, c_im.astype(f32))
         + d_skip.astype(f32).reshape(S5_GROUPS, S5_GROUP_DIM) * uf)
    y = jax.nn.gelu(y.reshape(bsz, s, D_WIDTH)).astype(u.dtype)
    return (y @ w1) * jax.nn.sigmoid(y @ w2)


def memory_cross_attention(h, mem_n, w_q, w_kv, w_o):
    b, s, _ = h.shape
    m = mem_n.shape[1]
    q = (h @ w_q).reshape(b, s, X_HEADS, X_HEAD_DIM)
    kv = (mem_n @ w_kv).reshape(b, m, 2, X_HEADS, X_HEAD_DIM)
    k, v = kv[:, :, 0], kv[:, :, 1]
    sc = jnp.einsum('bshd,bmhd->bhsm', q, k, preferred_element_type=jnp.float32) * (X_HEAD_DIM ** -0.5)
    p = jax.nn.softmax(sc, axis=-1).astype(v.dtype)
    o = jnp.einsum('bhsm,bmhd->bshd', p, v).reshape(b, s, D_MODEL)
    return o @ w_o


def setup_inputs(seed: int = 0) -> dict:
    key = jax.random.key(seed)
    ks = iter(jax.random.split(key, 40))

    def nrm(shape, scale):
        return scale * jax.random.normal(next(ks), shape, jnp.float32)

    def gain(shape):
        return 1.0 + nrm(shape, 0.05)

    return {
        'x': nrm((BATCH, SEQ, D_MODEL), 1.0),
        'mem': nrm((BATCH, MEM_LEN, D_MODEL), 1.0),
        'norm_ab': gain((N_EVEN, D_MODEL)),
        'w_in_ab': nrm((N_EVEN, D_MODEL, 4 * A_WIDTH + 2 * B_WIDTH), D_MODEL ** -0.5),
        'pool_w': nrm((N_EVEN, len(POOL_WINDOWS), B_GROUP, B_GROUP), B_GROUP ** -0.5),
        'pool_scale': gain((N_EVEN, B_WIDTH)),
        'w_out_ab': nrm((N_EVEN, A_WIDTH + B_WIDTH, D_MODEL), (A_WIDTH + B_WIDTH) ** -0.5),
        'norm_cd': gain((N_ODD, D_MODEL)),
        'w_in_cd': nrm((N_ODD, D_MODEL, 3 * C_WIDTH + 2 * D_WIDTH), D_MODEL ** -0.5),
        'sgu_ln_g': gain((N_ODD, C_WIDTH)),
        'sgu_ln_b': nrm((N_ODD, C_WIDTH), 0.02),
        'sgu_w': nrm((N_ODD, C_GROUPS, C_CHUNK, C_CHUNK), C_CHUNK ** -0.5),
        'sgu_b': gain((N_ODD, C_GROUPS, C_CHUNK)),
        's5_a_re': -0.5 + nrm((N_ODD, S5_GROUPS, S5_STATE), 0.01),
        's5_a_im': jnp.pi * jnp.arange(S5_STATE, dtype=jnp.float32)[None, None, :] + nrm((N_ODD, S5_GROUPS, S5_STATE), 0.01),
        's5_log_dt': jax.random.uniform(next(ks), (N_ODD, S5_GROUPS), jnp.float32, math.log(1e-3), math.log(1e-1)),
        's5_b_re': nrm((N_ODD, S5_GROUPS, S5_STATE, S5_GROUP_DIM), (2 * S5_GROUP_DIM) ** -0.5),
        's5_b_im': nrm((N_ODD, S5_GROUPS, S5_STATE, S5_GROUP_DIM), (2 * S5_GROUP_DIM) ** -0.5),
        's5_c_re': nrm((N_ODD, S5_GROUPS, S5_GROUP_DIM, S5_STATE), S5_STATE ** -0.5),
        's5_c_im': nrm((N_ODD, S5_GROUPS, S5_GROUP_DIM, S5_STATE), S5_STATE ** -0.5),
        's5_d': nrm((N_ODD, D_WIDTH), 1.0),
        'glu_w1': nrm((N_ODD, D_WIDTH, D_WIDTH), D_WIDTH ** -0.5),
        'glu_w2': nrm((N_ODD, D_WIDTH, D_WIDTH), D_WIDTH ** -0.5),
        'w_out_cd': nrm((N_ODD, C_WIDTH + D_WIDTH, D_MODEL), (C_WIDTH + D_WIDTH) ** -0.5),
        'norm_x': gain((DEPTH, D_MODEL)),
        'w_xq': nrm((DEPTH, D_MODEL, D_MODEL), D_MODEL ** -0.5),
        'w_xkv': nrm((DEPTH, D_MODEL, 2 * D_MODEL), D_MODEL ** -0.5),
        'w_xo': nrm((DEPTH, D_MODEL, D_MODEL), D_MODEL ** -0.5),
        'mem_norm': gain((D_MODEL,)),
        'final_norm': gain((D_MODEL,)),
    }


def reference(x, mem, norm_ab, w_in_ab, pool_w, pool_scale, w_out_ab,
              norm_cd, w_in_cd, sgu_ln_g, sgu_ln_b, sgu_w, sgu_b,
              s5_a_re, s5_a_im, s5_log_dt, s5_b_re, s5_b_im, s5_c_re, s5_c_im, s5_d,
              glu_w1, glu_w2, w_out_cd, norm_x, w_xq, w_xkv, w_xo, mem_norm, final_norm):
    b, s, _ = x.shape
    mem_n = rms_norm(mem, mem_norm)
    for layer in range(DEPTH):
        i = layer // 2
        if layer % 2 == 0:
            hn = rms_norm(x, norm_ab[i])
            z = hn @ w_in_ab[i]
            q, k, v, g_a, v_b, g_b = jnp.split(
                z, [A_WIDTH, 2 * A_WIDTH, 3 * A_WIDTH, 4 * A_WIDTH, 4 * A_WIDTH + B_WIDTH], axis=-1)
            q = q.reshape(b, s, A_HEADS, A_HEAD_DIM) * (A_HEAD_DIM ** -0.5)
            k = k.reshape(b, s, A_HEADS, A_HEAD_DIM)
            v = v.reshape(b, s, A_HEADS, A_HEAD_DIM)
            a_out = dilated_attention(q, k, v).reshape(b, s, A_WIDTH) * jax.nn.silu(g_a)
            b_out = multiscale_pool(v_b, pool_w[i], pool_scale[i]) * jax.nn.silu(g_b)
            y = jnp.concatenate([a_out, b_out], axis=-1) @ w_out_ab[i]
        else:
            hn = rms_norm(x, norm_cd[i])
            z = hn @ w_in_cd[i]
            u_c, v_c, g_c, x_d, g_d = jnp.split(
                z, [C_WIDTH, 2 * C_WIDTH, 3 * C_WIDTH, 3 * C_WIDTH + D_WIDTH], axis=-1)
            c_out = spatial_gating(u_c, v_c, sgu_ln_g[i], sgu_ln_b[i], sgu_w[i], sgu_b[i]) * jax.nn.silu(g_c)
            d_out = s5_ssm(x_d, s5_a_re[i], s5_a_im[i], s5_log_dt[i], s5_b_re[i], s5_b_im[i],
                           s5_c_re[i], s5_c_im[i], s5_d[i], glu_w1[i], glu_w2[i]) * jax.nn.silu(g_d)
            y = jnp.concatenate([c_out, d_out], axis=-1) @ w_out_cd[i]
        x = x + y
        x = x + memory_cross_attention(rms_norm(x, norm_x[layer]), mem_n, w_xq[layer], w_xkv[layer], w_xo[layer])
    return rms_norm(x, final_norm)
```

```python
import numpy as np
import concourse.bass as bass
import concourse.mybir as mybir
from concourse.bass_utils import run_bass_kernel_spmd
from contextlib import ExitStack

F32 = mybir.dt.float32
BF16 = mybir.dt.bfloat16
I32 = mybir.dt.int32
ALU = mybir.AluOpType
AF = mybir.ActivationFunctionType

ENGS = ('pe', 'act', 'dve', 'pool', 'sp')
EP = 20000
NEPOCH = 8
NSLOT = 8

SEQ = 2048
DM = 1024
NCH = 8
TQ = 4
EPS = 1e-6
TWO_PI = float(2 * np.pi)


class Buf:
    __slots__ = ('w', 'r', 'excl')

    def __init__(self, excl=False):
        self.w = None
        self.r = {}
        self.excl = excl


class _Rec:
    def __init__(self):
        self.call = None

    def __getattr__(self, name):
        def f(*a, **k):
            self.call = (name, a, k)
            return None
        return f


class Sched:
    def __init__(self, nc, es):
        self.nc = nc
        self.ops = {e: [] for e in ENGS}
        self.incs = {e: 0 for e in ENGS}
        self.waited = {e: {} for e in ENGS}
        self.sems = {}
        for e in ENGS:
            if e == 'sp':
                continue
            for k in range(NEPOCH):
                self.sems[(e, k)] = es.enter_context(nc.semaphore(f"s_{e}{k}"))
        self.dsem = [es.enter_context(nc.semaphore(f"s_dma{i}")) for i in range(NSLOT)]
        self.dcnt = [0] * NSLOT
        self.dnext = 0
        self.nops = 0

    def _collect(self, eng, reads, writes, extra=()):
        waits = {}

        def need(t):
            if t is None:
                return
            key, n = t
            if key == 'pe' and eng == 'pe':
                return
            if n > self.waited[eng].get(key, 0):
                if n > waits.get(key, 0):
                    waits[key] = n
        for b in reads:
            need(b.w)
            if b.excl:
                for k, t in b.r.items():
                    if k != eng:
                        need(t)
        for b in writes:
            need(b.w)
            for t in b.r.values():
                need(t)
        for t in extra:
            need(t)
        for key, n in waits.items():
            self.waited[eng][key] = n
        return list(waits.items())

    def op(self, eng, fn, reads=(), writes=(), inc=True):
        assert inc or eng == 'pe'
        waits = self._collect(eng, reads, writes)
        rec = _Rec()
        fn(rec)
        name_, a_, k_ = rec.call
        fn = (lambda e, name_=name_, a_=a_, k_=k_: getattr(e, name_)(*a_, **k_))
        n = self.incs[eng] + 1
        assert n <= EP * NEPOCH
        ticket = (eng, n)
        self.ops[eng].append((waits, fn, ('e', n) if inc else None))
        if inc:
            self.incs[eng] = n
        for b in reads:
            b.r[eng] = ticket
        for b in writes:
            b.w = ticket
            b.r = {}
        self.nops += 1
        return ticket

    def dma(self, out, in_, reads=(), writes=(), **kw):
        slot = self.dnext
        self.dnext = (slot + 1) % NSLOT
        prev = self.dcnt[slot]
        key = ('dma', slot)
        extra = [(key, prev)] if prev > 0 else []
        waits = self._collect('sp', reads, writes, extra)
        n = prev + 1
        self.dcnt[slot] = n
        ticket = (key, n)

        def fn(sp, out=out, in_=in_, kw=kw):
            return sp.dma_start(out=out, in_=in_, **kw)
        self.ops['sp'].append((waits, fn, ('d', slot)))
        for b in reads:
            b.r[key] = ticket
        for b in writes:
            b.w = ticket
            b.r = {}
        self.nops += 1
        return ticket

    def barrier(self):
        for eng in ENGS:
            waits = []
            for e2 in ENGS:
                if e2 != eng and e2 != 'sp' and self.incs[e2] > self.waited[eng].get(e2, 0):
                    waits.append((e2, self.incs[e2]))
                    self.waited[eng][e2] = self.incs[e2]
            for s_ in range(NSLOT):
                key = ('dma', s_)
                if self.dcnt[s_] > self.waited[eng].get(key, 0):
                    waits.append((key, self.dcnt[s_]))
                    self.waited[eng][key] = self.dcnt[s_]
            self.ops[eng].append((waits, None, None))

    def _wait(self, e, key, n):
        if isinstance(key, tuple):
            e.wait_ge(self.dsem[key[1]], 16 * n)
        else:
            k = (n - 1) // EP
            e.wait_ge(self.sems[(key, k)], (n - 1) % EP + 1)

    def emit(self):
        nc = self.nc
        fin = [(('dma', s), self.dcnt[s]) for s in range(NSLOT) if self.dcnt[s] > 0]
        with nc.Block() as block:
            def run(ename, e):
                for waits, fn, inc in self.ops[ename]:
                    for key, n in waits:
                        self._wait(e, key, n)
                    if fn is None:
                        continue
                    ins = fn(e)
                    if inc is not None:
                        if inc[0] == 'e':
                            n = inc[1]
                            ins.then_inc(self.sems[(ename, (n - 1) // EP)], 1)
                        else:
                            ins.then_inc(self.dsem[inc[1]], 16)
                if ename == 'sp':
                    for key, n in fin:
                        self._wait(e, key, n)

            @block.tensor
            def _(pe):
                run('pe', pe)

            @block.scalar
            def _(act):
                run('act', act)

            @block.vector
            def _(dve):
                run('dve', dve)

            @block.gpsimd
            def _(pool):
                run('pool', pool)

            @block.sync
            def _(sp):
                run('sp', sp)


PARAMS = [
    ('norm_ab', (2, 1024)), ('w_in_ab', (2, 1024, 6144)), ('pool_w', (2, 4, 256, 256)), ('pool_scale', (2, 1024)),
    ('w_out_ab', (2, 2048, 1024)), ('norm_cd', (2, 1024)), ('w_in_cd', (2, 1024, 4096)), ('sgu_ln_g', (2, 1024)),
    ('sgu_ln_b', (2, 1024)), ('sgu_w', (2, 4, 128, 128)), ('sgu_b', (2, 4, 128)), ('s5_a_re', (2, 32, 64)),
    ('s5_a_im', (2, 32, 64)), ('s5_log_dt', (2, 32)), ('s5_b_re', (2, 32, 64, 16)), ('s5_b_im', (2, 32, 64, 16)),
    ('s5_c_re', (2, 32, 16, 64)), ('s5_c_im', (2, 32, 16, 64)), ('s5_d', (2, 512)), ('glu_w1', (2, 512, 512)),
    ('glu_w2', (2, 512, 512)), ('w_out_cd', (2, 1536, 1024)), ('norm_x', (4, 1024)), ('w_xq', (4, 1024, 1024)),
    ('w_xkv', (4, 1024, 2048)), ('w_xo', (4, 1024, 1024)), ('mem_norm', (1024,)), ('final_norm', (1024,)),
]


def host_consts():
    c = {}
    c['c_ident'] = np.eye(128, dtype=np.float32)
    k = np.arange(128)[:, None]
    q = np.arange(128)[None, :]
    NEGM = -30000.0
    diag = np.where(q >= k, 0.0, NEGM)
    prev = np.where(q <= k, 0.0, NEGM)
    c['c_maskb'] = np.concatenate([diag, prev], axis=1).astype(np.float32)
    c['c_triu'] = (k <= q).astype(np.float32)
    c['c_iota'] = np.tile(np.arange(512, dtype=np.float32)[None, :], (128, 1))
    inv = np.zeros((4, 16), np.float32)
    for g, w in enumerate((2, 4, 8, 16)):
        inv[g] = 1.0 / np.minimum(np.arange(1, 17), w)
    c['c_invcnt'] = np.tile(inv.reshape(1, 64), (128, 1)).astype(np.float32)
    M = np.zeros((128, 4, 8), np.float32)
    for g2 in range(2):
        for gpl in range(4):
            M[g2 * 64:(g2 + 1) * 64, gpl, 2 * gpl + g2] = 1.0
    c['c_bm1'] = M.reshape(128, 32)
    M2 = np.zeros((128, 4, 2), np.float32)
    for gl in range(8):
        for gpl in range(4):
            for g2 in range(2):
                if gl == 2 * gpl + g2:
                    M2[gl * 16:(gl + 1) * 16, gpl, g2] = 1.0
    c['c_bm2'] = M2.reshape(128, 8)
    return c


class Kern:
    def __init__(self, nc, S, es, depth=4, wplan=None):
        self.nc, self.S, self.es, self.depth = nc, S, es, depth
        self.wplan = wplan
        self.wrec = []
        self.wissued = []
        d = {}
        d['x'] = nc.dram_tensor("x", [SEQ, DM], F32, kind="ExternalInput").ap()
        d['mem'] = nc.dram_tensor("mem", [256, DM], F32, kind="ExternalInput").ap()
        for name, shp in PARAMS:
            d[name] = nc.dram_tensor(name, list(shp), F32, kind="ExternalInput").ap()
        for name, arr in host_consts().items():
            d[name] = nc.dram_tensor(name, list(arr.shape), F32, kind="ExternalInput").ap()
        d['out'] = nc.dram_tensor("out", [SEQ, DM], F32, kind="ExternalOutput").ap()
        self.d = d
        self.psi = 0

    def sb(self, name, shape, dt=F32, es=None):
        self.uid = getattr(self, 'uid', 0) + 1
        return (es or self.es).enter_context(self.nc.sbuf_tensor(f"{name}_{self.uid}", shape, dt))

    def grid(self, *dims):
        if len(dims) == 1:
            return [Buf() for _ in range(dims[0])]
        return [self.grid(*dims[1:]) for _ in range(dims[0])]

    def ps(self):
        i = self.psi
        self.psi = (i + 1) % len(self.psr)
        j = self.psr[i]
        return self.P[j], self.bP[j]

    def ps_rot(self, banks):
        self.psr = list(banks)
        self.psi = 0

    def setup(self):
        nc, S, d = self.nc, self.S, self.d
        sb = self.sb
        self.P = [self.es.enter_context(nc.psum_tensor(f"P{i}", [128, 512], F32)) for i in range(8)]
        self.bP = [Buf(excl=True) for _ in range(8)]
        self.ps_rot(range(8))
        self.xT = sb("xT", [128, NCH, SEQ], F32)
        self.bx = self.grid(NCH, TQ)
        self.hnT = sb("hnT", [128, NCH, SEQ], BF16)
        self.bhn = self.grid(NCH, TQ)
        self.wst = [sb(f"wst{i}", [128, 2048], F32) for i in range(2)]
        self.bwst = [Buf() for _ in range(2)]
        self.wbf = [sb(f"wbf{i}", [128, 2048], BF16) for i in range(2)]
        self.bwbf = [Buf() for _ in range(2)]
        self.wi = 0
        self.memT = sb("memT", [128, NCH, 256], BF16)
        self.bmem = Buf()
        self.ident = sb("ident", [128, 128], F32)
        self.identb = sb("identb", [128, 128], BF16)
        self.onesb = sb("onesb", [128, 128], BF16)
        self.maskb = sb("maskb", [128, 256], BF16)
        self.triu = sb("triu", [128, 128], F32)
        self.iota = sb("iota", [128, 512], F32)
        self.invcnt = sb("invcnt", [128, 64], F32)
        self.bm1 = sb("bm1", [128, 32], F32)
        self.bm2 = sb("bm2", [128, 8], F32)
        self.halfpi = sb("halfpi", [128, 1], F32)
        self.bconst = Buf()
        tmp = self.wst[0]
        S.dma(self.ident[:], d['c_ident'], writes=[self.bconst])
        S.dma(self.triu[:], d['c_triu'], writes=[self.bconst])
        S.dma(self.iota[:], d['c_iota'], writes=[self.bconst])
        S.dma(self.invcnt[:], d['c_invcnt'], writes=[self.bconst])
        S.dma(self.bm1[:], d['c_bm1'], writes=[self.bconst])
        S.dma(self.bm2[:], d['c_bm2'], writes=[self.bconst])
        S.dma(tmp[:, 0:256], d['c_maskb'], writes=[self.bwst[0]])
        S.op('pool', lambda e: e.tensor_copy(out=self.maskb[:], in_=tmp[:, 0:256]), reads=[self.bwst[0]], writes=[self.bconst])
        S.op('pool', lambda e: e.tensor_copy(out=self.identb[:], in_=self.ident[:]), reads=[self.bconst], writes=[self.bconst])
        S.op('pool', lambda e: e.memset(self.onesb[:], 1.0), writes=[self.bconst])
        S.op('pool', lambda e: e.memset(self.halfpi[:], float(np.pi / 2)), writes=[self.bconst])
        self.magic = sb("magic", [128, 2], F32)
        S.op('pool', lambda e: e.memset(self.magic[:, 0:1], 12582912.0), writes=[self.bconst])
        S.op('pool', lambda e: e.memset(self.magic[:, 1:2], -12582912.0), writes=[self.bconst])
        rows = [('norm_ab', 2, 8), ('norm_cd', 2, 8), ('norm_x', 4, 8), ('pool_scale', 2, 8), ('mem_norm', 1, 8), ('final_norm', 1, 8), ('s5_d', 2, 4)]
        vst = sb("vst", [128, 128], F32)
        bvst = Buf()
        S.op('dve', lambda e: e.memset(vst[:], 0.0), writes=[bvst])
        vall = sb("vall", [128, 128], F32)
        r0 = 0
        self.vec = {}
        for name, n, ch in rows:
            src = d[name]
            if len(src.shape) == 2:
                src2 = src.rearrange("l (c p) -> (l c) p", p=128)
            else:
                src2 = src.rearrange("(c p) -> c p", p=128)
            S.dma(vst[r0:r0 + n * ch, :], src2, writes=[bvst])
            self.vec[name] = vall[:, r0:r0 + n * ch].rearrange("p (l c) -> p l c", c=ch)
            r0 += n * ch
        ps, bps = self.ps()
        S.op('pe', lambda e: e.transpose(out=ps[:, 0:128], in_=vst[:], identity=self.ident[:]), reads=[bvst, self.bconst], writes=[bps])
        S.op('dve', lambda e: e.tensor_copy(out=vall[:], in_=ps[:, 0:128]), reads=[bps], writes=[self.bconst])

    def wload(self, wap, r0, nr, c0, ncols, prefetch=True):
        key = (wap.tensor.name, int(wap.offset), r0, nr, c0, ncols)
        if self.wplan is None:
            self.wrec.append((key, (wap.tensor.name, int(wap.offset), int(wap.shape[0]), int(wap.shape[1]), r0, nr, c0, ncols)))
            return self._wload_now(wap, r0, nr, c0, ncols)
        if not self.wissued:
            self._wprefetch()
        k0, tile = self.wissued.pop(0)
        assert k0 == key, (k0, key)
        if prefetch:
            self._wprefetch()
        return tile

    def wfence(self):
        if self.wplan is None:
            self.wrec.append((None, None))
            return
        assert not self.wissued
        assert self.wplan and self.wplan[0][0] is None
        self.wplan.pop(0)

    def _wprefetch(self):
        if self.wissued or not self.wplan or self.wplan[0][0] is None:
            return
        key, (name, off, R, C, r0, nr, c0, ncols) = self.wplan.pop(0)
        full = self.d[name]
        nd_ = len(full.shape)
        flat = full if nd_ == 1 else full.rearrange(" ".join("abcd"[:nd_]) + " -> (" + " ".join("abcd"[:nd_]) + ")")
        wap = flat[off:off + R * C].rearrange("(r c) -> r c", c=C)
        self.wissued.append((key, self._wload_now(wap, r0, nr, c0, ncols)))

    def _wload_now(self, wap, r0, nr, c0, ncols):
        S = self.S
        P = min(nr, 128)
        kc = nr // P
        assert kc * ncols <= 2048
        i = self.wi
        self.wi = (i + 1) % 2
        st, bst, wb, bwb = self.wst[i], self.bwst[i], self.wbf[i], self.bwbf[i]
        n = kc * ncols
        src = wap[r0:r0 + nr, c0:c0 + ncols].rearrange("(kc p) c -> p kc c", p=P)
        S.dma(st[0:P, 0:n].rearrange("p (kc c) -> p kc c", c=ncols), src, writes=[bst])
        S.op('act', lambda e: e.activation(out=wb[0:P, 0:n], in_=st[0:P, 0:n], func=AF.Copy), reads=[bst], writes=[bwb])
        return wb[0:P, 0:n].rearrange("p (kc c) -> p kc c", c=ncols), bwb

    def mm_fm(self, ps_ap, bps, wt, bw, mcols, rhs_fn, rhs_bufs, kcs, first=True, last=True):
        S = self.S
        n = len(kcs)
        for i, kc in enumerate(kcs):
            rhs = rhs_fn(kc)
            S.op('pe', lambda e, kc=kc, rhs=rhs, i=i: e.matmul(ps_ap, lhsT=wt[:, kc, mcols], rhs=rhs,
                                                          start=(first and i == 0), stop=(last and i == n - 1)),
                 reads=[bw] + rhs_bufs(kc), writes=[bps], inc=(i == n - 1))

    def rmsnorm(self, src, bsrc, N, gcol, dst, bdst, es):
        S = self.S
        blk = min(N, 512)
        sq = [self.sb(f"rn_sq{i}_{N}", [128, blk], BF16, es) for i in range(2)]
        bsq = [Buf(), Buf()]
        rs = self.sb(f"rn_rs_{N}", [128, blk], F32, es)
        brs = Buf()
        for t in range(N // blk):
            sl = slice(t * blk, (t + 1) * blk)
            ps, bps = self.ps()
            for c in range(NCH):
                j = c % 2
                S.op('act', lambda e, c=c, j=j: e.activation(out=sq[j][:], in_=src[:, c, sl], func=AF.Square),
                     reads=[bsrc[c][t]], writes=[bsq[j]])
                S.op('pe', lambda e, c=c, j=j: e.matmul(ps[:, 0:blk], lhsT=self.onesb[:], rhs=sq[j][:], start=(c == 0), stop=(c == NCH - 1)),
                     reads=[bsq[j], self.bconst], writes=[bps], inc=True)
            S.op('dve', lambda e: e.tensor_scalar(out=rs[:], in0=ps[:, 0:blk], scalar1=1.0 / DM, scalar2=EPS, op0=ALU.mult, op1=ALU.add),
                 reads=[bps], writes=[brs])
            S.op('act', lambda e: e.activation(out=rs[:], in_=rs[:], func=AF.Ln), reads=[brs], writes=[brs])
            S.op('act', lambda e: e.activation(out=rs[:], in_=rs[:], func=AF.Exp, scale=-0.5), reads=[brs], writes=[brs])
            for c in range(NCH):
                S.op('dve', lambda e, c=c: e.scalar_tensor_tensor(out=dst[:, c, sl], in0=src[:, c, sl], scalar=gcol[:, c:c + 1], in1=rs[:],
                                                                   op0=ALU.mult, op1=ALU.mult),
                     reads=[bsrc[c][t], brs, self.bconst], writes=[bdst[c][t]])

    def load_T(self, src_ap, ntiles, dst, bdst_fn):
        S = self.S
        for n in range(ntiles):
            i = n % 2
            st, bst = self.wst[i], self.bwst[i]
            S.dma(st[:, 0:1024], src_ap[n * 128:(n + 1) * 128, :], writes=[bst])
            for h in range(2):
                ps, bps = self.ps()
                for j in range(4):
                    c = 4 * h + j
                    S.op('pe', lambda e, c=c, j=j: e.transpose(out=ps[:, j * 128:(j + 1) * 128], in_=st[:, c * 128:(c + 1) * 128], identity=self.ident[:]),
                         reads=[bst, self.bconst], writes=[bps], inc=(j == 3))
                for j in range(4):
                    c = 4 * h + j
                    if LT_MODE == 0 or (LT_MODE == 2 and j % 2 == 1):
                        S.op('dve', lambda e, c=c, j=j, ps=ps: e.tensor_copy(out=dst[:, c, n * 128:(n + 1) * 128], in_=ps[:, j * 128:(j + 1) * 128]),
                             reads=[bps], writes=bdst_fn(h, n))
                    else:
                        S.op('act', lambda e, c=c, j=j, ps=ps: e.activation(out=dst[:, c, n * 128:(n + 1) * 128], in_=ps[:, j * 128:(j + 1) * 128], func=AF.Copy),
                             reads=[bps], writes=bdst_fn(h, n))

    def outproj(self, wap, r0, nk, mix, bmix):
        S = self.S
        for oc in range(NCH):
            wt, bw = self.wload(wap, r0, nk * 128, oc * 128, 128)
            for tq in range(TQ):
                ps, bps = self.ps()
                sl = slice(tq * 512, (tq + 1) * 512)
                self.mm_fm(ps[:], bps, wt, bw, slice(0, 128), lambda kc: mix[:, kc, sl], lambda kc: [bmix[kc][tq]], list(range(nk)))
                S.op('dve', lambda e, oc=oc, sl=sl, ps=ps: e.tensor_tensor(out=self.xT[:, oc, sl], in0=ps[:], in1=self.xT[:, oc, sl], op=ALU.add),
                     reads=[bps, self.bx[oc][tq]], writes=[self.bx[oc][tq]])
        if BARRIERS:
            S.barrier()

    def proj_fm(self, wap, c0, dst_fn, bdst_fn, func=AF.Copy, eng='act'):
        S = self.S
        wt, bw = self.wload(wap, 0, 1024, c0, 128)
        for tq in range(TQ):
            ps, bps = self.ps()
            sl = slice(tq * 512, (tq + 1) * 512)
            self.mm_fm(ps[:], bps, wt, bw, slice(0, 128), lambda kc: self.hnT[:, kc, sl], lambda kc: [self.bhn[kc][tq]], list(range(NCH)))
            if eng == 'act':
                S.op('act', lambda e, tq=tq, ps=ps: e.activation(out=dst_fn(tq), in_=ps[:], func=func), reads=[bps], writes=bdst_fn(tq))
            else:
                S.op('dve', lambda e, tq=tq, ps=ps: e.tensor_copy(out=dst_fn(tq), in_=ps[:]), reads=[bps], writes=bdst_fn(tq))

    def even_layer(self, li):
        S, d = self.S, self.d
        w_in = d['w_in_ab'][li]
        w_out = d['w_out_ab'][li]
        with ExitStack() as es:
            if 'attnA' in STAGES:
                mix = self.sb("mixA", [128, NCH, SEQ], BF16, es)
                bmix = self.grid(NCH, TQ)
                self.attnA(li, w_in, mix, bmix)
                self.outproj(w_out, 0, 8, mix, bmix)
        if 'poolB' not in STAGES:
            return
        with ExitStack() as es:
            mix = self.sb("mixB", [128, NCH, SEQ], BF16, es)
            bmix = self.grid(NCH, TQ)
            self.poolB(li, w_in, mix, bmix)
            self.outproj(w_out, 1024, 8, mix, bmix)

    def attnA(self, li, w_in, mix, bmix):
        S = self.S
        with ExitStack() as es:
            qT = self.sb("qT", [128, SEQ], BF16, es)
            kT = self.sb("kT", [128, SEQ], BF16, es)
            gT = self.sb("gT", [128, SEQ], BF16, es)
            bq, bk, bg = self.grid(TQ), self.grid(TQ), self.grid(TQ)
            Va = self.sb("Vaug", [128, 3, 16, 2, 128], BF16, es)
            bVa = self.grid(3, 4)
            bVones = Buf()
            S.op('pool', lambda e: e.memset(Va[:, :, :, 0, 64:128], 1.0), writes=[bVones])
            S.op('pool', lambda e: e.memset(Va[:, :, :, 1, 0:64], 1.0), writes=[bVones])
            pT = [self.sb(f"pT{i}", [128, 256], BF16, es) for i in range(4)]
            bpT = [Buf() for _ in range(4)]
            rden = self.sb("rden", [128, 512], F32, es)
            tmpn = self.sb("tmpn", [128, 512], F32, es)
            brden, btmpn = Buf(), Buf()
            pti = 0
            for c in range(NCH):
                self.ps_rot(range(8))
                self.proj_fm(w_in, c * 128, lambda tq: qT[:, tq * 512:(tq + 1) * 512], lambda tq: [bq[tq]])
                self.proj_fm(w_in, 1024 + c * 128, lambda tq: kT[:, tq * 512:(tq + 1) * 512], lambda tq: [bk[tq]], eng='dve')
                self.proj_fm(w_in, 3072 + c * 128, lambda tq: gT[:, tq * 512:(tq + 1) * 512], lambda tq: [bg[tq]], func=AF.Silu)
                wv, bwv = self.wload(w_in, 0, 1024, 2048 + c * 128, 128)
                for o, dd in enumerate((1, 4, 16)):
                    nb = 16 // dd
                    for t4 in range(4):
                        ps, bps = self.ps()
                        for j in range(4):
                            ti = 4 * t4 + j
                            r, kb = ti // nb, ti % nb
                            st = r + dd * 128 * kb
                            tok = slice(st, st + dd * 127 + 1, dd)
                            tqs = sorted(set([(st) // 512, (st + dd * 127) // 512])) if dd < 16 else [0, 1, 2, 3]
                            for kc in range(NCH):
                                S.op('pe', lambda e, kc=kc, tok=tok, j=j, ps=ps: e.matmul(ps[:, j * 128:(j + 1) * 128], lhsT=self.hnT[:, kc, tok], rhs=wv[:, kc, :],
                                                                                    start=(kc == 0), stop=(kc == NCH - 1)),
                                     reads=[bwv] + [self.bhn[kc][q_] for q_ in tqs], writes=[bps], inc=(kc == NCH - 1 and j == 3))
                        psv = ps[:].rearrange("p (j h d) -> p j h d", j=4, h=2)
                        S.op('act', lambda e, o=o, t4=t4, psv=psv: e.activation(out=Va[:, o, 4 * t4:4 * t4 + 4, 0, 0:64], in_=psv[:, :, 0, :], func=AF.Copy),
                             reads=[bps], writes=[bVa[o][t4]])
                        S.op('dve', lambda e, o=o, t4=t4, psv=psv: e.tensor_copy(out=Va[:, o, 4 * t4:4 * t4 + 4, 1, 64:128], in_=psv[:, :, 1, :]),
                             reads=[bps], writes=[bVa[o][t4]])
                for hh in range(2):
                    hp = slice(hh * 64, hh * 64 + 64)
                    dp = slice(64 - hh * 64, 128 - hh * 64)
                    nd = [self.P[i] for i in range(4)]
                    bnd = [self.bP[i] for i in range(4)]
                    started = [False] * 4
                    self.ps_rot(range(4, 8))
                    tiles = []
                    for o, dd in enumerate((1, 4, 16)):
                        nb = 16 // dd
                        for r in range(dd):
                            for kb in range(nb):
                                tiles.append((o, dd, nb, r, kb))
                    LA = 3
                    pend = {}

                    def issue_S(idx):
                        nonlocal pti
                        o, dd, nb, r, kb = tiles[idx]
                        nq = 2 if kb < nb - 1 else 1
                        N = 128 * nq
                        kst = r + dd * 128 * kb
                        ktok = slice(kst, kst + dd * 127 + 1, dd)
                        qtok = slice(kst, kst + dd * (N - 1) + 1, dd)
                        ktqs = [kst // 512] if dd < 16 else [0, 1, 2, 3]
                        qtqs = sorted(set([kst // 512, (kst + dd * (N - 1)) // 512])) if dd < 16 else [0, 1, 2, 3]
                        ps, bps = self.ps()
                        S.op('pe', lambda e: e.matmul(ps[:, 0:N], lhsT=kT[hp, ktok], rhs=qT[hp, qtok], start=True, stop=False),
                             reads=[bk[q_] for q_ in ktqs] + [bq[q_] for q_ in qtqs], writes=[bps], inc=False)
                        S.op('pe', lambda e: e.matmul(ps[:, 0:N], lhsT=self.identb[:], rhs=self.maskb[:, 0:N], start=False, stop=True),
                             reads=[self.bconst], writes=[bps], inc=True)
                        p_, bp_ = pT[pti], bpT[pti]
                        pti = (pti + 1) % len(pT)
                        S.op('act', lambda e: e.activation(out=p_[:, 0:N], in_=ps[:, 0:N], func=AF.Exp, scale=0.125),
                             reads=[bps], writes=[bp_])
                        pend[idx] = (p_, bp_, nq)

                    def issue_PV(idx):
                        o, dd, nb, r, kb = tiles[idx]
                        p_, bp_, nq = pend.pop(idx)
                        ti = r * nb + kb
                        lhsV = Va[:, o, ti, hh, :]
                        allouts = []
                        for qi in range(nq):
                            i = kb + qi
                            if dd == 1:
                                allouts += [(i // 4, slice((i % 4) * 128, (i % 4) * 128 + 128), slice(qi * 128, qi * 128 + 128))]
                            elif dd == 4:
                                allouts += [(i, slice(r, 512, 4), slice(qi * 128, qi * 128 + 128))]
                            else:
                                allouts += [(j, slice(r, 512, 16), slice(j * 32, j * 32 + 32)) for j in range(4)]
                        for n_, (bank, ocols, pcols) in enumerate(allouts):
                            first = not started[bank]
                            started[bank] = True
                            S.op('pe', lambda e: e.matmul(nd[bank][:, ocols], lhsT=lhsV, rhs=p_[:, pcols], start=first, stop=False, skip_group_check=True),
                                 reads=[bp_, bVa[o][ti // 4], bVones], writes=[bnd[bank]], inc=(n_ == len(allouts) - 1))

                    for idx in range(len(tiles) + LA):
                        if idx < len(tiles):
                            issue_S(idx)
                        if idx >= LA:
                            issue_PV(idx - LA)
                    for tq in range(TQ if 'norm' not in ATT_SKIP else 0):
                        sl = slice(tq * 512, (tq + 1) * 512)
                        S.op('dve', lambda e, tq=tq: e.tensor_copy(out=rden[hp, :], in_=nd[tq][dp, :]), reads=[bnd[tq]], writes=[brden])
                        S.op('dve', lambda e: e.reciprocal(out=rden[hp, :], in_=rden[hp, :]), reads=[brden], writes=[brden])
                        S.op('dve', lambda e, tq=tq: e.tensor_tensor(out=tmpn[hp, :], in0=nd[tq][hp, :], in1=rden[hp, :], op=ALU.mult),
                             reads=[bnd[tq], brden], writes=[btmpn])
                        S.op('pool', lambda e, sl=sl: e.tensor_tensor(out=mix[hp, c, sl], in0=tmpn[hp, :], in1=gT[hp, sl], op=ALU.mult),
                             reads=[btmpn, bg[tq]], writes=[bmix[c][tq]])
            self.ps_rot(range(8))

    def poolB(self, li, w_in, mix, bmix):
        S = self.S
        with ExitStack() as es:
            vb = self.sb("vb", [128, 16 + SEQ], F32, es)
            sA = self.sb("sA", [128, 16 + SEQ], F32, es)
            sB = self.sb("sB", [128, 16 + SEQ], F32, es)
            bvb, bsA, bsB = Buf(), Buf(), Buf()
            pooled = self.sb("pooled", [128, 2, SEQ], BF16, es)
            bpo = self.grid(2, TQ)
            gb = self.sb("gbT", [128, 2, SEQ], BF16, es)
            bgb = self.grid(2, TQ)
            t16 = self.sb("t16", [128, 16], F32, es)
            bt16 = Buf()
            for t_ in (vb, sA, sB):
                S.op('pool', lambda e, t_=t_: e.memset(t_[:, 0:16], 0.0), writes=[bvb, bsA, bsB])
            for g in range(4):
                w = (2, 4, 8, 16)[g]
                for j in range(2):
                    cb = 2 * g + j
                    self.proj_fm(w_in, 4096 + cb * 128, lambda tq: vb[:, 16 + tq * 512:16 + (tq + 1) * 512], lambda tq: [bvb])
                    self.proj_fm(w_in, 5120 + cb * 128, lambda tq, j=j: gb[:, j, tq * 512:(tq + 1) * 512], lambda tq, j=j: [bgb[j][tq]], func=AF.Silu)
                    cur, bcur = vb, bvb
                    k = 1
                    nxt = [(sA, bsA), (sB, bsB)]
                    ni = 0
                    while k < w:
                        o_, bo_ = nxt[ni]
                        ni = 1 - ni
                        S.op('pool', lambda e, o_=o_, cur=cur, k=k: e.tensor_tensor(out=o_[:, 16:16 + SEQ], in0=cur[:, 16:16 + SEQ], in1=cur[:, 16 - k:16 - k + SEQ], op=ALU.add),
                             reads=[bcur], writes=[bo_])
                        cur, bcur = o_, bo_
                        k *= 2
                    S.op('dve', lambda e, cur=cur, j=j, w=w: e.scalar_tensor_tensor(out=pooled[:, j, :], in0=cur[:, 16:16 + SEQ], scalar=1.0 / w, in1=vb[:, 16:16 + SEQ],
                                                                                 op0=ALU.mult, op1=ALU.subtract),
                         reads=[bcur, bvb], writes=[bpo[j][t] for t in range(TQ)])
                    S.op('dve', lambda e, cur=cur, g=g: e.tensor_tensor(out=t16[:], in0=cur[:, 16:32], in1=self.invcnt[:, g * 16:(g + 1) * 16], op=ALU.mult),
                         reads=[bcur, self.bconst], writes=[bt16])
                    S.op('dve', lambda e, j=j: e.tensor_tensor(out=pooled[:, j, 0:16], in0=t16[:], in1=vb[:, 16:32], op=ALU.subtract),
                         reads=[bt16, bvb], writes=[bpo[j][0]])
                wp, bwp = self.wload(self.d['pool_w'][li][g], 0, 256, 0, 256)
                for oc2 in range(2):
                    cb = 2 * g + oc2
                    for tq in range(TQ):
                        sl = slice(tq * 512, (tq + 1) * 512)
                        ps, bps = self.ps()
                        self.mm_fm(ps[:], bps, wp, bwp, slice(oc2 * 128, oc2 * 128 + 128), lambda kc: pooled[:, kc, sl], lambda kc: [bpo[kc][tq]], [0, 1])
                        S.op('dve', lambda e, ps=ps, cb=cb, oc2=oc2, sl=sl: e.scalar_tensor_tensor(out=mix[:, cb, sl], in0=ps[:], scalar=self.vec['pool_scale'][:, li, cb:cb + 1],
                                                                                           in1=gb[:, oc2, sl], op0=ALU.mult, op1=ALU.mult),
                             reads=[bps, bgb[oc2][tq], self.bconst], writes=[bmix[cb][tq]])

    def odd_layer(self, li):
        d = self.d
        if 's5D' in STAGES:
            self.s5D(li, d['w_in_cd'][li], d['w_out_cd'][li])
        if 'sguC' in STAGES:
            self.sguC(li, d['w_in_cd'][li], d['w_out_cd'][li])

    def trig(self, y, by, yi, byi, yf, byf, cs, bcs, sn, bsn, on_act=False):
        S = self.S
        if on_act:
            S.op('act', lambda e: e.activation(out=yf, in_=y, func=AF.Identity, bias=self.magic[:, 0:1], scale=1.0), reads=[by, self.bconst], writes=[byf])
            S.op('act', lambda e: e.activation(out=yf, in_=yf, func=AF.Identity, bias=self.magic[:, 1:2], scale=1.0), reads=[byf, self.bconst], writes=[byf])
        else:
            S.op('dve', lambda e: e.tensor_copy(out=yi, in_=y), reads=[by], writes=[byi])
            S.op('dve', lambda e: e.tensor_copy(out=yf, in_=yi), reads=[byi], writes=[byf])
        S.op('pool', lambda e: e.tensor_tensor(out=yf, in0=y, in1=yf, op=ALU.subtract), reads=[by, byf], writes=[byf])
        S.op('act', lambda e: e.activation(out=sn, in_=yf, func=AF.Sin, scale=TWO_PI), reads=[byf], writes=[bsn])
        S.op('act', lambda e: e.activation(out=y, in_=yf, func=AF.Abs), reads=[byf], writes=[by])
        S.op('act', lambda e: e.activation(out=cs, in_=y, func=AF.Sin, scale=-TWO_PI, bias=self.halfpi[:, 0:1]), reads=[by, self.bconst], writes=[bcs])

    def s5D(self, li, w_in, w_out):
        S, d = self.S, self.d
        with ExitStack() as es:
            sb = lambda n, sh, dt=F32, es_=es: self.sb(n, sh, dt, es_)
            xd = sb("xdT", [128, 4, SEQ], BF16)
            bxd = self.grid(4, TQ)
            yg = sb("ygT", [128, 4, SEQ], BF16)
            byg = self.grid(4, TQ)
            for kc in range(4):
                self.proj_fm(w_in, 3072 + kc * 128, lambda tq, kc=kc: xd[:, kc, tq * 512:(tq + 1) * 512], lambda tq, kc=kc: [bxd[kc][tq]])
            ar = sb("s_ar", [128, 16]); ai = sb("s_ai", [128, 16]); ldt = sb("s_ldt", [128, 16])
            bprm = Buf()
            pst = sb("s_pst", [32, 128]); ldt0 = sb("s_ldt0", [128, 32])
            bpst = Buf()
            S.dma(pst[:, 0:64], d['s5_a_re'][li], writes=[bpst])
            S.dma(pst[:, 64:128], d['s5_a_im'][li], writes=[bpst])
            S.dma(ldt0[:], d['s5_log_dt'][li].partition_broadcast(128), writes=[bpst])
            for (dst_, c0) in ((ar, 0), (ai, 64)):
                ps, bps = self.ps()
                S.op('pe', lambda e, ps=ps, c0=c0: e.transpose(out=ps[0:64, 0:32], in_=pst[:, c0:c0 + 64], identity=self.ident[0:32, 0:32]), reads=[bpst, self.bconst], writes=[bps])
                S.op('dve', lambda e, ps=ps, dst_=dst_: e.tensor_copy(out=dst_[0:64, :], in_=ps[0:64, 0:32:2]), reads=[bps], writes=[bprm])
                S.op('dve', lambda e, ps=ps, dst_=dst_: e.tensor_copy(out=dst_[64:128, :], in_=ps[0:64, 1:32:2]), reads=[bps], writes=[bprm])
            S.op('dve', lambda e: e.tensor_copy(out=ldt[0:64, :], in_=ldt0[0:64, 0:32:2]), reads=[bpst], writes=[bprm])
            S.op('dve', lambda e: e.tensor_copy(out=ldt[64:128, :], in_=ldt0[64:128, 1:32:2]), reads=[bpst], writes=[bprm])
            names = ["dt", "dtar", "th", "rho", "c0", "s0", "abr", "abi", "inv", "t1", "t2", "cfr", "cfi", "yf", "thn", "y0"]
            T = {n: sb("s_" + n, [128, 16]) for n in names}
            yi0 = sb("s_yi", [128, 16], I32)
            B = {n: Buf() for n in names + ["yi"]}

            def dv(out, fn, reads, eng='dve'):
                S.op(eng, fn, reads=[B[r] if isinstance(r, str) else r for r in reads], writes=[B[out]])
            TT = lambda o, a, b_, op: (lambda e: e.tensor_tensor(out=T[o][:], in0=a[:], in1=b_[:], op=op))
            dv("dt", lambda e: e.activation(out=T["dt"][:], in_=ldt[:], func=AF.Exp), [bprm], 'act')
            dv("dtar", TT("dtar", T["dt"], ar, ALU.mult), ["dt", bprm])
            dv("th", TT("th", T["dt"], ai, ALU.mult), ["dt", bprm])
            dv("rho", lambda e: e.activation(out=T["rho"][:], in_=T["dtar"][:], func=AF.Exp), ["dtar"], 'act')
            dv("thn", lambda e: e.tensor_single_scalar(out=T["thn"][:], in_=T["th"][:], scalar=1.0 / TWO_PI, op=ALU.mult), ["th"])
            dv("y0", lambda e: e.tensor_copy(out=T["y0"][:], in_=T["thn"][:]), ["thn"])
            self.trig(T["y0"][:], B["y0"], yi0[:], B["yi"], T["yf"][:], B["yf"], T["c0"][:], B["c0"], T["s0"][:], B["s0"])
            dv("abr", TT("abr", T["rho"], T["c0"], ALU.mult), ["rho", "c0"])
            dv("abi", TT("abi", T["rho"], T["s0"], ALU.mult), ["rho", "s0"])
            dv("abr", lambda e: e.tensor_single_scalar(out=T["abr"][:], in_=T["abr"][:], scalar=-1.0, op=ALU.add), ["abr"])
            dv("t1", TT("t1", ar, ar, ALU.mult), [bprm])
            dv("t2", TT("t2", ai, ai, ALU.mult), [bprm])
            dv("inv", TT("inv", T["t1"], T["t2"], ALU.add), ["t1", "t2"])
            dv("inv", lambda e: e.reciprocal(out=T["inv"][:], in_=T["inv"][:]), ["inv"])
            dv("t1", TT("t1", T["abr"], ar, ALU.mult), ["abr", bprm])
            dv("t2", TT("t2", T["abi"], ai, ALU.mult), ["abi", bprm])
            dv("cfr", TT("cfr", T["t1"], T["t2"], ALU.add), ["t1", "t2"])
            dv("cfr", TT("cfr", T["cfr"], T["inv"], ALU.mult), ["cfr", "inv"])
            dv("t1", TT("t1", T["abi"], ar, ALU.mult), ["abi", bprm])
            dv("t2", TT("t2", T["abr"], ai, ALU.mult), ["abr", bprm])
            dv("cfi", TT("cfi", T["t1"], T["t2"], ALU.subtract), ["t1", "t2"])
            dv("cfi", TT("cfi", T["cfi"], T["inv"], ALU.mult), ["cfi", "inv"])
            off = sb("s_off", [128, 16, 4])
            boff = Buf()
            for tq in range(TQ):
                S.op('dve', lambda e, tq=tq: e.tensor_single_scalar(out=off[:, :, tq], in_=T["thn"][:], scalar=512.0 * tq, op=ALU.mult), reads=[B["thn"]], writes=[boff])
            Bre = sb("s_Bre", [128, 16, 16]); Bim = sb("s_Bim", [128, 16, 16])
            bB = Buf()
            S.dma(Bre[:], d['s5_b_re'][li].rearrange("(gp g2) p h -> (g2 p) gp h", g2=2), writes=[bB])
            S.dma(Bim[:], d['s5_b_im'][li].rearrange("(gp g2) p h -> (g2 p) gp h", g2=2), writes=[bB])
            bbr = sb("s_bbr", [128, 16, 16]); bbi = sb("s_bbi", [128, 16, 16]); bt = sb("s_bt", [128, 16, 16])
            bbb, bbt = Buf(), Buf()
            cfr_b = T["cfr"][:, :, None].to_broadcast([128, 16, 16])
            cfi_b = T["cfi"][:, :, None].to_broadcast([128, 16, 16])
            S.op('dve', lambda e: e.tensor_tensor(out=bbr[:], in0=Bre[:], in1=cfr_b, op=ALU.mult), reads=[bB, B["cfr"]], writes=[bbb])
            S.op('dve', lambda e: e.tensor_tensor(out=bt[:], in0=Bim[:], in1=cfi_b, op=ALU.mult), reads=[bB, B["cfi"]], writes=[bbt])
            S.op('dve', lambda e: e.tensor_tensor(out=bbr[:], in0=bbr[:], in1=bt[:], op=ALU.subtract), reads=[bbb, bbt], writes=[bbb])
            S.op('dve', lambda e: e.tensor_tensor(out=bbi[:], in0=Bim[:], in1=cfr_b, op=ALU.mult), reads=[bB, B["cfr"]], writes=[bbb])
            S.op('dve', lambda e: e.tensor_tensor(out=bt[:], in0=Bre[:], in1=cfi_b, op=ALU.mult), reads=[bB, B["cfi"], bbb], writes=[bbt])
            S.op('dve', lambda e: e.tensor_tensor(out=bbi[:], in0=bbi[:], in1=bt[:], op=ALU.add), reads=[bbb, bbt], writes=[bbb])
            Cre = sb("s_Cre", [128, 4, 64]); Cim = sb("s_Cim", [128, 4, 64])
            bC = Buf()
            S.dma(Cre[:], d['s5_c_re'][li].rearrange("(kc gl) h p -> (gl h) kc p", kc=4), writes=[bC])
            S.dma(Cim[:], d['s5_c_im'][li].rearrange("(kc gl) h p -> (gl h) kc p", kc=4), writes=[bC])
            Dg = sb("s_Dg", [128, 4, 128], BF16)
            bDg = Buf()
            for kc in range(4):
                S.op('dve', lambda e, kc=kc: e.tensor_single_scalar(out=Dg[:, kc, :], in_=self.ident[:], scalar=self.vec['s5_d'][:, li, kc:kc + 1], op=ALU.mult),
                     reads=[self.bconst], writes=[bDg])
            bm1 = self.bm1[:].rearrange("p (a b) -> p a b", a=4)
            bm2 = self.bm2[:].rearrange("p (a b) -> p a b", a=4)
            with ExitStack() as es2:
                sb2 = lambda n, sh, dt=F32: self.sb(n, sh, dt, es2)
                L = sb2("s_L", [128, 2, 4, 128], BF16)
                Cc = sb2("s_Cc", [128, 2, 4, 128], BF16)
                bL, bCc = Buf(), Buf()
                Z = [sb2(f"s_Z{i}", [128, 128]) for i in range(2)]
                bZ = [Buf(), Buf()]
                zi = 0
                NW = 512
                wkA = {}
                for n in ("y", "yf", "cs", "sn", "br", "bi", "t1", "t2", "t3", "t4"):
                    wkA[n] = (sb2("k_" + n, [128, NW])[:], Buf())
                wkA["yi"] = (sb2("k_yi", [128, NW], I32)[:], Buf())
                wkB = {}
                for j, n in enumerate(("y", "yf", "cs", "sn")):
                    wkB[n] = (self.wst[0][:, j * NW:(j + 1) * NW], Buf())
                for j, n in enumerate(("br", "bi", "t1", "t2")):
                    wkB[n] = (self.wst[1][:, j * NW:(j + 1) * NW], Buf())
                wb0f = self.wbf[0][:].bitcast(F32)
                for j, n in enumerate(("t3", "t4")):
                    wkB[n] = (wb0f[:, j * NW:(j + 1) * NW], Buf())
                wkB["yi"] = (self.wbf[1][:].bitcast(I32)[:, 0:NW], Buf())
                wks = [wkA, wkB]
                cur = [0]
                self.wfence()
                S.barrier()
                hrb = [sb2(f"k_hr{i}", [128, NW], BF16) for i in range(2)]
                hib = [sb2(f"k_hi{i}", [128, NW], BF16) for i in range(2)]
                bhr, bhi = [Buf(), Buf()], [Buf(), Buf()]
                car = sb2("k_car", [128, 16, 2])
                bcar = [[Buf(), Buf()] for _ in range(16)]
                hidx = 0
                W = lambda n: wks[cur[0]][n][0]
                Bk = lambda n: wks[cur[0]][n][1]
                for kc in range(4):
                    self.ps_rot(range(2, 8))
                    for gpl in range(4):
                        gp = 4 * kc + gpl
                        for ri, src in enumerate((bbr, bbi)):
                            z, bz = Z[zi], bZ[zi]
                            zi = 1 - zi
                            S.op('dve', lambda e, z=z, src=src, gp=gp, gpl=gpl: e.tensor_tensor(
                                out=z[:].rearrange("p (a b) -> p a b", a=8), in0=src[:, gp, None, :].to_broadcast([128, 8, 16]),
                                in1=bm1[:, gpl, :, None].to_broadcast([128, 8, 16]), op=ALU.mult), reads=[bbb, self.bconst], writes=[bz])
                            ps, bps = self.ps()
                            S.op('pe', lambda e, ps=ps, z=z: e.transpose(out=ps[:, 0:128], in_=z[:], identity=self.ident[:]), reads=[bz, self.bconst], writes=[bps])
                            S.op('act', lambda e, ps=ps, ri=ri, gpl=gpl: e.activation(out=L[:, ri, gpl, :], in_=ps[:, 0:128], func=AF.Copy), reads=[bps], writes=[bL])
                        for ri, src in enumerate((Cre, Cim)):
                            z, bz = Z[zi], bZ[zi]
                            zi = 1 - zi
                            S.op('dve', lambda e, z=z, src=src, kc=kc, gpl=gpl: e.tensor_tensor(
                                out=z[:].rearrange("p (a b) -> p a b", a=2), in0=src[:, kc, None, :].to_broadcast([128, 2, 64]),
                                in1=bm2[:, gpl, :, None].to_broadcast([128, 2, 64]), op=ALU.mult), reads=[bC, self.bconst], writes=[bz])
                            ps, bps = self.ps()
                            S.op('pe', lambda e, ps=ps, z=z: e.transpose(out=ps[:, 0:128], in_=z[:], identity=self.ident[:]), reads=[bz, self.bconst], writes=[bps])
                            S.op('dve', lambda e, ps=ps, ri=ri, gpl=gpl: e.tensor_single_scalar(out=Cc[:, ri, gpl, :], in_=ps[:, 0:128], scalar=(1.0 if ri == 0 else -1.0), op=ALU.mult),
                                 reads=[bps], writes=[bCc])
                    for tq in range(TQ):
                        sl = slice(tq * 512, (tq + 1) * 512)
                        psy, bpsy = self.P[tq % 2], self.bP[tq % 2]
                        for gpl in range(4):
                            gp = 4 * kc + gpl
                            cur[0] = 1 - cur[0]
                            pr, bpr = self.ps()
                            pi_, bpi = self.ps()
                            S.op('pe', lambda e, pr=pr, gpl=gpl, kc=kc, sl=sl: e.matmul(pr[:], lhsT=L[:, 0, gpl, :], rhs=xd[:, kc, sl], start=True, stop=True),
                                 reads=[bL, bxd[kc][tq]], writes=[bpr])
                            S.op('pe', lambda e, pi_=pi_, gpl=gpl, kc=kc, sl=sl: e.matmul(pi_[:], lhsT=L[:, 1, gpl, :], rhs=xd[:, kc, sl], start=True, stop=True),
                                 reads=[bL, bxd[kc][tq]], writes=[bpi])
                            S.op('act', lambda e, pr=pr: e.activation(out=W("br"), in_=pr[:], func=AF.Copy), reads=[bpr], writes=[Bk("br")])
                            S.op('act', lambda e, pi_=pi_: e.activation(out=W("bi"), in_=pi_[:], func=AF.Copy), reads=[bpi], writes=[Bk("bi")])
                            S.op('act', lambda e, gp=gp, tq=tq: e.activation(out=W("y"), in_=self.iota[:], func=AF.Identity, scale=T["thn"][:, gp:gp + 1], bias=off[:, gp, tq:tq + 1]),
                                 reads=[self.bconst, B["thn"], boff], writes=[Bk("y")])
                            self.trig(W("y"), Bk("y"), W("yi"), Bk("yi"), W("yf"), Bk("yf"), W("cs"), Bk("cs"), W("sn"), Bk("sn"), on_act=S5_ACT)
                            S.op('dve', lambda e: e.tensor_tensor(out=W("t1"), in0=W("br"), in1=W("cs"), op=ALU.mult), reads=[Bk("br"), Bk("cs")], writes=[Bk("t1")])
                            S.op('dve', lambda e: e.tensor_tensor(out=W("t2"), in0=W("bi"), in1=W("sn"), op=ALU.mult), reads=[Bk("bi"), Bk("sn")], writes=[Bk("t2")])
                            S.op('dve', lambda e: e.tensor_tensor(out=W("t1"), in0=W("t1"), in1=W("t2"), op=ALU.add), reads=[Bk("t1"), Bk("t2")], writes=[Bk("t1")])
                            S.op('pool', lambda e: e.tensor_tensor(out=W("t3"), in0=W("bi"), in1=W("cs"), op=ALU.mult), reads=[Bk("bi"), Bk("cs")], writes=[Bk("t3")])
                            S.op('pool', lambda e: e.tensor_tensor(out=W("t4"), in0=W("br"), in1=W("sn"), op=ALU.mult), reads=[Bk("br"), Bk("sn")], writes=[Bk("t4")])
                            S.op('pool', lambda e: e.tensor_tensor(out=W("t3"), in0=W("t3"), in1=W("t4"), op=ALU.subtract), reads=[Bk("t3"), Bk("t4")], writes=[Bk("t3")])
                            rho_b = T["rho"][:, gp:gp + 1].to_broadcast([128, NW])
                            for (src, dst, ci) in (("t1", "br", 0), ("t3", "bi", 1)):
                                init = 0.0 if tq == 0 else car[:, gp, ci:ci + 1]
                                S.op('dve', lambda e, src=src, dst=dst, init=init, rho_b=rho_b: e.tensor_tensor_scan(out=W(dst), data0=rho_b, data1=W(src), initial=init,
                                                                                                         op0=ALU.mult, op1=ALU.add),
                                     reads=[Bk(src), B["rho"], bcar[gp][ci]], writes=[Bk(dst)])
                                S.op('act', lambda e, dst=dst, gp=gp, ci=ci: e.activation(out=car[:, gp, ci:ci + 1], in_=W(dst)[:, NW - 1:NW], func=AF.Copy),
                                     reads=[Bk(dst)], writes=[bcar[gp][ci]])
                            hr, hi_, bhr_, bhi_ = hrb[hidx], hib[hidx], bhr[hidx], bhi[hidx]
                            hidx = 1 - hidx
                            S.op('dve', lambda e: e.tensor_tensor(out=W("t1"), in0=W("br"), in1=W("cs"), op=ALU.mult), reads=[Bk("br"), Bk("cs")], writes=[Bk("t1")])
                            S.op('dve', lambda e: e.tensor_tensor(out=W("t2"), in0=W("bi"), in1=W("sn"), op=ALU.mult), reads=[Bk("bi"), Bk("sn")], writes=[Bk("t2")])
                            S.op('dve', lambda e, hr=hr: e.tensor_tensor(out=hr[:], in0=W("t1"), in1=W("t2"), op=ALU.subtract), reads=[Bk("t1"), Bk("t2")], writes=[bhr_])
                            S.op('pool', lambda e: e.tensor_tensor(out=W("t3"), in0=W("bi"), in1=W("cs"), op=ALU.mult), reads=[Bk("bi"), Bk("cs")], writes=[Bk("t3")])
                            S.op('pool', lambda e: e.tensor_tensor(out=W("t4"), in0=W("br"), in1=W("sn"), op=ALU.mult), reads=[Bk("br"), Bk("sn")], writes=[Bk("t4")])
                            S.op('pool', lambda e, hi_=hi_: e.tensor_tensor(out=hi_[:], in0=W("t3"), in1=W("t4"), op=ALU.add), reads=[Bk("t3"), Bk("t4")], writes=[bhi_])
                            S.op('pe', lambda e, psy=psy, gpl=gpl, hr=hr: e.matmul(psy[:], lhsT=Cc[:, 0, gpl, :], rhs=hr[:], start=(gpl == 0), stop=False),
                                 reads=[bCc, bhr_], writes=[bpsy])
                            S.op('pe', lambda e, psy=psy, gpl=gpl, hi_=hi_: e.matmul(psy[:], lhsT=Cc[:, 1, gpl, :], rhs=hi_[:], start=False, stop=False),
                                 reads=[bCc, bhi_], writes=[bpsy])
                        S.op('pe', lambda e, psy=psy, kc=kc, sl=sl: e.matmul(psy[:], lhsT=Dg[:, kc, :], rhs=xd[:, kc, sl], start=False, stop=True),
                             reads=[bDg, bxd[kc][tq]], writes=[bpsy])
                        S.op('act', lambda e, psy=psy: e.activation(out=W("t4"), in_=psy[:], func=AF.Copy), reads=[bpsy], writes=[Bk("t4")])
                        S.op('dve', lambda e: e.tensor_tensor(out=W("t2"), in0=W("t4"), in1=W("t4"), op=ALU.mult), reads=[Bk("t4")], writes=[Bk("t2")])
                        S.op('dve', lambda e: e.tensor_scalar(out=W("t2"), in0=W("t2"), scalar1=0.044715, scalar2=1.0, op0=ALU.mult, op1=ALU.add), reads=[Bk("t2")], writes=[Bk("t2")])
                        S.op('dve', lambda e: e.tensor_tensor(out=W("t2"), in0=W("t2"), in1=W("t4"), op=ALU.mult), reads=[Bk("t2"), Bk("t4")], writes=[Bk("t2")])
                        S.op('act', lambda e: e.activation(out=W("t1"), in_=W("t2"), func=AF.Sigmoid, scale=1.5957691216057308), reads=[Bk("t2")], writes=[Bk("t1")])
                        S.op('dve', lambda e, kc=kc, sl=sl: e.tensor_tensor(out=yg[:, kc, sl], in0=W("t1"), in1=W("t4"), op=ALU.mult),
                             reads=[Bk("t1"), Bk("t4")], writes=[byg[kc][tq]])
                self.ps_rot(range(8))
            S.barrier()
            with ExitStack() as es3:
                sb3 = lambda n, sh, dt=F32: self.sb(n, sh, dt, es3)
                gd = sb3("gdT", [128, SEQ], BF16)
                bgd = self.grid(TQ)
                s2 = sb3("glu_s2", [128, 512]); t2_ = sb3("glu_t", [128, 512])
                bs2, bt2 = Buf(), Buf()
                for oc in range(4):
                    self.proj_fm(w_in, 3584 + oc * 128, lambda tq: gd[:, tq * 512:(tq + 1) * 512], lambda tq: [bgd[tq]], func=AF.Silu)
                    w1, bw1 = self.wload(d['glu_w1'][li], 0, 512, oc * 128, 128)
                    w2, bw2 = self.wload(d['glu_w2'][li], 0, 512, oc * 128, 128, prefetch=False)
                    for tq in range(TQ):
                        sl = slice(tq * 512, (tq + 1) * 512)
                        p1, bp1 = self.ps()
                        p2, bp2 = self.ps()
                        self.mm_fm(p1[:], bp1, w1, bw1, slice(0, 128), lambda kc: yg[:, kc, sl], lambda kc: [byg[kc][tq]], [0, 1, 2, 3])
                        self.mm_fm(p2[:], bp2, w2, bw2, slice(0, 128), lambda kc: yg[:, kc, sl], lambda kc: [byg[kc][tq]], [0, 1, 2, 3])
                        S.op('act', lambda e, p2=p2: e.activation(out=s2[:], in_=p2[:], func=AF.Sigmoid), reads=[bp2], writes=[bs2])
                        S.op('dve', lambda e, p1=p1: e.tensor_tensor(out=t2_[:], in0=p1[:], in1=s2[:], op=ALU.mult), reads=[bp1, bs2], writes=[bt2])
                        S.op('pool', lambda e, oc=oc, sl=sl: e.tensor_tensor(out=xd[:, oc, sl], in0=t2_[:], in1=gd[:, sl], op=ALU.mult),
                             reads=[bt2, bgd[tq]], writes=[bxd[oc][tq]])
            self.outproj(w_out, 1024, 4, xd, bxd)

    def sguC(self, li, w_in, w_out):
        S, d = self.S, self.d
        with ExitStack() as es:
            sb = lambda n, sh, dt=F32, es_=es: self.sb(n, sh, dt, es_)
            vn = sb("vn", [128, 16, DM], BF16)
            bvn = self.grid(16)
            with ExitStack() as es2:
                sb2 = lambda n, sh, dt=F32: self.sb(n, sh, dt, es2)
                ssum = sb2("c_ssum", [128, 16, 4]); ssq = sb2("c_ssq", [128, 16, 4])
                bst = Buf()
                junk = sb2("c_junk", [128, 256], BF16)
                bjunk = Buf()
                S.op('dve', lambda e: e.memset(ssum[:], 0.0), writes=[bst])
                S.op('dve', lambda e: e.memset(ssq[:], 0.0), writes=[bst])
                for q in range(4):
                    wt, bw = self.wload(w_in, 0, 1024, 1024 + q * 256, 256)
                    for n in range(16):
                        ps, bps = self.ps()
                        tok = slice(n * 128, (n + 1) * 128)
                        for kc in range(NCH):
                            S.op('pe', lambda e, ps=ps, kc=kc, tok=tok, wt=wt: e.matmul(ps[:, 0:256], lhsT=self.hnT[:, kc, tok], rhs=wt[:, kc, :], start=(kc == 0), stop=(kc == NCH - 1)),
                                 reads=[bw, self.bhn[kc][n // 4]], writes=[bps], inc=(kc == NCH - 1))
                        S.op('act', lambda e, ps=ps, n=n, q=q: e.activation(out=vn[:, n, q * 256:(q + 1) * 256], in_=ps[:, 0:256], func=AF.Copy, accum_out=ssum[:, n, q:q + 1]),
                             reads=[bps, bst], writes=[bvn[n], bst])
                        S.op('act', lambda e, ps=ps, n=n, q=q: e.activation(out=junk[:], in_=ps[:, 0:256], func=AF.Square, accum_out=ssq[:, n, q:q + 1]),
                             reads=[bps, bst], writes=[bjunk, bst])
                mean = sb2("c_mean", [128, 16]); var = sb2("c_var", [128, 16]); m2 = sb2("c_m2", [128, 16])
                S.op('dve', lambda e: e.tensor_tensor(out=ssum[:, :, 0:2], in0=ssum[:, :, 0:2], in1=ssum[:, :, 2:4], op=ALU.add), reads=[bst], writes=[bst])
                S.op('dve', lambda e: e.tensor_tensor(out=mean[:], in0=ssum[:, :, 0], in1=ssum[:, :, 1], op=ALU.add), reads=[bst], writes=[bst])
                S.op('dve', lambda e: e.tensor_single_scalar(out=mean[:], in_=mean[:], scalar=1.0 / DM, op=ALU.mult), reads=[bst], writes=[bst])
                S.op('dve', lambda e: e.tensor_tensor(out=ssq[:, :, 0:2], in0=ssq[:, :, 0:2], in1=ssq[:, :, 2:4], op=ALU.add), reads=[bst], writes=[bst])
                S.op('dve', lambda e: e.tensor_tensor(out=var[:], in0=ssq[:, :, 0], in1=ssq[:, :, 1], op=ALU.add), reads=[bst], writes=[bst])
                S.op('dve', lambda e: e.tensor_tensor(out=m2[:], in0=mean[:], in1=mean[:], op=ALU.mult), reads=[bst], writes=[bst])
                S.op('dve', lambda e: e.scalar_tensor_tensor(out=var[:], in0=var[:], scalar=1.0 / DM, in1=m2[:], op0=ALU.mult, op1=ALU.subtract), reads=[bst], writes=[bst])
                S.op('dve', lambda e: e.tensor_single_scalar(out=var[:], in_=var[:], scalar=EPS, op=ALU.add), reads=[bst], writes=[bst])
                S.op('act', lambda e: e.activation(out=var[:], in_=var[:], func=AF.Ln), reads=[bst], writes=[bst])
                S.op('act', lambda e: e.activation(out=var[:], in_=var[:], func=AF.Exp, scale=-0.5), reads=[bst], writes=[bst])
                lng = sb2("c_lng", [128, DM]); lnb = sb2("c_lnb", [128, DM])
                bln = Buf()
                S.dma(lng[:], d['sgu_ln_g'][li].partition_broadcast(128), writes=[bln])
                S.dma(lnb[:], d['sgu_ln_b'][li].partition_broadcast(128), writes=[bln])
                tmp = [sb2(f"c_tmp{i}", [128, DM]) for i in range(2)]
                btmp = [Buf(), Buf()]
                for n in range(16):
                    t_, bt_ = tmp[n % 2], btmp[n % 2]
                    S.op('dve', lambda e, n=n, t_=t_: e.tensor_scalar(out=t_[:], in0=vn[:, n, :], scalar1=mean[:, n:n + 1], scalar2=var[:, n:n + 1], op0=ALU.subtract, op1=ALU.mult),
                         reads=[bvn[n], bst], writes=[bt_])
                    S.op('pool', lambda e, t_=t_: e.tensor_tensor(out=t_[:], in0=t_[:], in1=lng[:], op=ALU.mult), reads=[bt_, bln], writes=[bt_])
                    S.op('pool', lambda e, n=n, t_=t_: e.tensor_tensor(out=vn[:, n, :], in0=t_[:], in1=lnb[:], op=ALU.add), reads=[bt_, bln], writes=[bvn[n]])
            S.barrier()
            mix = sb("mixC", [128, NCH, SEQ], BF16)
            bmix = self.grid(NCH, TQ)
            wsT = sb("c_wsT", [128, 4, 128], BF16)
            bws = Buf()
            bsb = sb("c_bsb", [128, 4, 128])
            bbs = Buf()
            for g in range(4):
                i = g % 2
                st, bst_ = self.wst[i], self.bwst[i]
                S.dma(st[:, 0:128], d['sgu_w'][li][g], writes=[bst_])
                ps, bps = self.ps()
                S.op('pe', lambda e, ps=ps, st=st: e.transpose(out=ps[:, 0:128], in_=st[:, 0:128], identity=self.ident[:]), reads=[bst_, self.bconst], writes=[bps])
                S.op('dve', lambda e, ps=ps, g=g: e.tensor_tensor(out=wsT[:, g, :], in0=ps[:, 0:128], in1=self.triu[:], op=ALU.mult), reads=[bps, self.bconst], writes=[bws])
                S.dma(bsb[:, g, :], d['sgu_b'][li][g].partition_broadcast(128), writes=[bbs])
            ta = [sb(f"c_ta{i}", [128, 512]) for i in range(2)]
            bta = [Buf(), Buf()]
            sg = [sb(f"c_sg{i}", [128, 512]) for i in range(2)]
            bsg = [Buf(), Buf()]
            k = 0
            for c in range(NCH):
                g = c // 2
                wu, bwu = self.wload(w_in, 0, 1024, c * 128, 128)
                wg, bwg = self.wload(w_in, 0, 1024, 2048 + c * 128, 128, prefetch=False)
                for tq in range(TQ):
                    sl = slice(tq * 512, (tq + 1) * 512)
                    a_, ba_, s_, bs_ = ta[k], bta[k], sg[k], bsg[k]
                    k = 1 - k
                    ps, bps = self.ps()
                    for j in range(4):
                        n = 4 * tq + j
                        S.op('pe', lambda e, ps=ps, j=j, n=n, c=c, g=g: e.matmul(ps[:, j * 128:(j + 1) * 128], lhsT=vn[:, n, c * 128:(c + 1) * 128], rhs=wsT[:, g, :], start=True, stop=True),
                             reads=[bvn[n], bws], writes=[bps], inc=(j == 3))
                    S.op('dve', lambda e, ps=ps, g=g, a_=a_: e.tensor_tensor(out=a_[:].rearrange("p (a b) -> p a b", a=4), in0=ps[:].rearrange("p (a b) -> p a b", a=4),
                                                                        in1=bsb[:, g, None, :].to_broadcast([128, 4, 128]), op=ALU.add), reads=[bps, bbs], writes=[ba_])
                    pu, bpu = self.ps()
                    self.mm_fm(pu[:], bpu, wu, bwu, slice(0, 128), lambda kc: self.hnT[:, kc, sl], lambda kc: [self.bhn[kc][tq]], list(range(NCH)))
                    S.op('dve', lambda e, pu=pu, a_=a_: e.tensor_tensor(out=a_[:], in0=pu[:], in1=a_[:], op=ALU.mult), reads=[bpu, ba_], writes=[ba_])
                    pg, bpg = self.ps()
                    self.mm_fm(pg[:], bpg, wg, bwg, slice(0, 128), lambda kc: self.hnT[:, kc, sl], lambda kc: [self.bhn[kc][tq]], list(range(NCH)))
                    S.op('act', lambda e, pg=pg, s_=s_: e.activation(out=s_[:], in_=pg[:], func=AF.Silu), reads=[bpg], writes=[bs_])
                    S.op('pool', lambda e, a_=a_, s_=s_, sl=sl, c=c: e.tensor_tensor(out=mix[:, c, sl], in0=a_[:], in1=s_[:], op=ALU.mult), reads=[ba_, bs_], writes=[bmix[c][tq]])
            self.outproj(w_out, 0, 8, mix, bmix)

    def cross(self, l):
        S, d = self.S, self.d
        with ExitStack() as es:
            sb = lambda n, sh, dt=F32: self.sb(n, sh, dt, es)
            with ExitStack() as es2:
                self.rmsnorm(self.xT, self.bx, SEQ, self.vec['norm_x'][:, l, :], self.hnT, self.bhn, es2)
            S.barrier()
            qT = sb("xqT", [128, 2, SEQ], BF16)
            bq = self.grid(2, TQ)
            mix = sb("mixX", [128, NCH, SEQ], BF16)
            bmix = self.grid(NCH, TQ)
            KT = sb("xKT", [128, NCH, 256], BF16)
            bKT = self.grid(NCH)
            Vx = sb("xV", [128, 2, DM], BF16)
            bVx = self.grid(2)
            pT = [sb(f"xpT{i}", [128, 2, 512], BF16) for i in range(2)]
            bpT = [Buf(), Buf()]
            rden = sb("xrden", [128, 512])
            brden = Buf()
            wkv = d['w_xkv'][l]
            for c in range(NCH):
                wt, bw = self.wload(wkv, 0, 1024, c * 128, 128)
                ps, bps = self.ps()
                self.mm_fm(ps[:, 0:256], bps, wt, bw, slice(0, 128), lambda kc: self.memT[:, kc, :], lambda kc: [self.bmem], list(range(NCH)))
                S.op('act', lambda e, ps=ps, c=c: e.activation(out=KT[:, c, :], in_=ps[:, 0:256], func=AF.Copy), reads=[bps], writes=[bKT[c]])
            for q in range(4):
                wt, bw = self.wload(wkv, 0, 1024, 1024 + q * 256, 256)
                for mt in range(2):
                    ps, bps = self.ps()
                    for kc in range(NCH):
                        S.op('pe', lambda e, ps=ps, kc=kc, mt=mt, wt=wt: e.matmul(ps[:, 0:256], lhsT=self.memT[:, kc, mt * 128:(mt + 1) * 128], rhs=wt[:, kc, :], start=(kc == 0), stop=(kc == NCH - 1)),
                             reads=[bw, self.bmem], writes=[bps], inc=(kc == NCH - 1))
                    S.op('act', lambda e, ps=ps, mt=mt, q=q: e.activation(out=Vx[:, mt, q * 256:(q + 1) * 256], in_=ps[:, 0:256], func=AF.Copy), reads=[bps], writes=[bVx[mt]])
            pi = 0
            for h in range(4):
                for k2 in range(2):
                    self.proj_fm(d['w_xq'][l], (2 * h + k2) * 128, lambda tq, k2=k2: qT[:, k2, tq * 512:(tq + 1) * 512], lambda tq, k2=k2: [bq[k2][tq]], eng=('act' if k2 == 0 else 'dve'))
                for tq in range(TQ):
                    sl = slice(tq * 512, (tq + 1) * 512)
                    p_, bp_ = pT[pi], bpT[pi]
                    pi = 1 - pi
                    for mt in range(2):
                        ps, bps = self.ps()
                        for k2 in range(2):
                            cc = 2 * h + k2
                            S.op('pe', lambda e, ps=ps, cc=cc, mt=mt, k2=k2, sl=sl: e.matmul(ps[:], lhsT=KT[:, cc, mt * 128:(mt + 1) * 128], rhs=qT[:, k2, sl], start=(k2 == 0), stop=(k2 == 1)),
                                 reads=[bKT[cc], bq[k2][tq]], writes=[bps], inc=(k2 == 1))
                        S.op('act', lambda e, ps=ps, mt=mt, p_=p_: e.activation(out=p_[:, mt, :], in_=ps[:], func=AF.Exp, scale=1.0 / 16.0), reads=[bps], writes=[bp_])
                    psd, bpsd = self.ps()
                    for mt in range(2):
                        S.op('pe', lambda e, psd=psd, mt=mt, p_=p_: e.matmul(psd[:], lhsT=self.onesb[:], rhs=p_[:, mt, :], start=(mt == 0), stop=(mt == 1)),
                             reads=[bp_, self.bconst], writes=[bpsd], inc=(mt == 1))
                    S.op('dve', lambda e, psd=psd: e.reciprocal(out=rden[:], in_=psd[:]), reads=[bpsd], writes=[brden])
                    for dc in range(2):
                        cc = 2 * h + dc
                        pso, bpso = self.ps()
                        for mt in range(2):
                            S.op('pe', lambda e, pso=pso, mt=mt, p_=p_, cc=cc: e.matmul(pso[:], lhsT=Vx[:, mt, cc * 128:(cc + 1) * 128], rhs=p_[:, mt, :], start=(mt == 0), stop=(mt == 1)),
                                 reads=[bp_, bVx[mt]], writes=[bpso], inc=(mt == 1))
                        S.op('dve', lambda e, pso=pso, cc=cc, sl=sl: e.tensor_tensor(out=mix[:, cc, sl], in0=pso[:], in1=rden[:], op=ALU.mult),
                             reads=[bpso, brden], writes=[bmix[cc][tq]])
            self.outproj(d['w_xo'][l], 0, 8, mix, bmix)

    def final(self):
        S, d = self.S, self.d
        with ExitStack() as es:
            sb = lambda n, sh, dt=F32: self.sb(n, sh, dt, es)
            blk = 512
            sq = [sb(f"f_sq{i}", [128, blk], BF16) for i in range(2)]
            bsq = [Buf(), Buf()]
            rs = sb("f_rs", [128, blk])
            brs = Buf()
            nrm = [sb(f"f_n{i}", [128, blk]) for i in range(2)]
            bnrm = [Buf(), Buf()]
            gcol = self.vec['final_norm'][:, 0, :]
            ost = [sb(f"f_o{i}", [128, DM]) for i in range(2)]
            bost = [Buf(), Buf()]
            for tq in range(TQ):
                sl = slice(tq * blk, (tq + 1) * blk)
                ps, bps = self.ps()
                for c in range(NCH):
                    j = c % 2
                    S.op('act', lambda e, c=c, j=j: e.activation(out=sq[j][:], in_=self.xT[:, c, sl], func=AF.Square), reads=[self.bx[c][tq]], writes=[bsq[j]])
                    S.op('pe', lambda e, c=c, j=j, ps=ps: e.matmul(ps[:], lhsT=self.onesb[:], rhs=sq[j][:], start=(c == 0), stop=(c == NCH - 1)),
                         reads=[bsq[j], self.bconst], writes=[bps], inc=True)
                S.op('dve', lambda e, ps=ps: e.tensor_scalar(out=rs[:], in0=ps[:], scalar1=1.0 / DM, scalar2=EPS, op0=ALU.mult, op1=ALU.add), reads=[bps], writes=[brs])
                S.op('act', lambda e: e.activation(out=rs[:], in_=rs[:], func=AF.Ln), reads=[brs], writes=[brs])
                S.op('act', lambda e: e.activation(out=rs[:], in_=rs[:], func=AF.Exp, scale=-0.5), reads=[brs], writes=[brs])
                pts = [self.ps() for _ in range(8)]
                for c in range(NCH):
                    n_, bn_ = nrm[c % 2], bnrm[c % 2]
                    S.op('dve', lambda e, c=c, n_=n_: e.scalar_tensor_tensor(out=n_[:], in0=self.xT[:, c, sl], scalar=gcol[:, c:c + 1], in1=rs[:], op0=ALU.mult, op1=ALU.mult),
                         reads=[self.bx[c][tq], brs, self.bconst], writes=[bn_])
                    for j in range(4):
                        pp, bpp = pts[2 * j + c // 4]
                        S.op('pe', lambda e, pp=pp, j=j, c=c, n_=n_: e.transpose(out=pp[:, (c % 4) * 128:(c % 4) * 128 + 128], in_=n_[:, j * 128:(j + 1) * 128], identity=self.ident[:]),
                             reads=[bn_, self.bconst], writes=[bpp], inc=True)
                for j in range(4):
                    n = 4 * tq + j
                    o_, bo_ = ost[n % 2], bost[n % 2]
                    for h in range(2):
                        pp, bpp = pts[2 * j + h]
                        if h == 0:
                            S.op('act', lambda e, pp=pp, o_=o_: e.activation(out=o_[:, 0:512], in_=pp[:], func=AF.Copy), reads=[bpp], writes=[bo_])
                        else:
                            S.op('dve', lambda e, pp=pp, o_=o_: e.tensor_copy(out=o_[:, 512:1024], in_=pp[:]), reads=[bpp], writes=[bo_])
                    S.dma(d['out'][n * 128:(n + 1) * 128, :], o_[:], reads=[bo_])

    def run(self):
        S, d = self.S, self.d
        self.setup()
        if CUT == 1:
            return
        self.load_T(d['x'], 16, self.xT, lambda h, n: [self.bx[c][n // 4] for c in range(4 * h, 4 * h + 4)])
        if CUT == 2:
            return
        with ExitStack() as es:
            mraw = self.sb("mraw", [128, NCH, 256], F32, es)
            bmr = [[Buf()] for _ in range(NCH)]
            self.load_T(d['mem'], 2, mraw, lambda h, n: [bmr[c][0] for c in range(4 * h, 4 * h + 4)])
            bm = [[self.bmem] for _ in range(NCH)]
            self.rmsnorm(mraw, bmr, 256, self.vec['mem_norm'][:, 0, :], self.memT, bm, es)
        S.barrier()
        if CUT == 3:
            return
        for layer in range(self.depth):
            i = layer // 2
            with ExitStack() as es:
                gname = 'norm_ab' if layer % 2 == 0 else 'norm_cd'
                self.rmsnorm(self.xT, self.bx, SEQ, self.vec[gname][:, i, :], self.hnT, self.bhn, es)
            S.barrier()
            if layer % 2 == 0:
                self.even_layer(i)
            else:
                self.odd_layer(i)
            S.barrier()
            if 'cross' in STAGES:
                self.cross(layer)
            S.barrier()
        self.final()


import os
_CACHE = {}
S5_ACT = int(os.environ.get('S5_ACT', '1'))
BARRIERS = int(os.environ.get('BARRIERS', '1'))
ATT_SKIP = set(os.environ.get('ATT_SKIP', '').split(','))
LT_MODE = 0
CUT = 0
STAGES = {'attnA', 'poolB', 'cross', 's5D', 'sguC'}


def build_nc(depth=4):
    key = (depth, tuple(sorted(STAGES)))
    if key in _CACHE:
        return _CACHE[key]
    plan = None
    for pass_ in range(2):
        nc = bass.Bass("TRN2", target_bir_lowering=False)
        with ExitStack() as es:
            S = Sched(nc, es)
            K = Kern(nc, S, es, depth, wplan=plan)
            K.run()
            if pass_ == 0:
                plan = [(k, sp) for k, sp in K.wrec]
                continue
            S.emit()
    _CACHE[key] = nc
    return nc


def kernel(**inputs):
    n = 8
    nc = build_nc(4)
    consts = host_consts()
    x = np.ascontiguousarray(np.asarray(inputs['x'], dtype=np.float32))
    mem = np.ascontiguousarray(np.asarray(inputs['mem'], dtype=np.float32))
    shared = {name: np.ascontiguousarray(np.asarray(inputs[name], dtype=np.float32)) for name, _ in PARAMS}
    shared.update(consts)
    in_maps = []
    for b in range(n):
        m = dict(shared)
        m['x'] = x[b]
        m['mem'] = mem[b]
        in_maps.append(m)
    res = run_bass_kernel_spmd(nc, in_maps, core_ids=list(range(n)))
    return np.stack([np.asarray(r['out'], dtype=np.float32) for r in res.results], axis=0)
```

```python
import numpy as np
import concourse.bass as bass
import concourse.mybir as mybir
from concourse.bass_utils import run_bass_kernel_spmd
from contextlib import ExitStack

F32 = mybir.dt.float32
BF16 = mybir.dt.bfloat16
I32 = mybir.dt.int32
ALU = mybir.AluOpType
AF = mybir.ActivationFunctionType

ENGS = ('pe', 'act', 'dve', 'pool', 'sp')
EP = 20000
NEPOCH = 8
NSLOT = 8

SEQ = 2048
DM = 1024
NCH = 8
TQ = 4
EPS = 1e-6
TWO_PI = float(2 * np.pi)


class Buf:
    __slots__ = ('w', 'r', 'excl')

    def __init__(self, excl=False):
        self.w = None
        self.r = {}
        self.excl = excl


class _Rec:
    def __init__(self):
        self.call = None

    def __getattr__(self, name):
        def f(*a, **k):
            self.call = (name, a, k)
            return None
        return f


class Sched:
    def __init__(self, nc, es):
        self.nc = nc
        self.ops = {e: [] for e in ENGS}
        self.incs = {e: 0 for e in ENGS}
        self.waited = {e: {} for e in ENGS}
        self.sems = {}
        for e in ENGS:
            if e == 'sp':
                continue
            for k in range(NEPOCH):
                self.sems[(e, k)] = es.enter_context(nc.semaphore(f"s_{e}{k}"))
        self.dsem = [es.enter_context(nc.semaphore(f"s_dma{i}")) for i in range(NSLOT)]
        self.dcnt = [0] * NSLOT
        self.dnext = 0
        self.nops = 0

    def _collect(self, eng, reads, writes, extra=()):
        waits = {}

        def need(t):
            if t is None:
                return
            key, n = t
            if key == 'pe' and eng == 'pe':
                return
            if n > self.waited[eng].get(key, 0):
                if n > waits.get(key, 0):
                    waits[key] = n
        for b in reads:
            need(b.w)
            if b.excl:
                for k, t in b.r.items():
                    if k != eng:
                        need(t)
        for b in writes:
            need(b.w)
            for t in b.r.values():
                need(t)
        for t in extra:
            need(t)
        for key, n in waits.items():
            self.waited[eng][key] = n
        return list(waits.items())

    def op(self, eng, fn, reads=(), writes=(), inc=True):
        assert inc or eng == 'pe'
        waits = self._collect(eng, reads, writes)
        rec = _Rec()
        fn(rec)
        name_, a_, k_ = rec.call
        fn = (lambda e, name_=name_, a_=a_, k_=k_: getattr(e, name_)(*a_, **k_))
        n = self.incs[eng] + 1
        assert n <= EP * NEPOCH
        ticket = (eng, n)
        self.ops[eng].append((waits, fn, ('e', n) if inc else None))
        if inc:
            self.incs[eng] = n
        for b in reads:
            b.r[eng] = ticket
        for b in writes:
            b.w = ticket
            b.r = {}
        self.nops += 1
        return ticket

    def dma(self, out, in_, reads=(), writes=(), **kw):
        slot = self.dnext
        self.dnext = (slot + 1) % NSLOT
        prev = self.dcnt[slot]
        key = ('dma', slot)
        extra = [(key, prev)] if prev > 0 else []
        waits = self._collect('sp', reads, writes, extra)
        n = prev + 1
        self.dcnt[slot] = n
        ticket = (key, n)

        def fn(sp, out=out, in_=in_, kw=kw):
            return sp.dma_start(out=out, in_=in_, **kw)
        self.ops['sp'].append((waits, fn, ('d', slot)))
        for b in reads:
            b.r[key] = ticket
        for b in writes:
            b.w = ticket
            b.r = {}
        self.nops += 1
        return ticket

    def barrier(self):
        for eng in ENGS:
            waits = []
            for e2 in ENGS:
                if e2 != eng and e2 != 'sp' and self.incs[e2] > self.waited[eng].get(e2, 0):
                    waits.append((e2, self.incs[e2]))
                    self.waited[eng][e2] = self.incs[e2]
            for s_ in range(NSLOT):
                key = ('dma', s_)
                if self.dcnt[s_] > self.waited[eng].get(key, 0):
                    waits.append((key, self.dcnt[s_]))
                    self.waited[eng][key] = self.dcnt[s_]
            self.ops[eng].append((waits, None, None))

    def _wait(self, e, key, n):
        if isinstance(key, tuple):
            e.wait_ge(self.dsem[key[1]], 16 * n)
        else:
            k = (n - 1) // EP
            e.wait_ge(self.sems[(key, k)], (n - 1) % EP + 1)

    def emit(self):
        nc = self.nc
        fin = [(('dma', s), self.dcnt[s]) for s in range(NSLOT) if self.dcnt[s] > 0]
        with nc.Block() as block:
            def run(ename, e):
                for waits, fn, inc in self.ops[ename]:
                    for key, n in waits:
                        self._wait(e, key, n)
                    if fn is None:
                        continue
                    ins = fn(e)
                    if inc is not None:
                        if inc[0] == 'e':
                            n = inc[1]
                            ins.then_inc(self.sems[(ename, (n - 1) // EP)], 1)
                        else:
                            ins.then_inc(self.dsem[inc[1]], 16)
                if ename == 'sp':
                    for key, n in fin:
                        self._wait(e, key, n)

            @block.tensor
            def _(pe):
                run('pe', pe)

            @block.scalar
            def _(act):
                run('act', act)

            @block.vector
            def _(dve):
                run('dve', dve)

            @block.gpsimd
            def _(pool):
                run('pool', pool)

            @block.sync
            def _(sp):
                run('sp', sp)


PARAMS = [
    ('norm_ab', (2, 1024)), ('w_in_ab', (2, 1024, 6144)), ('pool_w', (2, 4, 256, 256)), ('pool_scale', (2, 1024)),
    ('w_out_ab', (2, 2048, 1024)), ('norm_cd', (2, 1024)), ('w_in_cd', (2, 1024, 4096)), ('sgu_ln_g', (2, 1024)),
    ('sgu_ln_b', (2, 1024)), ('sgu_w', (2, 4, 128, 128)), ('sgu_b', (2, 4, 128)), ('s5_a_re', (2, 32, 64)),
    ('s5_a_im', (2, 32, 64)), ('s5_log_dt', (2, 32)), ('s5_b_re', (2, 32, 64, 16)), ('s5_b_im', (2, 32, 64, 16)),
    ('s5_c_re', (2, 32, 16, 64)), ('s5_c_im', (2, 32, 16, 64)), ('s5_d', (2, 512)), ('glu_w1', (2, 512, 512)),
    ('glu_w2', (2, 512, 512)), ('w_out_cd', (2, 1536, 1024)), ('norm_x', (4, 1024)), ('w_xq', (4, 1024, 1024)),
    ('w_xkv', (4, 1024, 2048)), ('w_xo', (4, 1024, 1024)), ('mem_norm', (1024,)), ('final_norm', (1024,)),
]


def host_consts():
    c = {}
    c['c_ident'] = np.eye(128, dtype=np.float32)
    k = np.arange(128)[:, None]
    q = np.arange(128)[None, :]
    NEGM = -30000.0
    diag = np.where(q >= k, 0.0, NEGM)
    prev = np.where(q <= k, 0.0, NEGM)
    c['c_maskb'] = np.concatenate([diag, prev], axis=1).astype(np.float32)
    c['c_mask01'] = (c['c_maskb'] == 0.0).astype(np.float32)
    c['c_triu'] = (k <= q).astype(np.float32)
    c['c_iota'] = np.tile(np.arange(512, dtype=np.float32)[None, :], (128, 1))
    inv = np.zeros((4, 16), np.float32)
    for g, w in enumerate((2, 4, 8, 16)):
        inv[g] = 1.0 / np.minimum(np.arange(1, 17), w)
    c['c_invcnt'] = np.tile(inv.reshape(1, 64), (128, 1)).astype(np.float32)
    M = np.zeros((128, 4, 8), np.float32)
    for g2 in range(2):
        for gpl in range(4):
            M[g2 * 64:(g2 + 1) * 64, gpl, 2 * gpl + g2] = 1.0
    c['c_bm1'] = M.reshape(128, 32)
    M2 = np.zeros((128, 4, 2), np.float32)
    for gl in range(8):
        for gpl in range(4):
            for g2 in range(2):
                if gl == 2 * gpl + g2:
                    M2[gl * 16:(gl + 1) * 16, gpl, g2] = 1.0
    c['c_bm2'] = M2.reshape(128, 8)
    return c


class Kern:
    def __init__(self, nc, S, es, depth=4, wplan=None):
        self.nc, self.S, self.es, self.depth = nc, S, es, depth
        self.wplan = wplan
        self.wrec = []
        self.wissued = []
        d = {}
        d['x'] = nc.dram_tensor("x", [SEQ, DM], F32, kind="ExternalInput").ap()
        d['mem'] = nc.dram_tensor("mem", [256, DM], F32, kind="ExternalInput").ap()
        for name, shp in PARAMS:
            d[name] = nc.dram_tensor(name, list(shp), F32, kind="ExternalInput").ap()
        for name, arr in host_consts().items():
            d[name] = nc.dram_tensor(name, list(arr.shape), F32, kind="ExternalInput").ap()
        d['out'] = nc.dram_tensor("out", [SEQ, DM], F32, kind="ExternalOutput").ap()
        self.d = d
        self.psi = 0

    def sb(self, name, shape, dt=F32, es=None):
        self.uid = getattr(self, 'uid', 0) + 1
        return (es or self.es).enter_context(self.nc.sbuf_tensor(f"{name}_{self.uid}", shape, dt))

    def grid(self, *dims):
        if len(dims) == 1:
            return [Buf() for _ in range(dims[0])]
        return [self.grid(*dims[1:]) for _ in range(dims[0])]

    def ps(self):
        i = self.psi
        self.psi = (i + 1) % len(self.psr)
        j = self.psr[i]
        return self.P[j], self.bP[j]

    def ps_rot(self, banks):
        self.psr = list(banks)
        self.psi = 0

    def setup(self):
        nc, S, d = self.nc, self.S, self.d
        sb = self.sb
        self.P = [self.es.enter_context(nc.psum_tensor(f"P{i}", [128, 512], F32)) for i in range(8)]
        self.bP = [Buf(excl=True) for _ in range(8)]
        self.ps_rot(range(8))
        self.xT = sb("xT", [128, NCH, SEQ], F32)
        self.bx = self.grid(NCH, TQ)
        self.hnT = sb("hnT", [128, NCH, SEQ], BF16)
        self.bhn = self.grid(NCH, TQ)
        self.wst = [sb(f"wst{i}", [128, 2048], F32) for i in range(2)]
        self.bwst = [Buf() for _ in range(2)]
        self.wbf = [sb(f"wbf{i}", [128, 2048], BF16) for i in range(2)]
        self.bwbf = [Buf() for _ in range(2)]
        self.wi = 0
        self.memT = sb("memT", [128, NCH, 256], BF16)
        self.bmem = Buf()
        self.ident = sb("ident", [128, 128], F32)
        self.identb = sb("identb", [128, 128], BF16)
        self.onesb = sb("onesb", [128, 128], BF16)
        self.maskb = sb("maskb", [128, 256], BF16)
        self.triu = sb("triu", [128, 128], F32)
        self.iota = sb("iota", [128, 512], F32)
        self.invcnt = sb("invcnt", [128, 64], F32)
        self.bm1 = sb("bm1", [128, 32], F32)
        self.bm2 = sb("bm2", [128, 8], F32)
        self.halfpi = sb("halfpi", [128, 1], F32)
        self.bconst = Buf()
        tmp = self.wst[0]
        S.dma(self.ident[:], d['c_ident'], writes=[self.bconst])
        S.dma(self.triu[:], d['c_triu'], writes=[self.bconst])
        S.dma(self.iota[:], d['c_iota'], writes=[self.bconst])
        S.dma(self.invcnt[:], d['c_invcnt'], writes=[self.bconst])
        S.dma(self.bm1[:], d['c_bm1'], writes=[self.bconst])
        S.dma(self.bm2[:], d['c_bm2'], writes=[self.bconst])
        S.dma(tmp[:, 0:256], d['c_maskb'], writes=[self.bwst[0]])
        S.op('pool', lambda e: e.tensor_copy(out=self.maskb[:], in_=tmp[:, 0:256]), reads=[self.bwst[0]], writes=[self.bconst])
        self.mask01 = sb("mask01", [128, 256], BF16)
        S.dma(tmp[:, 256:512], d['c_mask01'], writes=[self.bwst[0]])
        S.op('pool', lambda e: e.tensor_copy(out=self.mask01[:], in_=tmp[:, 256:512]), reads=[self.bwst[0]], writes=[self.bconst])
        S.op('pool', lambda e: e.tensor_copy(out=self.identb[:], in_=self.ident[:]), reads=[self.bconst], writes=[self.bconst])
        S.op('pool', lambda e: e.memset(self.onesb[:], 1.0), writes=[self.bconst])
        S.op('pool', lambda e: e.memset(self.halfpi[:], float(np.pi / 2)), writes=[self.bconst])
        self.magic = sb("magic", [128, 2], F32)
        S.op('pool', lambda e: e.memset(self.magic[:, 0:1], 12582912.0), writes=[self.bconst])
        S.op('pool', lambda e: e.memset(self.magic[:, 1:2], -12582912.0), writes=[self.bconst])
        rows = [('norm_ab', 2, 8), ('norm_cd', 2, 8), ('norm_x', 4, 8), ('pool_scale', 2, 8), ('mem_norm', 1, 8), ('final_norm', 1, 8), ('s5_d', 2, 4)]
        vst = sb("vst", [128, 128], F32)
        bvst = Buf()
        S.op('dve', lambda e: e.memset(vst[:], 0.0), writes=[bvst])
        vall = sb("vall", [128, 128], F32)
        r0 = 0
        self.vec = {}
        for name, n, ch in rows:
            src = d[name]
            if len(src.shape) == 2:
                src2 = src.rearrange("l (c p) -> (l c) p", p=128)
            else:
                src2 = src.rearrange("(c p) -> c p", p=128)
            S.dma(vst[r0:r0 + n * ch, :], src2, writes=[bvst])
            self.vec[name] = vall[:, r0:r0 + n * ch].rearrange("p (l c) -> p l c", c=ch)
            r0 += n * ch
        ps, bps = self.ps()
        S.op('pe', lambda e: e.transpose(out=ps[:, 0:128], in_=vst[:], identity=self.ident[:]), reads=[bvst, self.bconst], writes=[bps])
        S.op('dve', lambda e: e.tensor_copy(out=vall[:], in_=ps[:, 0:128]), reads=[bps], writes=[self.bconst])

    def wload(self, wap, r0, nr, c0, ncols, prefetch=True):
        key = (wap.tensor.name, int(wap.offset), r0, nr, c0, ncols)
        if self.wplan is None:
            self.wrec.append((key, (wap.tensor.name, int(wap.offset), int(wap.shape[0]), int(wap.shape[1]), r0, nr, c0, ncols)))
            return self._wload_now(wap, r0, nr, c0, ncols)
        if not self.wissued:
            self._wprefetch()
        k0, tile = self.wissued.pop(0)
        assert k0 == key, (k0, key)
        if prefetch:
            self._wprefetch()
        return tile

    def wfence(self):
        if self.wplan is None:
            self.wrec.append((None, None))
            return
        assert not self.wissued
        assert self.wplan and self.wplan[0][0] is None
        self.wplan.pop(0)

    def _wprefetch(self):
        if self.wissued or not self.wplan or self.wplan[0][0] is None:
            return
        key, (name, off, R, C, r0, nr, c0, ncols) = self.wplan.pop(0)
        full = self.d[name]
        nd_ = len(full.shape)
        flat = full if nd_ == 1 else full.rearrange(" ".join("abcd"[:nd_]) + " -> (" + " ".join("abcd"[:nd_]) + ")")
        wap = flat[off:off + R * C].rearrange("(r c) -> r c", c=C)
        self.wissued.append((key, self._wload_now(wap, r0, nr, c0, ncols)))

    def _wload_now(self, wap, r0, nr, c0, ncols):
        S = self.S
        P = min(nr, 128)
        kc = nr // P
        assert kc * ncols <= 2048
        i = self.wi
        self.wi = (i + 1) % 2
        st, bst, wb, bwb = self.wst[i], self.bwst[i], self.wbf[i], self.bwbf[i]
        n = kc * ncols
        src = wap[r0:r0 + nr, c0:c0 + ncols].rearrange("(kc p) c -> p kc c", p=P)
        S.dma(st[0:P, 0:n].rearrange("p (kc c) -> p kc c", c=ncols), src, writes=[bst])
        S.op('act', lambda e: e.activation(out=wb[0:P, 0:n], in_=st[0:P, 0:n], func=AF.Copy), reads=[bst], writes=[bwb])
        return wb[0:P, 0:n].rearrange("p (kc c) -> p kc c", c=ncols), bwb

    def mm_fm(self, ps_ap, bps, wt, bw, mcols, rhs_fn, rhs_bufs, kcs, first=True, last=True):
        S = self.S
        n = len(kcs)
        for i, kc in enumerate(kcs):
            rhs = rhs_fn(kc)
            S.op('pe', lambda e, kc=kc, rhs=rhs, i=i: e.matmul(ps_ap, lhsT=wt[:, kc, mcols], rhs=rhs,
                                                          start=(first and i == 0), stop=(last and i == n - 1)),
                 reads=[bw] + rhs_bufs(kc), writes=[bps], inc=(i == n - 1))

    def rmsnorm(self, src, bsrc, N, gcol, dst, bdst, es):
        S = self.S
        blk = min(N, 512)
        sq = [self.sb(f"rn_sq{i}_{N}", [128, blk], BF16, es) for i in range(2)]
        bsq = [Buf(), Buf()]
        rs = self.sb(f"rn_rs_{N}", [128, blk], F32, es)
        brs = Buf()
        for t in range(N // blk):
            sl = slice(t * blk, (t + 1) * blk)
            ps, bps = self.ps()
            for c in range(NCH):
                j = c % 2
                S.op('act', lambda e, c=c, j=j: e.activation(out=sq[j][:], in_=src[:, c, sl], func=AF.Square),
                     reads=[bsrc[c][t]], writes=[bsq[j]])
                S.op('pe', lambda e, c=c, j=j: e.matmul(ps[:, 0:blk], lhsT=self.onesb[:], rhs=sq[j][:], start=(c == 0), stop=(c == NCH - 1)),
                     reads=[bsq[j], self.bconst], writes=[bps], inc=True)
            S.op('dve', lambda e: e.tensor_scalar(out=rs[:], in0=ps[:, 0:blk], scalar1=1.0 / DM, scalar2=EPS, op0=ALU.mult, op1=ALU.add),
                 reads=[bps], writes=[brs])
            S.op('act', lambda e: e.activation(out=rs[:], in_=rs[:], func=AF.Ln), reads=[brs], writes=[brs])
            S.op('act', lambda e: e.activation(out=rs[:], in_=rs[:], func=AF.Exp, scale=-0.5), reads=[brs], writes=[brs])
            for c in range(NCH):
                S.op('dve', lambda e, c=c: e.scalar_tensor_tensor(out=dst[:, c, sl], in0=src[:, c, sl], scalar=gcol[:, c:c + 1], in1=rs[:],
                                                                   op0=ALU.mult, op1=ALU.mult),
                     reads=[bsrc[c][t], brs, self.bconst], writes=[bdst[c][t]])

    def load_T(self, src_ap, ntiles, dst, bdst_fn):
        S = self.S
        for n in range(ntiles):
            i = n % 2
            st, bst = self.wst[i], self.bwst[i]
            S.dma(st[:, 0:1024], src_ap[n * 128:(n + 1) * 128, :], writes=[bst])
            for h in range(2):
                ps, bps = self.ps()
                for j in range(4):
                    c = 4 * h + j
                    S.op('pe', lambda e, c=c, j=j: e.transpose(out=ps[:, j * 128:(j + 1) * 128], in_=st[:, c * 128:(c + 1) * 128], identity=self.ident[:]),
                         reads=[bst, self.bconst], writes=[bps], inc=(j == 3))
                for j in range(4):
                    c = 4 * h + j
                    if LT_MODE == 0 or (LT_MODE == 2 and j % 2 == 1):
                        S.op('dve', lambda e, c=c, j=j, ps=ps: e.tensor_copy(out=dst[:, c, n * 128:(n + 1) * 128], in_=ps[:, j * 128:(j + 1) * 128]),
                             reads=[bps], writes=bdst_fn(h, n))
                    else:
                        S.op('act', lambda e, c=c, j=j, ps=ps: e.activation(out=dst[:, c, n * 128:(n + 1) * 128], in_=ps[:, j * 128:(j + 1) * 128], func=AF.Copy),
                             reads=[bps], writes=bdst_fn(h, n))

    def outproj(self, wap, r0, nk, mix, bmix):
        S = self.S
        for oc in range(NCH):
            wt, bw = self.wload(wap, r0, nk * 128, oc * 128, 128)
            for tq in range(TQ):
                ps, bps = self.ps()
                sl = slice(tq * 512, (tq + 1) * 512)
                self.mm_fm(ps[:], bps, wt, bw, slice(0, 128), lambda kc: mix[:, kc, sl], lambda kc: [bmix[kc][tq]], list(range(nk)))
                S.op('dve', lambda e, oc=oc, sl=sl, ps=ps: e.tensor_tensor(out=self.xT[:, oc, sl], in0=ps[:], in1=self.xT[:, oc, sl], op=ALU.add),
                     reads=[bps, self.bx[oc][tq]], writes=[self.bx[oc][tq]])
        if BARRIERS:
            S.barrier()

    def proj_fm(self, wap, c0, dst_fn, bdst_fn, func=AF.Copy, eng='act'):
        S = self.S
        wt, bw = self.wload(wap, 0, 1024, c0, 128)
        for tq in range(TQ):
            ps, bps = self.ps()
            sl = slice(tq * 512, (tq + 1) * 512)
            self.mm_fm(ps[:], bps, wt, bw, slice(0, 128), lambda kc: self.hnT[:, kc, sl], lambda kc: [self.bhn[kc][tq]], list(range(NCH)))
            if eng == 'act':
                S.op('act', lambda e, tq=tq, ps=ps: e.activation(out=dst_fn(tq), in_=ps[:], func=func), reads=[bps], writes=bdst_fn(tq))
            else:
                S.op('dve', lambda e, tq=tq, ps=ps: e.tensor_copy(out=dst_fn(tq), in_=ps[:]), reads=[bps], writes=bdst_fn(tq))

    def even_layer(self, li):
        S, d = self.S, self.d
        w_in = d['w_in_ab'][li]
        w_out = d['w_out_ab'][li]
        with ExitStack() as es:
            if 'attnA' in STAGES:
                mix = self.sb("mixA", [128, NCH, SEQ], BF16, es)
                bmix = self.grid(NCH, TQ)
                self.attnA(li, w_in, mix, bmix)
                self.outproj(w_out, 0, 8, mix, bmix)
        if 'poolB' not in STAGES:
            return
        with ExitStack() as es:
            mix = self.sb("mixB", [128, NCH, SEQ], BF16, es)
            bmix = self.grid(NCH, TQ)
            self.poolB(li, w_in, mix, bmix)
            self.outproj(w_out, 1024, 8, mix, bmix)

    def attnA(self, li, w_in, mix, bmix):
        S = self.S
        with ExitStack() as es:
            qT = self.sb("qT", [128, SEQ], BF16, es)
            kT = self.sb("kT", [128, SEQ], BF16, es)
            gT = self.sb("gT", [128, SEQ], BF16, es)
            bq, bk, bg = self.grid(TQ), self.grid(TQ), self.grid(TQ)
            Va = self.sb("Vaug", [128, 3, 16, 2, 128], BF16, es)
            bVa = self.grid(3, 4)
            bVones = Buf()
            S.op('pool', lambda e: e.memset(Va[:, :, :, 0, 64:128], 1.0), writes=[bVones])
            S.op('pool', lambda e: e.memset(Va[:, :, :, 1, 0:64], 1.0), writes=[bVones])
            pT = [self.sb(f"pT{i}", [128, 256], BF16, es) for i in range(4)]
            bpT = [Buf() for _ in range(4)]
            eT = [self.sb(f"eT{i}", [128, 256], BF16, es) for i in range(4)]
            beT = [Buf() for _ in range(4)]
            rden = self.sb("rden", [128, 512], F32, es)
            tmpn = self.sb("tmpn", [128, 512], F32, es)
            brden, btmpn = Buf(), Buf()
            pti = 0
            for c in range(NCH):
                self.ps_rot(range(8))
                self.proj_fm(w_in, c * 128, lambda tq: qT[:, tq * 512:(tq + 1) * 512], lambda tq: [bq[tq]])
                self.proj_fm(w_in, 1024 + c * 128, lambda tq: kT[:, tq * 512:(tq + 1) * 512], lambda tq: [bk[tq]], eng='dve')
                self.proj_fm(w_in, 3072 + c * 128, lambda tq: gT[:, tq * 512:(tq + 1) * 512], lambda tq: [bg[tq]], func=AF.Silu)
                wv, bwv = self.wload(w_in, 0, 1024, 2048 + c * 128, 128)
                for o, dd in enumerate((1, 4, 16)):
                    nb = 16 // dd
                    for t4 in range(4):
                        ps, bps = self.ps()
                        for j in range(4):
                            ti = 4 * t4 + j
                            r, kb = ti // nb, ti % nb
                            st = r + dd * 128 * kb
                            tok = slice(st, st + dd * 127 + 1, dd)
                            tqs = sorted(set([(st) // 512, (st + dd * 127) // 512])) if dd < 16 else [0, 1, 2, 3]
                            for kc in range(NCH):
                                S.op('pe', lambda e, kc=kc, tok=tok, j=j, ps=ps: e.matmul(ps[:, j * 128:(j + 1) * 128], lhsT=self.hnT[:, kc, tok], rhs=wv[:, kc, :],
                                                                                    start=(kc == 0), stop=(kc == NCH - 1)),
                                     reads=[bwv] + [self.bhn[kc][q_] for q_ in tqs], writes=[bps], inc=(kc == NCH - 1 and j == 3))
                        psv = ps[:].rearrange("p (j h d) -> p j h d", j=4, h=2)
                        S.op('act', lambda e, o=o, t4=t4, psv=psv: e.activation(out=Va[:, o, 4 * t4:4 * t4 + 4, 0, 0:64], in_=psv[:, :, 0, :], func=AF.Copy),
                             reads=[bps], writes=[bVa[o][t4]])
                        S.op('dve', lambda e, o=o, t4=t4, psv=psv: e.tensor_copy(out=Va[:, o, 4 * t4:4 * t4 + 4, 1, 64:128], in_=psv[:, :, 1, :]),
                             reads=[bps], writes=[bVa[o][t4]])
                for hh in range(2):
                    hp = slice(hh * 64, hh * 64 + 64)
                    dp = slice(64 - hh * 64, 128 - hh * 64)
                    nd = [self.P[i] for i in range(4)]
                    bnd = [self.bP[i] for i in range(4)]
                    started = [False] * 4
                    self.ps_rot(range(4, 8))
                    tiles = []
                    for o, dd in enumerate((1, 4, 16)):
                        nb = 16 // dd
                        for r in range(dd):
                            for kb in range(nb):
                                tiles.append((o, dd, nb, r, kb))
                    LA = 3
                    pend = {}

                    def issue_S(idx):
                        nonlocal pti
                        o, dd, nb, r, kb = tiles[idx]
                        nq = 2 if kb < nb - 1 else 1
                        N = 128 * nq
                        kst = r + dd * 128 * kb
                        ktok = slice(kst, kst + dd * 127 + 1, dd)
                        qtok = slice(kst, kst + dd * (N - 1) + 1, dd)
                        ktqs = [kst // 512] if dd < 16 else [0, 1, 2, 3]
                        qtqs = sorted(set([kst // 512, (kst + dd * (N - 1)) // 512])) if dd < 16 else [0, 1, 2, 3]
                        ps, bps = self.ps()
                        S.op('pe', lambda e: e.matmul(ps[:, 0:N], lhsT=kT[hp, ktok], rhs=qT[hp, qtok], start=True, stop=True),
                             reads=[bk[q_] for q_ in ktqs] + [bq[q_] for q_ in qtqs], writes=[bps], inc=True)
                        p_, bp_ = pT[pti], bpT[pti]
                        e_, be_ = eT[pti], beT[pti]
                        pti = (pti + 1) % len(pT)
                        S.op('act', lambda e: e.activation(out=e_[:, 0:N], in_=ps[:, 0:N], func=AF.Exp, scale=0.125),
                             reads=[bps], writes=[be_])
                        S.op('dve', lambda e: e.tensor_tensor(out=p_[:, 0:N], in0=e_[:, 0:N], in1=self.mask01[:, 0:N], op=ALU.mult),
                             reads=[be_, self.bconst], writes=[bp_])
                        pend[idx] = (p_, bp_, nq)

                    def issue_PV(idx):
                        o, dd, nb, r, kb = tiles[idx]
                        p_, bp_, nq = pend.pop(idx)
                        ti = r * nb + kb
                        lhsV = Va[:, o, ti, hh, :]
                        allouts = []
                        for qi in range(nq):
                            i = kb + qi
                            if dd == 1:
                                allouts += [(i // 4, slice((i % 4) * 128, (i % 4) * 128 + 128), slice(qi * 128, qi * 128 + 128))]
                            elif dd == 4:
                                allouts += [(i, slice(r, 512, 4), slice(qi * 128, qi * 128 + 128))]
                            else:
                                allouts += [(j, slice(r, 512, 16), slice(j * 32, j * 32 + 32)) for j in range(4)]
                        for n_, (bank, ocols, pcols) in enumerate(allouts):
                            first = not started[bank]
                            started[bank] = True
                            S.op('pe', lambda e: e.matmul(nd[bank][:, ocols], lhsT=lhsV, rhs=p_[:, pcols], start=first, stop=False, skip_group_check=True),
                                 reads=[bp_, bVa[o][ti // 4], bVones], writes=[bnd[bank]], inc=(n_ == len(allouts) - 1))

                    for idx in range(len(tiles) + LA):
                        if idx < len(tiles):
                            issue_S(idx)
                        if idx >= LA:
                            issue_PV(idx - LA)
                    for tq in range(TQ if 'norm' not in ATT_SKIP else 0):
                        sl = slice(tq * 512, (tq + 1) * 512)
                        S.op('dve', lambda e, tq=tq: e.tensor_copy(out=rden[hp, :], in_=nd[tq][dp, :]), reads=[bnd[tq]], writes=[brden])
                        S.op('dve', lambda e: e.reciprocal(out=rden[hp, :], in_=rden[hp, :]), reads=[brden], writes=[brden])
                        S.op('dve', lambda e, tq=tq: e.tensor_tensor(out=tmpn[hp, :], in0=nd[tq][hp, :], in1=rden[hp, :], op=ALU.mult),
                             reads=[bnd[tq], brden], writes=[btmpn])
                        S.op('pool', lambda e, sl=sl: e.tensor_tensor(out=mix[hp, c, sl], in0=tmpn[hp, :], in1=gT[hp, sl], op=ALU.mult),
                             reads=[btmpn, bg[tq]], writes=[bmix[c][tq]])
            self.ps_rot(range(8))

    def poolB(self, li, w_in, mix, bmix):
        S = self.S
        with ExitStack() as es:
            vb = self.sb("vb", [128, 16 + SEQ], F32, es)
            sA = self.sb("sA", [128, 16 + SEQ], F32, es)
            sB = self.sb("sB", [128, 16 + SEQ], F32, es)
            bvb, bsA, bsB = Buf(), Buf(), Buf()
            pooled = self.sb("pooled", [128, 2, SEQ], BF16, es)
            bpo = self.grid(2, TQ)
            gb = self.sb("gbT", [128, 2, SEQ], BF16, es)
            bgb = self.grid(2, TQ)
            t16 = self.sb("t16", [128, 16], F32, es)
            bt16 = Buf()
            for t_ in (vb, sA, sB):
                S.op('pool', lambda e, t_=t_: e.memset(t_[:, 0:16], 0.0), writes=[bvb, bsA, bsB])
            for g in range(4):
                w = (2, 4, 8, 16)[g]
                for j in range(2):
                    cb = 2 * g + j
                    self.proj_fm(w_in, 4096 + cb * 128, lambda tq: vb[:, 16 + tq * 512:16 + (tq + 1) * 512], lambda tq: [bvb])
                    self.proj_fm(w_in, 5120 + cb * 128, lambda tq, j=j: gb[:, j, tq * 512:(tq + 1) * 512], lambda tq, j=j: [bgb[j][tq]], func=AF.Silu)
                    cur, bcur = vb, bvb
                    k = 1
                    nxt = [(sA, bsA), (sB, bsB)]
                    ni = 0
                    while k < w:
                        o_, bo_ = nxt[ni]
                        ni = 1 - ni
                        S.op('pool', lambda e, o_=o_, cur=cur, k=k: e.tensor_tensor(out=o_[:, 16:16 + SEQ], in0=cur[:, 16:16 + SEQ], in1=cur[:, 16 - k:16 - k + SEQ], op=ALU.add),
                             reads=[bcur], writes=[bo_])
                        cur, bcur = o_, bo_
                        k *= 2
                    S.op('dve', lambda e, cur=cur, j=j, w=w: e.scalar_tensor_tensor(out=pooled[:, j, :], in0=cur[:, 16:16 + SEQ], scalar=1.0 / w, in1=vb[:, 16:16 + SEQ],
                                                                                 op0=ALU.mult, op1=ALU.subtract),
                         reads=[bcur, bvb], writes=[bpo[j][t] for t in range(TQ)])
                    S.op('dve', lambda e, cur=cur, g=g: e.tensor_tensor(out=t16[:], in0=cur[:, 16:32], in1=self.invcnt[:, g * 16:(g + 1) * 16], op=ALU.mult),
                         reads=[bcur, self.bconst], writes=[bt16])
                    S.op('dve', lambda e, j=j: e.tensor_tensor(out=pooled[:, j, 0:16], in0=t16[:], in1=vb[:, 16:32], op=ALU.subtract),
                         reads=[bt16, bvb], writes=[bpo[j][0]])
                wp, bwp = self.wload(self.d['pool_w'][li][g], 0, 256, 0, 256)
                for oc2 in range(2):
                    cb = 2 * g + oc2
                    for tq in range(TQ):
                        sl = slice(tq * 512, (tq + 1) * 512)
                        ps, bps = self.ps()
                        self.mm_fm(ps[:], bps, wp, bwp, slice(oc2 * 128, oc2 * 128 + 128), lambda kc: pooled[:, kc, sl], lambda kc: [bpo[kc][tq]], [0, 1])
                        S.op('dve', lambda e, ps=ps, cb=cb, oc2=oc2, sl=sl: e.scalar_tensor_tensor(out=mix[:, cb, sl], in0=ps[:], scalar=self.vec['pool_scale'][:, li, cb:cb + 1],
                                                                                           in1=gb[:, oc2, sl], op0=ALU.mult, op1=ALU.mult),
                             reads=[bps, bgb[oc2][tq], self.bconst], writes=[bmix[cb][tq]])

    def odd_layer(self, li):
        d = self.d
        if 's5D' in STAGES:
            self.s5D(li, d['w_in_cd'][li], d['w_out_cd'][li])
        if 'sguC' in STAGES:
            self.sguC(li, d['w_in_cd'][li], d['w_out_cd'][li])

    def trig(self, y, by, yi, byi, yf, byf, cs, bcs, sn, bsn, on_act=False):
        S = self.S
        if on_act:
            S.op('act', lambda e: e.activation(out=yf, in_=y, func=AF.Identity, bias=self.magic[:, 0:1], scale=1.0), reads=[by, self.bconst], writes=[byf])
            S.op('act', lambda e: e.activation(out=yf, in_=yf, func=AF.Identity, bias=self.magic[:, 1:2], scale=1.0), reads=[byf, self.bconst], writes=[byf])
        else:
            S.op('dve', lambda e: e.tensor_copy(out=yi, in_=y), reads=[by], writes=[byi])
            S.op('dve', lambda e: e.tensor_copy(out=yf, in_=yi), reads=[byi], writes=[byf])
        S.op('dve', lambda e: e.tensor_tensor(out=yf, in0=y, in1=yf, op=ALU.subtract), reads=[by, byf], writes=[byf])
        S.op('act', lambda e: e.activation(out=sn, in_=yf, func=AF.Sin, scale=TWO_PI), reads=[byf], writes=[bsn])
        S.op('act', lambda e: e.activation(out=y, in_=yf, func=AF.Abs), reads=[byf], writes=[by])
        S.op('act', lambda e: e.activation(out=cs, in_=y, func=AF.Sin, scale=-TWO_PI, bias=self.halfpi[:, 0:1]), reads=[by, self.bconst], writes=[bcs])

    def s5D(self, li, w_in, w_out):
        S, d = self.S, self.d
        with ExitStack() as es:
            sb = lambda n, sh, dt=F32, es_=es: self.sb(n, sh, dt, es_)
            xd = sb("xdT", [128, 4, SEQ], BF16)
            bxd = self.grid(4, TQ)
            yg = sb("ygT", [128, 4, SEQ], BF16)
            byg = self.grid(4, TQ)
            for kc in range(4):
                self.proj_fm(w_in, 3072 + kc * 128, lambda tq, kc=kc: xd[:, kc, tq * 512:(tq + 1) * 512], lambda tq, kc=kc: [bxd[kc][tq]])
            ar = sb("s_ar", [128, 16]); ai = sb("s_ai", [128, 16]); ldt = sb("s_ldt", [128, 16])
            bprm = Buf()
            pst = sb("s_pst", [32, 128]); ldt0 = sb("s_ldt0", [128, 32])
            bpst = Buf()
            S.dma(pst[:, 0:64], d['s5_a_re'][li], writes=[bpst])
            S.dma(pst[:, 64:128], d['s5_a_im'][li], writes=[bpst])
            S.dma(ldt0[:], d['s5_log_dt'][li].partition_broadcast(128), writes=[bpst])
            for (dst_, c0) in ((ar, 0), (ai, 64)):
                ps, bps = self.ps()
                S.op('pe', lambda e, ps=ps, c0=c0: e.transpose(out=ps[0:64, 0:32], in_=pst[:, c0:c0 + 64], identity=self.ident[0:32, 0:32]), reads=[bpst, self.bconst], writes=[bps])
                S.op('dve', lambda e, ps=ps, dst_=dst_: e.tensor_copy(out=dst_[0:64, :], in_=ps[0:64, 0:32:2]), reads=[bps], writes=[bprm])
                S.op('dve', lambda e, ps=ps, dst_=dst_: e.tensor_copy(out=dst_[64:128, :], in_=ps[0:64, 1:32:2]), reads=[bps], writes=[bprm])
            S.op('dve', lambda e: e.tensor_copy(out=ldt[0:64, :], in_=ldt0[0:64, 0:32:2]), reads=[bpst], writes=[bprm])
            S.op('dve', lambda e: e.tensor_copy(out=ldt[64:128, :], in_=ldt0[64:128, 1:32:2]), reads=[bpst], writes=[bprm])
            names = ["dt", "dtar", "th", "rho", "c0", "s0", "abr", "abi", "inv", "t1", "t2", "cfr", "cfi", "yf", "thn", "y0"]
            T = {n: sb("s_" + n, [128, 16]) for n in names}
            yi0 = sb("s_yi", [128, 16], I32)
            B = {n: Buf() for n in names + ["yi"]}

            def dv(out, fn, reads, eng='dve'):
                S.op(eng, fn, reads=[B[r] if isinstance(r, str) else r for r in reads], writes=[B[out]])
            TT = lambda o, a, b_, op: (lambda e: e.tensor_tensor(out=T[o][:], in0=a[:], in1=b_[:], op=op))
            dv("dt", lambda e: e.activation(out=T["dt"][:], in_=ldt[:], func=AF.Exp), [bprm], 'act')
            dv("dtar", TT("dtar", T["dt"], ar, ALU.mult), ["dt", bprm])
            dv("th", TT("th", T["dt"], ai, ALU.mult), ["dt", bprm])
            dv("rho", lambda e: e.activation(out=T["rho"][:], in_=T["dtar"][:], func=AF.Exp), ["dtar"], 'act')
            dv("thn", lambda e: e.tensor_single_scalar(out=T["thn"][:], in_=T["th"][:], scalar=1.0 / TWO_PI, op=ALU.mult), ["th"])
            dv("y0", lambda e: e.tensor_copy(out=T["y0"][:], in_=T["thn"][:]), ["thn"])
            self.trig(T["y0"][:], B["y0"], yi0[:], B["yi"], T["yf"][:], B["yf"], T["c0"][:], B["c0"], T["s0"][:], B["s0"], on_act=True)
            dv("abr", TT("abr", T["rho"], T["c0"], ALU.mult), ["rho", "c0"])
            dv("abi", TT("abi", T["rho"], T["s0"], ALU.mult), ["rho", "s0"])
            dv("abr", lambda e: e.tensor_single_scalar(out=T["abr"][:], in_=T["abr"][:], scalar=-1.0, op=ALU.add), ["abr"])
            dv("t1", TT("t1", ar, ar, ALU.mult), [bprm])
            dv("t2", TT("t2", ai, ai, ALU.mult), [bprm])
            dv("inv", TT("inv", T["t1"], T["t2"], ALU.add), ["t1", "t2"])
            dv("inv", lambda e: e.reciprocal(out=T["inv"][:], in_=T["inv"][:]), ["inv"])
            dv("t1", TT("t1", T["abr"], ar, ALU.mult), ["abr", bprm])
            dv("t2", TT("t2", T["abi"], ai, ALU.mult), ["abi", bprm])
            dv("cfr", TT("cfr", T["t1"], T["t2"], ALU.add), ["t1", "t2"])
            dv("cfr", TT("cfr", T["cfr"], T["inv"], ALU.mult), ["cfr", "inv"])
            dv("t1", TT("t1", T["abi"], ar, ALU.mult), ["abi", bprm])
            dv("t2", TT("t2", T["abr"], ai, ALU.mult), ["abr", bprm])
            dv("cfi", TT("cfi", T["t1"], T["t2"], ALU.subtract), ["t1", "t2"])
            dv("cfi", TT("cfi", T["cfi"], T["inv"], ALU.mult), ["cfi", "inv"])
            off = sb("s_off", [128, 16, 4])
            boff = Buf()
            for tq in range(TQ):
                S.op('dve', lambda e, tq=tq: e.tensor_single_scalar(out=off[:, :, tq], in_=T["thn"][:], scalar=512.0 * tq, op=ALU.mult), reads=[B["thn"]], writes=[boff])
            Bre = sb("s_Bre", [128, 16, 16]); Bim = sb("s_Bim", [128, 16, 16])
            bB = Buf()
            S.dma(Bre[:], d['s5_b_re'][li].rearrange("(gp g2) p h -> (g2 p) gp h", g2=2), writes=[bB])
            S.dma(Bim[:], d['s5_b_im'][li].rearrange("(gp g2) p h -> (g2 p) gp h", g2=2), writes=[bB])
            bbr = sb("s_bbr", [128, 16, 16]); bbi = sb("s_bbi", [128, 16, 16]); bt = sb("s_bt", [128, 16, 16])
            bbb, bbt = Buf(), Buf()
            cfr_b = T["cfr"][:, :, None].to_broadcast([128, 16, 16])
            cfi_b = T["cfi"][:, :, None].to_broadcast([128, 16, 16])
            S.op('dve', lambda e: e.tensor_tensor(out=bbr[:], in0=Bre[:], in1=cfr_b, op=ALU.mult), reads=[bB, B["cfr"]], writes=[bbb])
            S.op('dve', lambda e: e.tensor_tensor(out=bt[:], in0=Bim[:], in1=cfi_b, op=ALU.mult), reads=[bB, B["cfi"]], writes=[bbt])
            S.op('dve', lambda e: e.tensor_tensor(out=bbr[:], in0=bbr[:], in1=bt[:], op=ALU.subtract), reads=[bbb, bbt], writes=[bbb])
            S.op('dve', lambda e: e.tensor_tensor(out=bbi[:], in0=Bim[:], in1=cfr_b, op=ALU.mult), reads=[bB, B["cfr"]], writes=[bbb])
            S.op('dve', lambda e: e.tensor_tensor(out=bt[:], in0=Bre[:], in1=cfi_b, op=ALU.mult), reads=[bB, B["cfi"], bbb], writes=[bbt])
            S.op('dve', lambda e: e.tensor_tensor(out=bbi[:], in0=bbi[:], in1=bt[:], op=ALU.add), reads=[bbb, bbt], writes=[bbb])
            Cre = sb("s_Cre", [128, 4, 64]); Cim = sb("s_Cim", [128, 4, 64])
            bC = Buf()
            S.dma(Cre[:], d['s5_c_re'][li].rearrange("(kc gl) h p -> (gl h) kc p", kc=4), writes=[bC])
            S.dma(Cim[:], d['s5_c_im'][li].rearrange("(kc gl) h p -> (gl h) kc p", kc=4), writes=[bC])
            Dg = sb("s_Dg", [128, 4, 128], BF16)
            bDg = Buf()
            for kc in range(4):
                S.op('dve', lambda e, kc=kc: e.tensor_single_scalar(out=Dg[:, kc, :], in_=self.ident[:], scalar=self.vec['s5_d'][:, li, kc:kc + 1], op=ALU.mult),
                     reads=[self.bconst], writes=[bDg])
            bm1 = self.bm1[:].rearrange("p (a b) -> p a b", a=4)
            bm2 = self.bm2[:].rearrange("p (a b) -> p a b", a=4)
            with ExitStack() as es2:
                sb2 = lambda n, sh, dt=F32: self.sb(n, sh, dt, es2)
                L = sb2("s_L", [128, 2, 4, 128], BF16)
                Cc = sb2("s_Cc", [128, 2, 4, 128], BF16)
                bL, bCc = Buf(), Buf()
                Z = [sb2(f"s_Z{i}", [128, 128]) for i in range(2)]
                bZ = [Buf(), Buf()]
                zi = 0
                NW = 512
                wkA = {}
                for n in ("y", "yf", "cs", "sn", "br", "bi", "t1", "t2", "t3", "t4"):
                    wkA[n] = (sb2("k_" + n, [128, NW])[:], Buf())
                wkA["yi"] = (sb2("k_yi", [128, NW], I32)[:], Buf())
                wkB = {}
                for j, n in enumerate(("y", "yf", "cs", "sn")):
                    wkB[n] = (self.wst[0][:, j * NW:(j + 1) * NW], Buf())
                for j, n in enumerate(("br", "bi", "t1", "t2")):
                    wkB[n] = (self.wst[1][:, j * NW:(j + 1) * NW], Buf())
                wb0f = self.wbf[0][:].bitcast(F32)
                for j, n in enumerate(("t3", "t4")):
                    wkB[n] = (wb0f[:, j * NW:(j + 1) * NW], Buf())
                wkB["yi"] = (self.wbf[1][:].bitcast(I32)[:, 0:NW], Buf())
                wks = [wkA, wkB]
                cur = [0]
                self.wfence()
                S.barrier()
                hrb = [sb2(f"k_hr{i}", [128, NW], BF16) for i in range(2)]
                hib = [sb2(f"k_hi{i}", [128, NW], BF16) for i in range(2)]
                bhr, bhi = [Buf(), Buf()], [Buf(), Buf()]
                car = sb2("k_car", [128, 16, 2])
                bcar = [[Buf(), Buf()] for _ in range(16)]
                hidx = 0
                W = lambda n: wks[cur[0]][n][0]
                Bk = lambda n: wks[cur[0]][n][1]
                for kc in range(4):
                    self.ps_rot(range(2, 8))
                    for gpl in range(4):
                        gp = 4 * kc + gpl
                        for ri, src in enumerate((bbr, bbi)):
                            z, bz = Z[zi], bZ[zi]
                            zi = 1 - zi
                            S.op('dve', lambda e, z=z, src=src, gp=gp, gpl=gpl: e.tensor_tensor(
                                out=z[:].rearrange("p (a b) -> p a b", a=8), in0=src[:, gp, None, :].to_broadcast([128, 8, 16]),
                                in1=bm1[:, gpl, :, None].to_broadcast([128, 8, 16]), op=ALU.mult), reads=[bbb, self.bconst], writes=[bz])
                            ps, bps = self.ps()
                            S.op('pe', lambda e, ps=ps, z=z: e.transpose(out=ps[:, 0:128], in_=z[:], identity=self.ident[:]), reads=[bz, self.bconst], writes=[bps])
                            S.op('act', lambda e, ps=ps, ri=ri, gpl=gpl: e.activation(out=L[:, ri, gpl, :], in_=ps[:, 0:128], func=AF.Copy), reads=[bps], writes=[bL])
                        for ri, src in enumerate((Cre, Cim)):
                            z, bz = Z[zi], bZ[zi]
                            zi = 1 - zi
                            S.op('dve', lambda e, z=z, src=src, kc=kc, gpl=gpl: e.tensor_tensor(
                                out=z[:].rearrange("p (a b) -> p a b", a=2), in0=src[:, kc, None, :].to_broadcast([128, 2, 64]),
                                in1=bm2[:, gpl, :, None].to_broadcast([128, 2, 64]), op=ALU.mult), reads=[bC, self.bconst], writes=[bz])
                            ps, bps = self.ps()
                            S.op('pe', lambda e, ps=ps, z=z: e.transpose(out=ps[:, 0:128], in_=z[:], identity=self.ident[:]), reads=[bz, self.bconst], writes=[bps])
                            S.op('dve', lambda e, ps=ps, ri=ri, gpl=gpl: e.tensor_single_scalar(out=Cc[:, ri, gpl, :], in_=ps[:, 0:128], scalar=(1.0 if ri == 0 else -1.0), op=ALU.mult),
                                 reads=[bps], writes=[bCc])
                    for tq in range(TQ):
                        sl = slice(tq * 512, (tq + 1) * 512)
                        psy, bpsy = self.P[tq % 2], self.bP[tq % 2]
                        for gpl in range(4):
                            gp = 4 * kc + gpl
                            cur[0] = 1 - cur[0]
                            pr, bpr = self.ps()
                            pi_, bpi = self.ps()
                            S.op('pe', lambda e, pr=pr, gpl=gpl, kc=kc, sl=sl: e.matmul(pr[:], lhsT=L[:, 0, gpl, :], rhs=xd[:, kc, sl], start=True, stop=True),
                                 reads=[bL, bxd[kc][tq]], writes=[bpr])
                            S.op('pe', lambda e, pi_=pi_, gpl=gpl, kc=kc, sl=sl: e.matmul(pi_[:], lhsT=L[:, 1, gpl, :], rhs=xd[:, kc, sl], start=True, stop=True),
                                 reads=[bL, bxd[kc][tq]], writes=[bpi])
                            S.op('act', lambda e, pr=pr: e.activation(out=W("br"), in_=pr[:], func=AF.Copy), reads=[bpr], writes=[Bk("br")])
                            S.op('act', lambda e, pi_=pi_: e.activation(out=W("bi"), in_=pi_[:], func=AF.Copy), reads=[bpi], writes=[Bk("bi")])
                            S.op('act', lambda e, gp=gp, tq=tq: e.activation(out=W("y"), in_=self.iota[:], func=AF.Identity, scale=T["thn"][:, gp:gp + 1], bias=off[:, gp, tq:tq + 1]),
                                 reads=[self.bconst, B["thn"], boff], writes=[Bk("y")])
                            self.trig(W("y"), Bk("y"), W("yi"), Bk("yi"), W("yf"), Bk("yf"), W("cs"), Bk("cs"), W("sn"), Bk("sn"), on_act=S5_ACT)
                            S.op('dve', lambda e: e.tensor_tensor(out=W("t1"), in0=W("br"), in1=W("cs"), op=ALU.mult), reads=[Bk("br"), Bk("cs")], writes=[Bk("t1")])
                            S.op('dve', lambda e: e.tensor_tensor(out=W("t2"), in0=W("bi"), in1=W("sn"), op=ALU.mult), reads=[Bk("bi"), Bk("sn")], writes=[Bk("t2")])
                            S.op('dve', lambda e: e.tensor_tensor(out=W("t1"), in0=W("t1"), in1=W("t2"), op=ALU.add), reads=[Bk("t1"), Bk("t2")], writes=[Bk("t1")])
                            S.op('dve', lambda e: e.tensor_tensor(out=W("t3"), in0=W("bi"), in1=W("cs"), op=ALU.mult), reads=[Bk("bi"), Bk("cs")], writes=[Bk("t3")])
                            S.op('dve', lambda e: e.tensor_tensor(out=W("t4"), in0=W("br"), in1=W("sn"), op=ALU.mult), reads=[Bk("br"), Bk("sn")], writes=[Bk("t4")])
                            S.op('dve', lambda e: e.tensor_tensor(out=W("t3"), in0=W("t3"), in1=W("t4"), op=ALU.subtract), reads=[Bk("t3"), Bk("t4")], writes=[Bk("t3")])
                            rho_b = T["rho"][:, gp:gp + 1].to_broadcast([128, NW])
                            for (src, dst, ci) in (("t1", "br", 0), ("t3", "bi", 1)):
                                init = 0.0 if tq == 0 else car[:, gp, ci:ci + 1]
                                S.op('dve', lambda e, src=src, dst=dst, init=init, rho_b=rho_b: e.tensor_tensor_scan(out=W(dst), data0=rho_b, data1=W(src), initial=init,
                                                                                                         op0=ALU.mult, op1=ALU.add),
                                     reads=[Bk(src), B["rho"], bcar[gp][ci]], writes=[Bk(dst)])
                                S.op('act', lambda e, dst=dst, gp=gp, ci=ci: e.activation(out=car[:, gp, ci:ci + 1], in_=W(dst)[:, NW - 1:NW], func=AF.Copy),
                                     reads=[Bk(dst)], writes=[bcar[gp][ci]])
                            hr, hi_, bhr_, bhi_ = hrb[hidx], hib[hidx], bhr[hidx], bhi[hidx]
                            hidx = 1 - hidx
                            S.op('dve', lambda e: e.tensor_tensor(out=W("t1"), in0=W("br"), in1=W("cs"), op=ALU.mult), reads=[Bk("br"), Bk("cs")], writes=[Bk("t1")])
                            S.op('dve', lambda e: e.tensor_tensor(out=W("t2"), in0=W("bi"), in1=W("sn"), op=ALU.mult), reads=[Bk("bi"), Bk("sn")], writes=[Bk("t2")])
                            S.op('dve', lambda e, hr=hr: e.tensor_tensor(out=hr[:], in0=W("t1"), in1=W("t2"), op=ALU.subtract), reads=[Bk("t1"), Bk("t2")], writes=[bhr_])
                            S.op('dve', lambda e: e.tensor_tensor(out=W("t3"), in0=W("bi"), in1=W("cs"), op=ALU.mult), reads=[Bk("bi"), Bk("cs")], writes=[Bk("t3")])
                            S.op('dve', lambda e: e.tensor_tensor(out=W("t4"), in0=W("br"), in1=W("sn"), op=ALU.mult), reads=[Bk("br"), Bk("sn")], writes=[Bk("t4")])
                            S.op('dve', lambda e, hi_=hi_: e.tensor_tensor(out=hi_[:], in0=W("t3"), in1=W("t4"), op=ALU.add), reads=[Bk("t3"), Bk("t4")], writes=[bhi_])
                            S.op('pe', lambda e, psy=psy, gpl=gpl, hr=hr: e.matmul(psy[:], lhsT=Cc[:, 0, gpl, :], rhs=hr[:], start=(gpl == 0), stop=False),
                                 reads=[bCc, bhr_], writes=[bpsy])
                            S.op('pe', lambda e, psy=psy, gpl=gpl, hi_=hi_: e.matmul(psy[:], lhsT=Cc[:, 1, gpl, :], rhs=hi_[:], start=False, stop=False),
                                 reads=[bCc, bhi_], writes=[bpsy])
                        S.op('pe', lambda e, psy=psy, kc=kc, sl=sl: e.matmul(psy[:], lhsT=Dg[:, kc, :], rhs=xd[:, kc, sl], start=False, stop=True),
                             reads=[bDg, bxd[kc][tq]], writes=[bpsy])
                        S.op('act', lambda e, psy=psy: e.activation(out=W("t4"), in_=psy[:], func=AF.Copy), reads=[bpsy], writes=[Bk("t4")])
                        S.op('dve', lambda e: e.tensor_tensor(out=W("t2"), in0=W("t4"), in1=W("t4"), op=ALU.mult), reads=[Bk("t4")], writes=[Bk("t2")])
                        S.op('dve', lambda e: e.tensor_scalar(out=W("t2"), in0=W("t2"), scalar1=0.044715, scalar2=1.0, op0=ALU.mult, op1=ALU.add), reads=[Bk("t2")], writes=[Bk("t2")])
                        S.op('dve', lambda e: e.tensor_tensor(out=W("t2"), in0=W("t2"), in1=W("t4"), op=ALU.mult), reads=[Bk("t2"), Bk("t4")], writes=[Bk("t2")])
                        S.op('act', lambda e: e.activation(out=W("t1"), in_=W("t2"), func=AF.Sigmoid, scale=1.5957691216057308), reads=[Bk("t2")], writes=[Bk("t1")])
                        S.op('dve', lambda e, kc=kc, sl=sl: e.tensor_tensor(out=yg[:, kc, sl], in0=W("t1"), in1=W("t4"), op=ALU.mult),
                             reads=[Bk("t1"), Bk("t4")], writes=[byg[kc][tq]])
                self.ps_rot(range(8))
            S.barrier()
            with ExitStack() as es3:
                sb3 = lambda n, sh, dt=F32: self.sb(n, sh, dt, es3)
                gd = sb3("gdT", [128, SEQ], BF16)
                bgd = self.grid(TQ)
                s2 = sb3("glu_s2", [128, 512]); t2_ = sb3("glu_t", [128, 512])
                bs2, bt2 = Buf(), Buf()
                for oc in range(4):
                    self.proj_fm(w_in, 3584 + oc * 128, lambda tq: gd[:, tq * 512:(tq + 1) * 512], lambda tq: [bgd[tq]], func=AF.Silu)
                    w1, bw1 = self.wload(d['glu_w1'][li], 0, 512, oc * 128, 128)
                    w2, bw2 = self.wload(d['glu_w2'][li], 0, 512, oc * 128, 128, prefetch=False)
                    for tq in range(TQ):
                        sl = slice(tq * 512, (tq + 1) * 512)
                        p1, bp1 = self.ps()
                        p2, bp2 = self.ps()
                        self.mm_fm(p1[:], bp1, w1, bw1, slice(0, 128), lambda kc: yg[:, kc, sl], lambda kc: [byg[kc][tq]], [0, 1, 2, 3])
                        self.mm_fm(p2[:], bp2, w2, bw2, slice(0, 128), lambda kc: yg[:, kc, sl], lambda kc: [byg[kc][tq]], [0, 1, 2, 3])
                        S.op('act', lambda e, p2=p2: e.activation(out=s2[:], in_=p2[:], func=AF.Sigmoid), reads=[bp2], writes=[bs2])
                        S.op('dve', lambda e, p1=p1: e.tensor_tensor(out=t2_[:], in0=p1[:], in1=s2[:], op=ALU.mult), reads=[bp1, bs2], writes=[bt2])
                        S.op('pool', lambda e, oc=oc, sl=sl: e.tensor_tensor(out=xd[:, oc, sl], in0=t2_[:], in1=gd[:, sl], op=ALU.mult),
                             reads=[bt2, bgd[tq]], writes=[bxd[oc][tq]])
            self.outproj(w_out, 1024, 4, xd, bxd)

    def sguC(self, li, w_in, w_out):
        S, d = self.S, self.d
        with ExitStack() as es:
            sb = lambda n, sh, dt=F32, es_=es: self.sb(n, sh, dt, es_)
            vn = sb("vn", [128, 16, DM], BF16)
            bvn = self.grid(16)
            with ExitStack() as es2:
                sb2 = lambda n, sh, dt=F32: self.sb(n, sh, dt, es2)
                ssum = sb2("c_ssum", [128, 16, 4]); ssq = sb2("c_ssq", [128, 16, 4])
                bst = Buf()
                junk = sb2("c_junk", [128, 256], BF16)
                bjunk = Buf()
                S.op('dve', lambda e: e.memset(ssum[:], 0.0), writes=[bst])
                S.op('dve', lambda e: e.memset(ssq[:], 0.0), writes=[bst])
                for q in range(4):
                    wt, bw = self.wload(w_in, 0, 1024, 1024 + q * 256, 256)
                    for n in range(16):
                        ps, bps = self.ps()
                        tok = slice(n * 128, (n + 1) * 128)
                        for kc in range(NCH):
                            S.op('pe', lambda e, ps=ps, kc=kc, tok=tok, wt=wt: e.matmul(ps[:, 0:256], lhsT=self.hnT[:, kc, tok], rhs=wt[:, kc, :], start=(kc == 0), stop=(kc == NCH - 1)),
                                 reads=[bw, self.bhn[kc][n // 4]], writes=[bps], inc=(kc == NCH - 1))
                        S.op('act', lambda e, ps=ps, n=n, q=q: e.activation(out=vn[:, n, q * 256:(q + 1) * 256], in_=ps[:, 0:256], func=AF.Copy, accum_out=ssum[:, n, q:q + 1]),
                             reads=[bps, bst], writes=[bvn[n], bst])
                        S.op('act', lambda e, ps=ps, n=n, q=q: e.activation(out=junk[:], in_=ps[:, 0:256], func=AF.Square, accum_out=ssq[:, n, q:q + 1]),
                             reads=[bps, bst], writes=[bjunk, bst])
                mean = sb2("c_mean", [128, 16]); var = sb2("c_var", [128, 16]); m2 = sb2("c_m2", [128, 16])
                S.op('dve', lambda e: e.tensor_tensor(out=ssum[:, :, 0:2], in0=ssum[:, :, 0:2], in1=ssum[:, :, 2:4], op=ALU.add), reads=[bst], writes=[bst])
                S.op('dve', lambda e: e.tensor_tensor(out=mean[:], in0=ssum[:, :, 0], in1=ssum[:, :, 1], op=ALU.add), reads=[bst], writes=[bst])
                S.op('dve', lambda e: e.tensor_single_scalar(out=mean[:], in_=mean[:], scalar=1.0 / DM, op=ALU.mult), reads=[bst], writes=[bst])
                S.op('dve', lambda e: e.tensor_tensor(out=ssq[:, :, 0:2], in0=ssq[:, :, 0:2], in1=ssq[:, :, 2:4], op=ALU.add), reads=[bst], writes=[bst])
                S.op('dve', lambda e: e.tensor_tensor(out=var[:], in0=ssq[:, :, 0], in1=ssq[:, :, 1], op=ALU.add), reads=[bst], writes=[bst])
                S.op('dve', lambda e: e.tensor_tensor(out=m2[:], in0=mean[:], in1=mean[:], op=ALU.mult), reads=[bst], writes=[bst])
                S.op('dve', lambda e: e.scalar_tensor_tensor(out=var[:], in0=var[:], scalar=1.0 / DM, in1=m2[:], op0=ALU.mult, op1=ALU.subtract), reads=[bst], writes=[bst])
                S.op('dve', lambda e: e.tensor_single_scalar(out=var[:], in_=var[:], scalar=EPS, op=ALU.add), reads=[bst], writes=[bst])
                S.op('act', lambda e: e.activation(out=var[:], in_=var[:], func=AF.Ln), reads=[bst], writes=[bst])
                S.op('act', lambda e: e.activation(out=var[:], in_=var[:], func=AF.Exp, scale=-0.5), reads=[bst], writes=[bst])
                lng = sb2("c_lng", [128, DM]); lnb = sb2("c_lnb", [128, DM])
                bln = Buf()
                S.dma(lng[:], d['sgu_ln_g'][li].partition_broadcast(128), writes=[bln])
                S.dma(lnb[:], d['sgu_ln_b'][li].partition_broadcast(128), writes=[bln])
                tmp = [sb2(f"c_tmp{i}", [128, DM]) for i in range(2)]
                btmp = [Buf(), Buf()]
                for n in range(16):
                    t_, bt_ = tmp[n % 2], btmp[n % 2]
                    S.op('dve', lambda e, n=n, t_=t_: e.tensor_scalar(out=t_[:], in0=vn[:, n, :], scalar1=mean[:, n:n + 1], scalar2=var[:, n:n + 1], op0=ALU.subtract, op1=ALU.mult),
                         reads=[bvn[n], bst], writes=[bt_])
                    S.op('pool', lambda e, t_=t_: e.tensor_tensor(out=t_[:], in0=t_[:], in1=lng[:], op=ALU.mult), reads=[bt_, bln], writes=[bt_])
                    S.op('pool', lambda e, n=n, t_=t_: e.tensor_tensor(out=vn[:, n, :], in0=t_[:], in1=lnb[:], op=ALU.add), reads=[bt_, bln], writes=[bvn[n]])
            S.barrier()
            mix = sb("mixC", [128, NCH, SEQ], BF16)
            bmix = self.grid(NCH, TQ)
            wsT = sb("c_wsT", [128, 4, 128], BF16)
            bws = Buf()
            bsb = sb("c_bsb", [128, 4, 128])
            bbs = Buf()
            for g in range(4):
                i = g % 2
                st, bst_ = self.wst[i], self.bwst[i]
                S.dma(st[:, 0:128], d['sgu_w'][li][g], writes=[bst_])
                ps, bps = self.ps()
                S.op('pe', lambda e, ps=ps, st=st: e.transpose(out=ps[:, 0:128], in_=st[:, 0:128], identity=self.ident[:]), reads=[bst_, self.bconst], writes=[bps])
                S.op('dve', lambda e, ps=ps, g=g: e.tensor_tensor(out=wsT[:, g, :], in0=ps[:, 0:128], in1=self.triu[:], op=ALU.mult), reads=[bps, self.bconst], writes=[bws])
                S.dma(bsb[:, g, :], d['sgu_b'][li][g].partition_broadcast(128), writes=[bbs])
            ta = [sb(f"c_ta{i}", [128, 512]) for i in range(2)]
            bta = [Buf(), Buf()]
            sg = [sb(f"c_sg{i}", [128, 512]) for i in range(2)]
            bsg = [Buf(), Buf()]
            k = 0
            for c in range(NCH):
                g = c // 2
                wu, bwu = self.wload(w_in, 0, 1024, c * 128, 128)
                wg, bwg = self.wload(w_in, 0, 1024, 2048 + c * 128, 128, prefetch=False)
                for tq in range(TQ):
                    sl = slice(tq * 512, (tq + 1) * 512)
                    a_, ba_, s_, bs_ = ta[k], bta[k], sg[k], bsg[k]
                    k = 1 - k
                    ps, bps = self.ps()
                    for j in range(4):
                        n = 4 * tq + j
                        S.op('pe', lambda e, ps=ps, j=j, n=n, c=c, g=g: e.matmul(ps[:, j * 128:(j + 1) * 128], lhsT=vn[:, n, c * 128:(c + 1) * 128], rhs=wsT[:, g, :], start=True, stop=True),
                             reads=[bvn[n], bws], writes=[bps], inc=(j == 3))
                    S.op('dve', lambda e, ps=ps, g=g, a_=a_: e.tensor_tensor(out=a_[:].rearrange("p (a b) -> p a b", a=4), in0=ps[:].rearrange("p (a b) -> p a b", a=4),
                                                                        in1=bsb[:, g, None, :].to_broadcast([128, 4, 128]), op=ALU.add), reads=[bps, bbs], writes=[ba_])
                    pu, bpu = self.ps()
                    self.mm_fm(pu[:], bpu, wu, bwu, slice(0, 128), lambda kc: self.hnT[:, kc, sl], lambda kc: [self.bhn[kc][tq]], list(range(NCH)))
                    S.op('dve', lambda e, pu=pu, a_=a_: e.tensor_tensor(out=a_[:], in0=pu[:], in1=a_[:], op=ALU.mult), reads=[bpu, ba_], writes=[ba_])
                    pg, bpg = self.ps()
                    self.mm_fm(pg[:], bpg, wg, bwg, slice(0, 128), lambda kc: self.hnT[:, kc, sl], lambda kc: [self.bhn[kc][tq]], list(range(NCH)))
                    S.op('act', lambda e, pg=pg, s_=s_: e.activation(out=s_[:], in_=pg[:], func=AF.Silu), reads=[bpg], writes=[bs_])
                    S.op('pool', lambda e, a_=a_, s_=s_, sl=sl, c=c: e.tensor_tensor(out=mix[:, c, sl], in0=a_[:], in1=s_[:], op=ALU.mult), reads=[ba_, bs_], writes=[bmix[c][tq]])
            self.outproj(w_out, 0, 8, mix, bmix)

    def cross(self, l):
        S, d = self.S, self.d
        with ExitStack() as es:
            sb = lambda n, sh, dt=F32: self.sb(n, sh, dt, es)
            with ExitStack() as es2:
                self.rmsnorm(self.xT, self.bx, SEQ, self.vec['norm_x'][:, l, :], self.hnT, self.bhn, es2)
            S.barrier()
            qT = sb("xqT", [128, 2, SEQ], BF16)
            bq = self.grid(2, TQ)
            mix = sb("mixX", [128, NCH, SEQ], BF16)
            bmix = self.grid(NCH, TQ)
            KT = sb("xKT", [128, NCH, 256], BF16)
            bKT = self.grid(NCH)
            Vx = sb("xV", [128, 2, DM], BF16)
            bVx = self.grid(2)
            pT = [sb(f"xpT{i}", [128, 2, 512], BF16) for i in range(2)]
            bpT = [Buf(), Buf()]
            rden = sb("xrden", [128, 512])
            brden = Buf()
            wkv = d['w_xkv'][l]
            for c in range(NCH):
                wt, bw = self.wload(wkv, 0, 1024, c * 128, 128)
                ps, bps = self.ps()
                self.mm_fm(ps[:, 0:256], bps, wt, bw, slice(0, 128), lambda kc: self.memT[:, kc, :], lambda kc: [self.bmem], list(range(NCH)))
                S.op('act', lambda e, ps=ps, c=c: e.activation(out=KT[:, c, :], in_=ps[:, 0:256], func=AF.Copy), reads=[bps], writes=[bKT[c]])
            for q in range(4):
                wt, bw = self.wload(wkv, 0, 1024, 1024 + q * 256, 256)
                for mt in range(2):
                    ps, bps = self.ps()
                    for kc in range(NCH):
                        S.op('pe', lambda e, ps=ps, kc=kc, mt=mt, wt=wt: e.matmul(ps[:, 0:256], lhsT=self.memT[:, kc, mt * 128:(mt + 1) * 128], rhs=wt[:, kc, :], start=(kc == 0), stop=(kc == NCH - 1)),
                             reads=[bw, self.bmem], writes=[bps], inc=(kc == NCH - 1))
                    S.op('act', lambda e, ps=ps, mt=mt, q=q: e.activation(out=Vx[:, mt, q * 256:(q + 1) * 256], in_=ps[:, 0:256], func=AF.Copy), reads=[bps], writes=[bVx[mt]])
            pi = 0
            for h in range(4):
                for k2 in range(2):
                    self.proj_fm(d['w_xq'][l], (2 * h + k2) * 128, lambda tq, k2=k2: qT[:, k2, tq * 512:(tq + 1) * 512], lambda tq, k2=k2: [bq[k2][tq]], eng=('act' if k2 == 0 else 'dve'))
                for tq in range(TQ):
                    sl = slice(tq * 512, (tq + 1) * 512)
                    p_, bp_ = pT[pi], bpT[pi]
                    pi = 1 - pi
                    for mt in range(2):
                        ps, bps = self.ps()
                        for k2 in range(2):
                            cc = 2 * h + k2
                            S.op('pe', lambda e, ps=ps, cc=cc, mt=mt, k2=k2, sl=sl: e.matmul(ps[:], lhsT=KT[:, cc, mt * 128:(mt + 1) * 128], rhs=qT[:, k2, sl], start=(k2 == 0), stop=(k2 == 1)),
                                 reads=[bKT[cc], bq[k2][tq]], writes=[bps], inc=(k2 == 1))
                        S.op('act', lambda e, ps=ps, mt=mt, p_=p_: e.activation(out=p_[:, mt, :], in_=ps[:], func=AF.Exp, scale=1.0 / 16.0), reads=[bps], writes=[bp_])
                    psd, bpsd = self.ps()
                    for mt in range(2):
                        S.op('pe', lambda e, psd=psd, mt=mt, p_=p_: e.matmul(psd[:], lhsT=self.onesb[:], rhs=p_[:, mt, :], start=(mt == 0), stop=(mt == 1)),
                             reads=[bp_, self.bconst], writes=[bpsd], inc=(mt == 1))
                    S.op('dve', lambda e, psd=psd: e.reciprocal(out=rden[:], in_=psd[:]), reads=[bpsd], writes=[brden])
                    for dc in range(2):
                        cc = 2 * h + dc
                        pso, bpso = self.ps()
                        for mt in range(2):
                            S.op('pe', lambda e, pso=pso, mt=mt, p_=p_, cc=cc: e.matmul(pso[:], lhsT=Vx[:, mt, cc * 128:(cc + 1) * 128], rhs=p_[:, mt, :], start=(mt == 0), stop=(mt == 1)),
                                 reads=[bp_, bVx[mt]], writes=[bpso], inc=(mt == 1))
                        S.op('dve', lambda e, pso=pso, cc=cc, sl=sl: e.tensor_tensor(out=mix[:, cc, sl], in0=pso[:], in1=rden[:], op=ALU.mult),
                             reads=[bpso, brden], writes=[bmix[cc][tq]])
            self.outproj(d['w_xo'][l], 0, 8, mix, bmix)

    def final(self):
        S, d = self.S, self.d
        with ExitStack() as es:
            sb = lambda n, sh, dt=F32: self.sb(n, sh, dt, es)
            blk = 512
            sq = [sb(f"f_sq{i}", [128, blk], BF16) for i in range(2)]
            bsq = [Buf(), Buf()]
            rs = sb("f_rs", [128, blk])
            brs = Buf()
            nrm = [sb(f"f_n{i}", [128, blk]) for i in range(2)]
            bnrm = [Buf(), Buf()]
            gcol = self.vec['final_norm'][:, 0, :]
            ost = [sb(f"f_o{i}", [128, DM]) for i in range(2)]
            bost = [Buf(), Buf()]
            for tq in range(TQ):
                sl = slice(tq * blk, (tq + 1) * blk)
                ps, bps = self.ps()
                for c in range(NCH):
                    j = c % 2
                    S.op('act', lambda e, c=c, j=j: e.activation(out=sq[j][:], in_=self.xT[:, c, sl], func=AF.Square), reads=[self.bx[c][tq]], writes=[bsq[j]])
                    S.op('pe', lambda e, c=c, j=j, ps=ps: e.matmul(ps[:], lhsT=self.onesb[:], rhs=sq[j][:], start=(c == 0), stop=(c == NCH - 1)),
                         reads=[bsq[j], self.bconst], writes=[bps], inc=True)
                S.op('dve', lambda e, ps=ps: e.tensor_scalar(out=rs[:], in0=ps[:], scalar1=1.0 / DM, scalar2=EPS, op0=ALU.mult, op1=ALU.add), reads=[bps], writes=[brs])
                S.op('act', lambda e: e.activation(out=rs[:], in_=rs[:], func=AF.Ln), reads=[brs], writes=[brs])
                S.op('act', lambda e: e.activation(out=rs[:], in_=rs[:], func=AF.Exp, scale=-0.5), reads=[brs], writes=[brs])
                pts = [self.ps() for _ in range(8)]
                for c in range(NCH):
                    n_, bn_ = nrm[c % 2], bnrm[c % 2]
                    S.op('dve', lambda e, c=c, n_=n_: e.scalar_tensor_tensor(out=n_[:], in0=self.xT[:, c, sl], scalar=gcol[:, c:c + 1], in1=rs[:], op0=ALU.mult, op1=ALU.mult),
                         reads=[self.bx[c][tq], brs, self.bconst], writes=[bn_])
                    for j in range(4):
                        pp, bpp = pts[2 * j + c // 4]
                        S.op('pe', lambda e, pp=pp, j=j, c=c, n_=n_: e.transpose(out=pp[:, (c % 4) * 128:(c % 4) * 128 + 128], in_=n_[:, j * 128:(j + 1) * 128], identity=self.ident[:]),
                             reads=[bn_, self.bconst], writes=[bpp], inc=True)
                for j in range(4):
                    n = 4 * tq + j
                    o_, bo_ = ost[n % 2], bost[n % 2]
                    for h in range(2):
                        pp, bpp = pts[2 * j + h]
                        if h == 0:
                            S.op('act', lambda e, pp=pp, o_=o_: e.activation(out=o_[:, 0:512], in_=pp[:], func=AF.Copy), reads=[bpp], writes=[bo_])
                        else:
                            S.op('dve', lambda e, pp=pp, o_=o_: e.tensor_copy(out=o_[:, 512:1024], in_=pp[:]), reads=[bpp], writes=[bo_])
                    S.dma(d['out'][n * 128:(n + 1) * 128, :], o_[:], reads=[bo_])

    def run(self):
        S, d = self.S, self.d
        self.setup()
        if CUT == 1:
            return
        self.load_T(d['x'], 16, self.xT, lambda h, n: [self.bx[c][n // 4] for c in range(4 * h, 4 * h + 4)])
        if CUT == 2:
            return
        with ExitStack() as es:
            mraw = self.sb("mraw", [128, NCH, 256], F32, es)
            bmr = [[Buf()] for _ in range(NCH)]
            self.load_T(d['mem'], 2, mraw, lambda h, n: [bmr[c][0] for c in range(4 * h, 4 * h + 4)])
            bm = [[self.bmem] for _ in range(NCH)]
            self.rmsnorm(mraw, bmr, 256, self.vec['mem_norm'][:, 0, :], self.memT, bm, es)
        S.barrier()
        if CUT == 3:
            return
        for layer in range(self.depth):
            i = layer // 2
            with ExitStack() as es:
                gname = 'norm_ab' if layer % 2 == 0 else 'norm_cd'
                self.rmsnorm(self.xT, self.bx, SEQ, self.vec[gname][:, i, :], self.hnT, self.bhn, es)
            S.barrier()
            if layer % 2 == 0:
                self.even_layer(i)
            else:
                self.odd_layer(i)
            S.barrier()
            if 'cross' in STAGES:
                self.cross(layer)
            S.barrier()
        self.final()


import os
_CACHE = {}
S5_ACT = int(os.environ.get('S5_ACT', '1'))
BARRIERS = int(os.environ.get('BARRIERS', '1'))
ATT_SKIP = set(os.environ.get('ATT_SKIP', '').split(','))
LT_MODE = 0
CUT = 0
STAGES = {'attnA', 'poolB', 'cross', 's5D', 'sguC'}


def build_nc(depth=4):
    key = (depth, tuple(sorted(STAGES)))
    if key in _CACHE:
        return _CACHE[key]
    plan = None
    for pass_ in range(2):
        nc = bass.Bass("TRN2", target_bir_lowering=False)
        with ExitStack() as es:
            S = Sched(nc, es)
            K = Kern(nc, S, es, depth, wplan=plan)
            K.run()
            if pass_ == 0:
                plan = [(k, sp) for k, sp in K.wrec]
                continue
            S.emit()
    _CACHE[key] = nc
    return nc


def kernel(**inputs):
    n = 8
    nc = build_nc(4)
    consts = host_consts()
    x = np.ascontiguousarray(np.asarray(inputs['x'], dtype=np.float32))
    mem = np.ascontiguousarray(np.asarray(inputs['mem'], dtype=np.float32))
    shared = {name: np.ascontiguousarray(np.asarray(inputs[name], dtype=np.float32)) for name, _ in PARAMS}
    shared.update(consts)
    in_maps = []
    for b in range(n):
        m = dict(shared)
        m['x'] = x[b]
        m['mem'] = mem[b]
        in_maps.append(m)
    res = run_bass_kernel_spmd(nc, in_maps, core_ids=list(range(n)))
    return np.stack([np.asarray(r['out'], dtype=np.float32) for r in res.results], axis=0)
```

```python
import numpy as np
import concourse.bass as bass
import concourse.mybir as mybir
from concourse.bass_utils import run_bass_kernel_spmd
from contextlib import ExitStack

F32 = mybir.dt.float32
BF16 = mybir.dt.bfloat16
I32 = mybir.dt.int32
ALU = mybir.AluOpType
AF = mybir.ActivationFunctionType

ENGS = ('pe', 'act', 'dve', 'pool', 'sp')
EP = 20000
NEPOCH = 8
NSLOT = 8

SEQ = 2048
DM = 1024
NCH = 8
TQ = 4
EPS = 1e-6
TWO_PI = float(2 * np.pi)


class Buf:
    __slots__ = ('w', 'r', 'excl')

    def __init__(self, excl=False):
        self.w = None
        self.r = {}
        self.excl = excl


class _Rec:
    def __init__(self):
        self.call = None

    def __getattr__(self, name):
        def f(*a, **k):
            self.call = (name, a, k)
            return None
        return f


class Sched:
    def __init__(self, nc, es):
        self.nc = nc
        self.ops = {e: [] for e in ENGS}
        self.incs = {e: 0 for e in ENGS}
        self.waited = {e: {} for e in ENGS}
        self.sems = {}
        for e in ENGS:
            if e == 'sp':
                continue
            for k in range(NEPOCH):
                self.sems[(e, k)] = es.enter_context(nc.semaphore(f"s_{e}{k}"))
        self.dsem = [es.enter_context(nc.semaphore(f"s_dma{i}")) for i in range(NSLOT)]
        self.dcnt = [0] * NSLOT
        self.dnext = 0
        self.nops = 0

    def _collect(self, eng, reads, writes, extra=()):
        waits = {}

        def need(t):
            if t is None:
                return
            key, n = t
            if key == 'pe' and eng == 'pe':
                return
            if n > self.waited[eng].get(key, 0):
                if n > waits.get(key, 0):
                    waits[key] = n
        for b in reads:
            need(b.w)
            if b.excl:
                for k, t in b.r.items():
                    if k != eng:
                        need(t)
        for b in writes:
            need(b.w)
            for t in b.r.values():
                need(t)
        for t in extra:
            need(t)
        for key, n in waits.items():
            self.waited[eng][key] = n
        return list(waits.items())

    def op(self, eng, fn, reads=(), writes=(), inc=True):
        assert inc or eng == 'pe'
        waits = self._collect(eng, reads, writes)
        rec = _Rec()
        fn(rec)
        name_, a_, k_ = rec.call
        fn = (lambda e, name_=name_, a_=a_, k_=k_: getattr(e, name_)(*a_, **k_))
        n = self.incs[eng] + 1
        assert n <= EP * NEPOCH
        ticket = (eng, n)
        self.ops[eng].append((waits, fn, ('e', n) if inc else None))
        if inc:
            self.incs[eng] = n
        for b in reads:
            b.r[eng] = ticket
        for b in writes:
            b.w = ticket
            b.r = {}
        self.nops += 1
        return ticket

    def dma(self, out, in_, reads=(), writes=(), **kw):
        slot = self.dnext
        self.dnext = (slot + 1) % NSLOT
        prev = self.dcnt[slot]
        key = ('dma', slot)
        extra = [(key, prev)] if prev > 0 else []
        waits = self._collect('sp', reads, writes, extra)
        n = prev + 1
        self.dcnt[slot] = n
        ticket = (key, n)

        def fn(sp, out=out, in_=in_, kw=kw):
            return sp.dma_start(out=out, in_=in_, **kw)
        self.ops['sp'].append((waits, fn, ('d', slot)))
        for b in reads:
            b.r[key] = ticket
        for b in writes:
            b.w = ticket
            b.r = {}
        self.nops += 1
        return ticket

    def barrier(self):
        for eng in ENGS:
            waits = []
            for e2 in ENGS:
                if e2 != eng and e2 != 'sp' and self.incs[e2] > self.waited[eng].get(e2, 0):
                    waits.append((e2, self.incs[e2]))
                    self.waited[eng][e2] = self.incs[e2]
            for s_ in range(NSLOT):
                key = ('dma', s_)
                if self.dcnt[s_] > self.waited[eng].get(key, 0):
                    waits.append((key, self.dcnt[s_]))
                    self.waited[eng][key] = self.dcnt[s_]
            self.ops[eng].append((waits, None, None))

    def _wait(self, e, key, n):
        if isinstance(key, tuple):
            e.wait_ge(self.dsem[key[1]], 16 * n)
        else:
            k = (n - 1) // EP
            e.wait_ge(self.sems[(key, k)], (n - 1) % EP + 1)

    def emit(self):
        nc = self.nc
        fin = [(('dma', s), self.dcnt[s]) for s in range(NSLOT) if self.dcnt[s] > 0]
        with nc.Block() as block:
            def run(ename, e):
                for waits, fn, inc in self.ops[ename]:
                    for key, n in waits:
                        self._wait(e, key, n)
                    if fn is None:
                        continue
                    ins = fn(e)
                    if inc is not None:
                        if inc[0] == 'e':
                            n = inc[1]
                            ins.then_inc(self.sems[(ename, (n - 1) // EP)], 1)
                        else:
                            ins.then_inc(self.dsem[inc[1]], 16)
                if ename == 'sp':
                    for key, n in fin:
                        self._wait(e, key, n)

            @block.tensor
            def _(pe):
                run('pe', pe)

            @block.scalar
            def _(act):
                run('act', act)

            @block.vector
            def _(dve):
                run('dve', dve)

            @block.gpsimd
            def _(pool):
                run('pool', pool)

            @block.sync
            def _(sp):
                run('sp', sp)


PARAMS = [
    ('norm_ab', (2, 1024)), ('w_in_ab', (2, 1024, 6144)), ('pool_w', (2, 4, 256, 256)), ('pool_scale', (2, 1024)),
    ('w_out_ab', (2, 2048, 1024)), ('norm_cd', (2, 1024)), ('w_in_cd', (2, 1024, 4096)), ('sgu_ln_g', (2, 1024)),
    ('sgu_ln_b', (2, 1024)), ('sgu_w', (2, 4, 128, 128)), ('sgu_b', (2, 4, 128)), ('s5_a_re', (2, 32, 64)),
    ('s5_a_im', (2, 32, 64)), ('s5_log_dt', (2, 32)), ('s5_b_re', (2, 32, 64, 16)), ('s5_b_im', (2, 32, 64, 16)),
    ('s5_c_re', (2, 32, 16, 64)), ('s5_c_im', (2, 32, 16, 64)), ('s5_d', (2, 512)), ('glu_w1', (2, 512, 512)),
    ('glu_w2', (2, 512, 512)), ('w_out_cd', (2, 1536, 1024)), ('norm_x', (4, 1024)), ('w_xq', (4, 1024, 1024)),
    ('w_xkv', (4, 1024, 2048)), ('w_xo', (4, 1024, 1024)), ('mem_norm', (1024,)), ('final_norm', (1024,)),
]


def host_consts():
    c = {}
    c['c_ident'] = np.eye(128, dtype=np.float32)
    k = np.arange(128)[:, None]
    q = np.arange(128)[None, :]
    NEGM = -30000.0
    diag = np.where(q >= k, 0.0, NEGM)
    prev = np.where(q <= k, 0.0, NEGM)
    c['c_maskb'] = np.concatenate([diag, prev], axis=1).astype(np.float32)
    c['c_mask01'] = (c['c_maskb'] == 0.0).astype(np.float32)
    c['c_triu'] = (k <= q).astype(np.float32)
    c['c_iota'] = np.tile(np.arange(512, dtype=np.float32)[None, :], (128, 1))
    inv = np.zeros((4, 16), np.float32)
    for g, w in enumerate((2, 4, 8, 16)):
        inv[g] = 1.0 / np.minimum(np.arange(1, 17), w)
    c['c_invcnt'] = np.tile(inv.reshape(1, 64), (128, 1)).astype(np.float32)
    M = np.zeros((128, 4, 8), np.float32)
    for g2 in range(2):
        for gpl in range(4):
            M[g2 * 64:(g2 + 1) * 64, gpl, 2 * gpl + g2] = 1.0
    c['c_bm1'] = M.reshape(128, 32)
    M2 = np.zeros((128, 4, 2), np.float32)
    for gl in range(8):
        for gpl in range(4):
            for g2 in range(2):
                if gl == 2 * gpl + g2:
                    M2[gl * 16:(gl + 1) * 16, gpl, g2] = 1.0
    c['c_bm2'] = M2.reshape(128, 8)
    return c


class Kern:
    def __init__(self, nc, S, es, depth=4, wplan=None):
        self.nc, self.S, self.es, self.depth = nc, S, es, depth
        self.wplan = wplan
        self.wrec = []
        self.wissued = []
        d = {}
        d['x'] = nc.dram_tensor("x", [SEQ, DM], F32, kind="ExternalInput").ap()
        d['mem'] = nc.dram_tensor("mem", [256, DM], F32, kind="ExternalInput").ap()
        for name, shp in PARAMS:
            d[name] = nc.dram_tensor(name, list(shp), F32, kind="ExternalInput").ap()
        for name, arr in host_consts().items():
            d[name] = nc.dram_tensor(name, list(arr.shape), F32, kind="ExternalInput").ap()
        d['out'] = nc.dram_tensor("out", [SEQ, DM], F32, kind="ExternalOutput").ap()
        self.d = d
        self.psi = 0

    def sb(self, name, shape, dt=F32, es=None):
        self.uid = getattr(self, 'uid', 0) + 1
        return (es or self.es).enter_context(self.nc.sbuf_tensor(f"{name}_{self.uid}", shape, dt))

    def grid(self, *dims):
        if len(dims) == 1:
            return [Buf() for _ in range(dims[0])]
        return [self.grid(*dims[1:]) for _ in range(dims[0])]

    def ps(self):
        i = self.psi
        self.psi = (i + 1) % len(self.psr)
        j = self.psr[i]
        return self.P[j], self.bP[j]

    def ps_rot(self, banks):
        self.psr = list(banks)
        self.psi = 0

    def setup(self):
        nc, S, d = self.nc, self.S, self.d
        sb = self.sb
        self.P = [self.es.enter_context(nc.psum_tensor(f"P{i}", [128, 512], F32)) for i in range(8)]
        self.bP = [Buf(excl=True) for _ in range(8)]
        self.ps_rot(range(8))
        self.xT = sb("xT", [128, NCH, SEQ], F32)
        self.bx = self.grid(NCH, TQ)
        self.hnT = sb("hnT", [128, NCH, SEQ], BF16)
        self.bhn = self.grid(NCH, TQ)
        self.wst = [sb(f"wst{i}", [128, 2048], F32) for i in range(2)]
        self.bwst = [Buf() for _ in range(2)]
        self.wbf = [sb(f"wbf{i}", [128, 2048], BF16) for i in range(2)]
        self.bwbf = [Buf() for _ in range(2)]
        self.wi = 0
        self.memT = sb("memT", [128, NCH, 256], BF16)
        self.bmem = Buf()
        self.ident = sb("ident", [128, 128], F32)
        self.identb = sb("identb", [128, 128], BF16)
        self.onesb = sb("onesb", [128, 128], BF16)
        self.maskb = sb("maskb", [128, 256], BF16)
        self.triu = sb("triu", [128, 128], F32)
        self.iota = sb("iota", [128, 512], F32)
        self.invcnt = sb("invcnt", [128, 64], F32)
        self.bm1 = sb("bm1", [128, 32], F32)
        self.bm2 = sb("bm2", [128, 8], F32)
        self.halfpi = sb("halfpi", [128, 1], F32)
        self.bconst = Buf()
        tmp = self.wst[0]
        S.dma(self.ident[:], d['c_ident'], writes=[self.bconst])
        S.dma(self.triu[:], d['c_triu'], writes=[self.bconst])
        S.dma(self.iota[:], d['c_iota'], writes=[self.bconst])
        S.dma(self.invcnt[:], d['c_invcnt'], writes=[self.bconst])
        S.dma(self.bm1[:], d['c_bm1'], writes=[self.bconst])
        S.dma(self.bm2[:], d['c_bm2'], writes=[self.bconst])
        S.dma(tmp[:, 0:256], d['c_maskb'], writes=[self.bwst[0]])
        S.op('pool', lambda e: e.tensor_copy(out=self.maskb[:], in_=tmp[:, 0:256]), reads=[self.bwst[0]], writes=[self.bconst])
        self.mask01 = sb("mask01", [128, 256], BF16)
        S.dma(tmp[:, 256:512], d['c_mask01'], writes=[self.bwst[0]])
        S.op('pool', lambda e: e.tensor_copy(out=self.mask01[:], in_=tmp[:, 256:512]), reads=[self.bwst[0]], writes=[self.bconst])
        S.op('pool', lambda e: e.tensor_copy(out=self.identb[:], in_=self.ident[:]), reads=[self.bconst], writes=[self.bconst])
        S.op('pool', lambda e: e.memset(self.onesb[:], 1.0), writes=[self.bconst])
        S.op('pool', lambda e: e.memset(self.halfpi[:], float(np.pi / 2)), writes=[self.bconst])
        self.magic = sb("magic", [128, 2], F32)
        S.op('pool', lambda e: e.memset(self.magic[:, 0:1], 12582912.0), writes=[self.bconst])
        S.op('pool', lambda e: e.memset(self.magic[:, 1:2], -12582912.0), writes=[self.bconst])
        rows = [('norm_ab', 2, 8), ('norm_cd', 2, 8), ('norm_x', 4, 8), ('pool_scale', 2, 8), ('mem_norm', 1, 8), ('final_norm', 1, 8), ('s5_d', 2, 4)]
        vst = sb("vst", [128, 128], F32)
        bvst = Buf()
        S.op('dve', lambda e: e.memset(vst[:], 0.0), writes=[bvst])
        vall = sb("vall", [128, 128], F32)
        r0 = 0
        self.vec = {}
        for name, n, ch in rows:
            src = d[name]
            if len(src.shape) == 2:
                src2 = src.rearrange("l (c p) -> (l c) p", p=128)
            else:
                src2 = src.rearrange("(c p) -> c p", p=128)
            S.dma(vst[r0:r0 + n * ch, :], src2, writes=[bvst])
            self.vec[name] = vall[:, r0:r0 + n * ch].rearrange("p (l c) -> p l c", c=ch)
            r0 += n * ch
        ps, bps = self.ps()
        S.op('pe', lambda e: e.transpose(out=ps[:, 0:128], in_=vst[:], identity=self.ident[:]), reads=[bvst, self.bconst], writes=[bps])
        S.op('dve', lambda e: e.tensor_copy(out=vall[:], in_=ps[:, 0:128]), reads=[bps], writes=[self.bconst])

    def wload(self, wap, r0, nr, c0, ncols, prefetch=True):
        key = (wap.tensor.name, int(wap.offset), r0, nr, c0, ncols)
        if self.wplan is None:
            self.wrec.append((key, (wap.tensor.name, int(wap.offset), int(wap.shape[0]), int(wap.shape[1]), r0, nr, c0, ncols)))
            return self._wload_now(wap, r0, nr, c0, ncols)
        if not self.wissued:
            self._wprefetch()
        k0, tile = self.wissued.pop(0)
        assert k0 == key, (k0, key)
        if prefetch:
            self._wprefetch()
        return tile

    def wfence(self):
        if self.wplan is None:
            self.wrec.append((None, None))
            return
        assert not self.wissued
        assert self.wplan and self.wplan[0][0] is None
        self.wplan.pop(0)

    def _wprefetch(self):
        if self.wissued or not self.wplan or self.wplan[0][0] is None:
            return
        key, (name, off, R, C, r0, nr, c0, ncols) = self.wplan.pop(0)
        full = self.d[name]
        nd_ = len(full.shape)
        flat = full if nd_ == 1 else full.rearrange(" ".join("abcd"[:nd_]) + " -> (" + " ".join("abcd"[:nd_]) + ")")
        wap = flat[off:off + R * C].rearrange("(r c) -> r c", c=C)
        self.wissued.append((key, self._wload_now(wap, r0, nr, c0, ncols)))

    def _wload_now(self, wap, r0, nr, c0, ncols):
        S = self.S
        P = min(nr, 128)
        kc = nr // P
        assert kc * ncols <= 2048
        i = self.wi
        self.wi = (i + 1) % 2
        st, bst, wb, bwb = self.wst[i], self.bwst[i], self.wbf[i], self.bwbf[i]
        n = kc * ncols
        src = wap[r0:r0 + nr, c0:c0 + ncols].rearrange("(kc p) c -> p kc c", p=P)
        S.dma(st[0:P, 0:n].rearrange("p (kc c) -> p kc c", c=ncols), src, writes=[bst])
        S.op('act', lambda e: e.activation(out=wb[0:P, 0:n], in_=st[0:P, 0:n], func=AF.Copy), reads=[bst], writes=[bwb])
        return wb[0:P, 0:n].rearrange("p (kc c) -> p kc c", c=ncols), bwb

    def mm_fm(self, ps_ap, bps, wt, bw, mcols, rhs_fn, rhs_bufs, kcs, first=True, last=True):
        S = self.S
        n = len(kcs)
        for i, kc in enumerate(kcs):
            rhs = rhs_fn(kc)
            S.op('pe', lambda e, kc=kc, rhs=rhs, i=i: e.matmul(ps_ap, lhsT=wt[:, kc, mcols], rhs=rhs,
                                                          start=(first and i == 0), stop=(last and i == n - 1)),
                 reads=[bw] + rhs_bufs(kc), writes=[bps], inc=(i == n - 1))

    def rmsnorm(self, src, bsrc, N, gcol, dst, bdst, es):
        S = self.S
        blk = min(N, 512)
        sq = [self.sb(f"rn_sq{i}_{N}", [128, blk], BF16, es) for i in range(2)]
        bsq = [Buf(), Buf()]
        rs = self.sb(f"rn_rs_{N}", [128, blk], F32, es)
        brs = Buf()
        for t in range(N // blk):
            sl = slice(t * blk, (t + 1) * blk)
            ps, bps = self.ps()
            for c in range(NCH):
                j = c % 2
                S.op('act', lambda e, c=c, j=j: e.activation(out=sq[j][:], in_=src[:, c, sl], func=AF.Square),
                     reads=[bsrc[c][t]], writes=[bsq[j]])
                S.op('pe', lambda e, c=c, j=j: e.matmul(ps[:, 0:blk], lhsT=self.onesb[:], rhs=sq[j][:], start=(c == 0), stop=(c == NCH - 1)),
                     reads=[bsq[j], self.bconst], writes=[bps], inc=True)
            S.op('dve', lambda e: e.tensor_scalar(out=rs[:], in0=ps[:, 0:blk], scalar1=1.0 / DM, scalar2=EPS, op0=ALU.mult, op1=ALU.add),
                 reads=[bps], writes=[brs])
            S.op('act', lambda e: e.activation(out=rs[:], in_=rs[:], func=AF.Ln), reads=[brs], writes=[brs])
            S.op('act', lambda e: e.activation(out=rs[:], in_=rs[:], func=AF.Exp, scale=-0.5), reads=[brs], writes=[brs])
            for c in range(NCH):
                S.op('dve', lambda e, c=c: e.scalar_tensor_tensor(out=dst[:, c, sl], in0=src[:, c, sl], scalar=gcol[:, c:c + 1], in1=rs[:],
                                                                   op0=ALU.mult, op1=ALU.mult),
                     reads=[bsrc[c][t], brs, self.bconst], writes=[bdst[c][t]])

    def load_T(self, src_ap, ntiles, dst, bdst_fn):
        S = self.S
        for n in range(ntiles):
            i = n % 2
            st, bst = self.wst[i], self.bwst[i]
            S.dma(st[:, 0:1024], src_ap[n * 128:(n + 1) * 128, :], writes=[bst])
            for h in range(2):
                ps, bps = self.ps()
                for j in range(4):
                    c = 4 * h + j
                    S.op('pe', lambda e, c=c, j=j: e.transpose(out=ps[:, j * 128:(j + 1) * 128], in_=st[:, c * 128:(c + 1) * 128], identity=self.ident[:]),
                         reads=[bst, self.bconst], writes=[bps], inc=(j == 3))
                for j in range(4):
                    c = 4 * h + j
                    if LT_MODE == 0 or (LT_MODE == 2 and j % 2 == 1):
                        S.op('dve', lambda e, c=c, j=j, ps=ps: e.tensor_copy(out=dst[:, c, n * 128:(n + 1) * 128], in_=ps[:, j * 128:(j + 1) * 128]),
                             reads=[bps], writes=bdst_fn(h, n))
                    else:
                        S.op('act', lambda e, c=c, j=j, ps=ps: e.activation(out=dst[:, c, n * 128:(n + 1) * 128], in_=ps[:, j * 128:(j + 1) * 128], func=AF.Copy),
                             reads=[bps], writes=bdst_fn(h, n))

    def outproj(self, wap, r0, nk, mix, bmix):
        S = self.S
        for oc in range(NCH):
            wt, bw = self.wload(wap, r0, nk * 128, oc * 128, 128)
            for tq in range(TQ):
                ps, bps = self.ps()
                sl = slice(tq * 512, (tq + 1) * 512)
                self.mm_fm(ps[:], bps, wt, bw, slice(0, 128), lambda kc: mix[:, kc, sl], lambda kc: [bmix[kc][tq]], list(range(nk)))
                S.op('dve', lambda e, oc=oc, sl=sl, ps=ps: e.tensor_tensor(out=self.xT[:, oc, sl], in0=ps[:], in1=self.xT[:, oc, sl], op=ALU.add),
                     reads=[bps, self.bx[oc][tq]], writes=[self.bx[oc][tq]])
        if BARRIERS:
            S.barrier()

    def proj_fm(self, wap, c0, dst_fn, bdst_fn, func=AF.Copy, eng='act'):
        S = self.S
        wt, bw = self.wload(wap, 0, 1024, c0, 128)
        for tq in range(TQ):
            ps, bps = self.ps()
            sl = slice(tq * 512, (tq + 1) * 512)
            self.mm_fm(ps[:], bps, wt, bw, slice(0, 128), lambda kc: self.hnT[:, kc, sl], lambda kc: [self.bhn[kc][tq]], list(range(NCH)))
            if eng == 'act':
                S.op('act', lambda e, tq=tq, ps=ps: e.activation(out=dst_fn(tq), in_=ps[:], func=func), reads=[bps], writes=bdst_fn(tq))
            else:
                S.op('dve', lambda e, tq=tq, ps=ps: e.tensor_copy(out=dst_fn(tq), in_=ps[:]), reads=[bps], writes=bdst_fn(tq))

    def even_layer(self, li):
        S, d = self.S, self.d
        w_in = d['w_in_ab'][li]
        w_out = d['w_out_ab'][li]
        with ExitStack() as es:
            if 'attnA' in STAGES:
                mix = self.sb("mixA", [128, NCH, SEQ], BF16, es)
                bmix = self.grid(NCH, TQ)
                self.attnA(li, w_in, mix, bmix)
                self.outproj(w_out, 0, 8, mix, bmix)
        if 'poolB' not in STAGES:
            return
        with ExitStack() as es:
            mix = self.sb("mixB", [128, NCH, SEQ], BF16, es)
            bmix = self.grid(NCH, TQ)
            self.poolB(li, w_in, mix, bmix)
            self.outproj(w_out, 1024, 8, mix, bmix)

    def attnA(self, li, w_in, mix, bmix):
        S = self.S
        with ExitStack() as es:
            qT = self.sb("qT", [128, SEQ], BF16, es)
            kT = self.sb("kT", [128, SEQ], BF16, es)
            gT = self.sb("gT", [128, SEQ], BF16, es)
            bq, bk, bg = self.grid(TQ), self.grid(TQ), self.grid(TQ)
            Va = self.sb("Vaug", [128, 3, 16, 2, 128], BF16, es)
            bVa = self.grid(3, 4)
            bVones = Buf()
            S.op('pool', lambda e: e.memset(Va[:, :, :, 0, 64:128], 1.0), writes=[bVones])
            S.op('pool', lambda e: e.memset(Va[:, :, :, 1, 0:64], 1.0), writes=[bVones])
            pT = [self.sb(f"pT{i}", [128, 256], BF16, es) for i in range(4)]
            bpT = [Buf() for _ in range(4)]
            eT = [self.sb(f"eT{i}", [128, 256], BF16, es) for i in range(4)]
            beT = [Buf() for _ in range(4)]
            rden = self.sb("rden", [128, 512], F32, es)
            tmpn = self.sb("tmpn", [128, 512], F32, es)
            brden, btmpn = Buf(), Buf()
            pti = 0
            for c in range(NCH):
                self.ps_rot(range(8))
                self.proj_fm(w_in, c * 128, lambda tq: qT[:, tq * 512:(tq + 1) * 512], lambda tq: [bq[tq]])
                self.proj_fm(w_in, 1024 + c * 128, lambda tq: kT[:, tq * 512:(tq + 1) * 512], lambda tq: [bk[tq]], eng='dve')
                self.proj_fm(w_in, 3072 + c * 128, lambda tq: gT[:, tq * 512:(tq + 1) * 512], lambda tq: [bg[tq]], func=AF.Silu)
                wv, bwv = self.wload(w_in, 0, 1024, 2048 + c * 128, 128)
                for o, dd in enumerate((1, 4, 16)):
                    nb = 16 // dd
                    for t4 in range(4):
                        ps, bps = self.ps()
                        for j in range(4):
                            ti = 4 * t4 + j
                            r, kb = ti // nb, ti % nb
                            st = r + dd * 128 * kb
                            tok = slice(st, st + dd * 127 + 1, dd)
                            tqs = sorted(set([(st) // 512, (st + dd * 127) // 512])) if dd < 16 else [0, 1, 2, 3]
                            for kc in range(NCH):
                                S.op('pe', lambda e, kc=kc, tok=tok, j=j, ps=ps: e.matmul(ps[:, j * 128:(j + 1) * 128], lhsT=self.hnT[:, kc, tok], rhs=wv[:, kc, :],
                                                                                    start=(kc == 0), stop=(kc == NCH - 1)),
                                     reads=[bwv] + [self.bhn[kc][q_] for q_ in tqs], writes=[bps], inc=(kc == NCH - 1 and j == 3))
                        psv = ps[:].rearrange("p (j h d) -> p j h d", j=4, h=2)
                        S.op('act', lambda e, o=o, t4=t4, psv=psv: e.activation(out=Va[:, o, 4 * t4:4 * t4 + 4, 0, 0:64], in_=psv[:, :, 0, :], func=AF.Copy),
                             reads=[bps], writes=[bVa[o][t4]])
                        S.op('dve', lambda e, o=o, t4=t4, psv=psv: e.tensor_copy(out=Va[:, o, 4 * t4:4 * t4 + 4, 1, 64:128], in_=psv[:, :, 1, :]),
                             reads=[bps], writes=[bVa[o][t4]])
                for hh in range(2):
                    hp = slice(hh * 64, hh * 64 + 64)
                    dp = slice(64 - hh * 64, 128 - hh * 64)
                    nd = [self.P[i] for i in range(4)]
                    bnd = [self.bP[i] for i in range(4)]
                    started = [False] * 4
                    self.ps_rot(range(4, 8))
                    tiles = []
                    for o, dd in enumerate((1, 4, 16)):
                        nb = 16 // dd
                        for r in range(dd):
                            for kb in range(nb):
                                tiles.append((o, dd, nb, r, kb))
                    LA = 3
                    pend = {}

                    def issue_S(idx):
                        nonlocal pti
                        o, dd, nb, r, kb = tiles[idx]
                        nq = 2 if kb < nb - 1 else 1
                        N = 128 * nq
                        kst = r + dd * 128 * kb
                        ktok = slice(kst, kst + dd * 127 + 1, dd)
                        qtok = slice(kst, kst + dd * (N - 1) + 1, dd)
                        ktqs = [kst // 512] if dd < 16 else [0, 1, 2, 3]
                        qtqs = sorted(set([kst // 512, (kst + dd * (N - 1)) // 512])) if dd < 16 else [0, 1, 2, 3]
                        ps, bps = self.ps()
                        S.op('pe', lambda e: e.matmul(ps[:, 0:N], lhsT=kT[hp, ktok], rhs=qT[hp, qtok], start=True, stop=True),
                             reads=[bk[q_] for q_ in ktqs] + [bq[q_] for q_ in qtqs], writes=[bps], inc=True)
                        p_, bp_ = pT[pti], bpT[pti]
                        e_, be_ = eT[pti], beT[pti]
                        pti = (pti + 1) % len(pT)
                        S.op('act', lambda e: e.activation(out=e_[:, 0:N], in_=ps[:, 0:N], func=AF.Exp, scale=0.125),
                             reads=[bps], writes=[be_])
                        S.op('dve', lambda e: e.tensor_tensor(out=p_[:, 0:N], in0=e_[:, 0:N], in1=self.mask01[:, 0:N], op=ALU.mult),
                             reads=[be_, self.bconst], writes=[bp_])
                        pend[idx] = (p_, bp_, nq)

                    def issue_PV(idx):
                        o, dd, nb, r, kb = tiles[idx]
                        p_, bp_, nq = pend.pop(idx)
                        ti = r * nb + kb
                        lhsV = Va[:, o, ti, hh, :]
                        allouts = []
                        if dd == 1 and nq == 2 and kb % 4 != 3:
                            allouts = [(kb // 4, slice((kb % 4) * 128, (kb % 4) * 128 + 256), slice(0, 256))]
                            nq = 0
                        for qi in range(nq):
                            i = kb + qi
                            if dd == 1:
                                allouts += [(i // 4, slice((i % 4) * 128, (i % 4) * 128 + 128), slice(qi * 128, qi * 128 + 128))]
                            elif dd == 4:
                                allouts += [(i, slice(r, 512, 4), slice(qi * 128, qi * 128 + 128))]
                            else:
                                allouts += [(j, slice(r, 512, 16), slice(j * 32, j * 32 + 32)) for j in range(4)]
                        for n_, (bank, ocols, pcols) in enumerate(allouts):
                            first = not started[bank]
                            started[bank] = True
                            S.op('pe', lambda e: e.matmul(nd[bank][:, ocols], lhsT=lhsV, rhs=p_[:, pcols], start=first, stop=False, skip_group_check=True),
                                 reads=[bp_, bVa[o][ti // 4], bVones], writes=[bnd[bank]], inc=(n_ == len(allouts) - 1))

                    for idx in range(len(tiles) + LA):
                        if idx < len(tiles):
                            issue_S(idx)
                        if idx >= LA:
                            issue_PV(idx - LA)
                    for tq in range(TQ if 'norm' not in ATT_SKIP else 0):
                        sl = slice(tq * 512, (tq + 1) * 512)
                        S.op('act', lambda e, tq=tq: e.activation(out=rden[hp, :], in_=nd[tq][dp, :], func=AF.Ln), reads=[bnd[tq]], writes=[brden])
                        S.op('act', lambda e: e.activation(out=rden[hp, :], in_=rden[hp, :], func=AF.Exp, scale=-1.0), reads=[brden], writes=[brden])
                        S.op('dve', lambda e, tq=tq: e.tensor_tensor(out=tmpn[hp, :], in0=nd[tq][hp, :], in1=rden[hp, :], op=ALU.mult),
                             reads=[bnd[tq], brden], writes=[btmpn])
                        S.op('pool', lambda e, sl=sl: e.tensor_tensor(out=mix[hp, c, sl], in0=tmpn[hp, :], in1=gT[hp, sl], op=ALU.mult),
                             reads=[btmpn, bg[tq]], writes=[bmix[c][tq]])
            self.ps_rot(range(8))

    def poolB(self, li, w_in, mix, bmix):
        S = self.S
        with ExitStack() as es:
            vb = self.sb("vb", [128, 16 + SEQ], F32, es)
            sA = self.sb("sA", [128, 16 + SEQ], F32, es)
            sB = self.sb("sB", [128, 16 + SEQ], F32, es)
            bvb, bsA, bsB = Buf(), Buf(), Buf()
            pooled = self.sb("pooled", [128, 2, SEQ], BF16, es)
            bpo = self.grid(2, TQ)
            gb = self.sb("gbT", [128, 2, SEQ], BF16, es)
            bgb = self.grid(2, TQ)
            t16 = self.sb("t16", [128, 16], F32, es)
            bt16 = Buf()
            for t_ in (vb, sA, sB):
                S.op('pool', lambda e, t_=t_: e.memset(t_[:, 0:16], 0.0), writes=[bvb, bsA, bsB])
            for g in range(4):
                w = (2, 4, 8, 16)[g]
                for j in range(2):
                    cb = 2 * g + j
                    self.proj_fm(w_in, 4096 + cb * 128, lambda tq: vb[:, 16 + tq * 512:16 + (tq + 1) * 512], lambda tq: [bvb])
                    self.proj_fm(w_in, 5120 + cb * 128, lambda tq, j=j: gb[:, j, tq * 512:(tq + 1) * 512], lambda tq, j=j: [bgb[j][tq]], func=AF.Silu)
                    cur, bcur = vb, bvb
                    k = 1
                    nxt = [(sA, bsA), (sB, bsB)]
                    ni = 0
                    while k < w:
                        o_, bo_ = nxt[ni]
                        ni = 1 - ni
                        S.op('pool', lambda e, o_=o_, cur=cur, k=k: e.tensor_tensor(out=o_[:, 16:16 + SEQ], in0=cur[:, 16:16 + SEQ], in1=cur[:, 16 - k:16 - k + SEQ], op=ALU.add),
                             reads=[bcur], writes=[bo_])
                        cur, bcur = o_, bo_
                        k *= 2
                    S.op('dve', lambda e, cur=cur, j=j, w=w: e.scalar_tensor_tensor(out=pooled[:, j, :], in0=cur[:, 16:16 + SEQ], scalar=1.0 / w, in1=vb[:, 16:16 + SEQ],
                                                                                 op0=ALU.mult, op1=ALU.subtract),
                         reads=[bcur, bvb], writes=[bpo[j][t] for t in range(TQ)])
                    S.op('dve', lambda e, cur=cur, g=g: e.tensor_tensor(out=t16[:], in0=cur[:, 16:32], in1=self.invcnt[:, g * 16:(g + 1) * 16], op=ALU.mult),
                         reads=[bcur, self.bconst], writes=[bt16])
                    S.op('dve', lambda e, j=j: e.tensor_tensor(out=pooled[:, j, 0:16], in0=t16[:], in1=vb[:, 16:32], op=ALU.subtract),
                         reads=[bt16, bvb], writes=[bpo[j][0]])
                wp, bwp = self.wload(self.d['pool_w'][li][g], 0, 256, 0, 256)
                for oc2 in range(2):
                    cb = 2 * g + oc2
                    for tq in range(TQ):
                        sl = slice(tq * 512, (tq + 1) * 512)
                        ps, bps = self.ps()
                        self.mm_fm(ps[:], bps, wp, bwp, slice(oc2 * 128, oc2 * 128 + 128), lambda kc: pooled[:, kc, sl], lambda kc: [bpo[kc][tq]], [0, 1])
                        S.op('dve', lambda e, ps=ps, cb=cb, oc2=oc2, sl=sl: e.scalar_tensor_tensor(out=mix[:, cb, sl], in0=ps[:], scalar=self.vec['pool_scale'][:, li, cb:cb + 1],
                                                                                           in1=gb[:, oc2, sl], op0=ALU.mult, op1=ALU.mult),
                             reads=[bps, bgb[oc2][tq], self.bconst], writes=[bmix[cb][tq]])

    def odd_layer(self, li):
        d = self.d
        if 's5D' in STAGES:
            self.s5D(li, d['w_in_cd'][li], d['w_out_cd'][li])
        if 'sguC' in STAGES:
            self.sguC(li, d['w_in_cd'][li], d['w_out_cd'][li])

    def trig(self, y, by, yi, byi, yf, byf, cs, bcs, sn, bsn, on_act=False):
        S = self.S
        if on_act:
            S.op('act', lambda e: e.activation(out=yf, in_=y, func=AF.Identity, bias=self.magic[:, 0:1], scale=1.0), reads=[by, self.bconst], writes=[byf])
            S.op('act', lambda e: e.activation(out=yf, in_=yf, func=AF.Identity, bias=self.magic[:, 1:2], scale=1.0), reads=[byf, self.bconst], writes=[byf])
        else:
            S.op('dve', lambda e: e.tensor_copy(out=yi, in_=y), reads=[by], writes=[byi])
            S.op('dve', lambda e: e.tensor_copy(out=yf, in_=yi), reads=[byi], writes=[byf])
        S.op('dve', lambda e: e.tensor_tensor(out=yf, in0=y, in1=yf, op=ALU.subtract), reads=[by, byf], writes=[byf])
        S.op('act', lambda e: e.activation(out=sn, in_=yf, func=AF.Sin, scale=TWO_PI), reads=[byf], writes=[bsn])
        S.op('act', lambda e: e.activation(out=y, in_=yf, func=AF.Abs), reads=[byf], writes=[by])
        S.op('act', lambda e: e.activation(out=cs, in_=y, func=AF.Sin, scale=-TWO_PI, bias=self.halfpi[:, 0:1]), reads=[by, self.bconst], writes=[bcs])

    def s5D(self, li, w_in, w_out):
        S, d = self.S, self.d
        with ExitStack() as es:
            sb = lambda n, sh, dt=F32, es_=es: self.sb(n, sh, dt, es_)
            xd = sb("xdT", [128, 4, SEQ], BF16)
            bxd = self.grid(4, TQ)
            yg = sb("ygT", [128, 4, SEQ], BF16)
            byg = self.grid(4, TQ)
            for kc in range(4):
                self.proj_fm(w_in, 3072 + kc * 128, lambda tq, kc=kc: xd[:, kc, tq * 512:(tq + 1) * 512], lambda tq, kc=kc: [bxd[kc][tq]])
            ar = sb("s_ar", [128, 16]); ai = sb("s_ai", [128, 16]); ldt = sb("s_ldt", [128, 16])
            bprm = Buf()
            pst = sb("s_pst", [32, 128]); ldt0 = sb("s_ldt0", [128, 32])
            bpst = Buf()
            S.dma(pst[:, 0:64], d['s5_a_re'][li], writes=[bpst])
            S.dma(pst[:, 64:128], d['s5_a_im'][li], writes=[bpst])
            S.dma(ldt0[:], d['s5_log_dt'][li].partition_broadcast(128), writes=[bpst])
            for (dst_, c0) in ((ar, 0), (ai, 64)):
                ps, bps = self.ps()
                S.op('pe', lambda e, ps=ps, c0=c0: e.transpose(out=ps[0:64, 0:32], in_=pst[:, c0:c0 + 64], identity=self.ident[0:32, 0:32]), reads=[bpst, self.bconst], writes=[bps])
                S.op('dve', lambda e, ps=ps, dst_=dst_: e.tensor_copy(out=dst_[0:64, :], in_=ps[0:64, 0:32:2]), reads=[bps], writes=[bprm])
                S.op('dve', lambda e, ps=ps, dst_=dst_: e.tensor_copy(out=dst_[64:128, :], in_=ps[0:64, 1:32:2]), reads=[bps], writes=[bprm])
            S.op('dve', lambda e: e.tensor_copy(out=ldt[0:64, :], in_=ldt0[0:64, 0:32:2]), reads=[bpst], writes=[bprm])
            S.op('dve', lambda e: e.tensor_copy(out=ldt[64:128, :], in_=ldt0[64:128, 1:32:2]), reads=[bpst], writes=[bprm])
            names = ["dt", "dtar", "th", "rho", "c0", "s0", "abr", "abi", "inv", "t1", "t2", "cfr", "cfi", "yf", "thn", "y0"]
            T = {n: sb("s_" + n, [128, 16]) for n in names}
            yi0 = sb("s_yi", [128, 16], I32)
            B = {n: Buf() for n in names + ["yi"]}

            def dv(out, fn, reads, eng='dve'):
                S.op(eng, fn, reads=[B[r] if isinstance(r, str) else r for r in reads], writes=[B[out]])
            TT = lambda o, a, b_, op: (lambda e: e.tensor_tensor(out=T[o][:], in0=a[:], in1=b_[:], op=op))
            dv("dt", lambda e: e.activation(out=T["dt"][:], in_=ldt[:], func=AF.Exp), [bprm], 'act')
            dv("dtar", TT("dtar", T["dt"], ar, ALU.mult), ["dt", bprm])
            dv("th", TT("th", T["dt"], ai, ALU.mult), ["dt", bprm])
            dv("rho", lambda e: e.activation(out=T["rho"][:], in_=T["dtar"][:], func=AF.Exp), ["dtar"], 'act')
            dv("thn", lambda e: e.tensor_single_scalar(out=T["thn"][:], in_=T["th"][:], scalar=1.0 / TWO_PI, op=ALU.mult), ["th"])
            dv("y0", lambda e: e.tensor_copy(out=T["y0"][:], in_=T["thn"][:]), ["thn"])
            self.trig(T["y0"][:], B["y0"], yi0[:], B["yi"], T["yf"][:], B["yf"], T["c0"][:], B["c0"], T["s0"][:], B["s0"], on_act=True)
            dv("abr", TT("abr", T["rho"], T["c0"], ALU.mult), ["rho", "c0"])
            dv("abi", TT("abi", T["rho"], T["s0"], ALU.mult), ["rho", "s0"])
            dv("abr", lambda e: e.tensor_single_scalar(out=T["abr"][:], in_=T["abr"][:], scalar=-1.0, op=ALU.add), ["abr"])
            dv("t1", TT("t1", ar, ar, ALU.mult), [bprm])
            dv("t2", TT("t2", ai, ai, ALU.mult), [bprm])
            dv("inv", TT("inv", T["t1"], T["t2"], ALU.add), ["t1", "t2"])
            dv("inv", lambda e: e.reciprocal(out=T["inv"][:], in_=T["inv"][:]), ["inv"])
            dv("t1", TT("t1", T["abr"], ar, ALU.mult), ["abr", bprm])
            dv("t2", TT("t2", T["abi"], ai, ALU.mult), ["abi", bprm])
            dv("cfr", TT("cfr", T["t1"], T["t2"], ALU.add), ["t1", "t2"])
            dv("cfr", TT("cfr", T["cfr"], T["inv"], ALU.mult), ["cfr", "inv"])
            dv("t1", TT("t1", T["abi"], ar, ALU.mult), ["abi", bprm])
            dv("t2", TT("t2", T["abr"], ai, ALU.mult), ["abr", bprm])
            dv("cfi", TT("cfi", T["t1"], T["t2"], ALU.subtract), ["t1", "t2"])
            dv("cfi", TT("cfi", T["cfi"], T["inv"], ALU.mult), ["cfi", "inv"])
            off = sb("s_off", [128, 16, 4])
            boff = Buf()
            for tq in range(TQ):
                S.op('dve', lambda e, tq=tq: e.tensor_single_scalar(out=off[:, :, tq], in_=T["thn"][:], scalar=512.0 * tq, op=ALU.mult), reads=[B["thn"]], writes=[boff])
            Bre = sb("s_Bre", [128, 16, 16]); Bim = sb("s_Bim", [128, 16, 16])
            bB = Buf()
            S.dma(Bre[:], d['s5_b_re'][li].rearrange("(gp g2) p h -> (g2 p) gp h", g2=2), writes=[bB])
            S.dma(Bim[:], d['s5_b_im'][li].rearrange("(gp g2) p h -> (g2 p) gp h", g2=2), writes=[bB])
            bbr = sb("s_bbr", [128, 16, 16]); bbi = sb("s_bbi", [128, 16, 16]); bt = sb("s_bt", [128, 16, 16])
            bbb, bbt = Buf(), Buf()
            cfr_b = T["cfr"][:, :, None].to_broadcast([128, 16, 16])
            cfi_b = T["cfi"][:, :, None].to_broadcast([128, 16, 16])
            S.op('dve', lambda e: e.tensor_tensor(out=bbr[:], in0=Bre[:], in1=cfr_b, op=ALU.mult), reads=[bB, B["cfr"]], writes=[bbb])
            S.op('dve', lambda e: e.tensor_tensor(out=bt[:], in0=Bim[:], in1=cfi_b, op=ALU.mult), reads=[bB, B["cfi"]], writes=[bbt])
            S.op('dve', lambda e: e.tensor_tensor(out=bbr[:], in0=bbr[:], in1=bt[:], op=ALU.subtract), reads=[bbb, bbt], writes=[bbb])
            S.op('dve', lambda e: e.tensor_tensor(out=bbi[:], in0=Bim[:], in1=cfr_b, op=ALU.mult), reads=[bB, B["cfr"]], writes=[bbb])
            S.op('dve', lambda e: e.tensor_tensor(out=bt[:], in0=Bre[:], in1=cfi_b, op=ALU.mult), reads=[bB, B["cfi"], bbb], writes=[bbt])
            S.op('dve', lambda e: e.tensor_tensor(out=bbi[:], in0=bbi[:], in1=bt[:], op=ALU.add), reads=[bbb, bbt], writes=[bbb])
            Cre = sb("s_Cre", [128, 4, 64]); Cim = sb("s_Cim", [128, 4, 64])
            bC = Buf()
            S.dma(Cre[:], d['s5_c_re'][li].rearrange("(kc gl) h p -> (gl h) kc p", kc=4), writes=[bC])
            S.dma(Cim[:], d['s5_c_im'][li].rearrange("(kc gl) h p -> (gl h) kc p", kc=4), writes=[bC])
            Dg = sb("s_Dg", [128, 4, 128], BF16)
            bDg = Buf()
            for kc in range(4):
                S.op('dve', lambda e, kc=kc: e.tensor_single_scalar(out=Dg[:, kc, :], in_=self.ident[:], scalar=self.vec['s5_d'][:, li, kc:kc + 1], op=ALU.mult),
                     reads=[self.bconst], writes=[bDg])
            bm1 = self.bm1[:].rearrange("p (a b) -> p a b", a=4)
            bm2 = self.bm2[:].rearrange("p (a b) -> p a b", a=4)
            with ExitStack() as es2:
                sb2 = lambda n, sh, dt=F32: self.sb(n, sh, dt, es2)
                L = sb2("s_L", [128, 2, 4, 128], BF16)
                Cc = sb2("s_Cc", [128, 2, 4, 128], BF16)
                bL, bCc = Buf(), Buf()
                Z = [sb2(f"s_Z{i}", [128, 128]) for i in range(2)]
                bZ = [Buf(), Buf()]
                zi = 0
                NW = 512
                wkA = {}
                for n in ("y", "yf", "cs", "sn", "br", "bi", "t1", "t2", "t3", "t4"):
                    wkA[n] = (sb2("k_" + n, [128, NW])[:], Buf())
                gl1 = sb2("k_gl1", [128, NW])
                wkA["yi"] = (gl1[:], Buf())
                wkB = {}
                for j, n in enumerate(("y", "yf", "cs", "sn")):
                    wkB[n] = (self.wst[0][:, j * NW:(j + 1) * NW], Buf())
                for j, n in enumerate(("br", "bi", "t1", "t2")):
                    wkB[n] = (self.wst[1][:, j * NW:(j + 1) * NW], Buf())
                wb0f = self.wbf[0][:].bitcast(F32)
                for j, n in enumerate(("t3", "t4")):
                    wkB[n] = (wb0f[:, j * NW:(j + 1) * NW], Buf())
                wkB["yi"] = (self.wbf[1][:].bitcast(I32)[:, 0:NW], Buf())
                wks = [wkA, wkB]
                cur = [0]
                wb1f = self.wbf[1][:].bitcast(F32)
                gl = [wb1f[:, NW:2 * NW], gl1[:]]
                bgl = [Buf(), Buf()]
                self.wfence()
                S.barrier()
                hrb = [sb2(f"k_hr{i}", [128, NW], BF16) for i in range(2)]
                hib = [sb2(f"k_hi{i}", [128, NW], BF16) for i in range(2)]
                bhr, bhi = [Buf(), Buf()], [Buf(), Buf()]
                car = sb2("k_car", [128, 16, 2])
                bcar = [[Buf(), Buf()] for _ in range(16)]
                hidx = 0
                W = lambda n: wks[cur[0]][n][0]
                Bk = lambda n: wks[cur[0]][n][1]
                for kc in range(4):
                    self.ps_rot(range(2, 8))
                    for gpl in range(4):
                        gp = 4 * kc + gpl
                        for ri, src in enumerate((bbr, bbi)):
                            z, bz = Z[zi], bZ[zi]
                            zi = 1 - zi
                            S.op('dve', lambda e, z=z, src=src, gp=gp, gpl=gpl: e.tensor_tensor(
                                out=z[:].rearrange("p (a b) -> p a b", a=8), in0=src[:, gp, None, :].to_broadcast([128, 8, 16]),
                                in1=bm1[:, gpl, :, None].to_broadcast([128, 8, 16]), op=ALU.mult), reads=[bbb, self.bconst], writes=[bz])
                            ps, bps = self.ps()
                            S.op('pe', lambda e, ps=ps, z=z: e.transpose(out=ps[:, 0:128], in_=z[:], identity=self.ident[:]), reads=[bz, self.bconst], writes=[bps])
                            S.op('act', lambda e, ps=ps, ri=ri, gpl=gpl: e.activation(out=L[:, ri, gpl, :], in_=ps[:, 0:128], func=AF.Copy), reads=[bps], writes=[bL])
                        for ri, src in enumerate((Cre, Cim)):
                            z, bz = Z[zi], bZ[zi]
                            zi = 1 - zi
                            S.op('dve', lambda e, z=z, src=src, kc=kc, gpl=gpl: e.tensor_tensor(
                                out=z[:].rearrange("p (a b) -> p a b", a=2), in0=src[:, kc, None, :].to_broadcast([128, 2, 64]),
                                in1=bm2[:, gpl, :, None].to_broadcast([128, 2, 64]), op=ALU.mult), reads=[bC, self.bconst], writes=[bz])
                            ps, bps = self.ps()
                            S.op('pe', lambda e, ps=ps, z=z: e.transpose(out=ps[:, 0:128], in_=z[:], identity=self.ident[:]), reads=[bz, self.bconst], writes=[bps])
                            S.op('dve', lambda e, ps=ps, ri=ri, gpl=gpl: e.tensor_single_scalar(out=Cc[:, ri, gpl, :], in_=ps[:, 0:128], scalar=(1.0 if ri == 0 else -1.0), op=ALU.mult),
                                 reads=[bps], writes=[bCc])
                    its = [(tq, gpl) for tq in range(TQ) for gpl in range(4)]

                    def stageA(n):
                        tq, gpl = its[n]
                        gp = 4 * kc + gpl
                        sl = slice(tq * 512, (tq + 1) * 512)
                        cur[0] = n % 2
                        pr, bpr = self.ps()
                        pi_, bpi = self.ps()
                        S.op('pe', lambda e: e.matmul(pr[:], lhsT=L[:, 0, gpl, :], rhs=xd[:, kc, sl], start=True, stop=True),
                             reads=[bL, bxd[kc][tq]], writes=[bpr])
                        S.op('pe', lambda e: e.matmul(pi_[:], lhsT=L[:, 1, gpl, :], rhs=xd[:, kc, sl], start=True, stop=True),
                             reads=[bL, bxd[kc][tq]], writes=[bpi])
                        S.op('act', lambda e: e.activation(out=W("br"), in_=pr[:], func=AF.Copy), reads=[bpr], writes=[Bk("br")])
                        S.op('act', lambda e: e.activation(out=W("bi"), in_=pi_[:], func=AF.Copy), reads=[bpi], writes=[Bk("bi")])
                        S.op('act', lambda e: e.activation(out=W("y"), in_=self.iota[:], func=AF.Identity, scale=T["thn"][:, gp:gp + 1], bias=off[:, gp, tq:tq + 1]),
                             reads=[self.bconst, B["thn"], boff], writes=[Bk("y")])
                        self.trig(W("y"), Bk("y"), W("yi"), Bk("yi"), W("yf"), Bk("yf"), W("cs"), Bk("cs"), W("sn"), Bk("sn"), on_act=True)

                    def stageB(n):
                        nonlocal hidx
                        tq, gpl = its[n]
                        gp = 4 * kc + gpl
                        sl = slice(tq * 512, (tq + 1) * 512)
                        cur[0] = n % 2
                        psy, bpsy = self.P[tq % 2], self.bP[tq % 2]
                        S.op('dve', lambda e: e.tensor_tensor(out=W("t1"), in0=W("br"), in1=W("cs"), op=ALU.mult), reads=[Bk("br"), Bk("cs")], writes=[Bk("t1")])
                        S.op('dve', lambda e: e.tensor_tensor(out=W("t2"), in0=W("bi"), in1=W("sn"), op=ALU.mult), reads=[Bk("bi"), Bk("sn")], writes=[Bk("t2")])
                        S.op('dve', lambda e: e.tensor_tensor(out=W("t1"), in0=W("t1"), in1=W("t2"), op=ALU.add), reads=[Bk("t1"), Bk("t2")], writes=[Bk("t1")])
                        S.op('dve', lambda e: e.tensor_tensor(out=W("t3"), in0=W("bi"), in1=W("cs"), op=ALU.mult), reads=[Bk("bi"), Bk("cs")], writes=[Bk("t3")])
                        S.op('dve', lambda e: e.tensor_tensor(out=W("t4"), in0=W("br"), in1=W("sn"), op=ALU.mult), reads=[Bk("br"), Bk("sn")], writes=[Bk("t4")])
                        S.op('dve', lambda e: e.tensor_tensor(out=W("t3"), in0=W("t3"), in1=W("t4"), op=ALU.subtract), reads=[Bk("t3"), Bk("t4")], writes=[Bk("t3")])
                        rho_b = T["rho"][:, gp:gp + 1].to_broadcast([128, NW])
                        for (src, dst, ci) in (("t1", "br", 0), ("t3", "bi", 1)):
                            init = 0.0 if tq == 0 else car[:, gp, ci:ci + 1]
                            S.op('dve', lambda e: e.tensor_tensor_scan(out=W(dst), data0=rho_b, data1=W(src), initial=init, op0=ALU.mult, op1=ALU.add),
                                 reads=[Bk(src), B["rho"], bcar[gp][ci]], writes=[Bk(dst)])
                            S.op('act', lambda e: e.activation(out=car[:, gp, ci:ci + 1], in_=W(dst)[:, NW - 1:NW], func=AF.Copy),
                                 reads=[Bk(dst)], writes=[bcar[gp][ci]])
                        hr, hi_, bhr_, bhi_ = hrb[hidx], hib[hidx], bhr[hidx], bhi[hidx]
                        hidx = 1 - hidx
                        S.op('dve', lambda e: e.tensor_tensor(out=W("t1"), in0=W("br"), in1=W("cs"), op=ALU.mult), reads=[Bk("br"), Bk("cs")], writes=[Bk("t1")])
                        S.op('dve', lambda e: e.tensor_tensor(out=W("t2"), in0=W("bi"), in1=W("sn"), op=ALU.mult), reads=[Bk("bi"), Bk("sn")], writes=[Bk("t2")])
                        S.op('dve', lambda e: e.tensor_tensor(out=hr[:], in0=W("t1"), in1=W("t2"), op=ALU.subtract), reads=[Bk("t1"), Bk("t2")], writes=[bhr_])
                        S.op('dve', lambda e: e.tensor_tensor(out=W("t3"), in0=W("bi"), in1=W("cs"), op=ALU.mult), reads=[Bk("bi"), Bk("cs")], writes=[Bk("t3")])
                        S.op('dve', lambda e: e.tensor_tensor(out=W("t4"), in0=W("br"), in1=W("sn"), op=ALU.mult), reads=[Bk("br"), Bk("sn")], writes=[Bk("t4")])
                        S.op('dve', lambda e: e.tensor_tensor(out=hi_[:], in0=W("t3"), in1=W("t4"), op=ALU.add), reads=[Bk("t3"), Bk("t4")], writes=[bhi_])
                        S.op('pe', lambda e: e.matmul(psy[:], lhsT=Cc[:, 0, gpl, :], rhs=hr[:], start=(gpl == 0), stop=False),
                             reads=[bCc, bhr_], writes=[bpsy])
                        S.op('pe', lambda e: e.matmul(psy[:], lhsT=Cc[:, 1, gpl, :], rhs=hi_[:], start=False, stop=False),
                             reads=[bCc, bhi_], writes=[bpsy])
                        if gpl == 3:
                            S.op('pe', lambda e: e.matmul(psy[:], lhsT=Dg[:, kc, :], rhs=xd[:, kc, sl], start=False, stop=True),
                                 reads=[bDg, bxd[kc][tq]], writes=[bpsy])
                            S.op('act', lambda e: e.activation(out=gl[0], in_=psy[:], func=AF.Copy), reads=[bpsy], writes=[bgl[0]])
                            S.op('dve', lambda e: e.tensor_tensor(out=gl[1], in0=gl[0], in1=gl[0], op=ALU.mult), reads=[bgl[0]], writes=[bgl[1]])
                            S.op('dve', lambda e: e.tensor_scalar(out=gl[1], in0=gl[1], scalar1=0.044715, scalar2=1.0, op0=ALU.mult, op1=ALU.add), reads=[bgl[1]], writes=[bgl[1]])
                            S.op('dve', lambda e: e.tensor_tensor(out=gl[1], in0=gl[1], in1=gl[0], op=ALU.mult), reads=[bgl[1], bgl[0]], writes=[bgl[1]])
                            S.op('act', lambda e: e.activation(out=gl[1], in_=gl[1], func=AF.Sigmoid, scale=1.5957691216057308), reads=[bgl[1]], writes=[bgl[1]])
                            S.op('dve', lambda e: e.tensor_tensor(out=yg[:, kc, sl], in0=gl[1], in1=gl[0], op=ALU.mult),
                                 reads=[bgl[1], bgl[0]], writes=[byg[kc][tq]])

                    for n in range(len(its) + 1):
                        if n < len(its):
                            stageA(n)
                        if n >= 1:
                            stageB(n - 1)
                self.ps_rot(range(8))
            S.barrier()
            with ExitStack() as es3:
                sb3 = lambda n, sh, dt=F32: self.sb(n, sh, dt, es3)
                gd = sb3("gdT", [128, SEQ], BF16)
                bgd = self.grid(TQ)
                s2 = sb3("glu_s2", [128, 512]); t2_ = sb3("glu_t", [128, 512])
                bs2, bt2 = Buf(), Buf()
                for oc in range(4):
                    self.proj_fm(w_in, 3584 + oc * 128, lambda tq: gd[:, tq * 512:(tq + 1) * 512], lambda tq: [bgd[tq]], func=AF.Silu)
                    w1, bw1 = self.wload(d['glu_w1'][li], 0, 512, oc * 128, 128)
                    w2, bw2 = self.wload(d['glu_w2'][li], 0, 512, oc * 128, 128, prefetch=False)
                    for tq in range(TQ):
                        sl = slice(tq * 512, (tq + 1) * 512)
                        p1, bp1 = self.ps()
                        p2, bp2 = self.ps()
                        self.mm_fm(p1[:], bp1, w1, bw1, slice(0, 128), lambda kc: yg[:, kc, sl], lambda kc: [byg[kc][tq]], [0, 1, 2, 3])
                        self.mm_fm(p2[:], bp2, w2, bw2, slice(0, 128), lambda kc: yg[:, kc, sl], lambda kc: [byg[kc][tq]], [0, 1, 2, 3])
                        S.op('act', lambda e, p2=p2: e.activation(out=s2[:], in_=p2[:], func=AF.Sigmoid), reads=[bp2], writes=[bs2])
                        S.op('dve', lambda e, p1=p1: e.tensor_tensor(out=t2_[:], in0=p1[:], in1=s2[:], op=ALU.mult), reads=[bp1, bs2], writes=[bt2])
                        S.op('pool', lambda e, oc=oc, sl=sl: e.tensor_tensor(out=xd[:, oc, sl], in0=t2_[:], in1=gd[:, sl], op=ALU.mult),
                             reads=[bt2, bgd[tq]], writes=[bxd[oc][tq]])
            self.outproj(w_out, 1024, 4, xd, bxd)

    def sguC(self, li, w_in, w_out):
        S, d = self.S, self.d
        with ExitStack() as es:
            sb = lambda n, sh, dt=F32, es_=es: self.sb(n, sh, dt, es_)
            vn = sb("vn", [128, 16, DM], BF16)
            bvn = self.grid(16)
            with ExitStack() as es2:
                sb2 = lambda n, sh, dt=F32: self.sb(n, sh, dt, es2)
                ssum = sb2("c_ssum", [128, 16, 4]); ssq = sb2("c_ssq", [128, 16, 4])
                bst = Buf()
                junk = sb2("c_junk", [128, 256], BF16)
                bjunk = Buf()
                S.op('dve', lambda e: e.memset(ssum[:], 0.0), writes=[bst])
                S.op('dve', lambda e: e.memset(ssq[:], 0.0), writes=[bst])
                for q in range(4):
                    wt, bw = self.wload(w_in, 0, 1024, 1024 + q * 256, 256)
                    for n in range(16):
                        ps, bps = self.ps()
                        tok = slice(n * 128, (n + 1) * 128)
                        for kc in range(NCH):
                            S.op('pe', lambda e, ps=ps, kc=kc, tok=tok, wt=wt: e.matmul(ps[:, 0:256], lhsT=self.hnT[:, kc, tok], rhs=wt[:, kc, :], start=(kc == 0), stop=(kc == NCH - 1)),
                                 reads=[bw, self.bhn[kc][n // 4]], writes=[bps], inc=(kc == NCH - 1))
                        S.op('act', lambda e, ps=ps, n=n, q=q: e.activation(out=vn[:, n, q * 256:(q + 1) * 256], in_=ps[:, 0:256], func=AF.Copy, accum_out=ssum[:, n, q:q + 1]),
                             reads=[bps, bst], writes=[bvn[n], bst])
                        S.op('act', lambda e, ps=ps, n=n, q=q: e.activation(out=junk[:], in_=ps[:, 0:256], func=AF.Square, accum_out=ssq[:, n, q:q + 1]),
                             reads=[bps, bst], writes=[bjunk, bst])
                mean = sb2("c_mean", [128, 16]); var = sb2("c_var", [128, 16]); m2 = sb2("c_m2", [128, 16])
                S.op('dve', lambda e: e.tensor_tensor(out=ssum[:, :, 0:2], in0=ssum[:, :, 0:2], in1=ssum[:, :, 2:4], op=ALU.add), reads=[bst], writes=[bst])
                S.op('dve', lambda e: e.tensor_tensor(out=mean[:], in0=ssum[:, :, 0], in1=ssum[:, :, 1], op=ALU.add), reads=[bst], writes=[bst])
                S.op('dve', lambda e: e.tensor_single_scalar(out=mean[:], in_=mean[:], scalar=1.0 / DM, op=ALU.mult), reads=[bst], writes=[bst])
                S.op('dve', lambda e: e.tensor_tensor(out=ssq[:, :, 0:2], in0=ssq[:, :, 0:2], in1=ssq[:, :, 2:4], op=ALU.add), reads=[bst], writes=[bst])
                S.op('dve', lambda e: e.tensor_tensor(out=var[:], in0=ssq[:, :, 0], in1=ssq[:, :, 1], op=ALU.add), reads=[bst], writes=[bst])
                S.op('dve', lambda e: e.tensor_tensor(out=m2[:], in0=mean[:], in1=mean[:], op=ALU.mult), reads=[bst], writes=[bst])
                S.op('dve', lambda e: e.scalar_tensor_tensor(out=var[:], in0=var[:], scalar=1.0 / DM, in1=m2[:], op0=ALU.mult, op1=ALU.subtract), reads=[bst], writes=[bst])
                S.op('dve', lambda e: e.tensor_single_scalar(out=var[:], in_=var[:], scalar=EPS, op=ALU.add), reads=[bst], writes=[bst])
                S.op('act', lambda e: e.activation(out=var[:], in_=var[:], func=AF.Ln), reads=[bst], writes=[bst])
                S.op('act', lambda e: e.activation(out=var[:], in_=var[:], func=AF.Exp, scale=-0.5), reads=[bst], writes=[bst])
                lng = sb2("c_lng", [128, DM]); lnb = sb2("c_lnb", [128, DM])
                bln = Buf()
                S.dma(lng[:], d['sgu_ln_g'][li].partition_broadcast(128), writes=[bln])
                S.dma(lnb[:], d['sgu_ln_b'][li].partition_broadcast(128), writes=[bln])
                tmp = [sb2(f"c_tmp{i}", [128, DM]) for i in range(2)]
                btmp = [Buf(), Buf()]
                for n in range(16):
                    t_, bt_ = tmp[n % 2], btmp[n % 2]
                    S.op('dve', lambda e, n=n, t_=t_: e.tensor_scalar(out=t_[:], in0=vn[:, n, :], scalar1=mean[:, n:n + 1], scalar2=var[:, n:n + 1], op0=ALU.subtract, op1=ALU.mult),
                         reads=[bvn[n], bst], writes=[bt_])
                    S.op('pool', lambda e, t_=t_: e.tensor_tensor(out=t_[:], in0=t_[:], in1=lng[:], op=ALU.mult), reads=[bt_, bln], writes=[bt_])
                    S.op('pool', lambda e, n=n, t_=t_: e.tensor_tensor(out=vn[:, n, :], in0=t_[:], in1=lnb[:], op=ALU.add), reads=[bt_, bln], writes=[bvn[n]])
            S.barrier()
            mix = sb("mixC", [128, NCH, SEQ], BF16)
            bmix = self.grid(NCH, TQ)
            wsT = sb("c_wsT", [128, 4, 128], BF16)
            bws = Buf()
            bsb = sb("c_bsb", [128, 4, 128])
            bbs = Buf()
            for g in range(4):
                i = g % 2
                st, bst_ = self.wst[i], self.bwst[i]
                S.dma(st[:, 0:128], d['sgu_w'][li][g], writes=[bst_])
                ps, bps = self.ps()
                S.op('pe', lambda e, ps=ps, st=st: e.transpose(out=ps[:, 0:128], in_=st[:, 0:128], identity=self.ident[:]), reads=[bst_, self.bconst], writes=[bps])
                S.op('dve', lambda e, ps=ps, g=g: e.tensor_tensor(out=wsT[:, g, :], in0=ps[:, 0:128], in1=self.triu[:], op=ALU.mult), reads=[bps, self.bconst], writes=[bws])
                S.dma(bsb[:, g, :], d['sgu_b'][li][g].partition_broadcast(128), writes=[bbs])
            ta = [sb(f"c_ta{i}", [128, 512]) for i in range(2)]
            bta = [Buf(), Buf()]
            sg = [sb(f"c_sg{i}", [128, 512]) for i in range(2)]
            bsg = [Buf(), Buf()]
            k = 0
            for c in range(NCH):
                g = c // 2
                wu, bwu = self.wload(w_in, 0, 1024, c * 128, 128)
                wg, bwg = self.wload(w_in, 0, 1024, 2048 + c * 128, 128, prefetch=False)
                for tq in range(TQ):
                    sl = slice(tq * 512, (tq + 1) * 512)
                    a_, ba_, s_, bs_ = ta[k], bta[k], sg[k], bsg[k]
                    k = 1 - k
                    ps, bps = self.ps()
                    for j in range(4):
                        n = 4 * tq + j
                        S.op('pe', lambda e, ps=ps, j=j, n=n, c=c, g=g: e.matmul(ps[:, j * 128:(j + 1) * 128], lhsT=vn[:, n, c * 128:(c + 1) * 128], rhs=wsT[:, g, :], start=True, stop=True),
                             reads=[bvn[n], bws], writes=[bps], inc=(j == 3))
                    S.op('dve', lambda e, ps=ps, g=g, a_=a_: e.tensor_tensor(out=a_[:].rearrange("p (a b) -> p a b", a=4), in0=ps[:].rearrange("p (a b) -> p a b", a=4),
                                                                        in1=bsb[:, g, None, :].to_broadcast([128, 4, 128]), op=ALU.add), reads=[bps, bbs], writes=[ba_])
                    pu, bpu = self.ps()
                    self.mm_fm(pu[:], bpu, wu, bwu, slice(0, 128), lambda kc: self.hnT[:, kc, sl], lambda kc: [self.bhn[kc][tq]], list(range(NCH)))
                    S.op('dve', lambda e, pu=pu, a_=a_: e.tensor_tensor(out=a_[:], in0=pu[:], in1=a_[:], op=ALU.mult), reads=[bpu, ba_], writes=[ba_])
                    pg, bpg = self.ps()
                    self.mm_fm(pg[:], bpg, wg, bwg, slice(0, 128), lambda kc: self.hnT[:, kc, sl], lambda kc: [self.bhn[kc][tq]], list(range(NCH)))
                    S.op('act', lambda e, pg=pg, s_=s_: e.activation(out=s_[:], in_=pg[:], func=AF.Silu), reads=[bpg], writes=[bs_])
                    S.op('pool', lambda e, a_=a_, s_=s_, sl=sl, c=c: e.tensor_tensor(out=mix[:, c, sl], in0=a_[:], in1=s_[:], op=ALU.mult), reads=[ba_, bs_], writes=[bmix[c][tq]])
            self.outproj(w_out, 0, 8, mix, bmix)

    def cross(self, l):
        S, d = self.S, self.d
        with ExitStack() as es:
            sb = lambda n, sh, dt=F32: self.sb(n, sh, dt, es)
            with ExitStack() as es2:
                self.rmsnorm(self.xT, self.bx, SEQ, self.vec['norm_x'][:, l, :], self.hnT, self.bhn, es2)
            S.barrier()
            qT = sb("xqT", [128, 2, SEQ], BF16)
            bq = self.grid(2, TQ)
            mix = sb("mixX", [128, NCH, SEQ], BF16)
            bmix = self.grid(NCH, TQ)
            KT = sb("xKT", [128, NCH, 256], BF16)
            bKT = self.grid(NCH)
            Vx = sb("xV", [128, 2, DM], BF16)
            bVx = self.grid(2)
            pT = [sb(f"xpT{i}", [128, 2, 512], BF16) for i in range(2)]
            bpT = [Buf(), Buf()]
            rden = sb("xrden", [128, 512])
            brden = Buf()
            wkv = d['w_xkv'][l]
            for c in range(NCH):
                wt, bw = self.wload(wkv, 0, 1024, c * 128, 128)
                ps, bps = self.ps()
                self.mm_fm(ps[:, 0:256], bps, wt, bw, slice(0, 128), lambda kc: self.memT[:, kc, :], lambda kc: [self.bmem], list(range(NCH)))
                S.op('act', lambda e, ps=ps, c=c: e.activation(out=KT[:, c, :], in_=ps[:, 0:256], func=AF.Copy), reads=[bps], writes=[bKT[c]])
            for q in range(4):
                wt, bw = self.wload(wkv, 0, 1024, 1024 + q * 256, 256)
                for mt in range(2):
                    ps, bps = self.ps()
                    for kc in range(NCH):
                        S.op('pe', lambda e, ps=ps, kc=kc, mt=mt, wt=wt: e.matmul(ps[:, 0:256], lhsT=self.memT[:, kc, mt * 128:(mt + 1) * 128], rhs=wt[:, kc, :], start=(kc == 0), stop=(kc == NCH - 1)),
                             reads=[bw, self.bmem], writes=[bps], inc=(kc == NCH - 1))
                    S.op('act', lambda e, ps=ps, mt=mt, q=q: e.activation(out=Vx[:, mt, q * 256:(q + 1) * 256], in_=ps[:, 0:256], func=AF.Copy), reads=[bps], writes=[bVx[mt]])
            pi = 0
            for h in range(4):
                for k2 in range(2):
                    self.proj_fm(d['w_xq'][l], (2 * h + k2) * 128, lambda tq, k2=k2: qT[:, k2, tq * 512:(tq + 1) * 512], lambda tq, k2=k2: [bq[k2][tq]], eng=('act' if k2 == 0 else 'dve'))
                for tq in range(TQ):
                    sl = slice(tq * 512, (tq + 1) * 512)
                    p_, bp_ = pT[pi], bpT[pi]
                    pi = 1 - pi
                    for mt in range(2):
                        ps, bps = self.ps()
                        for k2 in range(2):
                            cc = 2 * h + k2
                            S.op('pe', lambda e, ps=ps, cc=cc, mt=mt, k2=k2, sl=sl: e.matmul(ps[:], lhsT=KT[:, cc, mt * 128:(mt + 1) * 128], rhs=qT[:, k2, sl], start=(k2 == 0), stop=(k2 == 1)),
                                 reads=[bKT[cc], bq[k2][tq]], writes=[bps], inc=(k2 == 1))
                        S.op('act', lambda e, ps=ps, mt=mt, p_=p_: e.activation(out=p_[:, mt, :], in_=ps[:], func=AF.Exp, scale=1.0 / 16.0), reads=[bps], writes=[bp_])
                    psd, bpsd = self.ps()
                    for mt in range(2):
                        S.op('pe', lambda e, psd=psd, mt=mt, p_=p_: e.matmul(psd[:], lhsT=self.onesb[:], rhs=p_[:, mt, :], start=(mt == 0), stop=(mt == 1)),
                             reads=[bp_, self.bconst], writes=[bpsd], inc=(mt == 1))
                    S.op('dve', lambda e, psd=psd: e.reciprocal(out=rden[:], in_=psd[:]), reads=[bpsd], writes=[brden])
                    for dc in range(2):
                        cc = 2 * h + dc
                        pso, bpso = self.ps()
                        for mt in range(2):
                            S.op('pe', lambda e, pso=pso, mt=mt, p_=p_, cc=cc: e.matmul(pso[:], lhsT=Vx[:, mt, cc * 128:(cc + 1) * 128], rhs=p_[:, mt, :], start=(mt == 0), stop=(mt == 1)),
                                 reads=[bp_, bVx[mt]], writes=[bpso], inc=(mt == 1))
                        S.op('dve', lambda e, pso=pso, cc=cc, sl=sl: e.tensor_tensor(out=mix[:, cc, sl], in0=pso[:], in1=rden[:], op=ALU.mult),
                             reads=[bpso, brden], writes=[bmix[cc][tq]])
            self.outproj(d['w_xo'][l], 0, 8, mix, bmix)

    def final(self):
        S, d = self.S, self.d
        with ExitStack() as es:
            sb = lambda n, sh, dt=F32: self.sb(n, sh, dt, es)
            blk = 512
            sq = [sb(f"f_sq{i}", [128, blk], BF16) for i in range(2)]
            bsq = [Buf(), Buf()]
            rs = sb("f_rs", [128, blk])
            brs = Buf()
            nrm = [sb(f"f_n{i}", [128, blk]) for i in range(2)]
            bnrm = [Buf(), Buf()]
            gcol = self.vec['final_norm'][:, 0, :]
            ost = [sb(f"f_o{i}", [128, DM]) for i in range(2)]
            bost = [Buf(), Buf()]
            for tq in range(TQ):
                sl = slice(tq * blk, (tq + 1) * blk)
                ps, bps = self.ps()
                for c in range(NCH):
                    j = c % 2
                    S.op('act', lambda e, c=c, j=j: e.activation(out=sq[j][:], in_=self.xT[:, c, sl], func=AF.Square), reads=[self.bx[c][tq]], writes=[bsq[j]])
                    S.op('pe', lambda e, c=c, j=j, ps=ps: e.matmul(ps[:], lhsT=self.onesb[:], rhs=sq[j][:], start=(c == 0), stop=(c == NCH - 1)),
                         reads=[bsq[j], self.bconst], writes=[bps], inc=True)
                S.op('dve', lambda e, ps=ps: e.tensor_scalar(out=rs[:], in0=ps[:], scalar1=1.0 / DM, scalar2=EPS, op0=ALU.mult, op1=ALU.add), reads=[bps], writes=[brs])
                S.op('act', lambda e: e.activation(out=rs[:], in_=rs[:], func=AF.Ln), reads=[brs], writes=[brs])
                S.op('act', lambda e: e.activation(out=rs[:], in_=rs[:], func=AF.Exp, scale=-0.5), reads=[brs], writes=[brs])
                pts = [self.ps() for _ in range(8)]
                for c in range(NCH):
                    n_, bn_ = nrm[c % 2], bnrm[c % 2]
                    S.op('dve', lambda e, c=c, n_=n_: e.scalar_tensor_tensor(out=n_[:], in0=self.xT[:, c, sl], scalar=gcol[:, c:c + 1], in1=rs[:], op0=ALU.mult, op1=ALU.mult),
                         reads=[self.bx[c][tq], brs, self.bconst], writes=[bn_])
                    for j in range(4):
                        pp, bpp = pts[2 * j + c // 4]
                        S.op('pe', lambda e, pp=pp, j=j, c=c, n_=n_: e.transpose(out=pp[:, (c % 4) * 128:(c % 4) * 128 + 128], in_=n_[:, j * 128:(j + 1) * 128], identity=self.ident[:]),
                             reads=[bn_, self.bconst], writes=[bpp], inc=True)
                for j in range(4):
                    n = 4 * tq + j
                    o_, bo_ = ost[n % 2], bost[n % 2]
                    for h in range(2):
                        pp, bpp = pts[2 * j + h]
                        if h == 0:
                            S.op('act', lambda e, pp=pp, o_=o_: e.activation(out=o_[:, 0:512], in_=pp[:], func=AF.Copy), reads=[bpp], writes=[bo_])
                        else:
                            S.op('dve', lambda e, pp=pp, o_=o_: e.tensor_copy(out=o_[:, 512:1024], in_=pp[:]), reads=[bpp], writes=[bo_])
                    S.dma(d['out'][n * 128:(n + 1) * 128, :], o_[:], reads=[bo_])

    def run(self):
        S, d = self.S, self.d
        self.setup()
        if CUT == 1:
            return
        self.load_T(d['x'], 16, self.xT, lambda h, n: [self.bx[c][n // 4] for c in range(4 * h, 4 * h + 4)])
        if CUT == 2:
            return
        with ExitStack() as es:
            mraw = self.sb("mraw", [128, NCH, 256], F32, es)
            bmr = [[Buf()] for _ in range(NCH)]
            self.load_T(d['mem'], 2, mraw, lambda h, n: [bmr[c][0] for c in range(4 * h, 4 * h + 4)])
            bm = [[self.bmem] for _ in range(NCH)]
            self.rmsnorm(mraw, bmr, 256, self.vec['mem_norm'][:, 0, :], self.memT, bm, es)
        S.barrier()
        if CUT == 3:
            return
        for layer in range(self.depth):
            i = layer // 2
            with ExitStack() as es:
                gname = 'norm_ab' if layer % 2 == 0 else 'norm_cd'
                self.rmsnorm(self.xT, self.bx, SEQ, self.vec[gname][:, i, :], self.hnT, self.bhn, es)
            S.barrier()
            if layer % 2 == 0:
                self.even_layer(i)
            else:
                self.odd_layer(i)
            S.barrier()
            if 'cross' in STAGES:
                self.cross(layer)
            S.barrier()
        self.final()


import os
_CACHE = {}
NORM_DIV = int(os.environ.get('NORM_DIV', '1'))
S5_ACT = int(os.environ.get('S5_ACT', '1'))
BARRIERS = int(os.environ.get('BARRIERS', '1'))
ATT_SKIP = set(os.environ.get('ATT_SKIP', '').split(','))
LT_MODE = 0
CUT = 0
STAGES = {'attnA', 'poolB', 'cross', 's5D', 'sguC'}


def build_nc(depth=4):
    key = (depth, tuple(sorted(STAGES)))
    if key in _CACHE:
        return _CACHE[key]
    plan = None
    for pass_ in range(2):
        nc = bass.Bass("TRN2", target_bir_lowering=False)
        with ExitStack() as es:
            S = Sched(nc, es)
            K = Kern(nc, S, es, depth, wplan=plan)
            K.run()
            if pass_ == 0:
                plan = [(k, sp) for k, sp in K.wrec]
                continue
            S.emit()
    _CACHE[key] = nc
    return nc


def kernel(**inputs):
    n = 8
    nc = build_nc(4)
    consts = host_consts()
    x = np.ascontiguousarray(np.asarray(inputs['x'], dtype=np.float32))
    mem = np.ascontiguousarray(np.asarray(inputs['mem'], dtype=np.float32))
    shared = {name: np.ascontiguousarray(np.asarray(inputs[name], dtype=np.float32)) for name, _ in PARAMS}
    shared.update(consts)
    in_maps = []
    for b in range(n):
        m = dict(shared)
        m['x'] = x[b]
        m['mem'] = mem[b]
        in_maps.append(m)
    res = run_bass_kernel_spmd(nc, in_maps, core_ids=list(range(n)))
    return np.stack([np.asarray(r['out'], dtype=np.float32) for r in res.results], axis=0)
```

```python
import numpy as np
import concourse.bass as bass
import concourse.mybir as mybir
from concourse.bass_utils import run_bass_kernel_spmd
from contextlib import ExitStack

F32 = mybir.dt.float32
BF16 = mybir.dt.bfloat16
I32 = mybir.dt.int32
ALU = mybir.AluOpType
AF = mybir.ActivationFunctionType

ENGS = ('pe', 'act', 'dve', 'pool', 'sp')
EP = 20000
NEPOCH = 8
NSLOT = 8

SEQ = 2048
DM = 1024
NCH = 8
TQ = 4
EPS = 1e-6
TWO_PI = float(2 * np.pi)


class Buf:
    __slots__ = ('w', 'r', 'excl')

    def __init__(self, excl=False):
        self.w = None
        self.r = {}
        self.excl = excl


class _Rec:
    def __init__(self):
        self.call = None

    def __getattr__(self, name):
        def f(*a, **k):
            self.call = (name, a, k)
            return None
        return f


class Sched:
    def __init__(self, nc, es):
        self.nc = nc
        self.ops = {e: [] for e in ENGS}
        self.incs = {e: 0 for e in ENGS}
        self.waited = {e: {} for e in ENGS}
        self.sems = {}
        for e in ENGS:
            if e == 'sp':
                continue
            for k in range(NEPOCH):
                self.sems[(e, k)] = es.enter_context(nc.semaphore(f"s_{e}{k}"))
        self.dsem = [es.enter_context(nc.semaphore(f"s_dma{i}")) for i in range(NSLOT)]
        self.dcnt = [0] * NSLOT
        self.dnext = 0
        self.nops = 0

    def _collect(self, eng, reads, writes, extra=()):
        waits = {}

        def need(t):
            if t is None:
                return
            key, n = t
            if key == 'pe' and eng == 'pe':
                return
            if n > self.waited[eng].get(key, 0):
                if n > waits.get(key, 0):
                    waits[key] = n
        for b in reads:
            need(b.w)
            if b.excl:
                for k, t in b.r.items():
                    if k != eng:
                        need(t)
        for b in writes:
            need(b.w)
            for t in b.r.values():
                need(t)
        for t in extra:
            need(t)
        for key, n in waits.items():
            self.waited[eng][key] = n
        return list(waits.items())

    def op(self, eng, fn, reads=(), writes=(), inc=True):
        assert inc or eng == 'pe'
        waits = self._collect(eng, reads, writes)
        rec = _Rec()
        fn(rec)
        name_, a_, k_ = rec.call
        fn = (lambda e, name_=name_, a_=a_, k_=k_: getattr(e, name_)(*a_, **k_))
        n = self.incs[eng] + 1
        assert n <= EP * NEPOCH
        ticket = (eng, n)
        self.ops[eng].append((waits, fn, ('e', n) if inc else None))
        if inc:
            self.incs[eng] = n
        for b in reads:
            b.r[eng] = ticket
        for b in writes:
            b.w = ticket
            b.r = {}
        self.nops += 1
        return ticket

    def dma(self, out, in_, reads=(), writes=(), **kw):
        slot = self.dnext
        self.dnext = (slot + 1) % NSLOT
        prev = self.dcnt[slot]
        key = ('dma', slot)
        extra = [(key, prev)] if prev > 0 else []
        waits = self._collect('sp', reads, writes, extra)
        n = prev + 1
        self.dcnt[slot] = n
        ticket = (key, n)

        def fn(sp, out=out, in_=in_, kw=kw):
            return sp.dma_start(out=out, in_=in_, **kw)
        self.ops['sp'].append((waits, fn, ('d', slot)))
        for b in reads:
            b.r[key] = ticket
        for b in writes:
            b.w = ticket
            b.r = {}
        self.nops += 1
        return ticket

    def barrier(self):
        for eng in ENGS:
            waits = []
            for e2 in ENGS:
                if e2 != eng and e2 != 'sp' and self.incs[e2] > self.waited[eng].get(e2, 0):
                    waits.append((e2, self.incs[e2]))
                    self.waited[eng][e2] = self.incs[e2]
            for s_ in range(NSLOT):
                key = ('dma', s_)
                if self.dcnt[s_] > self.waited[eng].get(key, 0):
                    waits.append((key, self.dcnt[s_]))
                    self.waited[eng][key] = self.dcnt[s_]
            self.ops[eng].append((waits, None, None))

    def _wait(self, e, key, n):
        if isinstance(key, tuple):
            e.wait_ge(self.dsem[key[1]], 16 * n)
        else:
            k = (n - 1) // EP
            e.wait_ge(self.sems[(key, k)], (n - 1) % EP + 1)

    def emit(self):
        nc = self.nc
        fin = [(('dma', s), self.dcnt[s]) for s in range(NSLOT) if self.dcnt[s] > 0]
        with nc.Block() as block:
            def run(ename, e):
                for waits, fn, inc in self.ops[ename]:
                    for key, n in waits:
                        self._wait(e, key, n)
                    if fn is None:
                        continue
                    ins = fn(e)
                    if inc is not None:
                        if inc[0] == 'e':
                            n = inc[1]
                            ins.then_inc(self.sems[(ename, (n - 1) // EP)], 1)
                        else:
                            ins.then_inc(self.dsem[inc[1]], 16)
                if ename == 'sp':
                    for key, n in fin:
                        self._wait(e, key, n)

            @block.tensor
            def _(pe):
                run('pe', pe)

            @block.scalar
            def _(act):
                run('act', act)

            @block.vector
            def _(dve):
                run('dve', dve)

            @block.gpsimd
            def _(pool):
                run('pool', pool)

            @block.sync
            def _(sp):
                run('sp', sp)


PARAMS = [
    ('norm_ab', (2, 1024)), ('w_in_ab', (2, 1024, 6144)), ('pool_w', (2, 4, 256, 256)), ('pool_scale', (2, 1024)),
    ('w_out_ab', (2, 2048, 1024)), ('norm_cd', (2, 1024)), ('w_in_cd', (2, 1024, 4096)), ('sgu_ln_g', (2, 1024)),
    ('sgu_ln_b', (2, 1024)), ('sgu_w', (2, 4, 128, 128)), ('sgu_b', (2, 4, 128)), ('s5_a_re', (2, 32, 64)),
    ('s5_a_im', (2, 32, 64)), ('s5_log_dt', (2, 32)), ('s5_b_re', (2, 32, 64, 16)), ('s5_b_im', (2, 32, 64, 16)),
    ('s5_c_re', (2, 32, 16, 64)), ('s5_c_im', (2, 32, 16, 64)), ('s5_d', (2, 512)), ('glu_w1', (2, 512, 512)),
    ('glu_w2', (2, 512, 512)), ('w_out_cd', (2, 1536, 1024)), ('norm_x', (4, 1024)), ('w_xq', (4, 1024, 1024)),
    ('w_xkv', (4, 1024, 2048)), ('w_xo', (4, 1024, 1024)), ('mem_norm', (1024,)), ('final_norm', (1024,)),
]


def host_consts():
    c = {}
    c['c_ident'] = np.eye(128, dtype=np.float32)
    k = np.arange(128)[:, None]
    q = np.arange(128)[None, :]
    NEGM = -30000.0
    diag = np.where(q >= k, 0.0, NEGM)
    prev = np.where(q <= k, 0.0, NEGM)
    c['c_maskb'] = np.concatenate([diag, prev], axis=1).astype(np.float32)
    c['c_mask01'] = (c['c_maskb'] == 0.0).astype(np.float32)
    c['c_triu'] = (k <= q).astype(np.float32)
    c['c_iota'] = np.tile(np.arange(512, dtype=np.float32)[None, :], (128, 1))
    inv = np.zeros((4, 16), np.float32)
    for g, w in enumerate((2, 4, 8, 16)):
        inv[g] = 1.0 / np.minimum(np.arange(1, 17), w)
    c['c_invcnt'] = np.tile(inv.reshape(1, 64), (128, 1)).astype(np.float32)
    M = np.zeros((128, 4, 8), np.float32)
    for g2 in range(2):
        for gpl in range(4):
            M[g2 * 64:(g2 + 1) * 64, gpl, 2 * gpl + g2] = 1.0
    c['c_bm1'] = M.reshape(128, 32)
    M2 = np.zeros((128, 4, 2), np.float32)
    for gl in range(8):
        for gpl in range(4):
            for g2 in range(2):
                if gl == 2 * gpl + g2:
                    M2[gl * 16:(gl + 1) * 16, gpl, g2] = 1.0
    c['c_bm2'] = M2.reshape(128, 8)
    return c


class Kern:
    def __init__(self, nc, S, es, depth=4, wplan=None):
        self.nc, self.S, self.es, self.depth = nc, S, es, depth
        self.wplan = wplan
        self.wrec = []
        self.wissued = []
        d = {}
        d['x'] = nc.dram_tensor("x", [SEQ, DM], F32, kind="ExternalInput").ap()
        d['mem'] = nc.dram_tensor("mem", [256, DM], F32, kind="ExternalInput").ap()
        for name, shp in PARAMS:
            d[name] = nc.dram_tensor(name, list(shp), F32, kind="ExternalInput").ap()
        for name, arr in host_consts().items():
            d[name] = nc.dram_tensor(name, list(arr.shape), F32, kind="ExternalInput").ap()
        d['out'] = nc.dram_tensor("out", [SEQ, DM], F32, kind="ExternalOutput").ap()
        self.d = d
        self.psi = 0

    def sb(self, name, shape, dt=F32, es=None):
        self.uid = getattr(self, 'uid', 0) + 1
        return (es or self.es).enter_context(self.nc.sbuf_tensor(f"{name}_{self.uid}", shape, dt))

    def grid(self, *dims):
        if len(dims) == 1:
            return [Buf() for _ in range(dims[0])]
        return [self.grid(*dims[1:]) for _ in range(dims[0])]

    def ps(self):
        i = self.psi
        self.psi = (i + 1) % len(self.psr)
        j = self.psr[i]
        return self.P[j], self.bP[j]

    def ps_rot(self, banks):
        self.psr = list(banks)
        self.psi = 0

    def setup(self):
        nc, S, d = self.nc, self.S, self.d
        sb = self.sb
        self.P = [self.es.enter_context(nc.psum_tensor(f"P{i}", [128, 512], F32)) for i in range(8)]
        self.bP = [Buf(excl=True) for _ in range(8)]
        self.ps_rot(range(8))
        self.xT = sb("xT", [128, NCH, SEQ], F32)
        self.bx = self.grid(NCH, TQ)
        self.hnT = sb("hnT", [128, NCH, SEQ], BF16)
        self.bhn = self.grid(NCH, TQ)
        self.wst = [sb(f"wst{i}", [128, 2048], F32) for i in range(2)]
        self.bwst = [Buf() for _ in range(2)]
        self.wbf = [sb(f"wbf{i}", [128, 2048], BF16) for i in range(2)]
        self.bwbf = [Buf() for _ in range(2)]
        self.wi = 0
        self.memT = sb("memT", [128, NCH, 256], BF16)
        self.bmem = Buf()
        self.ident = sb("ident", [128, 128], F32)
        self.identb = sb("identb", [128, 128], BF16)
        self.onesb = sb("onesb", [128, 128], BF16)
        self.maskb = sb("maskb", [128, 256], BF16)
        self.triu = sb("triu", [128, 128], F32)
        self.iota = sb("iota", [128, 512], F32)
        self.invcnt = sb("invcnt", [128, 64], F32)
        self.bm1 = sb("bm1", [128, 32], F32)
        self.bm2 = sb("bm2", [128, 8], F32)
        self.halfpi = sb("halfpi", [128, 1], F32)
        self.bconst = Buf()
        tmp = self.wst[0]
        S.dma(self.ident[:], d['c_ident'], writes=[self.bconst])
        S.dma(self.triu[:], d['c_triu'], writes=[self.bconst])
        S.dma(self.iota[:], d['c_iota'], writes=[self.bconst])
        S.dma(self.invcnt[:], d['c_invcnt'], writes=[self.bconst])
        S.dma(self.bm1[:], d['c_bm1'], writes=[self.bconst])
        S.dma(self.bm2[:], d['c_bm2'], writes=[self.bconst])
        S.dma(tmp[:, 0:256], d['c_maskb'], writes=[self.bwst[0]])
        S.op('pool', lambda e: e.tensor_copy(out=self.maskb[:], in_=tmp[:, 0:256]), reads=[self.bwst[0]], writes=[self.bconst])
        self.mask01 = sb("mask01", [128, 256], BF16)
        S.dma(tmp[:, 256:512], d['c_mask01'], writes=[self.bwst[0]])
        S.op('pool', lambda e: e.tensor_copy(out=self.mask01[:], in_=tmp[:, 256:512]), reads=[self.bwst[0]], writes=[self.bconst])
        S.op('pool', lambda e: e.tensor_copy(out=self.identb[:], in_=self.ident[:]), reads=[self.bconst], writes=[self.bconst])
        S.op('pool', lambda e: e.memset(self.onesb[:], 1.0), writes=[self.bconst])
        S.op('pool', lambda e: e.memset(self.halfpi[:], float(np.pi / 2)), writes=[self.bconst])
        self.magic = sb("magic", [128, 2], F32)
        S.op('pool', lambda e: e.memset(self.magic[:, 0:1], 12582912.0), writes=[self.bconst])
        S.op('pool', lambda e: e.memset(self.magic[:, 1:2], -12582912.0), writes=[self.bconst])
        rows = [('norm_ab', 2, 8), ('norm_cd', 2, 8), ('norm_x', 4, 8), ('pool_scale', 2, 8), ('mem_norm', 1, 8), ('final_norm', 1, 8), ('s5_d', 2, 4)]
        vst = sb("vst", [128, 128], F32)
        bvst = Buf()
        S.op('dve', lambda e: e.memset(vst[:], 0.0), writes=[bvst])
        vall = sb("vall", [128, 128], F32)
        r0 = 0
        self.vec = {}
        for name, n, ch in rows:
            src = d[name]
            if len(src.shape) == 2:
                src2 = src.rearrange("l (c p) -> (l c) p", p=128)
            else:
                src2 = src.rearrange("(c p) -> c p", p=128)
            S.dma(vst[r0:r0 + n * ch, :], src2, writes=[bvst])
            self.vec[name] = vall[:, r0:r0 + n * ch].rearrange("p (l c) -> p l c", c=ch)
            r0 += n * ch
        ps, bps = self.ps()
        S.op('pe', lambda e: e.transpose(out=ps[:, 0:128], in_=vst[:], identity=self.ident[:]), reads=[bvst, self.bconst], writes=[bps])
        S.op('dve', lambda e: e.tensor_copy(out=vall[:], in_=ps[:, 0:128]), reads=[bps], writes=[self.bconst])

    def wload(self, wap, r0, nr, c0, ncols, prefetch=True):
        key = (wap.tensor.name, int(wap.offset), r0, nr, c0, ncols)
        if self.wplan is None:
            self.wrec.append((key, (wap.tensor.name, int(wap.offset), int(wap.shape[0]), int(wap.shape[1]), r0, nr, c0, ncols)))
            return self._wload_now(wap, r0, nr, c0, ncols)
        if not self.wissued:
            self._wprefetch()
        k0, tile = self.wissued.pop(0)
        assert k0 == key, (k0, key)
        if prefetch:
            self._wprefetch()
        return tile

    def wfence(self):
        if self.wplan is None:
            self.wrec.append((None, None))
            return
        assert not self.wissued
        assert self.wplan and self.wplan[0][0] is None
        self.wplan.pop(0)

    def _wprefetch(self):
        if self.wissued or not self.wplan or self.wplan[0][0] is None:
            return
        key, (name, off, R, C, r0, nr, c0, ncols) = self.wplan.pop(0)
        full = self.d[name]
        nd_ = len(full.shape)
        flat = full if nd_ == 1 else full.rearrange(" ".join("abcd"[:nd_]) + " -> (" + " ".join("abcd"[:nd_]) + ")")
        wap = flat[off:off + R * C].rearrange("(r c) -> r c", c=C)
        self.wissued.append((key, self._wload_now(wap, r0, nr, c0, ncols)))

    def _wload_now(self, wap, r0, nr, c0, ncols):
        S = self.S
        P = min(nr, 128)
        kc = nr // P
        assert kc * ncols <= 2048
        i = self.wi
        self.wi = (i + 1) % 2
        st, bst, wb, bwb = self.wst[i], self.bwst[i], self.wbf[i], self.bwbf[i]
        n = kc * ncols
        src = wap[r0:r0 + nr, c0:c0 + ncols].rearrange("(kc p) c -> p kc c", p=P)
        S.dma(st[0:P, 0:n].rearrange("p (kc c) -> p kc c", c=ncols), src, writes=[bst])
        S.op('act', lambda e: e.activation(out=wb[0:P, 0:n], in_=st[0:P, 0:n], func=AF.Copy), reads=[bst], writes=[bwb])
        return wb[0:P, 0:n].rearrange("p (kc c) -> p kc c", c=ncols), bwb

    def mm_fm(self, ps_ap, bps, wt, bw, mcols, rhs_fn, rhs_bufs, kcs, first=True, last=True):
        S = self.S
        n = len(kcs)
        for i, kc in enumerate(kcs):
            rhs = rhs_fn(kc)
            S.op('pe', lambda e, kc=kc, rhs=rhs, i=i: e.matmul(ps_ap, lhsT=wt[:, kc, mcols], rhs=rhs,
                                                          start=(first and i == 0), stop=(last and i == n - 1)),
                 reads=[bw] + rhs_bufs(kc), writes=[bps], inc=(i == n - 1))

    def rmsnorm(self, src, bsrc, N, gcol, dst, bdst, es):
        S = self.S
        blk = min(N, 512)
        sq = [self.sb(f"rn_sq{i}_{N}", [128, blk], BF16, es) for i in range(2)]
        bsq = [Buf(), Buf()]
        rs = self.sb(f"rn_rs_{N}", [128, blk], F32, es)
        brs = Buf()
        for t in range(N // blk):
            sl = slice(t * blk, (t + 1) * blk)
            ps, bps = self.ps()
            for c in range(NCH):
                j = c % 2
                S.op('act', lambda e, c=c, j=j: e.activation(out=sq[j][:], in_=src[:, c, sl], func=AF.Square),
                     reads=[bsrc[c][t]], writes=[bsq[j]])
                S.op('pe', lambda e, c=c, j=j: e.matmul(ps[:, 0:blk], lhsT=self.onesb[:], rhs=sq[j][:], start=(c == 0), stop=(c == NCH - 1)),
                     reads=[bsq[j], self.bconst], writes=[bps], inc=True)
            S.op('dve', lambda e: e.tensor_scalar(out=rs[:], in0=ps[:, 0:blk], scalar1=1.0 / DM, scalar2=EPS, op0=ALU.mult, op1=ALU.add),
                 reads=[bps], writes=[brs])
            S.op('act', lambda e: e.activation(out=rs[:], in_=rs[:], func=AF.Ln), reads=[brs], writes=[brs])
            S.op('act', lambda e: e.activation(out=rs[:], in_=rs[:], func=AF.Exp, scale=-0.5), reads=[brs], writes=[brs])
            for c in range(NCH):
                S.op('dve', lambda e, c=c: e.scalar_tensor_tensor(out=dst[:, c, sl], in0=src[:, c, sl], scalar=gcol[:, c:c + 1], in1=rs[:],
                                                                   op0=ALU.mult, op1=ALU.mult),
                     reads=[bsrc[c][t], brs, self.bconst], writes=[bdst[c][t]])

    def load_T(self, src_ap, ntiles, dst, bdst_fn):
        S = self.S
        for n in range(ntiles):
            i = n % 2
            st, bst = self.wst[i], self.bwst[i]
            S.dma(st[:, 0:1024], src_ap[n * 128:(n + 1) * 128, :], writes=[bst])
            for h in range(2):
                ps, bps = self.ps()
                for j in range(4):
                    c = 4 * h + j
                    S.op('pe', lambda e, c=c, j=j: e.transpose(out=ps[:, j * 128:(j + 1) * 128], in_=st[:, c * 128:(c + 1) * 128], identity=self.ident[:]),
                         reads=[bst, self.bconst], writes=[bps], inc=(j == 3))
                for j in range(4):
                    c = 4 * h + j
                    if LT_MODE == 0 or (LT_MODE == 2 and j % 2 == 1):
                        S.op('dve', lambda e, c=c, j=j, ps=ps: e.tensor_copy(out=dst[:, c, n * 128:(n + 1) * 128], in_=ps[:, j * 128:(j + 1) * 128]),
                             reads=[bps], writes=bdst_fn(h, n))
                    else:
                        S.op('act', lambda e, c=c, j=j, ps=ps: e.activation(out=dst[:, c, n * 128:(n + 1) * 128], in_=ps[:, j * 128:(j + 1) * 128], func=AF.Copy),
                             reads=[bps], writes=bdst_fn(h, n))

    def outproj(self, wap, r0, nk, mix, bmix):
        S = self.S
        for oc in range(NCH):
            wt, bw = self.wload(wap, r0, nk * 128, oc * 128, 128)
            for tq in range(TQ):
                ps, bps = self.ps()
                sl = slice(tq * 512, (tq + 1) * 512)
                self.mm_fm(ps[:], bps, wt, bw, slice(0, 128), lambda kc: mix[:, kc, sl], lambda kc: [bmix[kc][tq]], list(range(nk)))
                S.op('dve', lambda e, oc=oc, sl=sl, ps=ps: e.tensor_tensor(out=self.xT[:, oc, sl], in0=ps[:], in1=self.xT[:, oc, sl], op=ALU.add),
                     reads=[bps, self.bx[oc][tq]], writes=[self.bx[oc][tq]])
        if BARRIERS:
            S.barrier()

    def proj_fm(self, wap, c0, dst_fn, bdst_fn, func=AF.Copy, eng='act'):
        S = self.S
        wt, bw = self.wload(wap, 0, 1024, c0, 128)
        for tq in range(TQ):
            ps, bps = self.ps()
            sl = slice(tq * 512, (tq + 1) * 512)
            self.mm_fm(ps[:], bps, wt, bw, slice(0, 128), lambda kc: self.hnT[:, kc, sl], lambda kc: [self.bhn[kc][tq]], list(range(NCH)))
            if eng == 'act':
                S.op('act', lambda e, tq=tq, ps=ps: e.activation(out=dst_fn(tq), in_=ps[:], func=func), reads=[bps], writes=bdst_fn(tq))
            else:
                S.op('dve', lambda e, tq=tq, ps=ps: e.tensor_copy(out=dst_fn(tq), in_=ps[:]), reads=[bps], writes=bdst_fn(tq))

    def even_layer(self, li):
        S, d = self.S, self.d
        w_in = d['w_in_ab'][li]
        w_out = d['w_out_ab'][li]
        with ExitStack() as es:
            if 'attnA' in STAGES:
                mix = self.sb("mixA", [128, NCH, SEQ], BF16, es)
                bmix = self.grid(NCH, TQ)
                self.attnA(li, w_in, mix, bmix)
                self.outproj(w_out, 0, 8, mix, bmix)
        if 'poolB' not in STAGES:
            return
        with ExitStack() as es:
            mix = self.sb("mixB", [128, NCH, SEQ], BF16, es)
            bmix = self.grid(NCH, TQ)
            self.poolB(li, w_in, mix, bmix)
            self.outproj(w_out, 1024, 8, mix, bmix)

    def attnA(self, li, w_in, mix, bmix):
        S = self.S
        with ExitStack() as es:
            qT = self.sb("qT", [128, SEQ], BF16, es)
            kT = self.sb("kT", [128, SEQ], BF16, es)
            gT = self.sb("gT", [128, SEQ], BF16, es)
            bq, bk, bg = self.grid(TQ), self.grid(TQ), self.grid(TQ)
            Va = self.sb("Vaug", [128, 3, 16, 2, 128], BF16, es)
            bVa = self.grid(3, 4)
            bVones = Buf()
            S.op('pool', lambda e: e.memset(Va[:, :, :, 0, 64:128], 1.0), writes=[bVones])
            S.op('pool', lambda e: e.memset(Va[:, :, :, 1, 0:64], 1.0), writes=[bVones])
            pT = [self.sb(f"pT{i}", [128, 256], BF16, es) for i in range(4)]
            bpT = [Buf() for _ in range(4)]
            eT = [self.sb(f"eT{i}", [128, 256], BF16, es) for i in range(4)]
            beT = [Buf() for _ in range(4)]
            rden = self.sb("rden", [128, 512], F32, es)
            tmpn = self.sb("tmpn", [128, 512], F32, es)
            brden, btmpn = Buf(), Buf()
            pti = 0
            for c in range(NCH):
                self.ps_rot(range(8))
                self.proj_fm(w_in, c * 128, lambda tq: qT[:, tq * 512:(tq + 1) * 512], lambda tq: [bq[tq]])
                self.proj_fm(w_in, 1024 + c * 128, lambda tq: kT[:, tq * 512:(tq + 1) * 512], lambda tq: [bk[tq]], eng='dve')
                self.proj_fm(w_in, 3072 + c * 128, lambda tq: gT[:, tq * 512:(tq + 1) * 512], lambda tq: [bg[tq]], func=AF.Silu)
                wv, bwv = self.wload(w_in, 0, 1024, 2048 + c * 128, 128)
                for o, dd in enumerate((1, 4, 16)):
                    nb = 16 // dd
                    for t4 in range(4):
                        ps, bps = self.ps()
                        for j in range(4):
                            ti = 4 * t4 + j
                            r, kb = ti // nb, ti % nb
                            st = r + dd * 128 * kb
                            tok = slice(st, st + dd * 127 + 1, dd)
                            tqs = sorted(set([(st) // 512, (st + dd * 127) // 512])) if dd < 16 else [0, 1, 2, 3]
                            for kc in range(NCH):
                                S.op('pe', lambda e, kc=kc, tok=tok, j=j, ps=ps: e.matmul(ps[:, j * 128:(j + 1) * 128], lhsT=self.hnT[:, kc, tok], rhs=wv[:, kc, :],
                                                                                    start=(kc == 0), stop=(kc == NCH - 1)),
                                     reads=[bwv] + [self.bhn[kc][q_] for q_ in tqs], writes=[bps], inc=(kc == NCH - 1 and j == 3))
                        psv = ps[:].rearrange("p (j h d) -> p j h d", j=4, h=2)
                        S.op('act', lambda e, o=o, t4=t4, psv=psv: e.activation(out=Va[:, o, 4 * t4:4 * t4 + 4, 0, 0:64], in_=psv[:, :, 0, :], func=AF.Copy),
                             reads=[bps], writes=[bVa[o][t4]])
                        S.op('dve', lambda e, o=o, t4=t4, psv=psv: e.tensor_copy(out=Va[:, o, 4 * t4:4 * t4 + 4, 1, 64:128], in_=psv[:, :, 1, :]),
                             reads=[bps], writes=[bVa[o][t4]])
                for hh in range(2):
                    hp = slice(hh * 64, hh * 64 + 64)
                    dp = slice(64 - hh * 64, 128 - hh * 64)
                    nd = [self.P[i] for i in range(4)]
                    bnd = [self.bP[i] for i in range(4)]
                    started = [False] * 4
                    self.ps_rot(range(4, 8))
                    tiles = []
                    for o, dd in enumerate((1, 4, 16)):
                        nb = 16 // dd
                        for r in range(dd):
                            for kb in range(nb):
                                tiles.append((o, dd, nb, r, kb))
                    LA = 3
                    pend = {}

                    def issue_S(idx):
                        nonlocal pti
                        o, dd, nb, r, kb = tiles[idx]
                        nq = 2 if kb < nb - 1 else 1
                        N = 128 * nq
                        kst = r + dd * 128 * kb
                        ktok = slice(kst, kst + dd * 127 + 1, dd)
                        qtok = slice(kst, kst + dd * (N - 1) + 1, dd)
                        ktqs = [kst // 512] if dd < 16 else [0, 1, 2, 3]
                        qtqs = sorted(set([kst // 512, (kst + dd * (N - 1)) // 512])) if dd < 16 else [0, 1, 2, 3]
                        ps, bps = self.ps()
                        S.op('pe', lambda e: e.matmul(ps[:, 0:N], lhsT=kT[hp, ktok], rhs=qT[hp, qtok], start=True, stop=True),
                             reads=[bk[q_] for q_ in ktqs] + [bq[q_] for q_ in qtqs], writes=[bps], inc=True)
                        p_, bp_ = pT[pti], bpT[pti]
                        e_, be_ = eT[pti], beT[pti]
                        pti = (pti + 1) % len(pT)
                        S.op('act', lambda e: e.activation(out=e_[:, 0:N], in_=ps[:, 0:N], func=AF.Exp, scale=0.125),
                             reads=[bps], writes=[be_])
                        S.op('dve', lambda e: e.tensor_tensor(out=p_[:, 0:N], in0=e_[:, 0:N], in1=self.mask01[:, 0:N], op=ALU.mult),
                             reads=[be_, self.bconst], writes=[bp_])
                        pend[idx] = (p_, bp_, nq)

                    def issue_PV(idx):
                        o, dd, nb, r, kb = tiles[idx]
                        p_, bp_, nq = pend.pop(idx)
                        ti = r * nb + kb
                        lhsV = Va[:, o, ti, hh, :]
                        allouts = []
                        if dd == 1 and nq == 2 and kb % 4 != 3:
                            allouts = [(kb // 4, slice((kb % 4) * 128, (kb % 4) * 128 + 256), slice(0, 256))]
                            nq = 0
                        for qi in range(nq):
                            i = kb + qi
                            if dd == 1:
                                allouts += [(i // 4, slice((i % 4) * 128, (i % 4) * 128 + 128), slice(qi * 128, qi * 128 + 128))]
                            elif dd == 4:
                                allouts += [(i, slice(r, 512, 4), slice(qi * 128, qi * 128 + 128))]
                            else:
                                allouts += [(j, slice(r, 512, 16), slice(j * 32, j * 32 + 32)) for j in range(4)]
                        for n_, (bank, ocols, pcols) in enumerate(allouts):
                            first = not started[bank]
                            started[bank] = True
                            S.op('pe', lambda e: e.matmul(nd[bank][:, ocols], lhsT=lhsV, rhs=p_[:, pcols], start=first, stop=False, skip_group_check=True),
                                 reads=[bp_, bVa[o][ti // 4], bVones], writes=[bnd[bank]], inc=(n_ == len(allouts) - 1))

                    for idx in range(len(tiles) + LA):
                        if idx < len(tiles):
                            issue_S(idx)
                        if idx >= LA:
                            issue_PV(idx - LA)
                    for tq in range(TQ if 'norm' not in ATT_SKIP else 0):
                        sl = slice(tq * 512, (tq + 1) * 512)
                        S.op('act', lambda e, tq=tq: e.activation(out=rden[hp, :], in_=nd[tq][dp, :], func=AF.Ln), reads=[bnd[tq]], writes=[brden])
                        S.op('act', lambda e: e.activation(out=rden[hp, :], in_=rden[hp, :], func=AF.Exp, scale=-1.0), reads=[brden], writes=[brden])
                        S.op('dve', lambda e, tq=tq: e.tensor_tensor(out=tmpn[hp, :], in0=nd[tq][hp, :], in1=rden[hp, :], op=ALU.mult),
                             reads=[bnd[tq], brden], writes=[btmpn])
                        S.op('pool', lambda e, sl=sl: e.tensor_tensor(out=mix[hp, c, sl], in0=tmpn[hp, :], in1=gT[hp, sl], op=ALU.mult),
                             reads=[btmpn, bg[tq]], writes=[bmix[c][tq]])
            self.ps_rot(range(8))

    def poolB(self, li, w_in, mix, bmix):
        S = self.S
        with ExitStack() as es:
            vb = self.sb("vb", [128, 16 + SEQ], F32, es)
            sA = self.sb("sA", [128, 16 + SEQ], F32, es)
            sB = self.sb("sB", [128, 16 + SEQ], F32, es)
            bvb, bsA, bsB = Buf(), Buf(), Buf()
            pooled = self.sb("pooled", [128, 2, SEQ], BF16, es)
            bpo = self.grid(2, TQ)
            gb = self.sb("gbT", [128, 2, SEQ], BF16, es)
            bgb = self.grid(2, TQ)
            t16 = self.sb("t16", [128, 16], F32, es)
            bt16 = Buf()
            for t_ in (vb, sA, sB):
                S.op('pool', lambda e, t_=t_: e.memset(t_[:, 0:16], 0.0), writes=[bvb, bsA, bsB])
            for g in range(4):
                w = (2, 4, 8, 16)[g]
                for j in range(2):
                    cb = 2 * g + j
                    self.proj_fm(w_in, 4096 + cb * 128, lambda tq: vb[:, 16 + tq * 512:16 + (tq + 1) * 512], lambda tq: [bvb])
                    self.proj_fm(w_in, 5120 + cb * 128, lambda tq, j=j: gb[:, j, tq * 512:(tq + 1) * 512], lambda tq, j=j: [bgb[j][tq]], func=AF.Silu)
                    cur, bcur = vb, bvb
                    k = 1
                    nxt = [(sA, bsA), (sB, bsB)]
                    ni = 0
                    while k < w:
                        o_, bo_ = nxt[ni]
                        ni = 1 - ni
                        S.op('pool', lambda e, o_=o_, cur=cur, k=k: e.tensor_tensor(out=o_[:, 16:16 + SEQ], in0=cur[:, 16:16 + SEQ], in1=cur[:, 16 - k:16 - k + SEQ], op=ALU.add),
                             reads=[bcur], writes=[bo_])
                        cur, bcur = o_, bo_
                        k *= 2
                    S.op('dve', lambda e, cur=cur, j=j, w=w: e.scalar_tensor_tensor(out=pooled[:, j, :], in0=cur[:, 16:16 + SEQ], scalar=1.0 / w, in1=vb[:, 16:16 + SEQ],
                                                                                 op0=ALU.mult, op1=ALU.subtract),
                         reads=[bcur, bvb], writes=[bpo[j][t] for t in range(TQ)])
                    S.op('dve', lambda e, cur=cur, g=g: e.tensor_tensor(out=t16[:], in0=cur[:, 16:32], in1=self.invcnt[:, g * 16:(g + 1) * 16], op=ALU.mult),
                         reads=[bcur, self.bconst], writes=[bt16])
                    S.op('dve', lambda e, j=j: e.tensor_tensor(out=pooled[:, j, 0:16], in0=t16[:], in1=vb[:, 16:32], op=ALU.subtract),
                         reads=[bt16, bvb], writes=[bpo[j][0]])
                wp, bwp = self.wload(self.d['pool_w'][li][g], 0, 256, 0, 256)
                for oc2 in range(2):
                    cb = 2 * g + oc2
                    for tq in range(TQ):
                        sl = slice(tq * 512, (tq + 1) * 512)
                        ps, bps = self.ps()
                        self.mm_fm(ps[:], bps, wp, bwp, slice(oc2 * 128, oc2 * 128 + 128), lambda kc: pooled[:, kc, sl], lambda kc: [bpo[kc][tq]], [0, 1])
                        S.op('dve', lambda e, ps=ps, cb=cb, oc2=oc2, sl=sl: e.scalar_tensor_tensor(out=mix[:, cb, sl], in0=ps[:], scalar=self.vec['pool_scale'][:, li, cb:cb + 1],
                                                                                           in1=gb[:, oc2, sl], op0=ALU.mult, op1=ALU.mult),
                             reads=[bps, bgb[oc2][tq], self.bconst], writes=[bmix[cb][tq]])

    def odd_layer(self, li):
        d = self.d
        if 's5D' in STAGES:
            self.s5D(li, d['w_in_cd'][li], d['w_out_cd'][li])
        if 'sguC' in STAGES:
            self.sguC(li, d['w_in_cd'][li], d['w_out_cd'][li])

    def trig(self, y, by, yi, byi, yf, byf, cs, bcs, sn, bsn, on_act=False):
        S = self.S
        if on_act:
            S.op('act', lambda e: e.activation(out=yf, in_=y, func=AF.Identity, bias=self.magic[:, 0:1], scale=1.0), reads=[by, self.bconst], writes=[byf])
            S.op('act', lambda e: e.activation(out=yf, in_=yf, func=AF.Identity, bias=self.magic[:, 1:2], scale=1.0), reads=[byf, self.bconst], writes=[byf])
        else:
            S.op('dve', lambda e: e.tensor_copy(out=yi, in_=y), reads=[by], writes=[byi])
            S.op('dve', lambda e: e.tensor_copy(out=yf, in_=yi), reads=[byi], writes=[byf])
        S.op('dve', lambda e: e.tensor_tensor(out=yf, in0=y, in1=yf, op=ALU.subtract), reads=[by, byf], writes=[byf])
        S.op('act', lambda e: e.activation(out=sn, in_=yf, func=AF.Sin, scale=TWO_PI), reads=[byf], writes=[bsn])
        S.op('act', lambda e: e.activation(out=y, in_=yf, func=AF.Abs), reads=[byf], writes=[by])
        S.op('act', lambda e: e.activation(out=cs, in_=y, func=AF.Sin, scale=-TWO_PI, bias=self.halfpi[:, 0:1]), reads=[by, self.bconst], writes=[bcs])

    def s5D(self, li, w_in, w_out):
        S, d = self.S, self.d
        with ExitStack() as es:
            sb = lambda n, sh, dt=F32, es_=es: self.sb(n, sh, dt, es_)
            xd = sb("xdT", [128, 4, SEQ], BF16)
            bxd = self.grid(4, TQ)
            yg = sb("ygT", [128, 4, SEQ], BF16)
            byg = self.grid(4, TQ)
            for kc in range(4):
                self.proj_fm(w_in, 3072 + kc * 128, lambda tq, kc=kc: xd[:, kc, tq * 512:(tq + 1) * 512], lambda tq, kc=kc: [bxd[kc][tq]])
            ar = sb("s_ar", [128, 16]); ai = sb("s_ai", [128, 16]); ldt = sb("s_ldt", [128, 16])
            bprm = Buf()
            pst = sb("s_pst", [32, 128]); ldt0 = sb("s_ldt0", [128, 32])
            bpst = Buf()
            S.dma(pst[:, 0:64], d['s5_a_re'][li], writes=[bpst])
            S.dma(pst[:, 64:128], d['s5_a_im'][li], writes=[bpst])
            S.dma(ldt0[:], d['s5_log_dt'][li].partition_broadcast(128), writes=[bpst])
            for (dst_, c0) in ((ar, 0), (ai, 64)):
                ps, bps = self.ps()
                S.op('pe', lambda e, ps=ps, c0=c0: e.transpose(out=ps[0:64, 0:32], in_=pst[:, c0:c0 + 64], identity=self.ident[0:32, 0:32]), reads=[bpst, self.bconst], writes=[bps])
                S.op('dve', lambda e, ps=ps, dst_=dst_: e.tensor_copy(out=dst_[0:64, :], in_=ps[0:64, 0:32:2]), reads=[bps], writes=[bprm])
                S.op('dve', lambda e, ps=ps, dst_=dst_: e.tensor_copy(out=dst_[64:128, :], in_=ps[0:64, 1:32:2]), reads=[bps], writes=[bprm])
            S.op('dve', lambda e: e.tensor_copy(out=ldt[0:64, :], in_=ldt0[0:64, 0:32:2]), reads=[bpst], writes=[bprm])
            S.op('dve', lambda e: e.tensor_copy(out=ldt[64:128, :], in_=ldt0[64:128, 1:32:2]), reads=[bpst], writes=[bprm])
            names = ["dt", "dtar", "th", "rho", "c0", "s0", "abr", "abi", "inv", "t1", "t2", "cfr", "cfi", "yf", "thn", "y0"]
            T = {n: sb("s_" + n, [128, 16]) for n in names}
            yi0 = sb("s_yi", [128, 16], I32)
            B = {n: Buf() for n in names + ["yi"]}

            def dv(out, fn, reads, eng='dve'):
                S.op(eng, fn, reads=[B[r] if isinstance(r, str) else r for r in reads], writes=[B[out]])
            TT = lambda o, a, b_, op: (lambda e: e.tensor_tensor(out=T[o][:], in0=a[:], in1=b_[:], op=op))
            dv("dt", lambda e: e.activation(out=T["dt"][:], in_=ldt[:], func=AF.Exp), [bprm], 'act')
            dv("dtar", TT("dtar", T["dt"], ar, ALU.mult), ["dt", bprm])
            dv("th", TT("th", T["dt"], ai, ALU.mult), ["dt", bprm])
            dv("rho", lambda e: e.activation(out=T["rho"][:], in_=T["dtar"][:], func=AF.Exp), ["dtar"], 'act')
            dv("thn", lambda e: e.tensor_single_scalar(out=T["thn"][:], in_=T["th"][:], scalar=1.0 / TWO_PI, op=ALU.mult), ["th"])
            dv("y0", lambda e: e.tensor_copy(out=T["y0"][:], in_=T["thn"][:]), ["thn"])
            self.trig(T["y0"][:], B["y0"], yi0[:], B["yi"], T["yf"][:], B["yf"], T["c0"][:], B["c0"], T["s0"][:], B["s0"], on_act=True)
            dv("abr", TT("abr", T["rho"], T["c0"], ALU.mult), ["rho", "c0"])
            dv("abi", TT("abi", T["rho"], T["s0"], ALU.mult), ["rho", "s0"])
            dv("abr", lambda e: e.tensor_single_scalar(out=T["abr"][:], in_=T["abr"][:], scalar=-1.0, op=ALU.add), ["abr"])
            dv("t1", TT("t1", ar, ar, ALU.mult), [bprm])
            dv("t2", TT("t2", ai, ai, ALU.mult), [bprm])
            dv("inv", TT("inv", T["t1"], T["t2"], ALU.add), ["t1", "t2"])
            dv("inv", lambda e: e.reciprocal(out=T["inv"][:], in_=T["inv"][:]), ["inv"])
            dv("t1", TT("t1", T["abr"], ar, ALU.mult), ["abr", bprm])
            dv("t2", TT("t2", T["abi"], ai, ALU.mult), ["abi", bprm])
            dv("cfr", TT("cfr", T["t1"], T["t2"], ALU.add), ["t1", "t2"])
            dv("cfr", TT("cfr", T["cfr"], T["inv"], ALU.mult), ["cfr", "inv"])
            dv("t1", TT("t1", T["abi"], ar, ALU.mult), ["abi", bprm])
            dv("t2", TT("t2", T["abr"], ai, ALU.mult), ["abr", bprm])
            dv("cfi", TT("cfi", T["t1"], T["t2"], ALU.subtract), ["t1", "t2"])
            dv("cfi", TT("cfi", T["cfi"], T["inv"], ALU.mult), ["cfi", "inv"])
            off = sb("s_off", [128, 16, 4])
            boff = Buf()
            for tq in range(TQ):
                S.op('dve', lambda e, tq=tq: e.tensor_single_scalar(out=off[:, :, tq], in_=T["thn"][:], scalar=512.0 * tq, op=ALU.mult), reads=[B["thn"]], writes=[boff])
            Bre = sb("s_Bre", [128, 16, 16]); Bim = sb("s_Bim", [128, 16, 16])
            bB = Buf()
            S.dma(Bre[:], d['s5_b_re'][li].rearrange("(gp g2) p h -> (g2 p) gp h", g2=2), writes=[bB])
            S.dma(Bim[:], d['s5_b_im'][li].rearrange("(gp g2) p h -> (g2 p) gp h", g2=2), writes=[bB])
            bbr = sb("s_bbr", [128, 16, 16]); bbi = sb("s_bbi", [128, 16, 16]); bt = sb("s_bt", [128, 16, 16])
            bbb, bbt = Buf(), Buf()
            cfr_b = T["cfr"][:, :, None].to_broadcast([128, 16, 16])
            cfi_b = T["cfi"][:, :, None].to_broadcast([128, 16, 16])
            S.op('dve', lambda e: e.tensor_tensor(out=bbr[:], in0=Bre[:], in1=cfr_b, op=ALU.mult), reads=[bB, B["cfr"]], writes=[bbb])
            S.op('dve', lambda e: e.tensor_tensor(out=bt[:], in0=Bim[:], in1=cfi_b, op=ALU.mult), reads=[bB, B["cfi"]], writes=[bbt])
            S.op('dve', lambda e: e.tensor_tensor(out=bbr[:], in0=bbr[:], in1=bt[:], op=ALU.subtract), reads=[bbb, bbt], writes=[bbb])
            S.op('dve', lambda e: e.tensor_tensor(out=bbi[:], in0=Bim[:], in1=cfr_b, op=ALU.mult), reads=[bB, B["cfr"]], writes=[bbb])
            S.op('dve', lambda e: e.tensor_tensor(out=bt[:], in0=Bre[:], in1=cfi_b, op=ALU.mult), reads=[bB, B["cfi"], bbb], writes=[bbt])
            S.op('dve', lambda e: e.tensor_tensor(out=bbi[:], in0=bbi[:], in1=bt[:], op=ALU.add), reads=[bbb, bbt], writes=[bbb])
            Cre = sb("s_Cre", [128, 4, 64]); Cim = sb("s_Cim", [128, 4, 64])
            bC = Buf()
            S.dma(Cre[:], d['s5_c_re'][li].rearrange("(kc gl) h p -> (gl h) kc p", kc=4), writes=[bC])
            S.dma(Cim[:], d['s5_c_im'][li].rearrange("(kc gl) h p -> (gl h) kc p", kc=4), writes=[bC])
            Dg = sb("s_Dg", [128, 4, 128], BF16)
            bDg = Buf()
            for kc in range(4):
                S.op('dve', lambda e, kc=kc: e.tensor_single_scalar(out=Dg[:, kc, :], in_=self.ident[:], scalar=self.vec['s5_d'][:, li, kc:kc + 1], op=ALU.mult),
                     reads=[self.bconst], writes=[bDg])
            bm1 = self.bm1[:].rearrange("p (a b) -> p a b", a=4)
            bm2 = self.bm2[:].rearrange("p (a b) -> p a b", a=4)
            with ExitStack() as es2:
                sb2 = lambda n, sh, dt=F32: self.sb(n, sh, dt, es2)
                L = sb2("s_L", [128, 2, 4, 128], BF16)
                Cc = sb2("s_Cc", [128, 2, 4, 128], BF16)
                bL, bCc = Buf(), Buf()
                Z = [sb2(f"s_Z{i}", [128, 128]) for i in range(2)]
                bZ = [Buf(), Buf()]
                zi = 0
                NW = 512
                wkA = {}
                for n in ("y", "yf", "cs", "sn", "br", "bi", "t1", "t2", "t3", "t4"):
                    wkA[n] = (sb2("k_" + n, [128, NW])[:], Buf())
                gl1 = sb2("k_gl1", [128, NW])
                wkA["yi"] = (gl1[:], Buf())
                wkB = {}
                for j, n in enumerate(("y", "yf", "cs", "sn")):
                    wkB[n] = (self.wst[0][:, j * NW:(j + 1) * NW], Buf())
                for j, n in enumerate(("br", "bi", "t1", "t2")):
                    wkB[n] = (self.wst[1][:, j * NW:(j + 1) * NW], Buf())
                wb0f = self.wbf[0][:].bitcast(F32)
                for j, n in enumerate(("t3", "t4")):
                    wkB[n] = (wb0f[:, j * NW:(j + 1) * NW], Buf())
                wkB["yi"] = (self.wbf[1][:].bitcast(I32)[:, 0:NW], Buf())
                wks = [wkA, wkB]
                cur = [0]
                wb1f = self.wbf[1][:].bitcast(F32)
                gl = [wb1f[:, NW:2 * NW], gl1[:]]
                bgl = [Buf(), Buf()]
                self.wfence()
                S.barrier()
                hrb = [sb2(f"k_hr{i}", [128, NW], BF16) for i in range(2)]
                hib = [sb2(f"k_hi{i}", [128, NW], BF16) for i in range(2)]
                bhr, bhi = [Buf(), Buf()], [Buf(), Buf()]
                car = sb2("k_car", [128, 16, 2])
                bcar = [[Buf(), Buf()] for _ in range(16)]
                hidx = 0
                W = lambda n: wks[cur[0]][n][0]
                Bk = lambda n: wks[cur[0]][n][1]
                for kc in range(4):
                    self.ps_rot(range(2, 8))
                    for gpl in range(4):
                        gp = 4 * kc + gpl
                        for ri, src in enumerate((bbr, bbi)):
                            z, bz = Z[zi], bZ[zi]
                            zi = 1 - zi
                            S.op('dve', lambda e, z=z, src=src, gp=gp, gpl=gpl: e.tensor_tensor(
                                out=z[:].rearrange("p (a b) -> p a b", a=8), in0=src[:, gp, None, :].to_broadcast([128, 8, 16]),
                                in1=bm1[:, gpl, :, None].to_broadcast([128, 8, 16]), op=ALU.mult), reads=[bbb, self.bconst], writes=[bz])
                            ps, bps = self.ps()
                            S.op('pe', lambda e, ps=ps, z=z: e.transpose(out=ps[:, 0:128], in_=z[:], identity=self.ident[:]), reads=[bz, self.bconst], writes=[bps])
                            S.op('act', lambda e, ps=ps, ri=ri, gpl=gpl: e.activation(out=L[:, ri, gpl, :], in_=ps[:, 0:128], func=AF.Copy), reads=[bps], writes=[bL])
                        for ri, src in enumerate((Cre, Cim)):
                            z, bz = Z[zi], bZ[zi]
                            zi = 1 - zi
                            S.op('dve', lambda e, z=z, src=src, kc=kc, gpl=gpl: e.tensor_tensor(
                                out=z[:].rearrange("p (a b) -> p a b", a=2), in0=src[:, kc, None, :].to_broadcast([128, 2, 64]),
                                in1=bm2[:, gpl, :, None].to_broadcast([128, 2, 64]), op=ALU.mult), reads=[bC, self.bconst], writes=[bz])
                            ps, bps = self.ps()
                            S.op('pe', lambda e, ps=ps, z=z: e.transpose(out=ps[:, 0:128], in_=z[:], identity=self.ident[:]), reads=[bz, self.bconst], writes=[bps])
                            S.op('dve', lambda e, ps=ps, ri=ri, gpl=gpl: e.tensor_single_scalar(out=Cc[:, ri, gpl, :], in_=ps[:, 0:128], scalar=(1.0 if ri == 0 else -1.0), op=ALU.mult),
                                 reads=[bps], writes=[bCc])
                    its = [(tq, gpl) for tq in range(TQ) for gpl in range(4)]

                    def stageA(n):
                        tq, gpl = its[n]
                        gp = 4 * kc + gpl
                        sl = slice(tq * 512, (tq + 1) * 512)
                        cur[0] = n % 2
                        pr, bpr = self.ps()
                        pi_, bpi = self.ps()
                        S.op('pe', lambda e: e.matmul(pr[:], lhsT=L[:, 0, gpl, :], rhs=xd[:, kc, sl], start=True, stop=True),
                             reads=[bL, bxd[kc][tq]], writes=[bpr])
                        S.op('pe', lambda e: e.matmul(pi_[:], lhsT=L[:, 1, gpl, :], rhs=xd[:, kc, sl], start=True, stop=True),
                             reads=[bL, bxd[kc][tq]], writes=[bpi])
                        S.op('act', lambda e: e.activation(out=W("br"), in_=pr[:], func=AF.Copy), reads=[bpr], writes=[Bk("br")])
                        S.op('act', lambda e: e.activation(out=W("bi"), in_=pi_[:], func=AF.Copy), reads=[bpi], writes=[Bk("bi")])
                        S.op('act', lambda e: e.activation(out=W("y"), in_=self.iota[:], func=AF.Identity, scale=T["thn"][:, gp:gp + 1], bias=off[:, gp, tq:tq + 1]),
                             reads=[self.bconst, B["thn"], boff], writes=[Bk("y")])
                        self.trig(W("y"), Bk("y"), W("yi"), Bk("yi"), W("yf"), Bk("yf"), W("cs"), Bk("cs"), W("sn"), Bk("sn"), on_act=True)

                    def stageB(n):
                        nonlocal hidx
                        tq, gpl = its[n]
                        gp = 4 * kc + gpl
                        sl = slice(tq * 512, (tq + 1) * 512)
                        cur[0] = n % 2
                        psy, bpsy = self.P[tq % 2], self.bP[tq % 2]
                        S.op('dve', lambda e: e.tensor_tensor(out=W("t1"), in0=W("br"), in1=W("cs"), op=ALU.mult), reads=[Bk("br"), Bk("cs")], writes=[Bk("t1")])
                        S.op('dve', lambda e: e.tensor_tensor(out=W("t2"), in0=W("bi"), in1=W("sn"), op=ALU.mult), reads=[Bk("bi"), Bk("sn")], writes=[Bk("t2")])
                        S.op('dve', lambda e: e.tensor_tensor(out=W("t3"), in0=W("bi"), in1=W("cs"), op=ALU.mult), reads=[Bk("bi"), Bk("cs")], writes=[Bk("t3")])
                        S.op('dve', lambda e: e.tensor_tensor(out=W("t4"), in0=W("br"), in1=W("sn"), op=ALU.mult), reads=[Bk("br"), Bk("sn")], writes=[Bk("t4")])
                        S.op('dve', lambda e: e.tensor_tensor(out=W("t1"), in0=W("t1"), in1=W("t2"), op=ALU.add), reads=[Bk("t1"), Bk("t2")], writes=[Bk("t1")])
                        S.op('dve', lambda e: e.tensor_tensor(out=W("t3"), in0=W("t3"), in1=W("t4"), op=ALU.subtract), reads=[Bk("t3"), Bk("t4")], writes=[Bk("t3")])
                        rho_b = T["rho"][:, gp:gp + 1].to_broadcast([128, NW])
                        for (src, dst, ci) in (("t1", "br", 0), ("t3", "bi", 1)):
                            init = 0.0 if tq == 0 else car[:, gp, ci:ci + 1]
                            S.op('dve', lambda e: e.tensor_tensor_scan(out=W(dst), data0=rho_b, data1=W(src), initial=init, op0=ALU.mult, op1=ALU.add),
                                 reads=[Bk(src), B["rho"], bcar[gp][ci]], writes=[Bk(dst)])
                        for (dst, ci) in (("br", 0), ("bi", 1)):
                            S.op('act', lambda e: e.activation(out=car[:, gp, ci:ci + 1], in_=W(dst)[:, NW - 1:NW], func=AF.Copy),
                                 reads=[Bk(dst)], writes=[bcar[gp][ci]])
                        hr, hi_, bhr_, bhi_ = hrb[hidx], hib[hidx], bhr[hidx], bhi[hidx]
                        hidx = 1 - hidx
                        S.op('dve', lambda e: e.tensor_tensor(out=W("t1"), in0=W("br"), in1=W("cs"), op=ALU.mult), reads=[Bk("br"), Bk("cs")], writes=[Bk("t1")])
                        S.op('dve', lambda e: e.tensor_tensor(out=W("t4"), in0=W("br"), in1=W("sn"), op=ALU.mult), reads=[Bk("br"), Bk("sn")], writes=[Bk("t4")])
                        S.op('dve', lambda e: e.tensor_tensor(out=W("t2"), in0=W("bi"), in1=W("sn"), op=ALU.mult), reads=[Bk("bi"), Bk("sn")], writes=[Bk("t2")])
                        S.op('dve', lambda e: e.tensor_tensor(out=W("t3"), in0=W("bi"), in1=W("cs"), op=ALU.mult), reads=[Bk("bi"), Bk("cs")], writes=[Bk("t3")])
                        S.op('dve', lambda e: e.tensor_tensor(out=hr[:], in0=W("t1"), in1=W("t2"), op=ALU.subtract), reads=[Bk("t1"), Bk("t2")], writes=[bhr_])
                        S.op('dve', lambda e: e.tensor_tensor(out=hi_[:], in0=W("t3"), in1=W("t4"), op=ALU.add), reads=[Bk("t3"), Bk("t4")], writes=[bhi_])
                        S.op('pe', lambda e: e.matmul(psy[:], lhsT=Cc[:, 0, gpl, :], rhs=hr[:], start=(gpl == 0), stop=False),
                             reads=[bCc, bhr_], writes=[bpsy])
                        S.op('pe', lambda e: e.matmul(psy[:], lhsT=Cc[:, 1, gpl, :], rhs=hi_[:], start=False, stop=False),
                             reads=[bCc, bhi_], writes=[bpsy])
                        if gpl == 3:
                            S.op('pe', lambda e: e.matmul(psy[:], lhsT=Dg[:, kc, :], rhs=xd[:, kc, sl], start=False, stop=True),
                                 reads=[bDg, bxd[kc][tq]], writes=[bpsy])
                            S.op('act', lambda e: e.activation(out=gl[0], in_=psy[:], func=AF.Copy), reads=[bpsy], writes=[bgl[0]])
                            S.op('dve', lambda e: e.tensor_tensor(out=gl[1], in0=gl[0], in1=gl[0], op=ALU.mult), reads=[bgl[0]], writes=[bgl[1]])
                            S.op('dve', lambda e: e.tensor_scalar(out=gl[1], in0=gl[1], scalar1=0.044715, scalar2=1.0, op0=ALU.mult, op1=ALU.add), reads=[bgl[1]], writes=[bgl[1]])
                            S.op('dve', lambda e: e.tensor_tensor(out=gl[1], in0=gl[1], in1=gl[0], op=ALU.mult), reads=[bgl[1], bgl[0]], writes=[bgl[1]])
                            S.op('act', lambda e: e.activation(out=gl[1], in_=gl[1], func=AF.Sigmoid, scale=1.5957691216057308), reads=[bgl[1]], writes=[bgl[1]])
                            S.op('dve', lambda e: e.tensor_tensor(out=yg[:, kc, sl], in0=gl[1], in1=gl[0], op=ALU.mult),
                                 reads=[bgl[1], bgl[0]], writes=[byg[kc][tq]])

                    for n in range(len(its) + 1):
                        if n < len(its):
                            stageA(n)
                        if n >= 1:
                            stageB(n - 1)
                self.ps_rot(range(8))
            S.barrier()
            with ExitStack() as es3:
                sb3 = lambda n, sh, dt=F32: self.sb(n, sh, dt, es3)
                gd = sb3("gdT", [128, SEQ], BF16)
                bgd = self.grid(TQ)
                s2 = sb3("glu_s2", [128, 512]); t2_ = sb3("glu_t", [128, 512])
                bs2, bt2 = Buf(), Buf()
                for oc in range(4):
                    self.proj_fm(w_in, 3584 + oc * 128, lambda tq: gd[:, tq * 512:(tq + 1) * 512], lambda tq: [bgd[tq]], func=AF.Silu)
                    w1, bw1 = self.wload(d['glu_w1'][li], 0, 512, oc * 128, 128)
                    w2, bw2 = self.wload(d['glu_w2'][li], 0, 512, oc * 128, 128, prefetch=False)
                    for tq in range(TQ):
                        sl = slice(tq * 512, (tq + 1) * 512)
                        p1, bp1 = self.ps()
                        p2, bp2 = self.ps()
                        self.mm_fm(p1[:], bp1, w1, bw1, slice(0, 128), lambda kc: yg[:, kc, sl], lambda kc: [byg[kc][tq]], [0, 1, 2, 3])
                        self.mm_fm(p2[:], bp2, w2, bw2, slice(0, 128), lambda kc: yg[:, kc, sl], lambda kc: [byg[kc][tq]], [0, 1, 2, 3])
                        S.op('act', lambda e, p2=p2: e.activation(out=s2[:], in_=p2[:], func=AF.Sigmoid), reads=[bp2], writes=[bs2])
                        S.op('dve', lambda e, p1=p1: e.tensor_tensor(out=t2_[:], in0=p1[:], in1=s2[:], op=ALU.mult), reads=[bp1, bs2], writes=[bt2])
                        S.op('pool', lambda e, oc=oc, sl=sl: e.tensor_tensor(out=xd[:, oc, sl], in0=t2_[:], in1=gd[:, sl], op=ALU.mult),
                             reads=[bt2, bgd[tq]], writes=[bxd[oc][tq]])
            self.outproj(w_out, 1024, 4, xd, bxd)

    def sguC(self, li, w_in, w_out):
        S, d = self.S, self.d
        with ExitStack() as es:
            sb = lambda n, sh, dt=F32, es_=es: self.sb(n, sh, dt, es_)
            vn = sb("vn", [128, 16, DM], BF16)
            bvn = self.grid(16)
            with ExitStack() as es2:
                sb2 = lambda n, sh, dt=F32: self.sb(n, sh, dt, es2)
                ssum = sb2("c_ssum", [128, 16, 4]); ssq = sb2("c_ssq", [128, 16, 4])
                bst = Buf()
                junk = sb2("c_junk", [128, 256], BF16)
                bjunk = Buf()
                S.op('dve', lambda e: e.memset(ssum[:], 0.0), writes=[bst])
                S.op('dve', lambda e: e.memset(ssq[:], 0.0), writes=[bst])
                for q in range(4):
                    wt, bw = self.wload(w_in, 0, 1024, 1024 + q * 256, 256)
                    for n in range(16):
                        ps, bps = self.ps()
                        tok = slice(n * 128, (n + 1) * 128)
                        for kc in range(NCH):
                            S.op('pe', lambda e, ps=ps, kc=kc, tok=tok, wt=wt: e.matmul(ps[:, 0:256], lhsT=self.hnT[:, kc, tok], rhs=wt[:, kc, :], start=(kc == 0), stop=(kc == NCH - 1)),
                                 reads=[bw, self.bhn[kc][n // 4]], writes=[bps], inc=(kc == NCH - 1))
                        S.op('act', lambda e, ps=ps, n=n, q=q: e.activation(out=vn[:, n, q * 256:(q + 1) * 256], in_=ps[:, 0:256], func=AF.Copy, accum_out=ssum[:, n, q:q + 1]),
                             reads=[bps, bst], writes=[bvn[n], bst])
                        S.op('act', lambda e, ps=ps, n=n, q=q: e.activation(out=junk[:], in_=ps[:, 0:256], func=AF.Square, accum_out=ssq[:, n, q:q + 1]),
                             reads=[bps, bst], writes=[bjunk, bst])
                mean = sb2("c_mean", [128, 16]); var = sb2("c_var", [128, 16]); m2 = sb2("c_m2", [128, 16])
                S.op('dve', lambda e: e.tensor_tensor(out=ssum[:, :, 0:2], in0=ssum[:, :, 0:2], in1=ssum[:, :, 2:4], op=ALU.add), reads=[bst], writes=[bst])
                S.op('dve', lambda e: e.tensor_tensor(out=mean[:], in0=ssum[:, :, 0], in1=ssum[:, :, 1], op=ALU.add), reads=[bst], writes=[bst])
                S.op('dve', lambda e: e.tensor_single_scalar(out=mean[:], in_=mean[:], scalar=1.0 / DM, op=ALU.mult), reads=[bst], writes=[bst])
                S.op('dve', lambda e: e.tensor_tensor(out=ssq[:, :, 0:2], in0=ssq[:, :, 0:2], in1=ssq[:, :, 2:4], op=ALU.add), reads=[bst], writes=[bst])
                S.op('dve', lambda e: e.tensor_tensor(out=var[:], in0=ssq[:, :, 0], in1=ssq[:, :, 1], op=ALU.add), reads=[bst], writes=[bst])
                S.op('dve', lambda e: e.tensor_tensor(out=m2[:], in0=mean[:], in1=mean[:], op=ALU.mult), reads=[bst], writes=[bst])
                S.op('dve', lambda e: e.scalar_tensor_tensor(out=var[:], in0=var[:], scalar=1.0 / DM, in1=m2[:], op0=ALU.mult, op1=ALU.subtract), reads=[bst], writes=[bst])
                S.op('dve', lambda e: e.tensor_single_scalar(out=var[:], in_=var[:], scalar=EPS, op=ALU.add), reads=[bst], writes=[bst])
                S.op('act', lambda e: e.activation(out=var[:], in_=var[:], func=AF.Ln), reads=[bst], writes=[bst])
                S.op('act', lambda e: e.activation(out=var[:], in_=var[:], func=AF.Exp, scale=-0.5), reads=[bst], writes=[bst])
                lng = sb2("c_lng", [128, DM]); lnb = sb2("c_lnb", [128, DM])
                bln = Buf()
                S.dma(lng[:], d['sgu_ln_g'][li].partition_broadcast(128), writes=[bln])
                S.dma(lnb[:], d['sgu_ln_b'][li].partition_broadcast(128), writes=[bln])
                tmp = [sb2(f"c_tmp{i}", [128, DM]) for i in range(2)]
                btmp = [Buf(), Buf()]
                for n in range(16):
                    t_, bt_ = tmp[n % 2], btmp[n % 2]
                    S.op('dve', lambda e, n=n, t_=t_: e.tensor_scalar(out=t_[:], in0=vn[:, n, :], scalar1=mean[:, n:n + 1], scalar2=var[:, n:n + 1], op0=ALU.subtract, op1=ALU.mult),
                         reads=[bvn[n], bst], writes=[bt_])
                    S.op('pool', lambda e, t_=t_: e.tensor_tensor(out=t_[:], in0=t_[:], in1=lng[:], op=ALU.mult), reads=[bt_, bln], writes=[bt_])
                    S.op('pool', lambda e, n=n, t_=t_: e.tensor_tensor(out=vn[:, n, :], in0=t_[:], in1=lnb[:], op=ALU.add), reads=[bt_, bln], writes=[bvn[n]])
            S.barrier()
            mix = sb("mixC", [128, NCH, SEQ], BF16)
            bmix = self.grid(NCH, TQ)
            wsT = sb("c_wsT", [128, 4, 128], BF16)
            bws = Buf()
            bsb = sb("c_bsb", [128, 4, 128])
            bbs = Buf()
            for g in range(4):
                i = g % 2
                st, bst_ = self.wst[i], self.bwst[i]
                S.dma(st[:, 0:128], d['sgu_w'][li][g], writes=[bst_])
                ps, bps = self.ps()
                S.op('pe', lambda e, ps=ps, st=st: e.transpose(out=ps[:, 0:128], in_=st[:, 0:128], identity=self.ident[:]), reads=[bst_, self.bconst], writes=[bps])
                S.op('dve', lambda e, ps=ps, g=g: e.tensor_tensor(out=wsT[:, g, :], in0=ps[:, 0:128], in1=self.triu[:], op=ALU.mult), reads=[bps, self.bconst], writes=[bws])
                S.dma(bsb[:, g, :], d['sgu_b'][li][g].partition_broadcast(128), writes=[bbs])
            ta = [sb(f"c_ta{i}", [128, 512]) for i in range(2)]
            bta = [Buf(), Buf()]
            sg = [sb(f"c_sg{i}", [128, 512]) for i in range(2)]
            bsg = [Buf(), Buf()]
            k = 0
            for c in range(NCH):
                g = c // 2
                wu, bwu = self.wload(w_in, 0, 1024, c * 128, 128)
                wg, bwg = self.wload(w_in, 0, 1024, 2048 + c * 128, 128, prefetch=False)
                for tq in range(TQ):
                    sl = slice(tq * 512, (tq + 1) * 512)
                    a_, ba_, s_, bs_ = ta[k], bta[k], sg[k], bsg[k]
                    k = 1 - k
                    ps, bps = self.ps()
                    for j in range(4):
                        n = 4 * tq + j
                        S.op('pe', lambda e, ps=ps, j=j, n=n, c=c, g=g: e.matmul(ps[:, j * 128:(j + 1) * 128], lhsT=vn[:, n, c * 128:(c + 1) * 128], rhs=wsT[:, g, :], start=True, stop=True),
                             reads=[bvn[n], bws], writes=[bps], inc=(j == 3))
                    S.op('dve', lambda e, ps=ps, g=g, a_=a_: e.tensor_tensor(out=a_[:].rearrange("p (a b) -> p a b", a=4), in0=ps[:].rearrange("p (a b) -> p a b", a=4),
                                                                        in1=bsb[:, g, None, :].to_broadcast([128, 4, 128]), op=ALU.add), reads=[bps, bbs], writes=[ba_])
                    pu, bpu = self.ps()
                    self.mm_fm(pu[:], bpu, wu, bwu, slice(0, 128), lambda kc: self.hnT[:, kc, sl], lambda kc: [self.bhn[kc][tq]], list(range(NCH)))
                    S.op('dve', lambda e, pu=pu, a_=a_: e.tensor_tensor(out=a_[:], in0=pu[:], in1=a_[:], op=ALU.mult), reads=[bpu, ba_], writes=[ba_])
                    pg, bpg = self.ps()
                    self.mm_fm(pg[:], bpg, wg, bwg, slice(0, 128), lambda kc: self.hnT[:, kc, sl], lambda kc: [self.bhn[kc][tq]], list(range(NCH)))
                    S.op('act', lambda e, pg=pg, s_=s_: e.activation(out=s_[:], in_=pg[:], func=AF.Silu), reads=[bpg], writes=[bs_])
                    S.op('pool', lambda e, a_=a_, s_=s_, sl=sl, c=c: e.tensor_tensor(out=mix[:, c, sl], in0=a_[:], in1=s_[:], op=ALU.mult), reads=[ba_, bs_], writes=[bmix[c][tq]])
            self.outproj(w_out, 0, 8, mix, bmix)

    def cross(self, l):
        S, d = self.S, self.d
        with ExitStack() as es:
            sb = lambda n, sh, dt=F32: self.sb(n, sh, dt, es)
            with ExitStack() as es2:
                self.rmsnorm(self.xT, self.bx, SEQ, self.vec['norm_x'][:, l, :], self.hnT, self.bhn, es2)
            S.barrier()
            qT = sb("xqT", [128, 2, SEQ], BF16)
            bq = self.grid(2, TQ)
            mix = sb("mixX", [128, NCH, SEQ], BF16)
            bmix = self.grid(NCH, TQ)
            KT = sb("xKT", [128, NCH, 256], BF16)
            bKT = self.grid(NCH)
            Vx = sb("xV", [128, 2, DM], BF16)
            bVx = self.grid(2)
            pT = [sb(f"xpT{i}", [128, 2, 512], BF16) for i in range(2)]
            bpT = [Buf(), Buf()]
            rden = sb("xrden", [128, 512])
            brden = Buf()
            wkv = d['w_xkv'][l]
            for c in range(NCH):
                wt, bw = self.wload(wkv, 0, 1024, c * 128, 128)
                ps, bps = self.ps()
                self.mm_fm(ps[:, 0:256], bps, wt, bw, slice(0, 128), lambda kc: self.memT[:, kc, :], lambda kc: [self.bmem], list(range(NCH)))
                S.op('act', lambda e, ps=ps, c=c: e.activation(out=KT[:, c, :], in_=ps[:, 0:256], func=AF.Copy), reads=[bps], writes=[bKT[c]])
            for q in range(4):
                wt, bw = self.wload(wkv, 0, 1024, 1024 + q * 256, 256)
                for mt in range(2):
                    ps, bps = self.ps()
                    for kc in range(NCH):
                        S.op('pe', lambda e, ps=ps, kc=kc, mt=mt, wt=wt: e.matmul(ps[:, 0:256], lhsT=self.memT[:, kc, mt * 128:(mt + 1) * 128], rhs=wt[:, kc, :], start=(kc == 0), stop=(kc == NCH - 1)),
                             reads=[bw, self.bmem], writes=[bps], inc=(kc == NCH - 1))
                    S.op('act', lambda e, ps=ps, mt=mt, q=q: e.activation(out=Vx[:, mt, q * 256:(q + 1) * 256], in_=ps[:, 0:256], func=AF.Copy), reads=[bps], writes=[bVx[mt]])
            pi = 0
            for h in range(4):
                for k2 in range(2):
                    self.proj_fm(d['w_xq'][l], (2 * h + k2) * 128, lambda tq, k2=k2: qT[:, k2, tq * 512:(tq + 1) * 512], lambda tq, k2=k2: [bq[k2][tq]], eng=('act' if k2 == 0 else 'dve'))
                for tq in range(TQ):
                    sl = slice(tq * 512, (tq + 1) * 512)
                    p_, bp_ = pT[pi], bpT[pi]
                    pi = 1 - pi
                    for mt in range(2):
                        ps, bps = self.ps()
                        for k2 in range(2):
                            cc = 2 * h + k2
                            S.op('pe', lambda e, ps=ps, cc=cc, mt=mt, k2=k2, sl=sl: e.matmul(ps[:], lhsT=KT[:, cc, mt * 128:(mt + 1) * 128], rhs=qT[:, k2, sl], start=(k2 == 0), stop=(k2 == 1)),
                                 reads=[bKT[cc], bq[k2][tq]], writes=[bps], inc=(k2 == 1))
                        S.op('act', lambda e, ps=ps, mt=mt, p_=p_: e.activation(out=p_[:, mt, :], in_=ps[:], func=AF.Exp, scale=1.0 / 16.0), reads=[bps], writes=[bp_])
                    psd, bpsd = self.ps()
                    for mt in range(2):
                        S.op('pe', lambda e, psd=psd, mt=mt, p_=p_: e.matmul(psd[:], lhsT=self.onesb[:], rhs=p_[:, mt, :], start=(mt == 0), stop=(mt == 1)),
                             reads=[bp_, self.bconst], writes=[bpsd], inc=(mt == 1))
                    S.op('dve', lambda e, psd=psd: e.reciprocal(out=rden[:], in_=psd[:]), reads=[bpsd], writes=[brden])
                    for dc in range(2):
                        cc = 2 * h + dc
                        pso, bpso = self.ps()
                        for mt in range(2):
                            S.op('pe', lambda e, pso=pso, mt=mt, p_=p_, cc=cc: e.matmul(pso[:], lhsT=Vx[:, mt, cc * 128:(cc + 1) * 128], rhs=p_[:, mt, :], start=(mt == 0), stop=(mt == 1)),
                                 reads=[bp_, bVx[mt]], writes=[bpso], inc=(mt == 1))
                        S.op('dve', lambda e, pso=pso, cc=cc, sl=sl: e.tensor_tensor(out=mix[:, cc, sl], in0=pso[:], in1=rden[:], op=ALU.mult),
                             reads=[bpso, brden], writes=[bmix[cc][tq]])
            self.outproj(d['w_xo'][l], 0, 8, mix, bmix)

    def final(self):
        S, d = self.S, self.d
        with ExitStack() as es:
            sb = lambda n, sh, dt=F32: self.sb(n, sh, dt, es)
            blk = 512
            sq = [sb(f"f_sq{i}", [128, blk], BF16) for i in range(2)]
            bsq = [Buf(), Buf()]
            rs = sb("f_rs", [128, blk])
            brs = Buf()
            nrm = [sb(f"f_n{i}", [128, blk]) for i in range(2)]
            bnrm = [Buf(), Buf()]
            gcol = self.vec['final_norm'][:, 0, :]
            ost = [sb(f"f_o{i}", [128, DM]) for i in range(2)]
            bost = [Buf(), Buf()]
            for tq in range(TQ):
                sl = slice(tq * blk, (tq + 1) * blk)
                ps, bps = self.ps()
                for c in range(NCH):
                    j = c % 2
                    S.op('act', lambda e, c=c, j=j: e.activation(out=sq[j][:], in_=self.xT[:, c, sl], func=AF.Square), reads=[self.bx[c][tq]], writes=[bsq[j]])
                    S.op('pe', lambda e, c=c, j=j, ps=ps: e.matmul(ps[:], lhsT=self.onesb[:], rhs=sq[j][:], start=(c == 0), stop=(c == NCH - 1)),
                         reads=[bsq[j], self.bconst], writes=[bps], inc=True)
                S.op('dve', lambda e, ps=ps: e.tensor_scalar(out=rs[:], in0=ps[:], scalar1=1.0 / DM, scalar2=EPS, op0=ALU.mult, op1=ALU.add), reads=[bps], writes=[brs])
                S.op('act', lambda e: e.activation(out=rs[:], in_=rs[:], func=AF.Ln), reads=[brs], writes=[brs])
                S.op('act', lambda e: e.activation(out=rs[:], in_=rs[:], func=AF.Exp, scale=-0.5), reads=[brs], writes=[brs])
                pts = [self.ps() for _ in range(8)]
                for c in range(NCH):
                    n_, bn_ = nrm[c % 2], bnrm[c % 2]
                    S.op('dve', lambda e, c=c, n_=n_: e.scalar_tensor_tensor(out=n_[:], in0=self.xT[:, c, sl], scalar=gcol[:, c:c + 1], in1=rs[:], op0=ALU.mult, op1=ALU.mult),
                         reads=[self.bx[c][tq], brs, self.bconst], writes=[bn_])
                    for j in range(4):
                        pp, bpp = pts[2 * j + c // 4]
                        S.op('pe', lambda e, pp=pp, j=j, c=c, n_=n_: e.transpose(out=pp[:, (c % 4) * 128:(c % 4) * 128 + 128], in_=n_[:, j * 128:(j + 1) * 128], identity=self.ident[:]),
                             reads=[bn_, self.bconst], writes=[bpp], inc=True)
                for j in range(4):
                    n = 4 * tq + j
                    o_, bo_ = ost[n % 2], bost[n % 2]
                    for h in range(2):
                        pp, bpp = pts[2 * j + h]
                        if h == 0:
                            S.op('act', lambda e, pp=pp, o_=o_: e.activation(out=o_[:, 0:512], in_=pp[:], func=AF.Copy), reads=[bpp], writes=[bo_])
                        else:
                            S.op('dve', lambda e, pp=pp, o_=o_: e.tensor_copy(out=o_[:, 512:1024], in_=pp[:]), reads=[bpp], writes=[bo_])
                    S.dma(d['out'][n * 128:(n + 1) * 128, :], o_[:], reads=[bo_])

    def run(self):
        S, d = self.S, self.d
        self.setup()
        if CUT == 1:
            return
        self.load_T(d['x'], 16, self.xT, lambda h, n: [self.bx[c][n // 4] for c in range(4 * h, 4 * h + 4)])
        if CUT == 2:
            return
        with ExitStack() as es:
            mraw = self.sb("mraw", [128, NCH, 256], F32, es)
            bmr = [[Buf()] for _ in range(NCH)]
            self.load_T(d['mem'], 2, mraw, lambda h, n: [bmr[c][0] for c in range(4 * h, 4 * h + 4)])
            bm = [[self.bmem] for _ in range(NCH)]
            self.rmsnorm(mraw, bmr, 256, self.vec['mem_norm'][:, 0, :], self.memT, bm, es)
        S.barrier()
        if CUT == 3:
            return
        for layer in range(self.depth):
            i = layer // 2
            with ExitStack() as es:
                gname = 'norm_ab' if layer % 2 == 0 else 'norm_cd'
                self.rmsnorm(self.xT, self.bx, SEQ, self.vec[gname][:, i, :], self.hnT, self.bhn, es)
            S.barrier()
            if layer % 2 == 0:
                self.even_layer(i)
            else:
                self.odd_layer(i)
            S.barrier()
            if 'cross' in STAGES:
                self.cross(layer)
            S.barrier()
        self.final()


import os
_CACHE = {}
NORM_DIV = int(os.environ.get('NORM_DIV', '1'))
S5_ACT = int(os.environ.get('S5_ACT', '1'))
BARRIERS = int(os.environ.get('BARRIERS', '1'))
ATT_SKIP = set(os.environ.get('ATT_SKIP', '').split(','))
LT_MODE = 0
CUT = 0
STAGES = {'attnA', 'poolB', 'cross', 's5D', 'sguC'}


def build_nc(depth=4):
    key = (depth, tuple(sorted(STAGES)))
    if key in _CACHE:
        return _CACHE[key]
    plan = None
    for pass_ in range(2):
        nc = bass.Bass("TRN2", target_bir_lowering=False)
        with ExitStack() as es:
            S = Sched(nc, es)
            K = Kern(nc, S, es, depth, wplan=plan)
            K.run()
            if pass_ == 0:
                plan = [(k, sp) for k, sp in K.wrec]
                continue
            S.emit()
    _CACHE[key] = nc
    return nc


def kernel(**inputs):
    n = 8
    nc = build_nc(4)
    consts = host_consts()
    x = np.ascontiguousarray(np.asarray(inputs['x'], dtype=np.float32))
    mem = np.ascontiguousarray(np.asarray(inputs['mem'], dtype=np.float32))
    shared = {name: np.ascontiguousarray(np.asarray(inputs[name], dtype=np.float32)) for name, _ in PARAMS}
    shared.update(consts)
    in_maps = []
    for b in range(n):
        m = dict(shared)
        m['x'] = x[b]
        m['mem'] = mem[b]
        in_maps.append(m)
    res = run_bass_kernel_spmd(nc, in_maps, core_ids=list(range(n)))
    return np.stack([np.asarray(r['out'], dtype=np.float32) for r in res.results], axis=0)
```

```python
import numpy as np
import concourse.bass as bass
import concourse.mybir as mybir
from concourse.bass_utils import run_bass_kernel_spmd
from contextlib import ExitStack

F32 = mybir.dt.float32
BF16 = mybir.dt.bfloat16
I32 = mybir.dt.int32
ALU = mybir.AluOpType
AF = mybir.ActivationFunctionType

ENGS = ('pe', 'act', 'dve', 'pool', 'sp')
EP = 20000
NEPOCH = 8
NSLOT = 8

SEQ = 2048
DM = 1024
NCH = 8
TQ = 4
EPS = 1e-6
TWO_PI = float(2 * np.pi)


class Buf:
    __slots__ = ('w', 'r', 'excl')

    def __init__(self, excl=False):
        self.w = None
        self.r = {}
        self.excl = excl


class _Rec:
    def __init__(self):
        self.call = None

    def __getattr__(self, name):
        def f(*a, **k):
            self.call = (name, a, k)
            return None
        return f


class Sched:
    def __init__(self, nc, es):
        self.nc = nc
        self.ops = {e: [] for e in ENGS}
        self.incs = {e: 0 for e in ENGS}
        self.waited = {e: {} for e in ENGS}
        self.sems = {}
        for e in ENGS:
            if e == 'sp':
                continue
            for k in range(NEPOCH):
                self.sems[(e, k)] = es.enter_context(nc.semaphore(f"s_{e}{k}"))
        self.dsem = [es.enter_context(nc.semaphore(f"s_dma{i}")) for i in range(NSLOT)]
        self.dcnt = [0] * NSLOT
        self.dnext = 0
        self.nops = 0

    def _collect(self, eng, reads, writes, extra=()):
        waits = {}

        def need(t):
            if t is None:
                return
            key, n = t
            if key == 'pe' and eng == 'pe':
                return
            if n > self.waited[eng].get(key, 0):
                if n > waits.get(key, 0):
                    waits[key] = n
        for b in reads:
            need(b.w)
            if b.excl:
                for k, t in b.r.items():
                    if k != eng:
                        need(t)
        for b in writes:
            need(b.w)
            for t in b.r.values():
                need(t)
        for t in extra:
            need(t)
        for key, n in waits.items():
            self.waited[eng][key] = n
        return list(waits.items())

    def op(self, eng, fn, reads=(), writes=(), inc=True):
        assert inc or eng == 'pe'
        waits = self._collect(eng, reads, writes)
        rec = _Rec()
        fn(rec)
        name_, a_, k_ = rec.call
        fn = (lambda e, name_=name_, a_=a_, k_=k_: getattr(e, name_)(*a_, **k_))
        n = self.incs[eng] + 1
        assert n <= EP * NEPOCH
        ticket = (eng, n)
        self.ops[eng].append((waits, fn, ('e', n) if inc else None))
        if inc:
            self.incs[eng] = n
        for b in reads:
            b.r[eng] = ticket
        for b in writes:
            b.w = ticket
            b.r = {}
        self.nops += 1
        return ticket

    def dma(self, out, in_, reads=(), writes=(), **kw):
        slot = self.dnext
        self.dnext = (slot + 1) % NSLOT
        prev = self.dcnt[slot]
        key = ('dma', slot)
        extra = [(key, prev)] if prev > 0 else []
        waits = self._collect('sp', reads, writes, extra)
        n = prev + 1
        self.dcnt[slot] = n
        ticket = (key, n)

        def fn(sp, out=out, in_=in_, kw=kw):
            return sp.dma_start(out=out, in_=in_, **kw)
        self.ops['sp'].append((waits, fn, ('d', slot)))
        for b in reads:
            b.r[key] = ticket
        for b in writes:
            b.w = ticket
            b.r = {}
        self.nops += 1
        return ticket

    def barrier(self):
        for eng in ENGS:
            waits = []
            for e2 in ENGS:
                if e2 != eng and e2 != 'sp' and self.incs[e2] > self.waited[eng].get(e2, 0):
                    waits.append((e2, self.incs[e2]))
                    self.waited[eng][e2] = self.incs[e2]
            for s_ in range(NSLOT):
                key = ('dma', s_)
                if self.dcnt[s_] > self.waited[eng].get(key, 0):
                    waits.append((key, self.dcnt[s_]))
                    self.waited[eng][key] = self.dcnt[s_]
            self.ops[eng].append((waits, None, None))

    def _wait(self, e, key, n):
        if isinstance(key, tuple):
            e.wait_ge(self.dsem[key[1]], 16 * n)
        else:
            k = (n - 1) // EP
            e.wait_ge(self.sems[(key, k)], (n - 1) % EP + 1)

    def emit(self):
        nc = self.nc
        fin = [(('dma', s), self.dcnt[s]) for s in range(NSLOT) if self.dcnt[s] > 0]
        with nc.Block() as block:
            def run(ename, e):
                for waits, fn, inc in self.ops[ename]:
                    for key, n in waits:
                        self._wait(e, key, n)
                    if fn is None:
                        continue
                    ins = fn(e)
                    if inc is not None:
                        if inc[0] == 'e':
                            n = inc[1]
                            ins.then_inc(self.sems[(ename, (n - 1) // EP)], 1)
                        else:
                            ins.then_inc(self.dsem[inc[1]], 16)
                if ename == 'sp':
                    for key, n in fin:
                        self._wait(e, key, n)

            @block.tensor
            def _(pe):
                run('pe', pe)

            @block.scalar
            def _(act):
                run('act', act)

            @block.vector
            def _(dve):
                run('dve', dve)

            @block.gpsimd
            def _(pool):
                run('pool', pool)

            @block.sync
            def _(sp):
                run('sp', sp)


PARAMS = [
    ('norm_ab', (2, 1024)), ('w_in_ab', (2, 1024, 6144)), ('pool_w', (2, 4, 256, 256)), ('pool_scale', (2, 1024)),
    ('w_out_ab', (2, 2048, 1024)), ('norm_cd', (2, 1024)), ('w_in_cd', (2, 1024, 4096)), ('sgu_ln_g', (2, 1024)),
    ('sgu_ln_b', (2, 1024)), ('sgu_w', (2, 4, 128, 128)), ('sgu_b', (2, 4, 128)), ('s5_a_re', (2, 32, 64)),
    ('s5_a_im', (2, 32, 64)), ('s5_log_dt', (2, 32)), ('s5_b_re', (2, 32, 64, 16)), ('s5_b_im', (2, 32, 64, 16)),
    ('s5_c_re', (2, 32, 16, 64)), ('s5_c_im', (2, 32, 16, 64)), ('s5_d', (2, 512)), ('glu_w1', (2, 512, 512)),
    ('glu_w2', (2, 512, 512)), ('w_out_cd', (2, 1536, 1024)), ('norm_x', (4, 1024)), ('w_xq', (4, 1024, 1024)),
    ('w_xkv', (4, 1024, 2048)), ('w_xo', (4, 1024, 1024)), ('mem_norm', (1024,)), ('final_norm', (1024,)),
]


def host_consts():
    c = {}
    c['c_ident'] = np.eye(128, dtype=np.float32)
    k = np.arange(128)[:, None]
    q = np.arange(128)[None, :]
    NEGM = -30000.0
    diag = np.where(q >= k, 0.0, NEGM)
    prev = np.where(q <= k, 0.0, NEGM)
    c['c_maskb'] = np.concatenate([diag, prev], axis=1).astype(np.float32)
    c['c_mask01'] = (c['c_maskb'] == 0.0).astype(np.float32)
    c['c_triu'] = (k <= q).astype(np.float32)
    c['c_iota'] = np.tile(np.arange(512, dtype=np.float32)[None, :], (128, 1))
    inv = np.zeros((4, 16), np.float32)
    for g, w in enumerate((2, 4, 8, 16)):
        inv[g] = 1.0 / np.minimum(np.arange(1, 17), w)
    c['c_invcnt'] = np.tile(inv.reshape(1, 64), (128, 1)).astype(np.float32)
    M = np.zeros((128, 4, 8), np.float32)
    for g2 in range(2):
        for gpl in range(4):
            M[g2 * 64:(g2 + 1) * 64, gpl, 2 * gpl + g2] = 1.0
    c['c_bm1'] = M.reshape(128, 32)
    M2 = np.zeros((128, 4, 2), np.float32)
    for gl in range(8):
        for gpl in range(4):
            for g2 in range(2):
                if gl == 2 * gpl + g2:
                    M2[gl * 16:(gl + 1) * 16, gpl, g2] = 1.0
    c['c_bm2'] = M2.reshape(128, 8)
    return c


class Kern:
    def __init__(self, nc, S, es, depth=4, wplan=None):
        self.nc, self.S, self.es, self.depth = nc, S, es, depth
        self.wplan = wplan
        self.wrec = []
        self.wissued = []
        d = {}
        d['x'] = nc.dram_tensor("x", [SEQ, DM], F32, kind="ExternalInput").ap()
        d['mem'] = nc.dram_tensor("mem", [256, DM], F32, kind="ExternalInput").ap()
        for name, shp in PARAMS:
            d[name] = nc.dram_tensor(name, list(shp), F32, kind="ExternalInput").ap()
        for name, arr in host_consts().items():
            d[name] = nc.dram_tensor(name, list(arr.shape), F32, kind="ExternalInput").ap()
        d['out'] = nc.dram_tensor("out", [SEQ, DM], F32, kind="ExternalOutput").ap()
        self.d = d
        self.psi = 0

    def sb(self, name, shape, dt=F32, es=None):
        self.uid = getattr(self, 'uid', 0) + 1
        return (es or self.es).enter_context(self.nc.sbuf_tensor(f"{name}_{self.uid}", shape, dt))

    def grid(self, *dims):
        if len(dims) == 1:
            return [Buf() for _ in range(dims[0])]
        return [self.grid(*dims[1:]) for _ in range(dims[0])]

    def ps(self):
        i = self.psi
        self.psi = (i + 1) % len(self.psr)
        j = self.psr[i]
        return self.P[j], self.bP[j]

    def ps_rot(self, banks):
        self.psr = list(banks)
        self.psi = 0

    def setup(self):
        nc, S, d = self.nc, self.S, self.d
        sb = self.sb
        self.P = [self.es.enter_context(nc.psum_tensor(f"P{i}", [128, 512], F32)) for i in range(8)]
        self.bP = [Buf(excl=True) for _ in range(8)]
        self.ps_rot(range(8))
        self.xT = sb("xT", [128, NCH, SEQ], F32)
        self.bx = self.grid(NCH, TQ)
        self.hnT = sb("hnT", [128, NCH, SEQ], BF16)
        self.bhn = self.grid(NCH, TQ)
        self.wst = [sb(f"wst{i}", [128, 2048], F32) for i in range(2)]
        self.bwst = [Buf() for _ in range(2)]
        self.wbf = [sb(f"wbf{i}", [128, 2048], BF16) for i in range(2)]
        self.bwbf = [Buf() for _ in range(2)]
        self.wi = 0
        self.memT = sb("memT", [128, NCH, 256], BF16)
        self.bmem = Buf()
        self.ident = sb("ident", [128, 128], F32)
        self.identb = sb("identb", [128, 128], BF16)
        self.onesb = sb("onesb", [128, 128], BF16)
        self.maskb = sb("maskb", [128, 256], BF16)
        self.triu = sb("triu", [128, 128], F32)
        self.iota = sb("iota", [128, 512], F32)
        self.invcnt = sb("invcnt", [128, 64], F32)
        self.bm1 = sb("bm1", [128, 32], F32)
        self.bm2 = sb("bm2", [128, 8], F32)
        self.halfpi = sb("halfpi", [128, 1], F32)
        self.bconst = Buf()
        tmp = self.wst[0]
        S.dma(self.ident[:], d['c_ident'], writes=[self.bconst])
        S.dma(self.triu[:], d['c_triu'], writes=[self.bconst])
        S.dma(self.iota[:], d['c_iota'], writes=[self.bconst])
        S.dma(self.invcnt[:], d['c_invcnt'], writes=[self.bconst])
        S.dma(self.bm1[:], d['c_bm1'], writes=[self.bconst])
        S.dma(self.bm2[:], d['c_bm2'], writes=[self.bconst])
        S.dma(tmp[:, 0:256], d['c_maskb'], writes=[self.bwst[0]])
        S.op('pool', lambda e: e.tensor_copy(out=self.maskb[:], in_=tmp[:, 0:256]), reads=[self.bwst[0]], writes=[self.bconst])
        self.mask01 = sb("mask01", [128, 256], BF16)
        S.dma(tmp[:, 256:512], d['c_mask01'], writes=[self.bwst[0]])
        S.op('pool', lambda e: e.tensor_copy(out=self.mask01[:], in_=tmp[:, 256:512]), reads=[self.bwst[0]], writes=[self.bconst])
        S.op('pool', lambda e: e.tensor_copy(out=self.identb[:], in_=self.ident[:]), reads=[self.bconst], writes=[self.bconst])
        S.op('pool', lambda e: e.memset(self.onesb[:], 1.0), writes=[self.bconst])
        S.op('pool', lambda e: e.memset(self.halfpi[:], float(np.pi / 2)), writes=[self.bconst])
        self.magic = sb("magic", [128, 2], F32)
        S.op('pool', lambda e: e.memset(self.magic[:, 0:1], 12582912.0), writes=[self.bconst])
        S.op('pool', lambda e: e.memset(self.magic[:, 1:2], -12582912.0), writes=[self.bconst])
        rows = [('norm_ab', 2, 8), ('norm_cd', 2, 8), ('norm_x', 4, 8), ('pool_scale', 2, 8), ('mem_norm', 1, 8), ('final_norm', 1, 8), ('s5_d', 2, 4)]
        vst = sb("vst", [128, 128], F32)
        bvst = Buf()
        S.op('dve', lambda e: e.memset(vst[:], 0.0), writes=[bvst])
        vall = sb("vall", [128, 128], F32)
        r0 = 0
        self.vec = {}
        for name, n, ch in rows:
            src = d[name]
            if len(src.shape) == 2:
                src2 = src.rearrange("l (c p) -> (l c) p", p=128)
            else:
                src2 = src.rearrange("(c p) -> c p", p=128)
            S.dma(vst[r0:r0 + n * ch, :], src2, writes=[bvst])
            self.vec[name] = vall[:, r0:r0 + n * ch].rearrange("p (l c) -> p l c", c=ch)
            r0 += n * ch
        ps, bps = self.ps()
        S.op('pe', lambda e: e.transpose(out=ps[:, 0:128], in_=vst[:], identity=self.ident[:]), reads=[bvst, self.bconst], writes=[bps])
        S.op('dve', lambda e: e.tensor_copy(out=vall[:], in_=ps[:, 0:128]), reads=[bps], writes=[self.bconst])

    def wload(self, wap, r0, nr, c0, ncols, prefetch=True):
        key = (wap.tensor.name, int(wap.offset), r0, nr, c0, ncols)
        if self.wplan is None:
            self.wrec.append((key, (wap.tensor.name, int(wap.offset), int(wap.shape[0]), int(wap.shape[1]), r0, nr, c0, ncols)))
            return self._wload_now(wap, r0, nr, c0, ncols)
        if not self.wissued:
            self._wprefetch()
        k0, tile = self.wissued.pop(0)
        assert k0 == key, (k0, key)
        if prefetch:
            self._wprefetch()
        return tile

    def wfence(self):
        if self.wplan is None:
            self.wrec.append((None, None))
            return
        assert not self.wissued
        assert self.wplan and self.wplan[0][0] is None
        self.wplan.pop(0)

    def _wprefetch(self):
        if self.wissued or not self.wplan or self.wplan[0][0] is None:
            return
        key, (name, off, R, C, r0, nr, c0, ncols) = self.wplan.pop(0)
        full = self.d[name]
        nd_ = len(full.shape)
        flat = full if nd_ == 1 else full.rearrange(" ".join("abcd"[:nd_]) + " -> (" + " ".join("abcd"[:nd_]) + ")")
        wap = flat[off:off + R * C].rearrange("(r c) -> r c", c=C)
        self.wissued.append((key, self._wload_now(wap, r0, nr, c0, ncols)))

    def _wload_now(self, wap, r0, nr, c0, ncols):
        S = self.S
        P = min(nr, 128)
        kc = nr // P
        assert kc * ncols <= 2048
        i = self.wi
        self.wi = (i + 1) % 2
        st, bst, wb, bwb = self.wst[i], self.bwst[i], self.wbf[i], self.bwbf[i]
        n = kc * ncols
        src = wap[r0:r0 + nr, c0:c0 + ncols].rearrange("(kc p) c -> p kc c", p=P)
        S.dma(st[0:P, 0:n].rearrange("p (kc c) -> p kc c", c=ncols), src, writes=[bst])
        S.op('act', lambda e: e.activation(out=wb[0:P, 0:n], in_=st[0:P, 0:n], func=AF.Copy), reads=[bst], writes=[bwb])
        return wb[0:P, 0:n].rearrange("p (kc c) -> p kc c", c=ncols), bwb

    def mm_fm(self, ps_ap, bps, wt, bw, mcols, rhs_fn, rhs_bufs, kcs, first=True, last=True):
        S = self.S
        n = len(kcs)
        for i, kc in enumerate(kcs):
            rhs = rhs_fn(kc)
            S.op('pe', lambda e, kc=kc, rhs=rhs, i=i: e.matmul(ps_ap, lhsT=wt[:, kc, mcols], rhs=rhs,
                                                          start=(first and i == 0), stop=(last and i == n - 1)),
                 reads=[bw] + rhs_bufs(kc), writes=[bps], inc=(i == n - 1))

    def rmsnorm(self, src, bsrc, N, gcol, dst, bdst, es):
        S = self.S
        blk = min(N, 512)
        sq = [self.sb(f"rn_sq{i}_{N}", [128, blk], BF16, es) for i in range(2)]
        bsq = [Buf(), Buf()]
        rs = self.sb(f"rn_rs_{N}", [128, blk], F32, es)
        brs = Buf()
        for t in range(N // blk):
            sl = slice(t * blk, (t + 1) * blk)
            ps, bps = self.ps()
            for c in range(NCH):
                j = c % 2
                S.op('act', lambda e, c=c, j=j: e.activation(out=sq[j][:], in_=src[:, c, sl], func=AF.Square),
                     reads=[bsrc[c][t]], writes=[bsq[j]])
                S.op('pe', lambda e, c=c, j=j: e.matmul(ps[:, 0:blk], lhsT=self.onesb[:], rhs=sq[j][:], start=(c == 0), stop=(c == NCH - 1)),
                     reads=[bsq[j], self.bconst], writes=[bps], inc=True)
            S.op('dve', lambda e: e.tensor_scalar(out=rs[:], in0=ps[:, 0:blk], scalar1=1.0 / DM, scalar2=EPS, op0=ALU.mult, op1=ALU.add),
                 reads=[bps], writes=[brs])
            S.op('act', lambda e: e.activation(out=rs[:], in_=rs[:], func=AF.Ln), reads=[brs], writes=[brs])
            S.op('act', lambda e: e.activation(out=rs[:], in_=rs[:], func=AF.Exp, scale=-0.5), reads=[brs], writes=[brs])
            for c in range(NCH):
                S.op('dve', lambda e, c=c: e.scalar_tensor_tensor(out=dst[:, c, sl], in0=src[:, c, sl], scalar=gcol[:, c:c + 1], in1=rs[:],
                                                                   op0=ALU.mult, op1=ALU.mult),
                     reads=[bsrc[c][t], brs, self.bconst], writes=[bdst[c][t]])

    def load_T(self, src_ap, ntiles, dst, bdst_fn):
        S = self.S
        for n in range(ntiles):
            i = n % 2
            st, bst = self.wst[i], self.bwst[i]
            S.dma(st[:, 0:1024], src_ap[n * 128:(n + 1) * 128, :], writes=[bst])
            for h in range(2):
                ps, bps = self.ps()
                for j in range(4):
                    c = 4 * h + j
                    S.op('pe', lambda e, c=c, j=j: e.transpose(out=ps[:, j * 128:(j + 1) * 128], in_=st[:, c * 128:(c + 1) * 128], identity=self.ident[:]),
                         reads=[bst, self.bconst], writes=[bps], inc=(j == 3))
                for j in range(4):
                    c = 4 * h + j
                    if LT_MODE == 0 or (LT_MODE == 2 and j % 2 == 1):
                        S.op('dve', lambda e, c=c, j=j, ps=ps: e.tensor_copy(out=dst[:, c, n * 128:(n + 1) * 128], in_=ps[:, j * 128:(j + 1) * 128]),
                             reads=[bps], writes=bdst_fn(h, n))
                    else:
                        S.op('act', lambda e, c=c, j=j, ps=ps: e.activation(out=dst[:, c, n * 128:(n + 1) * 128], in_=ps[:, j * 128:(j + 1) * 128], func=AF.Copy),
                             reads=[bps], writes=bdst_fn(h, n))

    def outproj(self, wap, r0, nk, mix, bmix):
        S = self.S
        for oc in range(NCH):
            wt, bw = self.wload(wap, r0, nk * 128, oc * 128, 128)
            for tq in range(TQ):
                ps, bps = self.ps()
                sl = slice(tq * 512, (tq + 1) * 512)
                self.mm_fm(ps[:], bps, wt, bw, slice(0, 128), lambda kc: mix[:, kc, sl], lambda kc: [bmix[kc][tq]], list(range(nk)))
                S.op('dve', lambda e, oc=oc, sl=sl, ps=ps: e.tensor_tensor(out=self.xT[:, oc, sl], in0=ps[:], in1=self.xT[:, oc, sl], op=ALU.add),
                     reads=[bps, self.bx[oc][tq]], writes=[self.bx[oc][tq]])
        if BARRIERS:
            S.barrier()

    def proj_fm(self, wap, c0, dst_fn, bdst_fn, func=AF.Copy, eng='act'):
        S = self.S
        wt, bw = self.wload(wap, 0, 1024, c0, 128)
        for tq in range(TQ):
            ps, bps = self.ps()
            sl = slice(tq * 512, (tq + 1) * 512)
            self.mm_fm(ps[:], bps, wt, bw, slice(0, 128), lambda kc: self.hnT[:, kc, sl], lambda kc: [self.bhn[kc][tq]], list(range(NCH)))
            if eng == 'act':
                S.op('act', lambda e, tq=tq, ps=ps: e.activation(out=dst_fn(tq), in_=ps[:], func=func), reads=[bps], writes=bdst_fn(tq))
            else:
                S.op('dve', lambda e, tq=tq, ps=ps: e.tensor_copy(out=dst_fn(tq), in_=ps[:]), reads=[bps], writes=bdst_fn(tq))

    def even_layer(self, li):
        S, d = self.S, self.d
        w_in = d['w_in_ab'][li]
        w_out = d['w_out_ab'][li]
        with ExitStack() as es:
            if 'attnA' in STAGES:
                mix = self.sb("mixA", [128, NCH, SEQ], BF16, es)
                bmix = self.grid(NCH, TQ)
                self.attnA(li, w_in, mix, bmix)
                self.outproj(w_out, 0, 8, mix, bmix)
        if 'poolB' not in STAGES:
            return
        with ExitStack() as es:
            mix = self.sb("mixB", [128, NCH, SEQ], BF16, es)
            bmix = self.grid(NCH, TQ)
            self.poolB(li, w_in, mix, bmix)
            self.outproj(w_out, 1024, 8, mix, bmix)

    def attnA(self, li, w_in, mix, bmix):
        S = self.S
        with ExitStack() as es:
            qT = self.sb("qT", [128, SEQ], BF16, es)
            kT = self.sb("kT", [128, SEQ], BF16, es)
            gT = self.sb("gT", [128, SEQ], BF16, es)
            bq, bk, bg = self.grid(TQ), self.grid(TQ), self.grid(TQ)
            Va = self.sb("Vaug", [128, 3, 16, 2, 128], BF16, es)
            bVa = self.grid(3, 4)
            bVones = Buf()
            S.op('pool', lambda e: e.memset(Va[:, :, :, 0, 64:128], 1.0), writes=[bVones])
            S.op('pool', lambda e: e.memset(Va[:, :, :, 1, 0:64], 1.0), writes=[bVones])
            pT = [self.sb(f"pT{i}", [128, 256], BF16, es) for i in range(4)]
            bpT = [Buf() for _ in range(4)]
            eT = [self.sb(f"eT{i}", [128, 256], BF16, es) for i in range(4)]
            beT = [Buf() for _ in range(4)]
            rden = self.sb("rden", [128, 512], F32, es)
            tmpn = self.sb("tmpn", [128, 512], F32, es)
            brden, btmpn = Buf(), Buf()
            pti = 0
            for c in range(NCH):
                self.ps_rot(range(8))
                self.proj_fm(w_in, c * 128, lambda tq: qT[:, tq * 512:(tq + 1) * 512], lambda tq: [bq[tq]])
                self.proj_fm(w_in, 1024 + c * 128, lambda tq: kT[:, tq * 512:(tq + 1) * 512], lambda tq: [bk[tq]], eng='dve')
                self.proj_fm(w_in, 3072 + c * 128, lambda tq: gT[:, tq * 512:(tq + 1) * 512], lambda tq: [bg[tq]], func=AF.Silu)
                wv, bwv = self.wload(w_in, 0, 1024, 2048 + c * 128, 128)
                for o, dd in enumerate((1, 4, 16)):
                    nb = 16 // dd
                    for t4 in range(4):
                        ps, bps = self.ps()
                        for j in range(4):
                            ti = 4 * t4 + j
                            r, kb = ti // nb, ti % nb
                            st = r + dd * 128 * kb
                            tok = slice(st, st + dd * 127 + 1, dd)
                            tqs = sorted(set([(st) // 512, (st + dd * 127) // 512])) if dd < 16 else [0, 1, 2, 3]
                            for kc in range(NCH):
                                S.op('pe', lambda e, kc=kc, tok=tok, j=j, ps=ps: e.matmul(ps[:, j * 128:(j + 1) * 128], lhsT=self.hnT[:, kc, tok], rhs=wv[:, kc, :],
                                                                                    start=(kc == 0), stop=(kc == NCH - 1)),
                                     reads=[bwv] + [self.bhn[kc][q_] for q_ in tqs], writes=[bps], inc=(kc == NCH - 1 and j == 3))
                        psv = ps[:].rearrange("p (j h d) -> p j h d", j=4, h=2)
                        S.op('act', lambda e, o=o, t4=t4, psv=psv: e.activation(out=Va[:, o, 4 * t4:4 * t4 + 4, 0, 0:64], in_=psv[:, :, 0, :], func=AF.Copy),
                             reads=[bps], writes=[bVa[o][t4]])
                        S.op('dve', lambda e, o=o, t4=t4, psv=psv: e.tensor_copy(out=Va[:, o, 4 * t4:4 * t4 + 4, 1, 64:128], in_=psv[:, :, 1, :]),
                             reads=[bps], writes=[bVa[o][t4]])
                for hh in range(2):
                    hp = slice(hh * 64, hh * 64 + 64)
                    dp = slice(64 - hh * 64, 128 - hh * 64)
                    nd = [self.P[i] for i in range(4)]
                    bnd = [self.bP[i] for i in range(4)]
                    started = [False] * 4
                    self.ps_rot(range(4, 8))
                    tiles = []
                    for o, dd in enumerate((1, 4, 16)):
                        nb = 16 // dd
                        for r in range(dd):
                            for kb in range(nb):
                                tiles.append((o, dd, nb, r, kb))
                    LA = 3
                    pend = {}

                    def issue_S(idx):
                        nonlocal pti
                        o, dd, nb, r, kb = tiles[idx]
                        nq = 2 if kb < nb - 1 else 1
                        N = 128 * nq
                        kst = r + dd * 128 * kb
                        ktok = slice(kst, kst + dd * 127 + 1, dd)
                        qtok = slice(kst, kst + dd * (N - 1) + 1, dd)
                        ktqs = [kst // 512] if dd < 16 else [0, 1, 2, 3]
                        qtqs = sorted(set([kst // 512, (kst + dd * (N - 1)) // 512])) if dd < 16 else [0, 1, 2, 3]
                        ps, bps = self.ps()
                        S.op('pe', lambda e: e.matmul(ps[:, 0:N], lhsT=kT[hp, ktok], rhs=qT[hp, qtok], start=True, stop=True),
                             reads=[bk[q_] for q_ in ktqs] + [bq[q_] for q_ in qtqs], writes=[bps], inc=True)
                        p_, bp_ = pT[pti], bpT[pti]
                        e_, be_ = eT[pti], beT[pti]
                        pti = (pti + 1) % len(pT)
                        S.op('act', lambda e: e.activation(out=e_[:, 0:N], in_=ps[:, 0:N], func=AF.Exp, scale=0.125),
                             reads=[bps], writes=[be_])
                        S.op('dve', lambda e: e.tensor_tensor(out=p_[:, 0:N], in0=e_[:, 0:N], in1=self.mask01[:, 0:N], op=ALU.mult),
                             reads=[be_, self.bconst], writes=[bp_])
                        pend[idx] = (p_, bp_, nq)

                    def issue_PV(idx):
                        o, dd, nb, r, kb = tiles[idx]
                        p_, bp_, nq = pend.pop(idx)
                        ti = r * nb + kb
                        lhsV = Va[:, o, ti, hh, :]
                        allouts = []
                        if dd == 1 and nq == 2 and kb % 4 != 3:
                            allouts = [(kb // 4, slice((kb % 4) * 128, (kb % 4) * 128 + 256), slice(0, 256))]
                            nq = 0
                        for qi in range(nq):
                            i = kb + qi
                            if dd == 1:
                                allouts += [(i // 4, slice((i % 4) * 128, (i % 4) * 128 + 128), slice(qi * 128, qi * 128 + 128))]
                            elif dd == 4:
                                allouts += [(i, slice(r, 512, 4), slice(qi * 128, qi * 128 + 128))]
                            else:
                                allouts += [(j, slice(r, 512, 16), slice(j * 32, j * 32 + 32)) for j in range(4)]
                        for n_, (bank, ocols, pcols) in enumerate(allouts):
                            first = not started[bank]
                            started[bank] = True
                            S.op('pe', lambda e: e.matmul(nd[bank][:, ocols], lhsT=lhsV, rhs=p_[:, pcols], start=first, stop=False, skip_group_check=True),
                                 reads=[bp_, bVa[o][ti // 4], bVones], writes=[bnd[bank]], inc=(n_ == len(allouts) - 1))

                    for idx in range(len(tiles) + LA):
                        if idx < len(tiles):
                            issue_S(idx)
                        if idx >= LA:
                            issue_PV(idx - LA)
                    for tq in range(TQ if 'norm' not in ATT_SKIP else 0):
                        sl = slice(tq * 512, (tq + 1) * 512)
                        S.op('act', lambda e, tq=tq: e.activation(out=rden[hp, :], in_=nd[tq][dp, :], func=AF.Ln), reads=[bnd[tq]], writes=[brden])
                        S.op('act', lambda e: e.activation(out=rden[hp, :], in_=rden[hp, :], func=AF.Exp, scale=-1.0), reads=[brden], writes=[brden])
                        S.op('dve', lambda e, tq=tq: e.tensor_tensor(out=tmpn[hp, :], in0=nd[tq][hp, :], in1=rden[hp, :], op=ALU.mult),
                             reads=[bnd[tq], brden], writes=[btmpn])
                        S.op('pool', lambda e, sl=sl: e.tensor_tensor(out=mix[hp, c, sl], in0=tmpn[hp, :], in1=gT[hp, sl], op=ALU.mult),
                             reads=[btmpn, bg[tq]], writes=[bmix[c][tq]])
            self.ps_rot(range(8))

    def poolB(self, li, w_in, mix, bmix):
        S = self.S
        with ExitStack() as es:
            vb = self.sb("vb", [128, 16 + SEQ], F32, es)
            sA = self.sb("sA", [128, 16 + SEQ], F32, es)
            sB = self.sb("sB", [128, 16 + SEQ], F32, es)
            bvb, bsA, bsB = Buf(), Buf(), Buf()
            pooled = self.sb("pooled", [128, 2, SEQ], BF16, es)
            bpo = self.grid(2, TQ)
            gb = self.sb("gbT", [128, 2, SEQ], BF16, es)
            bgb = self.grid(2, TQ)
            t16 = self.sb("t16", [128, 16], F32, es)
            bt16 = Buf()
            for t_ in (vb, sA, sB):
                S.op('pool', lambda e, t_=t_: e.memset(t_[:, 0:16], 0.0), writes=[bvb, bsA, bsB])
            for g in range(4):
                w = (2, 4, 8, 16)[g]
                for j in range(2):
                    cb = 2 * g + j
                    self.proj_fm(w_in, 4096 + cb * 128, lambda tq: vb[:, 16 + tq * 512:16 + (tq + 1) * 512], lambda tq: [bvb])
                    self.proj_fm(w_in, 5120 + cb * 128, lambda tq, j=j: gb[:, j, tq * 512:(tq + 1) * 512], lambda tq, j=j: [bgb[j][tq]], func=AF.Silu)
                    cur, bcur = vb, bvb
                    k = 1
                    nxt = [(sA, bsA), (sB, bsB)]
                    ni = 0
                    while k < w:
                        o_, bo_ = nxt[ni]
                        ni = 1 - ni
                        S.op('pool', lambda e, o_=o_, cur=cur, k=k: e.tensor_tensor(out=o_[:, 16:16 + SEQ], in0=cur[:, 16:16 + SEQ], in1=cur[:, 16 - k:16 - k + SEQ], op=ALU.add),
                             reads=[bcur], writes=[bo_])
                        cur, bcur = o_, bo_
                        k *= 2
                    S.op('dve', lambda e, cur=cur, j=j, w=w: e.scalar_tensor_tensor(out=pooled[:, j, :], in0=cur[:, 16:16 + SEQ], scalar=1.0 / w, in1=vb[:, 16:16 + SEQ],
                                                                                 op0=ALU.mult, op1=ALU.subtract),
                         reads=[bcur, bvb], writes=[bpo[j][t] for t in range(TQ)])
                    S.op('dve', lambda e, cur=cur, g=g: e.tensor_tensor(out=t16[:], in0=cur[:, 16:32], in1=self.invcnt[:, g * 16:(g + 1) * 16], op=ALU.mult),
                         reads=[bcur, self.bconst], writes=[bt16])
                    S.op('dve', lambda e, j=j: e.tensor_tensor(out=pooled[:, j, 0:16], in0=t16[:], in1=vb[:, 16:32], op=ALU.subtract),
                         reads=[bt16, bvb], writes=[bpo[j][0]])
                wp, bwp = self.wload(self.d['pool_w'][li][g], 0, 256, 0, 256)
                for oc2 in range(2):
                    cb = 2 * g + oc2
                    for tq in range(TQ):
                        sl = slice(tq * 512, (tq + 1) * 512)
                        ps, bps = self.ps()
                        self.mm_fm(ps[:], bps, wp, bwp, slice(oc2 * 128, oc2 * 128 + 128), lambda kc: pooled[:, kc, sl], lambda kc: [bpo[kc][tq]], [0, 1])
                        S.op('dve', lambda e, ps=ps, cb=cb, oc2=oc2, sl=sl: e.scalar_tensor_tensor(out=mix[:, cb, sl], in0=ps[:], scalar=self.vec['pool_scale'][:, li, cb:cb + 1],
                                                                                           in1=gb[:, oc2, sl], op0=ALU.mult, op1=ALU.mult),
                             reads=[bps, bgb[oc2][tq], self.bconst], writes=[bmix[cb][tq]])

    def odd_layer(self, li):
        d = self.d
        if 's5D' in STAGES:
            self.s5D(li, d['w_in_cd'][li], d['w_out_cd'][li])
        if 'sguC' in STAGES:
            self.sguC(li, d['w_in_cd'][li], d['w_out_cd'][li])

    def trig(self, y, by, yi, byi, yf, byf, cs, bcs, sn, bsn, on_act=False, f_eng='dve'):
        S = self.S
        if on_act:
            S.op('act', lambda e: e.activation(out=yf, in_=y, func=AF.Identity, bias=self.magic[:, 0:1], scale=1.0), reads=[by, self.bconst], writes=[byf])
            S.op('act', lambda e: e.activation(out=yf, in_=yf, func=AF.Identity, bias=self.magic[:, 1:2], scale=1.0), reads=[byf, self.bconst], writes=[byf])
        else:
            S.op('dve', lambda e: e.tensor_copy(out=yi, in_=y), reads=[by], writes=[byi])
            S.op('dve', lambda e: e.tensor_copy(out=yf, in_=yi), reads=[byi], writes=[byf])
        S.op(f_eng, lambda e: e.tensor_tensor(out=yf, in0=y, in1=yf, op=ALU.subtract), reads=[by, byf], writes=[byf])
        S.op('act', lambda e: e.activation(out=sn, in_=yf, func=AF.Sin, scale=TWO_PI), reads=[byf], writes=[bsn])
        S.op('act', lambda e: e.activation(out=y, in_=yf, func=AF.Abs), reads=[byf], writes=[by])
        S.op('act', lambda e: e.activation(out=cs, in_=y, func=AF.Sin, scale=-TWO_PI, bias=self.halfpi[:, 0:1]), reads=[by, self.bconst], writes=[bcs])

    def s5D(self, li, w_in, w_out):
        S, d = self.S, self.d
        with ExitStack() as es:
            sb = lambda n, sh, dt=F32, es_=es: self.sb(n, sh, dt, es_)
            xd = sb("xdT", [128, 4, SEQ], BF16)
            bxd = self.grid(4, TQ)
            yg = sb("ygT", [128, 4, SEQ], BF16)
            byg = self.grid(4, TQ)
            for kc in range(4):
                self.proj_fm(w_in, 3072 + kc * 128, lambda tq, kc=kc: xd[:, kc, tq * 512:(tq + 1) * 512], lambda tq, kc=kc: [bxd[kc][tq]])
            ar = sb("s_ar", [128, 16]); ai = sb("s_ai", [128, 16]); ldt = sb("s_ldt", [128, 16])
            bprm = Buf()
            pst = sb("s_pst", [32, 128]); ldt0 = sb("s_ldt0", [128, 32])
            bpst = Buf()
            S.dma(pst[:, 0:64], d['s5_a_re'][li], writes=[bpst])
            S.dma(pst[:, 64:128], d['s5_a_im'][li], writes=[bpst])
            S.dma(ldt0[:], d['s5_log_dt'][li].partition_broadcast(128), writes=[bpst])
            for (dst_, c0) in ((ar, 0), (ai, 64)):
                ps, bps = self.ps()
                S.op('pe', lambda e, ps=ps, c0=c0: e.transpose(out=ps[0:64, 0:32], in_=pst[:, c0:c0 + 64], identity=self.ident[0:32, 0:32]), reads=[bpst, self.bconst], writes=[bps])
                S.op('dve', lambda e, ps=ps, dst_=dst_: e.tensor_copy(out=dst_[0:64, :], in_=ps[0:64, 0:32:2]), reads=[bps], writes=[bprm])
                S.op('dve', lambda e, ps=ps, dst_=dst_: e.tensor_copy(out=dst_[64:128, :], in_=ps[0:64, 1:32:2]), reads=[bps], writes=[bprm])
            S.op('dve', lambda e: e.tensor_copy(out=ldt[0:64, :], in_=ldt0[0:64, 0:32:2]), reads=[bpst], writes=[bprm])
            S.op('dve', lambda e: e.tensor_copy(out=ldt[64:128, :], in_=ldt0[64:128, 1:32:2]), reads=[bpst], writes=[bprm])
            names = ["dt", "dtar", "th", "rho", "c0", "s0", "abr", "abi", "inv", "t1", "t2", "cfr", "cfi", "yf", "thn", "y0"]
            T = {n: sb("s_" + n, [128, 16]) for n in names}
            yi0 = sb("s_yi", [128, 16], I32)
            B = {n: Buf() for n in names + ["yi"]}

            def dv(out, fn, reads, eng='dve'):
                S.op(eng, fn, reads=[B[r] if isinstance(r, str) else r for r in reads], writes=[B[out]])
            TT = lambda o, a, b_, op: (lambda e: e.tensor_tensor(out=T[o][:], in0=a[:], in1=b_[:], op=op))
            dv("dt", lambda e: e.activation(out=T["dt"][:], in_=ldt[:], func=AF.Exp), [bprm], 'act')
            dv("dtar", TT("dtar", T["dt"], ar, ALU.mult), ["dt", bprm])
            dv("th", TT("th", T["dt"], ai, ALU.mult), ["dt", bprm])
            dv("rho", lambda e: e.activation(out=T["rho"][:], in_=T["dtar"][:], func=AF.Exp), ["dtar"], 'act')
            dv("thn", lambda e: e.tensor_single_scalar(out=T["thn"][:], in_=T["th"][:], scalar=1.0 / TWO_PI, op=ALU.mult), ["th"])
            dv("y0", lambda e: e.tensor_copy(out=T["y0"][:], in_=T["thn"][:]), ["thn"])
            self.trig(T["y0"][:], B["y0"], yi0[:], B["yi"], T["yf"][:], B["yf"], T["c0"][:], B["c0"], T["s0"][:], B["s0"], on_act=True)
            dv("abr", TT("abr", T["rho"], T["c0"], ALU.mult), ["rho", "c0"])
            dv("abi", TT("abi", T["rho"], T["s0"], ALU.mult), ["rho", "s0"])
            dv("abr", lambda e: e.tensor_single_scalar(out=T["abr"][:], in_=T["abr"][:], scalar=-1.0, op=ALU.add), ["abr"])
            dv("t1", TT("t1", ar, ar, ALU.mult), [bprm])
            dv("t2", TT("t2", ai, ai, ALU.mult), [bprm])
            dv("inv", TT("inv", T["t1"], T["t2"], ALU.add), ["t1", "t2"])
            dv("inv", lambda e: e.reciprocal(out=T["inv"][:], in_=T["inv"][:]), ["inv"])
            dv("t1", TT("t1", T["abr"], ar, ALU.mult), ["abr", bprm])
            dv("t2", TT("t2", T["abi"], ai, ALU.mult), ["abi", bprm])
            dv("cfr", TT("cfr", T["t1"], T["t2"], ALU.add), ["t1", "t2"])
            dv("cfr", TT("cfr", T["cfr"], T["inv"], ALU.mult), ["cfr", "inv"])
            dv("t1", TT("t1", T["abi"], ar, ALU.mult), ["abi", bprm])
            dv("t2", TT("t2", T["abr"], ai, ALU.mult), ["abr", bprm])
            dv("cfi", TT("cfi", T["t1"], T["t2"], ALU.subtract), ["t1", "t2"])
            dv("cfi", TT("cfi", T["cfi"], T["inv"], ALU.mult), ["cfi", "inv"])
            off = sb("s_off", [128, 16, 4])
            boff = Buf()
            for tq in range(TQ):
                S.op('dve', lambda e, tq=tq: e.tensor_single_scalar(out=off[:, :, tq], in_=T["thn"][:], scalar=512.0 * tq, op=ALU.mult), reads=[B["thn"]], writes=[boff])
            Bre = sb("s_Bre", [128, 16, 16]); Bim = sb("s_Bim", [128, 16, 16])
            bB = Buf()
            S.dma(Bre[:], d['s5_b_re'][li].rearrange("(gp g2) p h -> (g2 p) gp h", g2=2), writes=[bB])
            S.dma(Bim[:], d['s5_b_im'][li].rearrange("(gp g2) p h -> (g2 p) gp h", g2=2), writes=[bB])
            bbr = sb("s_bbr", [128, 16, 16]); bbi = sb("s_bbi", [128, 16, 16]); bt = sb("s_bt", [128, 16, 16])
            bbb, bbt = Buf(), Buf()
            cfr_b = T["cfr"][:, :, None].to_broadcast([128, 16, 16])
            cfi_b = T["cfi"][:, :, None].to_broadcast([128, 16, 16])
            S.op('dve', lambda e: e.tensor_tensor(out=bbr[:], in0=Bre[:], in1=cfr_b, op=ALU.mult), reads=[bB, B["cfr"]], writes=[bbb])
            S.op('dve', lambda e: e.tensor_tensor(out=bt[:], in0=Bim[:], in1=cfi_b, op=ALU.mult), reads=[bB, B["cfi"]], writes=[bbt])
            S.op('dve', lambda e: e.tensor_tensor(out=bbr[:], in0=bbr[:], in1=bt[:], op=ALU.subtract), reads=[bbb, bbt], writes=[bbb])
            S.op('dve', lambda e: e.tensor_tensor(out=bbi[:], in0=Bim[:], in1=cfr_b, op=ALU.mult), reads=[bB, B["cfr"]], writes=[bbb])
            S.op('dve', lambda e: e.tensor_tensor(out=bt[:], in0=Bre[:], in1=cfi_b, op=ALU.mult), reads=[bB, B["cfi"], bbb], writes=[bbt])
            S.op('dve', lambda e: e.tensor_tensor(out=bbi[:], in0=bbi[:], in1=bt[:], op=ALU.add), reads=[bbb, bbt], writes=[bbb])
            Cre = sb("s_Cre", [128, 4, 64]); Cim = sb("s_Cim", [128, 4, 64])
            bC = Buf()
            S.dma(Cre[:], d['s5_c_re'][li].rearrange("(kc gl) h p -> (gl h) kc p", kc=4), writes=[bC])
            S.dma(Cim[:], d['s5_c_im'][li].rearrange("(kc gl) h p -> (gl h) kc p", kc=4), writes=[bC])
            Dg = sb("s_Dg", [128, 4, 128], BF16)
            bDg = Buf()
            for kc in range(4):
                S.op('dve', lambda e, kc=kc: e.tensor_single_scalar(out=Dg[:, kc, :], in_=self.ident[:], scalar=self.vec['s5_d'][:, li, kc:kc + 1], op=ALU.mult),
                     reads=[self.bconst], writes=[bDg])
            bm1 = self.bm1[:].rearrange("p (a b) -> p a b", a=4)
            bm2 = self.bm2[:].rearrange("p (a b) -> p a b", a=4)
            with ExitStack() as es2:
                sb2 = lambda n, sh, dt=F32: self.sb(n, sh, dt, es2)
                L = sb2("s_L", [128, 2, 4, 128], BF16)
                Cc = sb2("s_Cc", [128, 2, 4, 128], BF16)
                bL, bCc = Buf(), Buf()
                Z = [sb2(f"s_Z{i}", [128, 128]) for i in range(2)]
                bZ = [Buf(), Buf()]
                zi = 0
                NW = 512
                wkA = {}
                for n in ("y", "yf", "cs", "sn", "br", "bi", "t1", "t2", "t3", "t4"):
                    wkA[n] = (sb2("k_" + n, [128, NW])[:], Buf())
                gl1 = sb2("k_gl1", [128, NW])
                wkA["yi"] = (gl1[:], Buf())
                wkB = {}
                for j, n in enumerate(("y", "yf", "cs", "sn")):
                    wkB[n] = (self.wst[0][:, j * NW:(j + 1) * NW], Buf())
                for j, n in enumerate(("br", "bi", "t1", "t2")):
                    wkB[n] = (self.wst[1][:, j * NW:(j + 1) * NW], Buf())
                wb0f = self.wbf[0][:].bitcast(F32)
                for j, n in enumerate(("t3", "t4")):
                    wkB[n] = (wb0f[:, j * NW:(j + 1) * NW], Buf())
                wkB["yi"] = (self.wbf[1][:].bitcast(I32)[:, 0:NW], Buf())
                wks = [wkA, wkB]
                cur = [0]
                wb1f = self.wbf[1][:].bitcast(F32)
                gl = [wb1f[:, NW:2 * NW], gl1[:]]
                bgl = [Buf(), Buf()]
                self.wfence()
                S.barrier()
                hrb = [sb2(f"k_hr{i}", [128, NW], BF16) for i in range(2)]
                hib = [sb2(f"k_hi{i}", [128, NW], BF16) for i in range(2)]
                bhr, bhi = [Buf(), Buf()], [Buf(), Buf()]
                car = sb2("k_car", [128, 16, 2])
                bcar = [[Buf(), Buf()] for _ in range(16)]
                hidx = 0
                W = lambda n: wks[cur[0]][n][0]
                Bk = lambda n: wks[cur[0]][n][1]
                for kc in range(4):
                    self.ps_rot(range(2, 8))
                    for gpl in range(4):
                        gp = 4 * kc + gpl
                        for ri, src in enumerate((bbr, bbi)):
                            z, bz = Z[zi], bZ[zi]
                            zi = 1 - zi
                            S.op('dve', lambda e, z=z, src=src, gp=gp, gpl=gpl: e.tensor_tensor(
                                out=z[:].rearrange("p (a b) -> p a b", a=8), in0=src[:, gp, None, :].to_broadcast([128, 8, 16]),
                                in1=bm1[:, gpl, :, None].to_broadcast([128, 8, 16]), op=ALU.mult), reads=[bbb, self.bconst], writes=[bz])
                            ps, bps = self.ps()
                            S.op('pe', lambda e, ps=ps, z=z: e.transpose(out=ps[:, 0:128], in_=z[:], identity=self.ident[:]), reads=[bz, self.bconst], writes=[bps])
                            S.op('act', lambda e, ps=ps, ri=ri, gpl=gpl: e.activation(out=L[:, ri, gpl, :], in_=ps[:, 0:128], func=AF.Copy), reads=[bps], writes=[bL])
                        for ri, src in enumerate((Cre, Cim)):
                            z, bz = Z[zi], bZ[zi]
                            zi = 1 - zi
                            S.op('dve', lambda e, z=z, src=src, kc=kc, gpl=gpl: e.tensor_tensor(
                                out=z[:].rearrange("p (a b) -> p a b", a=2), in0=src[:, kc, None, :].to_broadcast([128, 2, 64]),
                                in1=bm2[:, gpl, :, None].to_broadcast([128, 2, 64]), op=ALU.mult), reads=[bC, self.bconst], writes=[bz])
                            ps, bps = self.ps()
                            S.op('pe', lambda e, ps=ps, z=z: e.transpose(out=ps[:, 0:128], in_=z[:], identity=self.ident[:]), reads=[bz, self.bconst], writes=[bps])
                            S.op('dve', lambda e, ps=ps, ri=ri, gpl=gpl: e.tensor_single_scalar(out=Cc[:, ri, gpl, :], in_=ps[:, 0:128], scalar=(1.0 if ri == 0 else -1.0), op=ALU.mult),
                                 reads=[bps], writes=[bCc])
                    its = [(tq, gpl) for tq in range(TQ) for gpl in range(4)]

                    def stageA(n):
                        tq, gpl = its[n]
                        gp = 4 * kc + gpl
                        sl = slice(tq * 512, (tq + 1) * 512)
                        cur[0] = n % 2
                        S.op('act', lambda e: e.activation(out=W("y"), in_=self.iota[:], func=AF.Identity, scale=T["thn"][:, gp:gp + 1], bias=off[:, gp, tq:tq + 1]),
                             reads=[self.bconst, B["thn"], boff], writes=[Bk("y")])
                        self.trig(W("y"), Bk("y"), W("yi"), Bk("yi"), W("yf"), Bk("yf"), W("cs"), Bk("cs"), W("sn"), Bk("sn"), on_act=True, f_eng='pool')
                        pr, bpr = self.ps()
                        pi_, bpi = self.ps()
                        S.op('pe', lambda e: e.matmul(pr[:], lhsT=L[:, 0, gpl, :], rhs=xd[:, kc, sl], start=True, stop=True),
                             reads=[bL, bxd[kc][tq]], writes=[bpr])
                        S.op('pe', lambda e: e.matmul(pi_[:], lhsT=L[:, 1, gpl, :], rhs=xd[:, kc, sl], start=True, stop=True),
                             reads=[bL, bxd[kc][tq]], writes=[bpi])
                        S.op('act', lambda e: e.activation(out=W("br"), in_=pr[:], func=AF.Copy), reads=[bpr], writes=[Bk("br")])
                        S.op('act', lambda e: e.activation(out=W("bi"), in_=pi_[:], func=AF.Copy), reads=[bpi], writes=[Bk("bi")])

                    def stageB(n):
                        nonlocal hidx
                        tq, gpl = its[n]
                        gp = 4 * kc + gpl
                        sl = slice(tq * 512, (tq + 1) * 512)
                        cur[0] = n % 2
                        psy, bpsy = self.P[tq % 2], self.bP[tq % 2]
                        S.op('dve', lambda e: e.tensor_tensor(out=W("t1"), in0=W("br"), in1=W("cs"), op=ALU.mult), reads=[Bk("br"), Bk("cs")], writes=[Bk("t1")])
                        S.op('dve', lambda e: e.tensor_tensor(out=W("t2"), in0=W("bi"), in1=W("sn"), op=ALU.mult), reads=[Bk("bi"), Bk("sn")], writes=[Bk("t2")])
                        S.op('dve', lambda e: e.tensor_tensor(out=W("t3"), in0=W("bi"), in1=W("cs"), op=ALU.mult), reads=[Bk("bi"), Bk("cs")], writes=[Bk("t3")])
                        S.op('dve', lambda e: e.tensor_tensor(out=W("t4"), in0=W("br"), in1=W("sn"), op=ALU.mult), reads=[Bk("br"), Bk("sn")], writes=[Bk("t4")])
                        S.op('dve', lambda e: e.tensor_tensor(out=W("t1"), in0=W("t1"), in1=W("t2"), op=ALU.add), reads=[Bk("t1"), Bk("t2")], writes=[Bk("t1")])
                        S.op('dve', lambda e: e.tensor_tensor(out=W("t3"), in0=W("t3"), in1=W("t4"), op=ALU.subtract), reads=[Bk("t3"), Bk("t4")], writes=[Bk("t3")])
                        rho_b = T["rho"][:, gp:gp + 1].to_broadcast([128, NW])
                        for (src, dst, ci) in (("t1", "br", 0), ("t3", "bi", 1)):
                            init = 0.0 if tq == 0 else car[:, gp, ci:ci + 1]
                            S.op('dve', lambda e: e.tensor_tensor_scan(out=W(dst), data0=rho_b, data1=W(src), initial=init, op0=ALU.mult, op1=ALU.add),
                                 reads=[Bk(src), B["rho"], bcar[gp][ci]], writes=[Bk(dst)])
                        for (dst, ci) in (("br", 0), ("bi", 1)):
                            S.op('act', lambda e: e.activation(out=car[:, gp, ci:ci + 1], in_=W(dst)[:, NW - 1:NW], func=AF.Copy),
                                 reads=[Bk(dst)], writes=[bcar[gp][ci]])
                        hr, hi_, bhr_, bhi_ = hrb[hidx], hib[hidx], bhr[hidx], bhi[hidx]
                        hidx = 1 - hidx
                        S.op('dve', lambda e: e.tensor_tensor(out=W("t1"), in0=W("br"), in1=W("cs"), op=ALU.mult), reads=[Bk("br"), Bk("cs")], writes=[Bk("t1")])
                        S.op('dve', lambda e: e.tensor_tensor(out=W("t4"), in0=W("br"), in1=W("sn"), op=ALU.mult), reads=[Bk("br"), Bk("sn")], writes=[Bk("t4")])
                        S.op('dve', lambda e: e.tensor_tensor(out=W("t2"), in0=W("bi"), in1=W("sn"), op=ALU.mult), reads=[Bk("bi"), Bk("sn")], writes=[Bk("t2")])
                        S.op('dve', lambda e: e.tensor_tensor(out=W("t3"), in0=W("bi"), in1=W("cs"), op=ALU.mult), reads=[Bk("bi"), Bk("cs")], writes=[Bk("t3")])
                        S.op('dve', lambda e: e.tensor_tensor(out=hr[:], in0=W("t1"), in1=W("t2"), op=ALU.subtract), reads=[Bk("t1"), Bk("t2")], writes=[bhr_])
                        S.op('dve', lambda e: e.tensor_tensor(out=hi_[:], in0=W("t3"), in1=W("t4"), op=ALU.add), reads=[Bk("t3"), Bk("t4")], writes=[bhi_])
                        S.op('pe', lambda e: e.matmul(psy[:], lhsT=Cc[:, 0, gpl, :], rhs=hr[:], start=(gpl == 0), stop=False),
                             reads=[bCc, bhr_], writes=[bpsy])
                        S.op('pe', lambda e: e.matmul(psy[:], lhsT=Cc[:, 1, gpl, :], rhs=hi_[:], start=False, stop=False),
                             reads=[bCc, bhi_], writes=[bpsy])
                        if gpl == 3:
                            S.op('pe', lambda e: e.matmul(psy[:], lhsT=Dg[:, kc, :], rhs=xd[:, kc, sl], start=False, stop=True),
                                 reads=[bDg, bxd[kc][tq]], writes=[bpsy])
                            S.op('act', lambda e: e.activation(out=gl[0], in_=psy[:], func=AF.Copy), reads=[bpsy], writes=[bgl[0]])
                            S.op('dve', lambda e: e.tensor_tensor(out=gl[1], in0=gl[0], in1=gl[0], op=ALU.mult), reads=[bgl[0]], writes=[bgl[1]])
                            S.op('dve', lambda e: e.tensor_scalar(out=gl[1], in0=gl[1], scalar1=0.044715, scalar2=1.0, op0=ALU.mult, op1=ALU.add), reads=[bgl[1]], writes=[bgl[1]])
                            S.op('dve', lambda e: e.tensor_tensor(out=gl[1], in0=gl[1], in1=gl[0], op=ALU.mult), reads=[bgl[1], bgl[0]], writes=[bgl[1]])
                            S.op('act', lambda e: e.activation(out=gl[1], in_=gl[1], func=AF.Sigmoid, scale=1.5957691216057308), reads=[bgl[1]], writes=[bgl[1]])
                            S.op('dve', lambda e: e.tensor_tensor(out=yg[:, kc, sl], in0=gl[1], in1=gl[0], op=ALU.mult),
                                 reads=[bgl[1], bgl[0]], writes=[byg[kc][tq]])

                    for n in range(len(its) + 1):
                        if n < len(its):
                            stageA(n)
                        if n >= 1:
                            stageB(n - 1)
                self.ps_rot(range(8))
            S.barrier()
            with ExitStack() as es3:
                sb3 = lambda n, sh, dt=F32: self.sb(n, sh, dt, es3)
                gd = sb3("gdT", [128, SEQ], BF16)
                bgd = self.grid(TQ)
                s2 = sb3("glu_s2", [128, 512]); t2_ = sb3("glu_t", [128, 512])
                bs2, bt2 = Buf(), Buf()
                for oc in range(4):
                    self.proj_fm(w_in, 3584 + oc * 128, lambda tq: gd[:, tq * 512:(tq + 1) * 512], lambda tq: [bgd[tq]], func=AF.Silu)
                    w1, bw1 = self.wload(d['glu_w1'][li], 0, 512, oc * 128, 128)
                    w2, bw2 = self.wload(d['glu_w2'][li], 0, 512, oc * 128, 128, prefetch=False)
                    for tq in range(TQ):
                        sl = slice(tq * 512, (tq + 1) * 512)
                        p1, bp1 = self.ps()
                        p2, bp2 = self.ps()
                        self.mm_fm(p1[:], bp1, w1, bw1, slice(0, 128), lambda kc: yg[:, kc, sl], lambda kc: [byg[kc][tq]], [0, 1, 2, 3])
                        self.mm_fm(p2[:], bp2, w2, bw2, slice(0, 128), lambda kc: yg[:, kc, sl], lambda kc: [byg[kc][tq]], [0, 1, 2, 3])
                        S.op('act', lambda e, p2=p2: e.activation(out=s2[:], in_=p2[:], func=AF.Sigmoid), reads=[bp2], writes=[bs2])
                        S.op('dve', lambda e, p1=p1: e.tensor_tensor(out=t2_[:], in0=p1[:], in1=s2[:], op=ALU.mult), reads=[bp1, bs2], writes=[bt2])
                        S.op('pool', lambda e, oc=oc, sl=sl: e.tensor_tensor(out=xd[:, oc, sl], in0=t2_[:], in1=gd[:, sl], op=ALU.mult),
                             reads=[bt2, bgd[tq]], writes=[bxd[oc][tq]])
            self.outproj(w_out, 1024, 4, xd, bxd)

    def sguC(self, li, w_in, w_out):
        S, d = self.S, self.d
        with ExitStack() as es:
            sb = lambda n, sh, dt=F32, es_=es: self.sb(n, sh, dt, es_)
            vn = sb("vn", [128, 16, DM], BF16)
            bvn = self.grid(16)
            with ExitStack() as es2:
                sb2 = lambda n, sh, dt=F32: self.sb(n, sh, dt, es2)
                ssum = sb2("c_ssum", [128, 16, 4]); ssq = sb2("c_ssq", [128, 16, 4])
                bst = Buf()
                junk = sb2("c_junk", [128, 256], BF16)
                bjunk = Buf()
                S.op('dve', lambda e: e.memset(ssum[:], 0.0), writes=[bst])
                S.op('dve', lambda e: e.memset(ssq[:], 0.0), writes=[bst])
                for q in range(4):
                    wt, bw = self.wload(w_in, 0, 1024, 1024 + q * 256, 256)
                    for n in range(16):
                        ps, bps = self.ps()
                        tok = slice(n * 128, (n + 1) * 128)
                        for kc in range(NCH):
                            S.op('pe', lambda e, ps=ps, kc=kc, tok=tok, wt=wt: e.matmul(ps[:, 0:256], lhsT=self.hnT[:, kc, tok], rhs=wt[:, kc, :], start=(kc == 0), stop=(kc == NCH - 1)),
                                 reads=[bw, self.bhn[kc][n // 4]], writes=[bps], inc=(kc == NCH - 1))
                        S.op('act', lambda e, ps=ps, n=n, q=q: e.activation(out=vn[:, n, q * 256:(q + 1) * 256], in_=ps[:, 0:256], func=AF.Copy, accum_out=ssum[:, n, q:q + 1]),
                             reads=[bps, bst], writes=[bvn[n], bst])
                        S.op('act', lambda e, ps=ps, n=n, q=q: e.activation(out=junk[:], in_=ps[:, 0:256], func=AF.Square, accum_out=ssq[:, n, q:q + 1]),
                             reads=[bps, bst], writes=[bjunk, bst])
                mean = sb2("c_mean", [128, 16]); var = sb2("c_var", [128, 16]); m2 = sb2("c_m2", [128, 16])
                S.op('dve', lambda e: e.tensor_tensor(out=ssum[:, :, 0:2], in0=ssum[:, :, 0:2], in1=ssum[:, :, 2:4], op=ALU.add), reads=[bst], writes=[bst])
                S.op('dve', lambda e: e.tensor_tensor(out=mean[:], in0=ssum[:, :, 0], in1=ssum[:, :, 1], op=ALU.add), reads=[bst], writes=[bst])
                S.op('dve', lambda e: e.tensor_single_scalar(out=mean[:], in_=mean[:], scalar=1.0 / DM, op=ALU.mult), reads=[bst], writes=[bst])
                S.op('dve', lambda e: e.tensor_tensor(out=ssq[:, :, 0:2], in0=ssq[:, :, 0:2], in1=ssq[:, :, 2:4], op=ALU.add), reads=[bst], writes=[bst])
                S.op('dve', lambda e: e.tensor_tensor(out=var[:], in0=ssq[:, :, 0], in1=ssq[:, :, 1], op=ALU.add), reads=[bst], writes=[bst])
                S.op('dve', lambda e: e.tensor_tensor(out=m2[:], in0=mean[:], in1=mean[:], op=ALU.mult), reads=[bst], writes=[bst])
                S.op('dve', lambda e: e.scalar_tensor_tensor(out=var[:], in0=var[:], scalar=1.0 / DM, in1=m2[:], op0=ALU.mult, op1=ALU.subtract), reads=[bst], writes=[bst])
                S.op('dve', lambda e: e.tensor_single_scalar(out=var[:], in_=var[:], scalar=EPS, op=ALU.add), reads=[bst], writes=[bst])
                S.op('act', lambda e: e.activation(out=var[:], in_=var[:], func=AF.Ln), reads=[bst], writes=[bst])
                S.op('act', lambda e: e.activation(out=var[:], in_=var[:], func=AF.Exp, scale=-0.5), reads=[bst], writes=[bst])
                lng = sb2("c_lng", [128, DM]); lnb = sb2("c_lnb", [128, DM])
                bln = Buf()
                S.dma(lng[:], d['sgu_ln_g'][li].partition_broadcast(128), writes=[bln])
                S.dma(lnb[:], d['sgu_ln_b'][li].partition_broadcast(128), writes=[bln])
                tmp = [sb2(f"c_tmp{i}", [128, DM]) for i in range(2)]
                btmp = [Buf(), Buf()]
                for n in range(16):
                    t_, bt_ = tmp[n % 2], btmp[n % 2]
                    S.op('dve', lambda e, n=n, t_=t_: e.tensor_scalar(out=t_[:], in0=vn[:, n, :], scalar1=mean[:, n:n + 1], scalar2=var[:, n:n + 1], op0=ALU.subtract, op1=ALU.mult),
                         reads=[bvn[n], bst], writes=[bt_])
                    S.op('pool', lambda e, t_=t_: e.tensor_tensor(out=t_[:], in0=t_[:], in1=lng[:], op=ALU.mult), reads=[bt_, bln], writes=[bt_])
                    S.op('pool', lambda e, n=n, t_=t_: e.tensor_tensor(out=vn[:, n, :], in0=t_[:], in1=lnb[:], op=ALU.add), reads=[bt_, bln], writes=[bvn[n]])
            S.barrier()
            mix = sb("mixC", [128, NCH, SEQ], BF16)
            bmix = self.grid(NCH, TQ)
            wsT = sb("c_wsT", [128, 4, 128], BF16)
            bws = Buf()
            bsb = sb("c_bsb", [128, 4, 128])
            bbs = Buf()
            for g in range(4):
                i = g % 2
                st, bst_ = self.wst[i], self.bwst[i]
                S.dma(st[:, 0:128], d['sgu_w'][li][g], writes=[bst_])
                ps, bps = self.ps()
                S.op('pe', lambda e, ps=ps, st=st: e.transpose(out=ps[:, 0:128], in_=st[:, 0:128], identity=self.ident[:]), reads=[bst_, self.bconst], writes=[bps])
                S.op('dve', lambda e, ps=ps, g=g: e.tensor_tensor(out=wsT[:, g, :], in0=ps[:, 0:128], in1=self.triu[:], op=ALU.mult), reads=[bps, self.bconst], writes=[bws])
                S.dma(bsb[:, g, :], d['sgu_b'][li][g].partition_broadcast(128), writes=[bbs])
            ta = [sb(f"c_ta{i}", [128, 512]) for i in range(2)]
            bta = [Buf(), Buf()]
            sg = [sb(f"c_sg{i}", [128, 512]) for i in range(2)]
            bsg = [Buf(), Buf()]
            k = 0
            for c in range(NCH):
                g = c // 2
                wu, bwu = self.wload(w_in, 0, 1024, c * 128, 128)
                wg, bwg = self.wload(w_in, 0, 1024, 2048 + c * 128, 128, prefetch=False)
                for tq in range(TQ):
                    sl = slice(tq * 512, (tq + 1) * 512)
                    a_, ba_, s_, bs_ = ta[k], bta[k], sg[k], bsg[k]
                    k = 1 - k
                    ps, bps = self.ps()
                    for j in range(4):
                        n = 4 * tq + j
                        S.op('pe', lambda e, ps=ps, j=j, n=n, c=c, g=g: e.matmul(ps[:, j * 128:(j + 1) * 128], lhsT=vn[:, n, c * 128:(c + 1) * 128], rhs=wsT[:, g, :], start=True, stop=True),
                             reads=[bvn[n], bws], writes=[bps], inc=(j == 3))
                    S.op('dve', lambda e, ps=ps, g=g, a_=a_: e.tensor_tensor(out=a_[:].rearrange("p (a b) -> p a b", a=4), in0=ps[:].rearrange("p (a b) -> p a b", a=4),
                                                                        in1=bsb[:, g, None, :].to_broadcast([128, 4, 128]), op=ALU.add), reads=[bps, bbs], writes=[ba_])
                    pu, bpu = self.ps()
                    self.mm_fm(pu[:], bpu, wu, bwu, slice(0, 128), lambda kc: self.hnT[:, kc, sl], lambda kc: [self.bhn[kc][tq]], list(range(NCH)))
                    S.op('dve', lambda e, pu=pu, a_=a_: e.tensor_tensor(out=a_[:], in0=pu[:], in1=a_[:], op=ALU.mult), reads=[bpu, ba_], writes=[ba_])
                    pg, bpg = self.ps()
                    self.mm_fm(pg[:], bpg, wg, bwg, slice(0, 128), lambda kc: self.hnT[:, kc, sl], lambda kc: [self.bhn[kc][tq]], list(range(NCH)))
                    S.op('act', lambda e, pg=pg, s_=s_: e.activation(out=s_[:], in_=pg[:], func=AF.Silu), reads=[bpg], writes=[bs_])
                    S.op('pool', lambda e, a_=a_, s_=s_, sl=sl, c=c: e.tensor_tensor(out=mix[:, c, sl], in0=a_[:], in1=s_[:], op=ALU.mult), reads=[ba_, bs_], writes=[bmix[c][tq]])
            self.outproj(w_out, 0, 8, mix, bmix)

    def cross(self, l):
        S, d = self.S, self.d
        with ExitStack() as es:
            sb = lambda n, sh, dt=F32: self.sb(n, sh, dt, es)
            with ExitStack() as es2:
                self.rmsnorm(self.xT, self.bx, SEQ, self.vec['norm_x'][:, l, :], self.hnT, self.bhn, es2)
            S.barrier()
            qT = sb("xqT", [128, 2, SEQ], BF16)
            bq = self.grid(2, TQ)
            mix = sb("mixX", [128, NCH, SEQ], BF16)
            bmix = self.grid(NCH, TQ)
            KT = sb("xKT", [128, NCH, 256], BF16)
            bKT = self.grid(NCH)
            Vx = sb("xV", [128, 2, DM], BF16)
            bVx = self.grid(2)
            pT = [sb(f"xpT{i}", [128, 2, 512], BF16) for i in range(2)]
            bpT = [Buf(), Buf()]
            rden = sb("xrden", [128, 512])
            brden = Buf()
            wkv = d['w_xkv'][l]
            for c in range(NCH):
                wt, bw = self.wload(wkv, 0, 1024, c * 128, 128)
                ps, bps = self.ps()
                self.mm_fm(ps[:, 0:256], bps, wt, bw, slice(0, 128), lambda kc: self.memT[:, kc, :], lambda kc: [self.bmem], list(range(NCH)))
                S.op('act', lambda e, ps=ps, c=c: e.activation(out=KT[:, c, :], in_=ps[:, 0:256], func=AF.Copy), reads=[bps], writes=[bKT[c]])
            for q in range(4):
                wt, bw = self.wload(wkv, 0, 1024, 1024 + q * 256, 256)
                for mt in range(2):
                    ps, bps = self.ps()
                    for kc in range(NCH):
                        S.op('pe', lambda e, ps=ps, kc=kc, mt=mt, wt=wt: e.matmul(ps[:, 0:256], lhsT=self.memT[:, kc, mt * 128:(mt + 1) * 128], rhs=wt[:, kc, :], start=(kc == 0), stop=(kc == NCH - 1)),
                             reads=[bw, self.bmem], writes=[bps], inc=(kc == NCH - 1))
                    S.op('act', lambda e, ps=ps, mt=mt, q=q: e.activation(out=Vx[:, mt, q * 256:(q + 1) * 256], in_=ps[:, 0:256], func=AF.Copy), reads=[bps], writes=[bVx[mt]])
            pi = 0
            for h in range(4):
                for k2 in range(2):
                    self.proj_fm(d['w_xq'][l], (2 * h + k2) * 128, lambda tq, k2=k2: qT[:, k2, tq * 512:(tq + 1) * 512], lambda tq, k2=k2: [bq[k2][tq]], eng=('act' if k2 == 0 else 'dve'))
                for tq in range(TQ):
                    sl = slice(tq * 512, (tq + 1) * 512)
                    p_, bp_ = pT[pi], bpT[pi]
                    pi = 1 - pi
                    for mt in range(2):
                        ps, bps = self.ps()
                        for k2 in range(2):
                            cc = 2 * h + k2
                            S.op('pe', lambda e, ps=ps, cc=cc, mt=mt, k2=k2, sl=sl: e.matmul(ps[:], lhsT=KT[:, cc, mt * 128:(mt + 1) * 128], rhs=qT[:, k2, sl], start=(k2 == 0), stop=(k2 == 1)),
                                 reads=[bKT[cc], bq[k2][tq]], writes=[bps], inc=(k2 == 1))
                        S.op('act', lambda e, ps=ps, mt=mt, p_=p_: e.activation(out=p_[:, mt, :], in_=ps[:], func=AF.Exp, scale=1.0 / 16.0), reads=[bps], writes=[bp_])
                    psd, bpsd = self.ps()
                    for mt in range(2):
                        S.op('pe', lambda e, psd=psd, mt=mt, p_=p_: e.matmul(psd[:], lhsT=self.onesb[:], rhs=p_[:, mt, :], start=(mt == 0), stop=(mt == 1)),
                             reads=[bp_, self.bconst], writes=[bpsd], inc=(mt == 1))
                    S.op('dve', lambda e, psd=psd: e.reciprocal(out=rden[:], in_=psd[:]), reads=[bpsd], writes=[brden])
                    for dc in range(2):
                        cc = 2 * h + dc
                        pso, bpso = self.ps()
                        for mt in range(2):
                            S.op('pe', lambda e, pso=pso, mt=mt, p_=p_, cc=cc: e.matmul(pso[:], lhsT=Vx[:, mt, cc * 128:(cc + 1) * 128], rhs=p_[:, mt, :], start=(mt == 0), stop=(mt == 1)),
                                 reads=[bp_, bVx[mt]], writes=[bpso], inc=(mt == 1))
                        S.op('dve', lambda e, pso=pso, cc=cc, sl=sl: e.tensor_tensor(out=mix[:, cc, sl], in0=pso[:], in1=rden[:], op=ALU.mult),
                             reads=[bpso, brden], writes=[bmix[cc][tq]])
            self.outproj(d['w_xo'][l], 0, 8, mix, bmix)

    def final(self):
        S, d = self.S, self.d
        with ExitStack() as es:
            sb = lambda n, sh, dt=F32: self.sb(n, sh, dt, es)
            blk = 512
            sq = [sb(f"f_sq{i}", [128, blk], BF16) for i in range(2)]
            bsq = [Buf(), Buf()]
            rs = sb("f_rs", [128, blk])
            brs = Buf()
            nrm = [sb(f"f_n{i}", [128, blk]) for i in range(2)]
            bnrm = [Buf(), Buf()]
            gcol = self.vec['final_norm'][:, 0, :]
            ost = [sb(f"f_o{i}", [128, DM]) for i in range(2)]
            bost = [Buf(), Buf()]
            for tq in range(TQ):
                sl = slice(tq * blk, (tq + 1) * blk)
                ps, bps = self.ps()
                for c in range(NCH):
                    j = c % 2
                    S.op('act', lambda e, c=c, j=j: e.activation(out=sq[j][:], in_=self.xT[:, c, sl], func=AF.Square), reads=[self.bx[c][tq]], writes=[bsq[j]])
                    S.op('pe', lambda e, c=c, j=j, ps=ps: e.matmul(ps[:], lhsT=self.onesb[:], rhs=sq[j][:], start=(c == 0), stop=(c == NCH - 1)),
                         reads=[bsq[j], self.bconst], writes=[bps], inc=True)
                S.op('dve', lambda e, ps=ps: e.tensor_scalar(out=rs[:], in0=ps[:], scalar1=1.0 / DM, scalar2=EPS, op0=ALU.mult, op1=ALU.add), reads=[bps], writes=[brs])
                S.op('act', lambda e: e.activation(out=rs[:], in_=rs[:], func=AF.Ln), reads=[brs], writes=[brs])
                S.op('act', lambda e: e.activation(out=rs[:], in_=rs[:], func=AF.Exp, scale=-0.5), reads=[brs], writes=[brs])
                pts = [self.ps() for _ in range(8)]
                for c in range(NCH):
                    n_, bn_ = nrm[c % 2], bnrm[c % 2]
                    S.op('dve', lambda e, c=c, n_=n_: e.scalar_tensor_tensor(out=n_[:], in0=self.xT[:, c, sl], scalar=gcol[:, c:c + 1], in1=rs[:], op0=ALU.mult, op1=ALU.mult),
                         reads=[self.bx[c][tq], brs, self.bconst], writes=[bn_])
                    for j in range(4):
                        pp, bpp = pts[2 * j + c // 4]
                        S.op('pe', lambda e, pp=pp, j=j, c=c, n_=n_: e.transpose(out=pp[:, (c % 4) * 128:(c % 4) * 128 + 128], in_=n_[:, j * 128:(j + 1) * 128], identity=self.ident[:]),
                             reads=[bn_, self.bconst], writes=[bpp], inc=True)
                for j in range(4):
                    n = 4 * tq + j
                    o_, bo_ = ost[n % 2], bost[n % 2]
                    for h in range(2):
                        pp, bpp = pts[2 * j + h]
                        if h == 0:
                            S.op('act', lambda e, pp=pp, o_=o_: e.activation(out=o_[:, 0:512], in_=pp[:], func=AF.Copy), reads=[bpp], writes=[bo_])
                        else:
                            S.op('dve', lambda e, pp=pp, o_=o_: e.tensor_copy(out=o_[:, 512:1024], in_=pp[:]), reads=[bpp], writes=[bo_])
                    S.dma(d['out'][n * 128:(n + 1) * 128, :], o_[:], reads=[bo_])

    def run(self):
        S, d = self.S, self.d
        self.setup()
        if CUT == 1:
            return
        self.load_T(d['x'], 16, self.xT, lambda h, n: [self.bx[c][n // 4] for c in range(4 * h, 4 * h + 4)])
        if CUT == 2:
            return
        with ExitStack() as es:
            mraw = self.sb("mraw", [128, NCH, 256], F32, es)
            bmr = [[Buf()] for _ in range(NCH)]
            self.load_T(d['mem'], 2, mraw, lambda h, n: [bmr[c][0] for c in range(4 * h, 4 * h + 4)])
            bm = [[self.bmem] for _ in range(NCH)]
            self.rmsnorm(mraw, bmr, 256, self.vec['mem_norm'][:, 0, :], self.memT, bm, es)
        S.barrier()
        if CUT == 3:
            return
        for layer in range(self.depth):
            i = layer // 2
            with ExitStack() as es:
                gname = 'norm_ab' if layer % 2 == 0 else 'norm_cd'
                self.rmsnorm(self.xT, self.bx, SEQ, self.vec[gname][:, i, :], self.hnT, self.bhn, es)
            S.barrier()
            if layer % 2 == 0:
                self.even_layer(i)
            else:
                self.odd_layer(i)
            S.barrier()
            if 'cross' in STAGES:
                self.cross(layer)
            S.barrier()
        self.final()


import os
_CACHE = {}
NORM_DIV = int(os.environ.get('NORM_DIV', '1'))
S5_ACT = int(os.environ.get('S5_ACT', '1'))
BARRIERS = int(os.environ.get('BARRIERS', '1'))
ATT_SKIP = set(os.environ.get('ATT_SKIP', '').split(','))
LT_MODE = 0
CUT = 0
STAGES = {'attnA', 'poolB', 'cross', 's5D', 'sguC'}


def build_nc(depth=4):
    key = (depth, tuple(sorted(STAGES)))
    if key in _CACHE:
        return _CACHE[key]
    plan = None
    for pass_ in range(2):
        nc = bass.Bass("TRN2", target_bir_lowering=False)
        with ExitStack() as es:
            S = Sched(nc, es)
            K = Kern(nc, S, es, depth, wplan=plan)
            K.run()
            if pass_ == 0:
                plan = [(k, sp) for k, sp in K.wrec]
                continue
            S.emit()
    _CACHE[key] = nc
    return nc


def kernel(**inputs):
    n = 8
    nc = build_nc(4)
    consts = host_consts()
    x = np.ascontiguousarray(np.asarray(inputs['x'], dtype=np.float32))
    mem = np.ascontiguousarray(np.asarray(inputs['mem'], dtype=np.float32))
    shared = {name: np.ascontiguousarray(np.asarray(inputs[name], dtype=np.float32)) for name, _ in PARAMS}
    shared.update(consts)
    in_maps = []
    for b in range(n):
        m = dict(shared)
        m['x'] = x[b]
        m['mem'] = mem[b]
        in_maps.append(m)
    res = run_bass_kernel_spmd(nc, in_maps, core_ids=list(range(n)))
    return np.stack([np.asarray(r['out'], dtype=np.float32) for r in res.results], axis=0)
```

```python
import numpy as np
import concourse.bass as bass
import concourse.mybir as mybir
from concourse.bass_utils import run_bass_kernel_spmd
from contextlib import ExitStack

F32 = mybir.dt.float32
BF16 = mybir.dt.bfloat16
I32 = mybir.dt.int32
ALU = mybir.AluOpType
AF = mybir.ActivationFunctionType

ENGS = ('pe', 'act', 'dve', 'pool', 'sp')
EP = 20000
NEPOCH = 8
NSLOT = 8

SEQ = 2048
DM = 1024
NCH = 8
TQ = 4
EPS = 1e-6
TWO_PI = float(2 * np.pi)


class Buf:
    __slots__ = ('w', 'r', 'excl')

    def __init__(self, excl=False):
        self.w = None
        self.r = {}
        self.excl = excl


class _Rec:
    def __init__(self):
        self.call = None

    def __getattr__(self, name):
        def f(*a, **k):
            self.call = (name, a, k)
            return None
        return f


class Sched:
    def __init__(self, nc, es):
        self.nc = nc
        self.ops = {e: [] for e in ENGS}
        self.incs = {e: 0 for e in ENGS}
        self.waited = {e: {} for e in ENGS}
        self.sems = {}
        for e in ENGS:
            if e == 'sp':
                continue
            for k in range(NEPOCH):
                self.sems[(e, k)] = es.enter_context(nc.semaphore(f"s_{e}{k}"))
        self.dsem = [es.enter_context(nc.semaphore(f"s_dma{i}")) for i in range(NSLOT)]
        self.dcnt = [0] * NSLOT
        self.dnext = 0
        self.nops = 0

    def _collect(self, eng, reads, writes, extra=()):
        waits = {}

        def need(t):
            if t is None:
                return
            key, n = t
            if key == 'pe' and eng == 'pe':
                return
            if n > self.waited[eng].get(key, 0):
                if n > waits.get(key, 0):
                    waits[key] = n
        for b in reads:
            need(b.w)
            if b.excl:
                for k, t in b.r.items():
                    if k != eng:
                        need(t)
        for b in writes:
            need(b.w)
            for t in b.r.values():
                need(t)
        for t in extra:
            need(t)
        for key, n in waits.items():
            self.waited[eng][key] = n
        return list(waits.items())

    def op(self, eng, fn, reads=(), writes=(), inc=True):
        assert inc or eng == 'pe'
        waits = self._collect(eng, reads, writes)
        rec = _Rec()
        fn(rec)
        name_, a_, k_ = rec.call
        fn = (lambda e, name_=name_, a_=a_, k_=k_: getattr(e, name_)(*a_, **k_))
        n = self.incs[eng] + 1
        assert n <= EP * NEPOCH
        ticket = (eng, n)
        self.ops[eng].append((waits, fn, ('e', n) if inc else None))
        if inc:
            self.incs[eng] = n
        for b in reads:
            b.r[eng] = ticket
        for b in writes:
            b.w = ticket
            b.r = {}
        self.nops += 1
        return ticket

    def dma(self, out, in_, reads=(), writes=(), **kw):
        slot = self.dnext
        self.dnext = (slot + 1) % NSLOT
        prev = self.dcnt[slot]
        key = ('dma', slot)
        extra = [(key, prev)] if prev > 0 else []
        waits = self._collect('sp', reads, writes, extra)
        n = prev + 1
        self.dcnt[slot] = n
        ticket = (key, n)

        def fn(sp, out=out, in_=in_, kw=kw):
            return sp.dma_start(out=out, in_=in_, **kw)
        self.ops['sp'].append((waits, fn, ('d', slot)))
        for b in reads:
            b.r[key] = ticket
        for b in writes:
            b.w = ticket
            b.r = {}
        self.nops += 1
        return ticket

    def barrier(self):
        for eng in ENGS:
            waits = []
            for e2 in ENGS:
                if e2 != eng and e2 != 'sp' and self.incs[e2] > self.waited[eng].get(e2, 0):
                    waits.append((e2, self.incs[e2]))
                    self.waited[eng][e2] = self.incs[e2]
            for s_ in range(NSLOT):
                key = ('dma', s_)
                if self.dcnt[s_] > self.waited[eng].get(key, 0):
                    waits.append((key, self.dcnt[s_]))
                    self.waited[eng][key] = self.dcnt[s_]
            self.ops[eng].append((waits, None, None))

    def _wait(self, e, key, n):
        if isinstance(key, tuple):
            e.wait_ge(self.dsem[key[1]], 16 * n)
        else:
            k = (n - 1) // EP
            e.wait_ge(self.sems[(key, k)], (n - 1) % EP + 1)

    def emit(self):
        nc = self.nc
        fin = [(('dma', s), self.dcnt[s]) for s in range(NSLOT) if self.dcnt[s] > 0]
        with nc.Block() as block:
            def run(ename, e):
                for waits, fn, inc in self.ops[ename]:
                    for key, n in waits:
                        self._wait(e, key, n)
                    if fn is None:
                        continue
                    ins = fn(e)
                    if inc is not None:
                        if inc[0] == 'e':
                            n = inc[1]
                            ins.then_inc(self.sems[(ename, (n - 1) // EP)], 1)
                        else:
                            ins.then_inc(self.dsem[inc[1]], 16)
                if ename == 'sp':
                    for key, n in fin:
                        self._wait(e, key, n)

            @block.tensor
            def _(pe):
                run('pe', pe)

            @block.scalar
            def _(act):
                run('act', act)

            @block.vector
            def _(dve):
                run('dve', dve)

            @block.gpsimd
            def _(pool):
                run('pool', pool)

            @block.sync
            def _(sp):
                run('sp', sp)


PARAMS = [
    ('norm_ab', (2, 1024)), ('w_in_ab', (2, 1024, 6144)), ('pool_w', (2, 4, 256, 256)), ('pool_scale', (2, 1024)),
    ('w_out_ab', (2, 2048, 1024)), ('norm_cd', (2, 1024)), ('w_in_cd', (2, 1024, 4096)), ('sgu_ln_g', (2, 1024)),
    ('sgu_ln_b', (2, 1024)), ('sgu_w', (2, 4, 128, 128)), ('sgu_b', (2, 4, 128)), ('s5_a_re', (2, 32, 64)),
    ('s5_a_im', (2, 32, 64)), ('s5_log_dt', (2, 32)), ('s5_b_re', (2, 32, 64, 16)), ('s5_b_im', (2, 32, 64, 16)),
    ('s5_c_re', (2, 32, 16, 64)), ('s5_c_im', (2, 32, 16, 64)), ('s5_d', (2, 512)), ('glu_w1', (2, 512, 512)),
    ('glu_w2', (2, 512, 512)), ('w_out_cd', (2, 1536, 1024)), ('norm_x', (4, 1024)), ('w_xq', (4, 1024, 1024)),
    ('w_xkv', (4, 1024, 2048)), ('w_xo', (4, 1024, 1024)), ('mem_norm', (1024,)), ('final_norm', (1024,)),
]


def host_consts():
    c = {}
    c['c_ident'] = np.eye(128, dtype=np.float32)
    k = np.arange(128)[:, None]
    q = np.arange(128)[None, :]
    NEGM = -30000.0
    diag = np.where(q >= k, 0.0, NEGM)
    prev = np.where(q <= k, 0.0, NEGM)
    c['c_maskb'] = np.concatenate([diag, prev], axis=1).astype(np.float32)
    c['c_mask01'] = (c['c_maskb'] == 0.0).astype(np.float32)
    c['c_triu'] = (k <= q).astype(np.float32)
    c['c_iota'] = np.tile(np.arange(512, dtype=np.float32)[None, :], (128, 1))
    inv = np.zeros((4, 16), np.float32)
    for g, w in enumerate((2, 4, 8, 16)):
        inv[g] = 1.0 / np.minimum(np.arange(1, 17), w)
    c['c_invcnt'] = np.tile(inv.reshape(1, 64), (128, 1)).astype(np.float32)
    M = np.zeros((128, 4, 8), np.float32)
    for g2 in range(2):
        for gpl in range(4):
            M[g2 * 64:(g2 + 1) * 64, gpl, 2 * gpl + g2] = 1.0
    c['c_bm1'] = M.reshape(128, 32)
    M2 = np.zeros((128, 4, 2), np.float32)
    for gl in range(8):
        for gpl in range(4):
            for g2 in range(2):
                if gl == 2 * gpl + g2:
                    M2[gl * 16:(gl + 1) * 16, gpl, g2] = 1.0
    c['c_bm2'] = M2.reshape(128, 8)
    return c


class Kern:
    def __init__(self, nc, S, es, depth=4, wplan=None):
        self.nc, self.S, self.es, self.depth = nc, S, es, depth
        self.wplan = wplan
        self.wrec = []
        self.wissued = []
        d = {}
        d['x'] = nc.dram_tensor("x", [SEQ, DM], F32, kind="ExternalInput").ap()
        d['mem'] = nc.dram_tensor("mem", [256, DM], F32, kind="ExternalInput").ap()
        for name, shp in PARAMS:
            d[name] = nc.dram_tensor(name, list(shp), F32, kind="ExternalInput").ap()
        for name, arr in host_consts().items():
            d[name] = nc.dram_tensor(name, list(arr.shape), F32, kind="ExternalInput").ap()
        d['out'] = nc.dram_tensor("out", [SEQ, DM], F32, kind="ExternalOutput").ap()
        self.d = d
        self.psi = 0

    def sb(self, name, shape, dt=F32, es=None):
        self.uid = getattr(self, 'uid', 0) + 1
        return (es or self.es).enter_context(self.nc.sbuf_tensor(f"{name}_{self.uid}", shape, dt))

    def grid(self, *dims):
        if len(dims) == 1:
            return [Buf() for _ in range(dims[0])]
        return [self.grid(*dims[1:]) for _ in range(dims[0])]

    def ps(self):
        i = self.psi
        self.psi = (i + 1) % len(self.psr)
        j = self.psr[i]
        return self.P[j], self.bP[j]

    def ps_rot(self, banks):
        self.psr = list(banks)
        self.psi = 0

    def setup(self):
        nc, S, d = self.nc, self.S, self.d
        sb = self.sb
        self.P = [self.es.enter_context(nc.psum_tensor(f"P{i}", [128, 512], F32)) for i in range(8)]
        self.bP = [Buf(excl=True) for _ in range(8)]
        self.ps_rot(range(8))
        self.xT = sb("xT", [128, NCH, SEQ], F32)
        self.bx = self.grid(NCH, TQ)
        self.hnT = sb("hnT", [128, NCH, SEQ], BF16)
        self.bhn = self.grid(NCH, TQ)
        self.wst = [sb(f"wst{i}", [128, 2048], F32) for i in range(2)]
        self.bwst = [Buf() for _ in range(2)]
        self.wbf = [sb(f"wbf{i}", [128, 2048], BF16) for i in range(2)]
        self.bwbf = [Buf() for _ in range(2)]
        self.wi = 0
        self.memT = sb("memT", [128, NCH, 256], BF16)
        self.bmem = Buf()
        self.ident = sb("ident", [128, 128], F32)
        self.identb = sb("identb", [128, 128], BF16)
        self.onesb = sb("onesb", [128, 128], BF16)
        self.maskb = sb("maskb", [128, 256], BF16)
        self.triu = sb("triu", [128, 128], F32)
        self.iota = sb("iota", [128, 512], F32)
        self.invcnt = sb("invcnt", [128, 64], F32)
        self.bm1 = sb("bm1", [128, 32], F32)
        self.bm2 = sb("bm2", [128, 8], F32)
        self.halfpi = sb("halfpi", [128, 1], F32)
        self.bconst = Buf()
        tmp = self.wst[0]
        S.dma(self.ident[:], d['c_ident'], writes=[self.bconst])
        S.dma(self.triu[:], d['c_triu'], writes=[self.bconst])
        S.dma(self.iota[:], d['c_iota'], writes=[self.bconst])
        S.dma(self.invcnt[:], d['c_invcnt'], writes=[self.bconst])
        S.dma(self.bm1[:], d['c_bm1'], writes=[self.bconst])
        S.dma(self.bm2[:], d['c_bm2'], writes=[self.bconst])
        S.dma(tmp[:, 0:256], d['c_maskb'], writes=[self.bwst[0]])
        S.op('pool', lambda e: e.tensor_copy(out=self.maskb[:], in_=tmp[:, 0:256]), reads=[self.bwst[0]], writes=[self.bconst])
        self.mask01 = sb("mask01", [128, 256], BF16)
        S.dma(tmp[:, 256:512], d['c_mask01'], writes=[self.bwst[0]])
        S.op('pool', lambda e: e.tensor_copy(out=self.mask01[:], in_=tmp[:, 256:512]), reads=[self.bwst[0]], writes=[self.bconst])
        S.op('pool', lambda e: e.tensor_copy(out=self.identb[:], in_=self.ident[:]), reads=[self.bconst], writes=[self.bconst])
        S.op('pool', lambda e: e.memset(self.onesb[:], 1.0), writes=[self.bconst])
        S.op('pool', lambda e: e.memset(self.halfpi[:], float(np.pi / 2)), writes=[self.bconst])
        self.magic = sb("magic", [128, 2], F32)
        S.op('pool', lambda e: e.memset(self.magic[:, 0:1], 12582912.0), writes=[self.bconst])
        S.op('pool', lambda e: e.memset(self.magic[:, 1:2], -12582912.0), writes=[self.bconst])
        rows = [('norm_ab', 2, 8), ('norm_cd', 2, 8), ('norm_x', 4, 8), ('pool_scale', 2, 8), ('mem_norm', 1, 8), ('final_norm', 1, 8), ('s5_d', 2, 4)]
        vst = sb("vst", [128, 128], F32)
        bvst = Buf()
        S.op('dve', lambda e: e.memset(vst[:], 0.0), writes=[bvst])
        vall = sb("vall", [128, 128], F32)
        r0 = 0
        self.vec = {}
        for name, n, ch in rows:
            src = d[name]
            if len(src.shape) == 2:
                src2 = src.rearrange("l (c p) -> (l c) p", p=128)
            else:
                src2 = src.rearrange("(c p) -> c p", p=128)
            S.dma(vst[r0:r0 + n * ch, :], src2, writes=[bvst])
            self.vec[name] = vall[:, r0:r0 + n * ch].rearrange("p (l c) -> p l c", c=ch)
            r0 += n * ch
        ps, bps = self.ps()
        S.op('pe', lambda e: e.transpose(out=ps[:, 0:128], in_=vst[:], identity=self.ident[:]), reads=[bvst, self.bconst], writes=[bps])
        S.op('dve', lambda e: e.tensor_copy(out=vall[:], in_=ps[:, 0:128]), reads=[bps], writes=[self.bconst])

    def wload(self, wap, r0, nr, c0, ncols, prefetch=True):
        key = (wap.tensor.name, int(wap.offset), r0, nr, c0, ncols)
        if self.wplan is None:
            self.wrec.append((key, (wap.tensor.name, int(wap.offset), int(wap.shape[0]), int(wap.shape[1]), r0, nr, c0, ncols)))
            return self._wload_now(wap, r0, nr, c0, ncols)
        if not self.wissued:
            self._wprefetch()
        k0, tile = self.wissued.pop(0)
        assert k0 == key, (k0, key)
        if prefetch:
            self._wprefetch()
        return tile

    def wfence(self):
        if self.wplan is None:
            self.wrec.append((None, None))
            return
        assert not self.wissued
        assert self.wplan and self.wplan[0][0] is None
        self.wplan.pop(0)

    def _wprefetch(self):
        if self.wissued or not self.wplan or self.wplan[0][0] is None:
            return
        key, (name, off, R, C, r0, nr, c0, ncols) = self.wplan.pop(0)
        full = self.d[name]
        nd_ = len(full.shape)
        flat = full if nd_ == 1 else full.rearrange(" ".join("abcd"[:nd_]) + " -> (" + " ".join("abcd"[:nd_]) + ")")
        wap = flat[off:off + R * C].rearrange("(r c) -> r c", c=C)
        self.wissued.append((key, self._wload_now(wap, r0, nr, c0, ncols)))

    def _wload_now(self, wap, r0, nr, c0, ncols):
        S = self.S
        P = min(nr, 128)
        kc = nr // P
        assert kc * ncols <= 2048
        i = self.wi
        self.wi = (i + 1) % 2
        st, bst, wb, bwb = self.wst[i], self.bwst[i], self.wbf[i], self.bwbf[i]
        n = kc * ncols
        src = wap[r0:r0 + nr, c0:c0 + ncols].rearrange("(kc p) c -> p kc c", p=P)
        S.dma(st[0:P, 0:n].rearrange("p (kc c) -> p kc c", c=ncols), src, writes=[bst])
        S.op('act', lambda e: e.activation(out=wb[0:P, 0:n], in_=st[0:P, 0:n], func=AF.Copy), reads=[bst], writes=[bwb])
        return wb[0:P, 0:n].rearrange("p (kc c) -> p kc c", c=ncols), bwb

    def mm_fm(self, ps_ap, bps, wt, bw, mcols, rhs_fn, rhs_bufs, kcs, first=True, last=True):
        S = self.S
        n = len(kcs)
        for i, kc in enumerate(kcs):
            rhs = rhs_fn(kc)
            S.op('pe', lambda e, kc=kc, rhs=rhs, i=i: e.matmul(ps_ap, lhsT=wt[:, kc, mcols], rhs=rhs,
                                                          start=(first and i == 0), stop=(last and i == n - 1)),
                 reads=[bw] + rhs_bufs(kc), writes=[bps], inc=(i == n - 1))

    def rmsnorm(self, src, bsrc, N, gcol, dst, bdst, es):
        S = self.S
        blk = min(N, 512)
        sq = [self.sb(f"rn_sq{i}_{N}", [128, blk], BF16, es) for i in range(2)]
        bsq = [Buf(), Buf()]
        rs = self.sb(f"rn_rs_{N}", [128, blk], F32, es)
        brs = Buf()
        for t in range(N // blk):
            sl = slice(t * blk, (t + 1) * blk)
            ps, bps = self.ps()
            for c in range(NCH):
                j = c % 2
                S.op('act', lambda e, c=c, j=j: e.activation(out=sq[j][:], in_=src[:, c, sl], func=AF.Square),
                     reads=[bsrc[c][t]], writes=[bsq[j]])
                S.op('pe', lambda e, c=c, j=j: e.matmul(ps[:, 0:blk], lhsT=self.onesb[:], rhs=sq[j][:], start=(c == 0), stop=(c == NCH - 1)),
                     reads=[bsq[j], self.bconst], writes=[bps], inc=True)
            S.op('dve', lambda e: e.tensor_scalar(out=rs[:], in0=ps[:, 0:blk], scalar1=1.0 / DM, scalar2=EPS, op0=ALU.mult, op1=ALU.add),
                 reads=[bps], writes=[brs])
            S.op('act', lambda e: e.activation(out=rs[:], in_=rs[:], func=AF.Ln), reads=[brs], writes=[brs])
            S.op('act', lambda e: e.activation(out=rs[:], in_=rs[:], func=AF.Exp, scale=-0.5), reads=[brs], writes=[brs])
            for c in range(NCH):
                S.op('dve', lambda e, c=c: e.scalar_tensor_tensor(out=dst[:, c, sl], in0=src[:, c, sl], scalar=gcol[:, c:c + 1], in1=rs[:],
                                                                   op0=ALU.mult, op1=ALU.mult),
                     reads=[bsrc[c][t], brs, self.bconst], writes=[bdst[c][t]])

    def load_T(self, src_ap, ntiles, dst, bdst_fn):
        S = self.S
        for n in range(ntiles):
            i = n % 2
            st, bst = self.wst[i], self.bwst[i]
            S.dma(st[:, 0:1024], src_ap[n * 128:(n + 1) * 128, :], writes=[bst])
            for h in range(2):
                ps, bps = self.ps()
                for j in range(4):
                    c = 4 * h + j
                    S.op('pe', lambda e, c=c, j=j: e.transpose(out=ps[:, j * 128:(j + 1) * 128], in_=st[:, c * 128:(c + 1) * 128], identity=self.ident[:]),
                         reads=[bst, self.bconst], writes=[bps], inc=(j == 3))
                for j in range(4):
                    c = 4 * h + j
                    if LT_MODE == 0 or (LT_MODE == 2 and j % 2 == 1):
                        S.op('dve', lambda e, c=c, j=j, ps=ps: e.tensor_copy(out=dst[:, c, n * 128:(n + 1) * 128], in_=ps[:, j * 128:(j + 1) * 128]),
                             reads=[bps], writes=bdst_fn(h, n))
                    else:
                        S.op('act', lambda e, c=c, j=j, ps=ps: e.activation(out=dst[:, c, n * 128:(n + 1) * 128], in_=ps[:, j * 128:(j + 1) * 128], func=AF.Copy),
                             reads=[bps], writes=bdst_fn(h, n))

    def outproj(self, wap, r0, nk, mix, bmix):
        S = self.S
        for oc in range(NCH):
            wt, bw = self.wload(wap, r0, nk * 128, oc * 128, 128)
            for tq in range(TQ):
                ps, bps = self.ps()
                sl = slice(tq * 512, (tq + 1) * 512)
                self.mm_fm(ps[:], bps, wt, bw, slice(0, 128), lambda kc: mix[:, kc, sl], lambda kc: [bmix[kc][tq]], list(range(nk)))
                S.op('dve', lambda e, oc=oc, sl=sl, ps=ps: e.tensor_tensor(out=self.xT[:, oc, sl], in0=ps[:], in1=self.xT[:, oc, sl], op=ALU.add),
                     reads=[bps, self.bx[oc][tq]], writes=[self.bx[oc][tq]])
        if BARRIERS:
            S.barrier()

    def proj_fm(self, wap, c0, dst_fn, bdst_fn, func=AF.Copy, eng='act'):
        S = self.S
        wt, bw = self.wload(wap, 0, 1024, c0, 128)
        for tq in range(TQ):
            ps, bps = self.ps()
            sl = slice(tq * 512, (tq + 1) * 512)
            self.mm_fm(ps[:], bps, wt, bw, slice(0, 128), lambda kc: self.hnT[:, kc, sl], lambda kc: [self.bhn[kc][tq]], list(range(NCH)))
            if eng == 'act':
                S.op('act', lambda e, tq=tq, ps=ps: e.activation(out=dst_fn(tq), in_=ps[:], func=func), reads=[bps], writes=bdst_fn(tq))
            else:
                S.op('dve', lambda e, tq=tq, ps=ps: e.tensor_copy(out=dst_fn(tq), in_=ps[:]), reads=[bps], writes=bdst_fn(tq))

    def even_layer(self, li):
        S, d = self.S, self.d
        w_in = d['w_in_ab'][li]
        w_out = d['w_out_ab'][li]
        with ExitStack() as es:
            if 'attnA' in STAGES:
                mix = self.sb("mixA", [128, NCH, SEQ], BF16, es)
                bmix = self.grid(NCH, TQ)
                self.attnA(li, w_in, mix, bmix)
                self.outproj(w_out, 0, 8, mix, bmix)
        if 'poolB' not in STAGES:
            return
        with ExitStack() as es:
            mix = self.sb("mixB", [128, NCH, SEQ], BF16, es)
            bmix = self.grid(NCH, TQ)
            self.poolB(li, w_in, mix, bmix)
            self.outproj(w_out, 1024, 8, mix, bmix)

    def attnA(self, li, w_in, mix, bmix):
        S = self.S
        with ExitStack() as es:
            qT = self.sb("qT", [128, SEQ], BF16, es)
            kT = self.sb("kT", [128, SEQ], BF16, es)
            gT = self.sb("gT", [128, SEQ], BF16, es)
            bq, bk, bg = self.grid(TQ), self.grid(TQ), self.grid(TQ)
            Va = self.sb("Vaug", [128, 3, 16, 2, 128], BF16, es)
            bVa = self.grid(3, 4)
            bVones = Buf()
            S.op('pool', lambda e: e.memset(Va[:, :, :, 0, 64:128], 1.0), writes=[bVones])
            S.op('pool', lambda e: e.memset(Va[:, :, :, 1, 0:64], 1.0), writes=[bVones])
            pT = [self.sb(f"pT{i}", [128, 256], BF16, es) for i in range(4)]
            bpT = [Buf() for _ in range(4)]
            eT = [self.sb(f"eT{i}", [128, 256], BF16, es) for i in range(4)]
            beT = [Buf() for _ in range(4)]
            rden = self.sb("rden", [128, 512], F32, es)
            tmpn = self.sb("tmpn", [128, 512], F32, es)
            brden, btmpn = Buf(), Buf()
            pti = 0
            for c in range(NCH):
                self.ps_rot(range(8))
                self.proj_fm(w_in, c * 128, lambda tq: qT[:, tq * 512:(tq + 1) * 512], lambda tq: [bq[tq]])
                self.proj_fm(w_in, 1024 + c * 128, lambda tq: kT[:, tq * 512:(tq + 1) * 512], lambda tq: [bk[tq]], eng='dve')
                self.proj_fm(w_in, 3072 + c * 128, lambda tq: gT[:, tq * 512:(tq + 1) * 512], lambda tq: [bg[tq]], func=AF.Silu)
                wv, bwv = self.wload(w_in, 0, 1024, 2048 + c * 128, 128)
                for o, dd in enumerate((1, 4, 16)):
                    nb = 16 // dd
                    for t4 in range(4):
                        ps, bps = self.ps()
                        for j in range(4):
                            ti = 4 * t4 + j
                            r, kb = ti // nb, ti % nb
                            st = r + dd * 128 * kb
                            tok = slice(st, st + dd * 127 + 1, dd)
                            tqs = sorted(set([(st) // 512, (st + dd * 127) // 512])) if dd < 16 else [0, 1, 2, 3]
                            for kc in range(NCH):
                                S.op('pe', lambda e, kc=kc, tok=tok, j=j, ps=ps: e.matmul(ps[:, j * 128:(j + 1) * 128], lhsT=self.hnT[:, kc, tok], rhs=wv[:, kc, :],
                                                                                    start=(kc == 0), stop=(kc == NCH - 1)),
                                     reads=[bwv] + [self.bhn[kc][q_] for q_ in tqs], writes=[bps], inc=(kc == NCH - 1 and j == 3))
                        psv = ps[:].rearrange("p (j h d) -> p j h d", j=4, h=2)
                        S.op('act', lambda e, o=o, t4=t4, psv=psv: e.activation(out=Va[:, o, 4 * t4:4 * t4 + 4, 0, 0:64], in_=psv[:, :, 0, :], func=AF.Copy),
                             reads=[bps], writes=[bVa[o][t4]])
                        S.op('dve', lambda e, o=o, t4=t4, psv=psv: e.tensor_copy(out=Va[:, o, 4 * t4:4 * t4 + 4, 1, 64:128], in_=psv[:, :, 1, :]),
                             reads=[bps], writes=[bVa[o][t4]])
                for hh in range(2):
                    hp = slice(hh * 64, hh * 64 + 64)
                    dp = slice(64 - hh * 64, 128 - hh * 64)
                    nd = [self.P[i] for i in range(4)]
                    bnd = [self.bP[i] for i in range(4)]
                    started = [False] * 4
                    self.ps_rot(range(4, 8))
                    tiles = []
                    for o, dd in enumerate((1, 4, 16)):
                        nb = 16 // dd
                        for r in range(dd):
                            for kb in range(nb):
                                tiles.append((o, dd, nb, r, kb))
                    LA = 3
                    pend = {}

                    def issue_S(idx):
                        nonlocal pti
                        o, dd, nb, r, kb = tiles[idx]
                        nq = 2 if kb < nb - 1 else 1
                        N = 128 * nq
                        kst = r + dd * 128 * kb
                        ktok = slice(kst, kst + dd * 127 + 1, dd)
                        qtok = slice(kst, kst + dd * (N - 1) + 1, dd)
                        ktqs = [kst // 512] if dd < 16 else [0, 1, 2, 3]
                        qtqs = sorted(set([kst // 512, (kst + dd * (N - 1)) // 512])) if dd < 16 else [0, 1, 2, 3]
                        ps, bps = self.ps()
                        S.op('pe', lambda e: e.matmul(ps[:, 0:N], lhsT=kT[hp, ktok], rhs=qT[hp, qtok], start=True, stop=True),
                             reads=[bk[q_] for q_ in ktqs] + [bq[q_] for q_ in qtqs], writes=[bps], inc=True)
                        p_, bp_ = pT[pti], bpT[pti]
                        e_, be_ = eT[pti], beT[pti]
                        pti = (pti + 1) % len(pT)
                        S.op('act', lambda e: e.activation(out=e_[:, 0:N], in_=ps[:, 0:N], func=AF.Exp, scale=0.125),
                             reads=[bps], writes=[be_])
                        S.op('dve', lambda e: e.tensor_tensor(out=p_[:, 0:N], in0=e_[:, 0:N], in1=self.mask01[:, 0:N], op=ALU.mult),
                             reads=[be_, self.bconst], writes=[bp_])
                        pend[idx] = (p_, bp_, nq)

                    def issue_PV(idx):
                        o, dd, nb, r, kb = tiles[idx]
                        p_, bp_, nq = pend.pop(idx)
                        ti = r * nb + kb
                        lhsV = Va[:, o, ti, hh, :]
                        allouts = []
                        if dd == 1 and nq == 2 and kb % 4 != 3:
                            allouts = [(kb // 4, slice((kb % 4) * 128, (kb % 4) * 128 + 256), slice(0, 256))]
                            nq = 0
                        for qi in range(nq):
                            i = kb + qi
                            if dd == 1:
                                allouts += [(i // 4, slice((i % 4) * 128, (i % 4) * 128 + 128), slice(qi * 128, qi * 128 + 128))]
                            elif dd == 4:
                                allouts += [(i, slice(r, 512, 4), slice(qi * 128, qi * 128 + 128))]
                            else:
                                allouts += [(j, slice(r, 512, 16), slice(j * 32, j * 32 + 32)) for j in range(4)]
                        for n_, (bank, ocols, pcols) in enumerate(allouts):
                            first = not started[bank]
                            started[bank] = True
                            S.op('pe', lambda e: e.matmul(nd[bank][:, ocols], lhsT=lhsV, rhs=p_[:, pcols], start=first, stop=False, skip_group_check=True),
                                 reads=[bp_, bVa[o][ti // 4], bVones], writes=[bnd[bank]], inc=(n_ == len(allouts) - 1))

                    for idx in range(len(tiles) + LA):
                        if idx < len(tiles):
                            issue_S(idx)
                        if idx >= LA:
                            issue_PV(idx - LA)
                    for tq in range(TQ if 'norm' not in ATT_SKIP else 0):
                        sl = slice(tq * 512, (tq + 1) * 512)
                        S.op('act', lambda e, tq=tq: e.activation(out=rden[hp, :], in_=nd[tq][dp, :], func=AF.Ln), reads=[bnd[tq]], writes=[brden])
                        S.op('act', lambda e: e.activation(out=rden[hp, :], in_=rden[hp, :], func=AF.Exp, scale=-1.0), reads=[brden], writes=[brden])
                        S.op('dve', lambda e, tq=tq: e.tensor_tensor(out=tmpn[hp, :], in0=nd[tq][hp, :], in1=rden[hp, :], op=ALU.mult),
                             reads=[bnd[tq], brden], writes=[btmpn])
                        S.op('pool', lambda e, sl=sl: e.tensor_tensor(out=mix[hp, c, sl], in0=tmpn[hp, :], in1=gT[hp, sl], op=ALU.mult),
                             reads=[btmpn, bg[tq]], writes=[bmix[c][tq]])
            self.ps_rot(range(8))

    def poolB(self, li, w_in, mix, bmix):
        S = self.S
        with ExitStack() as es:
            vb = self.sb("vb", [128, 16 + SEQ], F32, es)
            sA = self.sb("sA", [128, 16 + SEQ], F32, es)
            sB = self.sb("sB", [128, 16 + SEQ], F32, es)
            bvb, bsA, bsB = Buf(), Buf(), Buf()
            pooled = self.sb("pooled", [128, 2, SEQ], BF16, es)
            bpo = self.grid(2, TQ)
            gb = self.sb("gbT", [128, 2, SEQ], BF16, es)
            bgb = self.grid(2, TQ)
            t16 = self.sb("t16", [128, 16], F32, es)
            bt16 = Buf()
            for t_ in (vb, sA, sB):
                S.op('pool', lambda e, t_=t_: e.memset(t_[:, 0:16], 0.0), writes=[bvb, bsA, bsB])
            for g in range(4):
                w = (2, 4, 8, 16)[g]
                for j in range(2):
                    cb = 2 * g + j
                    self.proj_fm(w_in, 4096 + cb * 128, lambda tq: vb[:, 16 + tq * 512:16 + (tq + 1) * 512], lambda tq: [bvb])
                    self.proj_fm(w_in, 5120 + cb * 128, lambda tq, j=j: gb[:, j, tq * 512:(tq + 1) * 512], lambda tq, j=j: [bgb[j][tq]], func=AF.Silu)
                    cur, bcur = vb, bvb
                    k = 1
                    nxt = [(sA, bsA), (sB, bsB)]
                    ni = 0
                    while k < w:
                        o_, bo_ = nxt[ni]
                        ni = 1 - ni
                        S.op('dve', lambda e, o_=o_, cur=cur, k=k: e.tensor_tensor(out=o_[:, 16:16 + SEQ], in0=cur[:, 16:16 + SEQ], in1=cur[:, 16 - k:16 - k + SEQ], op=ALU.add),
                             reads=[bcur], writes=[bo_])
                        cur, bcur = o_, bo_
                        k *= 2
                    S.op('dve', lambda e, cur=cur, j=j, w=w: e.scalar_tensor_tensor(out=pooled[:, j, :], in0=cur[:, 16:16 + SEQ], scalar=1.0 / w, in1=vb[:, 16:16 + SEQ],
                                                                                 op0=ALU.mult, op1=ALU.subtract),
                         reads=[bcur, bvb], writes=[bpo[j][t] for t in range(TQ)])
                    S.op('dve', lambda e, cur=cur, g=g: e.tensor_tensor(out=t16[:], in0=cur[:, 16:32], in1=self.invcnt[:, g * 16:(g + 1) * 16], op=ALU.mult),
                         reads=[bcur, self.bconst], writes=[bt16])
                    S.op('dve', lambda e, j=j: e.tensor_tensor(out=pooled[:, j, 0:16], in0=t16[:], in1=vb[:, 16:32], op=ALU.subtract),
                         reads=[bt16, bvb], writes=[bpo[j][0]])
                wp, bwp = self.wload(self.d['pool_w'][li][g], 0, 256, 0, 256)
                for oc2 in range(2):
                    cb = 2 * g + oc2
                    for tq in range(TQ):
                        sl = slice(tq * 512, (tq + 1) * 512)
                        ps, bps = self.ps()
                        self.mm_fm(ps[:], bps, wp, bwp, slice(oc2 * 128, oc2 * 128 + 128), lambda kc: pooled[:, kc, sl], lambda kc: [bpo[kc][tq]], [0, 1])
                        S.op('dve', lambda e, ps=ps, cb=cb, oc2=oc2, sl=sl: e.scalar_tensor_tensor(out=mix[:, cb, sl], in0=ps[:], scalar=self.vec['pool_scale'][:, li, cb:cb + 1],
                                                                                           in1=gb[:, oc2, sl], op0=ALU.mult, op1=ALU.mult),
                             reads=[bps, bgb[oc2][tq], self.bconst], writes=[bmix[cb][tq]])

    def odd_layer(self, li):
        d = self.d
        if 's5D' in STAGES:
            self.s5D(li, d['w_in_cd'][li], d['w_out_cd'][li])
        if 'sguC' in STAGES:
            self.sguC(li, d['w_in_cd'][li], d['w_out_cd'][li])

    def trig(self, y, by, yi, byi, yf, byf, cs, bcs, sn, bsn, on_act=False, f_eng='dve'):
        S = self.S
        if on_act:
            S.op('act', lambda e: e.activation(out=yf, in_=y, func=AF.Identity, bias=self.magic[:, 0:1], scale=1.0), reads=[by, self.bconst], writes=[byf])
            S.op('act', lambda e: e.activation(out=yf, in_=yf, func=AF.Identity, bias=self.magic[:, 1:2], scale=1.0), reads=[byf, self.bconst], writes=[byf])
        else:
            S.op('dve', lambda e: e.tensor_copy(out=yi, in_=y), reads=[by], writes=[byi])
            S.op('dve', lambda e: e.tensor_copy(out=yf, in_=yi), reads=[byi], writes=[byf])
        S.op(f_eng, lambda e: e.tensor_tensor(out=yf, in0=y, in1=yf, op=ALU.subtract), reads=[by, byf], writes=[byf])
        S.op('act', lambda e: e.activation(out=sn, in_=yf, func=AF.Sin, scale=TWO_PI), reads=[byf], writes=[bsn])
        S.op('act', lambda e: e.activation(out=y, in_=yf, func=AF.Abs), reads=[byf], writes=[by])
        S.op('act', lambda e: e.activation(out=cs, in_=y, func=AF.Sin, scale=-TWO_PI, bias=self.halfpi[:, 0:1]), reads=[by, self.bconst], writes=[bcs])

    def s5D(self, li, w_in, w_out):
        S, d = self.S, self.d
        with ExitStack() as es:
            sb = lambda n, sh, dt=F32, es_=es: self.sb(n, sh, dt, es_)
            xd = sb("xdT", [128, 4, SEQ], BF16)
            bxd = self.grid(4, TQ)
            yg = sb("ygT", [128, 4, SEQ], BF16)
            byg = self.grid(4, TQ)
            for kc in range(4):
                self.proj_fm(w_in, 3072 + kc * 128, lambda tq, kc=kc: xd[:, kc, tq * 512:(tq + 1) * 512], lambda tq, kc=kc: [bxd[kc][tq]])
            ar = sb("s_ar", [128, 16]); ai = sb("s_ai", [128, 16]); ldt = sb("s_ldt", [128, 16])
            bprm = Buf()
            pst = sb("s_pst", [32, 128]); ldt0 = sb("s_ldt0", [128, 32])
            bpst = Buf()
            S.dma(pst[:, 0:64], d['s5_a_re'][li], writes=[bpst])
            S.dma(pst[:, 64:128], d['s5_a_im'][li], writes=[bpst])
            S.dma(ldt0[:], d['s5_log_dt'][li].partition_broadcast(128), writes=[bpst])
            for (dst_, c0) in ((ar, 0), (ai, 64)):
                ps, bps = self.ps()
                S.op('pe', lambda e, ps=ps, c0=c0: e.transpose(out=ps[0:64, 0:32], in_=pst[:, c0:c0 + 64], identity=self.ident[0:32, 0:32]), reads=[bpst, self.bconst], writes=[bps])
                S.op('dve', lambda e, ps=ps, dst_=dst_: e.tensor_copy(out=dst_[0:64, :], in_=ps[0:64, 0:32:2]), reads=[bps], writes=[bprm])
                S.op('dve', lambda e, ps=ps, dst_=dst_: e.tensor_copy(out=dst_[64:128, :], in_=ps[0:64, 1:32:2]), reads=[bps], writes=[bprm])
            S.op('dve', lambda e: e.tensor_copy(out=ldt[0:64, :], in_=ldt0[0:64, 0:32:2]), reads=[bpst], writes=[bprm])
            S.op('dve', lambda e: e.tensor_copy(out=ldt[64:128, :], in_=ldt0[64:128, 1:32:2]), reads=[bpst], writes=[bprm])
            names = ["dt", "dtar", "th", "rho", "c0", "s0", "abr", "abi", "inv", "t1", "t2", "cfr", "cfi", "yf", "thn", "y0"]
            T = {n: sb("s_" + n, [128, 16]) for n in names}
            yi0 = sb("s_yi", [128, 16], I32)
            B = {n: Buf() for n in names + ["yi"]}

            def dv(out, fn, reads, eng='dve'):
                S.op(eng, fn, reads=[B[r] if isinstance(r, str) else r for r in reads], writes=[B[out]])
            TT = lambda o, a, b_, op: (lambda e: e.tensor_tensor(out=T[o][:], in0=a[:], in1=b_[:], op=op))
            dv("dt", lambda e: e.activation(out=T["dt"][:], in_=ldt[:], func=AF.Exp), [bprm], 'act')
            dv("dtar", TT("dtar", T["dt"], ar, ALU.mult), ["dt", bprm])
            dv("th", TT("th", T["dt"], ai, ALU.mult), ["dt", bprm])
            dv("rho", lambda e: e.activation(out=T["rho"][:], in_=T["dtar"][:], func=AF.Exp), ["dtar"], 'act')
            dv("thn", lambda e: e.tensor_single_scalar(out=T["thn"][:], in_=T["th"][:], scalar=1.0 / TWO_PI, op=ALU.mult), ["th"])
            dv("y0", lambda e: e.tensor_copy(out=T["y0"][:], in_=T["thn"][:]), ["thn"])
            self.trig(T["y0"][:], B["y0"], yi0[:], B["yi"], T["yf"][:], B["yf"], T["c0"][:], B["c0"], T["s0"][:], B["s0"], on_act=True)
            dv("abr", TT("abr", T["rho"], T["c0"], ALU.mult), ["rho", "c0"])
            dv("abi", TT("abi", T["rho"], T["s0"], ALU.mult), ["rho", "s0"])
            dv("abr", lambda e: e.tensor_single_scalar(out=T["abr"][:], in_=T["abr"][:], scalar=-1.0, op=ALU.add), ["abr"])
            dv("t1", TT("t1", ar, ar, ALU.mult), [bprm])
            dv("t2", TT("t2", ai, ai, ALU.mult), [bprm])
            dv("inv", TT("inv", T["t1"], T["t2"], ALU.add), ["t1", "t2"])
            dv("inv", lambda e: e.reciprocal(out=T["inv"][:], in_=T["inv"][:]), ["inv"])
            dv("t1", TT("t1", T["abr"], ar, ALU.mult), ["abr", bprm])
            dv("t2", TT("t2", T["abi"], ai, ALU.mult), ["abi", bprm])
            dv("cfr", TT("cfr", T["t1"], T["t2"], ALU.add), ["t1", "t2"])
            dv("cfr", TT("cfr", T["cfr"], T["inv"], ALU.mult), ["cfr", "inv"])
            dv("t1", TT("t1", T["abi"], ar, ALU.mult), ["abi", bprm])
            dv("t2", TT("t2", T["abr"], ai, ALU.mult), ["abr", bprm])
            dv("cfi", TT("cfi", T["t1"], T["t2"], ALU.subtract), ["t1", "t2"])
            dv("cfi", TT("cfi", T["cfi"], T["inv"], ALU.mult), ["cfi", "inv"])
            off = sb("s_off", [128, 16, 4])
            boff = Buf()
            for tq in range(TQ):
                S.op('dve', lambda e, tq=tq: e.tensor_single_scalar(out=off[:, :, tq], in_=T["thn"][:], scalar=512.0 * tq, op=ALU.mult), reads=[B["thn"]], writes=[boff])
            Bre = sb("s_Bre", [128, 16, 16]); Bim = sb("s_Bim", [128, 16, 16])
            bB = Buf()
            S.dma(Bre[:], d['s5_b_re'][li].rearrange("(gp g2) p h -> (g2 p) gp h", g2=2), writes=[bB])
            S.dma(Bim[:], d['s5_b_im'][li].rearrange("(gp g2) p h -> (g2 p) gp h", g2=2), writes=[bB])
            bbr = sb("s_bbr", [128, 16, 16]); bbi = sb("s_bbi", [128, 16, 16]); bt = sb("s_bt", [128, 16, 16])
            bbb, bbt = Buf(), Buf()
            cfr_b = T["cfr"][:, :, None].to_broadcast([128, 16, 16])
            cfi_b = T["cfi"][:, :, None].to_broadcast([128, 16, 16])
            S.op('dve', lambda e: e.tensor_tensor(out=bbr[:], in0=Bre[:], in1=cfr_b, op=ALU.mult), reads=[bB, B["cfr"]], writes=[bbb])
            S.op('dve', lambda e: e.tensor_tensor(out=bt[:], in0=Bim[:], in1=cfi_b, op=ALU.mult), reads=[bB, B["cfi"]], writes=[bbt])
            S.op('dve', lambda e: e.tensor_tensor(out=bbr[:], in0=bbr[:], in1=bt[:], op=ALU.subtract), reads=[bbb, bbt], writes=[bbb])
            S.op('dve', lambda e: e.tensor_tensor(out=bbi[:], in0=Bim[:], in1=cfr_b, op=ALU.mult), reads=[bB, B["cfr"]], writes=[bbb])
            S.op('dve', lambda e: e.tensor_tensor(out=bt[:], in0=Bre[:], in1=cfi_b, op=ALU.mult), reads=[bB, B["cfi"], bbb], writes=[bbt])
            S.op('dve', lambda e: e.tensor_tensor(out=bbi[:], in0=bbi[:], in1=bt[:], op=ALU.add), reads=[bbb, bbt], writes=[bbb])
            Cre = sb("s_Cre", [128, 4, 64]); Cim = sb("s_Cim", [128, 4, 64])
            bC = Buf()
            S.dma(Cre[:], d['s5_c_re'][li].rearrange("(kc gl) h p -> (gl h) kc p", kc=4), writes=[bC])
            S.dma(Cim[:], d['s5_c_im'][li].rearrange("(kc gl) h p -> (gl h) kc p", kc=4), writes=[bC])
            Dg = sb("s_Dg", [128, 4, 128], BF16)
            bDg = Buf()
            for kc in range(4):
                S.op('dve', lambda e, kc=kc: e.tensor_single_scalar(out=Dg[:, kc, :], in_=self.ident[:], scalar=self.vec['s5_d'][:, li, kc:kc + 1], op=ALU.mult),
                     reads=[self.bconst], writes=[bDg])
            bm1 = self.bm1[:].rearrange("p (a b) -> p a b", a=4)
            bm2 = self.bm2[:].rearrange("p (a b) -> p a b", a=4)
            with ExitStack() as es2:
                sb2 = lambda n, sh, dt=F32: self.sb(n, sh, dt, es2)
                L = sb2("s_L", [128, 2, 4, 128], BF16)
                Cc = sb2("s_Cc", [128, 2, 4, 128], BF16)
                bL, bCc = Buf(), Buf()
                Z = [sb2(f"s_Z{i}", [128, 128]) for i in range(2)]
                bZ = [Buf(), Buf()]
                zi = 0
                NW = 512
                wkA = {}
                for n in ("y", "yf", "cs", "sn", "br", "bi", "t1", "t2", "t3", "t4"):
                    wkA[n] = (sb2("k_" + n, [128, NW])[:], Buf())
                gl1 = sb2("k_gl1", [128, NW])
                wkA["yi"] = (gl1[:], Buf())
                wkB = {}
                for j, n in enumerate(("y", "yf", "cs", "sn")):
                    wkB[n] = (self.wst[0][:, j * NW:(j + 1) * NW], Buf())
                for j, n in enumerate(("br", "bi", "t1", "t2")):
                    wkB[n] = (self.wst[1][:, j * NW:(j + 1) * NW], Buf())
                wb0f = self.wbf[0][:].bitcast(F32)
                for j, n in enumerate(("t3", "t4")):
                    wkB[n] = (wb0f[:, j * NW:(j + 1) * NW], Buf())
                wkB["yi"] = (self.wbf[1][:].bitcast(I32)[:, 0:NW], Buf())
                wks = [wkA, wkB]
                cur = [0]
                wb1f = self.wbf[1][:].bitcast(F32)
                gl = [wb1f[:, NW:2 * NW], gl1[:]]
                bgl = [Buf(), Buf()]
                self.wfence()
                S.barrier()
                hrb = [sb2(f"k_hr{i}", [128, NW], BF16) for i in range(2)]
                hib = [sb2(f"k_hi{i}", [128, NW], BF16) for i in range(2)]
                bhr, bhi = [Buf(), Buf()], [Buf(), Buf()]
                car = sb2("k_car", [128, 16, 2])
                bcar = [[Buf(), Buf()] for _ in range(16)]
                hidx = 0
                W = lambda n: wks[cur[0]][n][0]
                Bk = lambda n: wks[cur[0]][n][1]
                for kc in range(4):
                    self.ps_rot(range(2, 8))
                    for gpl in range(4):
                        gp = 4 * kc + gpl
                        for ri, src in enumerate((bbr, bbi)):
                            z, bz = Z[zi], bZ[zi]
                            zi = 1 - zi
                            S.op('dve', lambda e, z=z, src=src, gp=gp, gpl=gpl: e.tensor_tensor(
                                out=z[:].rearrange("p (a b) -> p a b", a=8), in0=src[:, gp, None, :].to_broadcast([128, 8, 16]),
                                in1=bm1[:, gpl, :, None].to_broadcast([128, 8, 16]), op=ALU.mult), reads=[bbb, self.bconst], writes=[bz])
                            ps, bps = self.ps()
                            S.op('pe', lambda e, ps=ps, z=z: e.transpose(out=ps[:, 0:128], in_=z[:], identity=self.ident[:]), reads=[bz, self.bconst], writes=[bps])
                            S.op('act', lambda e, ps=ps, ri=ri, gpl=gpl: e.activation(out=L[:, ri, gpl, :], in_=ps[:, 0:128], func=AF.Copy), reads=[bps], writes=[bL])
                        for ri, src in enumerate((Cre, Cim)):
                            z, bz = Z[zi], bZ[zi]
                            zi = 1 - zi
                            S.op('dve', lambda e, z=z, src=src, kc=kc, gpl=gpl: e.tensor_tensor(
                                out=z[:].rearrange("p (a b) -> p a b", a=2), in0=src[:, kc, None, :].to_broadcast([128, 2, 64]),
                                in1=bm2[:, gpl, :, None].to_broadcast([128, 2, 64]), op=ALU.mult), reads=[bC, self.bconst], writes=[bz])
                            ps, bps = self.ps()
                            S.op('pe', lambda e, ps=ps, z=z: e.transpose(out=ps[:, 0:128], in_=z[:], identity=self.ident[:]), reads=[bz, self.bconst], writes=[bps])
                            S.op('dve', lambda e, ps=ps, ri=ri, gpl=gpl: e.tensor_single_scalar(out=Cc[:, ri, gpl, :], in_=ps[:, 0:128], scalar=(1.0 if ri == 0 else -1.0), op=ALU.mult),
                                 reads=[bps], writes=[bCc])
                    its = [(tq, gpl) for tq in range(TQ) for gpl in range(4)]

                    def stageA(n):
                        tq, gpl = its[n]
                        gp = 4 * kc + gpl
                        sl = slice(tq * 512, (tq + 1) * 512)
                        cur[0] = n % 2
                        S.op('act', lambda e: e.activation(out=W("y"), in_=self.iota[:], func=AF.Identity, scale=T["thn"][:, gp:gp + 1], bias=off[:, gp, tq:tq + 1]),
                             reads=[self.bconst, B["thn"], boff], writes=[Bk("y")])
                        self.trig(W("y"), Bk("y"), W("yi"), Bk("yi"), W("yf"), Bk("yf"), W("cs"), Bk("cs"), W("sn"), Bk("sn"), on_act=True, f_eng='pool')
                        pr, bpr = self.ps()
                        pi_, bpi = self.ps()
                        S.op('pe', lambda e: e.matmul(pr[:], lhsT=L[:, 0, gpl, :], rhs=xd[:, kc, sl], start=True, stop=True),
                             reads=[bL, bxd[kc][tq]], writes=[bpr])
                        S.op('pe', lambda e: e.matmul(pi_[:], lhsT=L[:, 1, gpl, :], rhs=xd[:, kc, sl], start=True, stop=True),
                             reads=[bL, bxd[kc][tq]], writes=[bpi])
                        S.op('act', lambda e: e.activation(out=W("br"), in_=pr[:], func=AF.Copy), reads=[bpr], writes=[Bk("br")])
                        S.op('act', lambda e: e.activation(out=W("bi"), in_=pi_[:], func=AF.Copy), reads=[bpi], writes=[Bk("bi")])

                    def stageB(n):
                        nonlocal hidx
                        tq, gpl = its[n]
                        gp = 4 * kc + gpl
                        sl = slice(tq * 512, (tq + 1) * 512)
                        cur[0] = n % 2
                        psy, bpsy = self.P[tq % 2], self.bP[tq % 2]
                        S.op('dve', lambda e: e.tensor_tensor(out=W("t1"), in0=W("br"), in1=W("cs"), op=ALU.mult), reads=[Bk("br"), Bk("cs")], writes=[Bk("t1")])
                        S.op('dve', lambda e: e.tensor_tensor(out=W("t2"), in0=W("bi"), in1=W("sn"), op=ALU.mult), reads=[Bk("bi"), Bk("sn")], writes=[Bk("t2")])
                        S.op('dve', lambda e: e.tensor_tensor(out=W("t3"), in0=W("bi"), in1=W("cs"), op=ALU.mult), reads=[Bk("bi"), Bk("cs")], writes=[Bk("t3")])
                        S.op('dve', lambda e: e.tensor_tensor(out=W("t4"), in0=W("br"), in1=W("sn"), op=ALU.mult), reads=[Bk("br"), Bk("sn")], writes=[Bk("t4")])
                        S.op('dve', lambda e: e.tensor_tensor(out=W("t1"), in0=W("t1"), in1=W("t2"), op=ALU.add), reads=[Bk("t1"), Bk("t2")], writes=[Bk("t1")])
                        S.op('dve', lambda e: e.tensor_tensor(out=W("t3"), in0=W("t3"), in1=W("t4"), op=ALU.subtract), reads=[Bk("t3"), Bk("t4")], writes=[Bk("t3")])
                        rho_b = T["rho"][:, gp:gp + 1].to_broadcast([128, NW])
                        for (src, dst, ci) in (("t1", "br", 0), ("t3", "bi", 1)):
                            init = 0.0 if tq == 0 else car[:, gp, ci:ci + 1]
                            S.op('dve', lambda e: e.tensor_tensor_scan(out=W(dst), data0=rho_b, data1=W(src), initial=init, op0=ALU.mult, op1=ALU.add),
                                 reads=[Bk(src), B["rho"], bcar[gp][ci]], writes=[Bk(dst)])
                        for (dst, ci) in (("br", 0), ("bi", 1)):
                            S.op('act', lambda e: e.activation(out=car[:, gp, ci:ci + 1], in_=W(dst)[:, NW - 1:NW], func=AF.Copy),
                                 reads=[Bk(dst)], writes=[bcar[gp][ci]])
                        hr, hi_, bhr_, bhi_ = hrb[hidx], hib[hidx], bhr[hidx], bhi[hidx]
                        hidx = 1 - hidx
                        S.op('dve', lambda e: e.tensor_tensor(out=W("t1"), in0=W("br"), in1=W("cs"), op=ALU.mult), reads=[Bk("br"), Bk("cs")], writes=[Bk("t1")])
                        S.op('dve', lambda e: e.tensor_tensor(out=W("t4"), in0=W("br"), in1=W("sn"), op=ALU.mult), reads=[Bk("br"), Bk("sn")], writes=[Bk("t4")])
                        S.op('dve', lambda e: e.tensor_tensor(out=W("t2"), in0=W("bi"), in1=W("sn"), op=ALU.mult), reads=[Bk("bi"), Bk("sn")], writes=[Bk("t2")])
                        S.op('dve', lambda e: e.tensor_tensor(out=W("t3"), in0=W("bi"), in1=W("cs"), op=ALU.mult), reads=[Bk("bi"), Bk("cs")], writes=[Bk("t3")])
                        S.op('dve', lambda e: e.tensor_tensor(out=hr[:], in0=W("t1"), in1=W("t2"), op=ALU.subtract), reads=[Bk("t1"), Bk("t2")], writes=[bhr_])
                        S.op('dve', lambda e: e.tensor_tensor(out=hi_[:], in0=W("t3"), in1=W("t4"), op=ALU.add), reads=[Bk("t3"), Bk("t4")], writes=[bhi_])
                        S.op('pe', lambda e: e.matmul(psy[:], lhsT=Cc[:, 0, gpl, :], rhs=hr[:], start=(gpl == 0), stop=False),
                             reads=[bCc, bhr_], writes=[bpsy])
                        S.op('pe', lambda e: e.matmul(psy[:], lhsT=Cc[:, 1, gpl, :], rhs=hi_[:], start=False, stop=False),
                             reads=[bCc, bhi_], writes=[bpsy])
                        if gpl == 3:
                            S.op('pe', lambda e: e.matmul(psy[:], lhsT=Dg[:, kc, :], rhs=xd[:, kc, sl], start=False, stop=True),
                                 reads=[bDg, bxd[kc][tq]], writes=[bpsy])
                            S.op('act', lambda e: e.activation(out=gl[0], in_=psy[:], func=AF.Copy), reads=[bpsy], writes=[bgl[0]])
                            S.op('dve', lambda e: e.tensor_tensor(out=gl[1], in0=gl[0], in1=gl[0], op=ALU.mult), reads=[bgl[0]], writes=[bgl[1]])
                            S.op('dve', lambda e: e.tensor_scalar(out=gl[1], in0=gl[1], scalar1=0.044715, scalar2=1.0, op0=ALU.mult, op1=ALU.add), reads=[bgl[1]], writes=[bgl[1]])
                            S.op('dve', lambda e: e.tensor_tensor(out=gl[1], in0=gl[1], in1=gl[0], op=ALU.mult), reads=[bgl[1], bgl[0]], writes=[bgl[1]])
                            S.op('act', lambda e: e.activation(out=gl[1], in_=gl[1], func=AF.Sigmoid, scale=1.5957691216057308), reads=[bgl[1]], writes=[bgl[1]])
                            S.op('dve', lambda e: e.tensor_tensor(out=yg[:, kc, sl], in0=gl[1], in1=gl[0], op=ALU.mult),
                                 reads=[bgl[1], bgl[0]], writes=[byg[kc][tq]])

                    for n in range(len(its) + 1):
                        if n < len(its):
                            stageA(n)
                        if n >= 1:
                            stageB(n - 1)
                self.ps_rot(range(8))
            S.barrier()
            with ExitStack() as es3:
                sb3 = lambda n, sh, dt=F32: self.sb(n, sh, dt, es3)
                gd = sb3("gdT", [128, SEQ], BF16)
                bgd = self.grid(TQ)
                s2 = sb3("glu_s2", [128, 512]); t2_ = sb3("glu_t", [128, 512])
                bs2, bt2 = Buf(), Buf()
                for oc in range(4):
                    self.proj_fm(w_in, 3584 + oc * 128, lambda tq: gd[:, tq * 512:(tq + 1) * 512], lambda tq: [bgd[tq]], func=AF.Silu)
                    w1, bw1 = self.wload(d['glu_w1'][li], 0, 512, oc * 128, 128)
                    w2, bw2 = self.wload(d['glu_w2'][li], 0, 512, oc * 128, 128, prefetch=False)
                    for tq in range(TQ):
                        sl = slice(tq * 512, (tq + 1) * 512)
                        p1, bp1 = self.ps()
                        p2, bp2 = self.ps()
                        self.mm_fm(p1[:], bp1, w1, bw1, slice(0, 128), lambda kc: yg[:, kc, sl], lambda kc: [byg[kc][tq]], [0, 1, 2, 3])
                        self.mm_fm(p2[:], bp2, w2, bw2, slice(0, 128), lambda kc: yg[:, kc, sl], lambda kc: [byg[kc][tq]], [0, 1, 2, 3])
                        S.op('act', lambda e, p2=p2: e.activation(out=s2[:], in_=p2[:], func=AF.Sigmoid), reads=[bp2], writes=[bs2])
                        S.op('dve', lambda e, p1=p1: e.tensor_tensor(out=t2_[:], in0=p1[:], in1=s2[:], op=ALU.mult), reads=[bp1, bs2], writes=[bt2])
                        S.op('pool', lambda e, oc=oc, sl=sl: e.tensor_tensor(out=xd[:, oc, sl], in0=t2_[:], in1=gd[:, sl], op=ALU.mult),
                             reads=[bt2, bgd[tq]], writes=[bxd[oc][tq]])
            self.outproj(w_out, 1024, 4, xd, bxd)

    def sguC(self, li, w_in, w_out):
        S, d = self.S, self.d
        with ExitStack() as es:
            sb = lambda n, sh, dt=F32, es_=es: self.sb(n, sh, dt, es_)
            vn = sb("vn", [128, 16, DM], BF16)
            bvn = self.grid(16)
            with ExitStack() as es2:
                sb2 = lambda n, sh, dt=F32: self.sb(n, sh, dt, es2)
                ssum = sb2("c_ssum", [128, 16, 4]); ssq = sb2("c_ssq", [128, 16, 4])
                bst = Buf()
                junk = sb2("c_junk", [128, 256], BF16)
                bjunk = Buf()
                S.op('dve', lambda e: e.memset(ssum[:], 0.0), writes=[bst])
                S.op('dve', lambda e: e.memset(ssq[:], 0.0), writes=[bst])
                for q in range(4):
                    wt, bw = self.wload(w_in, 0, 1024, 1024 + q * 256, 256)
                    for n in range(16):
                        ps, bps = self.ps()
                        tok = slice(n * 128, (n + 1) * 128)
                        for kc in range(NCH):
                            S.op('pe', lambda e, ps=ps, kc=kc, tok=tok, wt=wt: e.matmul(ps[:, 0:256], lhsT=self.hnT[:, kc, tok], rhs=wt[:, kc, :], start=(kc == 0), stop=(kc == NCH - 1)),
                                 reads=[bw, self.bhn[kc][n // 4]], writes=[bps], inc=(kc == NCH - 1))
                        S.op('act', lambda e, ps=ps, n=n, q=q: e.activation(out=vn[:, n, q * 256:(q + 1) * 256], in_=ps[:, 0:256], func=AF.Copy, accum_out=ssum[:, n, q:q + 1]),
                             reads=[bps, bst], writes=[bvn[n], bst])
                        S.op('act', lambda e, ps=ps, n=n, q=q: e.activation(out=junk[:], in_=ps[:, 0:256], func=AF.Square, accum_out=ssq[:, n, q:q + 1]),
                             reads=[bps, bst], writes=[bjunk, bst])
                mean = sb2("c_mean", [128, 16]); var = sb2("c_var", [128, 16]); m2 = sb2("c_m2", [128, 16])
                S.op('dve', lambda e: e.tensor_tensor(out=ssum[:, :, 0:2], in0=ssum[:, :, 0:2], in1=ssum[:, :, 2:4], op=ALU.add), reads=[bst], writes=[bst])
                S.op('dve', lambda e: e.tensor_tensor(out=mean[:], in0=ssum[:, :, 0], in1=ssum[:, :, 1], op=ALU.add), reads=[bst], writes=[bst])
                S.op('dve', lambda e: e.tensor_single_scalar(out=mean[:], in_=mean[:], scalar=1.0 / DM, op=ALU.mult), reads=[bst], writes=[bst])
                S.op('dve', lambda e: e.tensor_tensor(out=ssq[:, :, 0:2], in0=ssq[:, :, 0:2], in1=ssq[:, :, 2:4], op=ALU.add), reads=[bst], writes=[bst])
                S.op('dve', lambda e: e.tensor_tensor(out=var[:], in0=ssq[:, :, 0], in1=ssq[:, :, 1], op=ALU.add), reads=[bst], writes=[bst])
                S.op('dve', lambda e: e.tensor_tensor(out=m2[:], in0=mean[:], in1=mean[:], op=ALU.mult), reads=[bst], writes=[bst])
                S.op('dve', lambda e: e.scalar_tensor_tensor(out=var[:], in0=var[:], scalar=1.0 / DM, in1=m2[:], op0=ALU.mult, op1=ALU.subtract), reads=[bst], writes=[bst])
                S.op('dve', lambda e: e.tensor_single_scalar(out=var[:], in_=var[:], scalar=EPS, op=ALU.add), reads=[bst], writes=[bst])
                S.op('act', lambda e: e.activation(out=var[:], in_=var[:], func=AF.Ln), reads=[bst], writes=[bst])
                S.op('act', lambda e: e.activation(out=var[:], in_=var[:], func=AF.Exp, scale=-0.5), reads=[bst], writes=[bst])
                lng = sb2("c_lng", [128, DM]); lnb = sb2("c_lnb", [128, DM])
                bln = Buf()
                S.dma(lng[:], d['sgu_ln_g'][li].partition_broadcast(128), writes=[bln])
                S.dma(lnb[:], d['sgu_ln_b'][li].partition_broadcast(128), writes=[bln])
                tmp = [sb2(f"c_tmp{i}", [128, DM]) for i in range(2)]
                btmp = [Buf(), Buf()]
                for n in range(16):
                    t_, bt_ = tmp[n % 2], btmp[n % 2]
                    S.op('dve', lambda e, n=n, t_=t_: e.tensor_scalar(out=t_[:], in0=vn[:, n, :], scalar1=mean[:, n:n + 1], scalar2=var[:, n:n + 1], op0=ALU.subtract, op1=ALU.mult),
                         reads=[bvn[n], bst], writes=[bt_])
                    S.op('pool', lambda e, t_=t_: e.tensor_tensor(out=t_[:], in0=t_[:], in1=lng[:], op=ALU.mult), reads=[bt_, bln], writes=[bt_])
                    S.op('pool', lambda e, n=n, t_=t_: e.tensor_tensor(out=vn[:, n, :], in0=t_[:], in1=lnb[:], op=ALU.add), reads=[bt_, bln], writes=[bvn[n]])
            S.barrier()
            mix = sb("mixC", [128, NCH, SEQ], BF16)
            bmix = self.grid(NCH, TQ)
            wsT = sb("c_wsT", [128, 4, 128], BF16)
            bws = Buf()
            bsb = sb("c_bsb", [128, 4, 128])
            bbs = Buf()
            for g in range(4):
                i = g % 2
                st, bst_ = self.wst[i], self.bwst[i]
                S.dma(st[:, 0:128], d['sgu_w'][li][g], writes=[bst_])
                ps, bps = self.ps()
                S.op('pe', lambda e, ps=ps, st=st: e.transpose(out=ps[:, 0:128], in_=st[:, 0:128], identity=self.ident[:]), reads=[bst_, self.bconst], writes=[bps])
                S.op('dve', lambda e, ps=ps, g=g: e.tensor_tensor(out=wsT[:, g, :], in0=ps[:, 0:128], in1=self.triu[:], op=ALU.mult), reads=[bps, self.bconst], writes=[bws])
                S.dma(bsb[:, g, :], d['sgu_b'][li][g].partition_broadcast(128), writes=[bbs])
            ta = [sb(f"c_ta{i}", [128, 512]) for i in range(2)]
            bta = [Buf(), Buf()]
            sg = [sb(f"c_sg{i}", [128, 512]) for i in range(2)]
            bsg = [Buf(), Buf()]
            k = 0
            for c in range(NCH):
                g = c // 2
                wu, bwu = self.wload(w_in, 0, 1024, c * 128, 128)
                wg, bwg = self.wload(w_in, 0, 1024, 2048 + c * 128, 128, prefetch=False)
                for tq in range(TQ):
                    sl = slice(tq * 512, (tq + 1) * 512)
                    a_, ba_, s_, bs_ = ta[k], bta[k], sg[k], bsg[k]
                    k = 1 - k
                    ps, bps = self.ps()
                    for j in range(4):
                        n = 4 * tq + j
                        S.op('pe', lambda e, ps=ps, j=j, n=n, c=c, g=g: e.matmul(ps[:, j * 128:(j + 1) * 128], lhsT=vn[:, n, c * 128:(c + 1) * 128], rhs=wsT[:, g, :], start=True, stop=True),
                             reads=[bvn[n], bws], writes=[bps], inc=(j == 3))
                    S.op('dve', lambda e, ps=ps, g=g, a_=a_: e.tensor_tensor(out=a_[:].rearrange("p (a b) -> p a b", a=4), in0=ps[:].rearrange("p (a b) -> p a b", a=4),
                                                                        in1=bsb[:, g, None, :].to_broadcast([128, 4, 128]), op=ALU.add), reads=[bps, bbs], writes=[ba_])
                    pu, bpu = self.ps()
                    self.mm_fm(pu[:], bpu, wu, bwu, slice(0, 128), lambda kc: self.hnT[:, kc, sl], lambda kc: [self.bhn[kc][tq]], list(range(NCH)))
                    S.op('dve', lambda e, pu=pu, a_=a_: e.tensor_tensor(out=a_[:], in0=pu[:], in1=a_[:], op=ALU.mult), reads=[bpu, ba_], writes=[ba_])
                    pg, bpg = self.ps()
                    self.mm_fm(pg[:], bpg, wg, bwg, slice(0, 128), lambda kc: self.hnT[:, kc, sl], lambda kc: [self.bhn[kc][tq]], list(range(NCH)))
                    S.op('act', lambda e, pg=pg, s_=s_: e.activation(out=s_[:], in_=pg[:], func=AF.Silu), reads=[bpg], writes=[bs_])
                    S.op('pool', lambda e, a_=a_, s_=s_, sl=sl, c=c: e.tensor_tensor(out=mix[:, c, sl], in0=a_[:], in1=s_[:], op=ALU.mult), reads=[ba_, bs_], writes=[bmix[c][tq]])
            self.outproj(w_out, 0, 8, mix, bmix)

    def cross(self, l):
        S, d = self.S, self.d
        with ExitStack() as es:
            sb = lambda n, sh, dt=F32: self.sb(n, sh, dt, es)
            with ExitStack() as es2:
                self.rmsnorm(self.xT, self.bx, SEQ, self.vec['norm_x'][:, l, :], self.hnT, self.bhn, es2)
            S.barrier()
            qT = sb("xqT", [128, 2, SEQ], BF16)
            bq = self.grid(2, TQ)
            mix = sb("mixX", [128, NCH, SEQ], BF16)
            bmix = self.grid(NCH, TQ)
            KT = sb("xKT", [128, NCH, 256], BF16)
            bKT = self.grid(NCH)
            Vx = sb("xV", [128, 2, DM], BF16)
            bVx = self.grid(2)
            pT = [sb(f"xpT{i}", [128, 2, 512], BF16) for i in range(2)]
            bpT = [Buf(), Buf()]
            rden = sb("xrden", [128, 512])
            brden = Buf()
            wkv = d['w_xkv'][l]
            for c in range(NCH):
                wt, bw = self.wload(wkv, 0, 1024, c * 128, 128)
                ps, bps = self.ps()
                self.mm_fm(ps[:, 0:256], bps, wt, bw, slice(0, 128), lambda kc: self.memT[:, kc, :], lambda kc: [self.bmem], list(range(NCH)))
                S.op('act', lambda e, ps=ps, c=c: e.activation(out=KT[:, c, :], in_=ps[:, 0:256], func=AF.Copy), reads=[bps], writes=[bKT[c]])
            for q in range(4):
                wt, bw = self.wload(wkv, 0, 1024, 1024 + q * 256, 256)
                for mt in range(2):
                    ps, bps = self.ps()
                    for kc in range(NCH):
                        S.op('pe', lambda e, ps=ps, kc=kc, mt=mt, wt=wt: e.matmul(ps[:, 0:256], lhsT=self.memT[:, kc, mt * 128:(mt + 1) * 128], rhs=wt[:, kc, :], start=(kc == 0), stop=(kc == NCH - 1)),
                             reads=[bw, self.bmem], writes=[bps], inc=(kc == NCH - 1))
                    S.op('act', lambda e, ps=ps, mt=mt, q=q: e.activation(out=Vx[:, mt, q * 256:(q + 1) * 256], in_=ps[:, 0:256], func=AF.Copy), reads=[bps], writes=[bVx[mt]])
            pi = 0
            for h in range(4):
                for k2 in range(2):
                    self.proj_fm(d['w_xq'][l], (2 * h + k2) * 128, lambda tq, k2=k2: qT[:, k2, tq * 512:(tq + 1) * 512], lambda tq, k2=k2: [bq[k2][tq]], eng=('act' if k2 == 0 else 'dve'))
                for tq in range(TQ):
                    sl = slice(tq * 512, (tq + 1) * 512)
                    p_, bp_ = pT[pi], bpT[pi]
                    pi = 1 - pi
                    for mt in range(2):
                        ps, bps = self.ps()
                        for k2 in range(2):
                            cc = 2 * h + k2
                            S.op('pe', lambda e, ps=ps, cc=cc, mt=mt, k2=k2, sl=sl: e.matmul(ps[:], lhsT=KT[:, cc, mt * 128:(mt + 1) * 128], rhs=qT[:, k2, sl], start=(k2 == 0), stop=(k2 == 1)),
                                 reads=[bKT[cc], bq[k2][tq]], writes=[bps], inc=(k2 == 1))
                        S.op('act', lambda e, ps=ps, mt=mt, p_=p_: e.activation(out=p_[:, mt, :], in_=ps[:], func=AF.Exp, scale=1.0 / 16.0), reads=[bps], writes=[bp_])
                    psd, bpsd = self.ps()
                    for mt in range(2):
                        S.op('pe', lambda e, psd=psd, mt=mt, p_=p_: e.matmul(psd[:], lhsT=self.onesb[:], rhs=p_[:, mt, :], start=(mt == 0), stop=(mt == 1)),
                             reads=[bp_, self.bconst], writes=[bpsd], inc=(mt == 1))
                    S.op('dve', lambda e, psd=psd: e.reciprocal(out=rden[:], in_=psd[:]), reads=[bpsd], writes=[brden])
                    for dc in range(2):
                        cc = 2 * h + dc
                        pso, bpso = self.ps()
                        for mt in range(2):
                            S.op('pe', lambda e, pso=pso, mt=mt, p_=p_, cc=cc: e.matmul(pso[:], lhsT=Vx[:, mt, cc * 128:(cc + 1) * 128], rhs=p_[:, mt, :], start=(mt == 0), stop=(mt == 1)),
                                 reads=[bp_, bVx[mt]], writes=[bpso], inc=(mt == 1))
                        S.op('dve', lambda e, pso=pso, cc=cc, sl=sl: e.tensor_tensor(out=mix[:, cc, sl], in0=pso[:], in1=rden[:], op=ALU.mult),
                             reads=[bpso, brden], writes=[bmix[cc][tq]])
            self.outproj(d['w_xo'][l], 0, 8, mix, bmix)

    def final(self):
        S, d = self.S, self.d
        with ExitStack() as es:
            sb = lambda n, sh, dt=F32: self.sb(n, sh, dt, es)
            blk = 512
            sq = [sb(f"f_sq{i}", [128, blk], BF16) for i in range(2)]
            bsq = [Buf(), Buf()]
            rs = sb("f_rs", [128, blk])
            brs = Buf()
            nrm = [sb(f"f_n{i}", [128, blk]) for i in range(2)]
            bnrm = [Buf(), Buf()]
            gcol = self.vec['final_norm'][:, 0, :]
            ost = [sb(f"f_o{i}", [128, DM]) for i in range(2)]
            bost = [Buf(), Buf()]
            for tq in range(TQ):
                sl = slice(tq * blk, (tq + 1) * blk)
                ps, bps = self.ps()
                for c in range(NCH):
                    j = c % 2
                    S.op('act', lambda e, c=c, j=j: e.activation(out=sq[j][:], in_=self.xT[:, c, sl], func=AF.Square), reads=[self.bx[c][tq]], writes=[bsq[j]])
                    S.op('pe', lambda e, c=c, j=j, ps=ps: e.matmul(ps[:], lhsT=self.onesb[:], rhs=sq[j][:], start=(c == 0), stop=(c == NCH - 1)),
                         reads=[bsq[j], self.bconst], writes=[bps], inc=True)
                S.op('dve', lambda e, ps=ps: e.tensor_scalar(out=rs[:], in0=ps[:], scalar1=1.0 / DM, scalar2=EPS, op0=ALU.mult, op1=ALU.add), reads=[bps], writes=[brs])
                S.op('act', lambda e: e.activation(out=rs[:], in_=rs[:], func=AF.Ln), reads=[brs], writes=[brs])
                S.op('act', lambda e: e.activation(out=rs[:], in_=rs[:], func=AF.Exp, scale=-0.5), reads=[brs], writes=[brs])
                pts = [self.ps() for _ in range(8)]
                for c in range(NCH):
                    n_, bn_ = nrm[c % 2], bnrm[c % 2]
                    S.op('dve', lambda e, c=c, n_=n_: e.scalar_tensor_tensor(out=n_[:], in0=self.xT[:, c, sl], scalar=gcol[:, c:c + 1], in1=rs[:], op0=ALU.mult, op1=ALU.mult),
                         reads=[self.bx[c][tq], brs, self.bconst], writes=[bn_])
                    for j in range(4):
                        pp, bpp = pts[2 * j + c // 4]
                        S.op('pe', lambda e, pp=pp, j=j, c=c, n_=n_: e.transpose(out=pp[:, (c % 4) * 128:(c % 4) * 128 + 128], in_=n_[:, j * 128:(j + 1) * 128], identity=self.ident[:]),
                             reads=[bn_, self.bconst], writes=[bpp], inc=True)
                for j in range(4):
                    n = 4 * tq + j
                    o_, bo_ = ost[n % 2], bost[n % 2]
                    for h in range(2):
                        pp, bpp = pts[2 * j + h]
                        if h == 0:
                            S.op('act', lambda e, pp=pp, o_=o_: e.activation(out=o_[:, 0:512], in_=pp[:], func=AF.Copy), reads=[bpp], writes=[bo_])
                        else:
                            S.op('dve', lambda e, pp=pp, o_=o_: e.tensor_copy(out=o_[:, 512:1024], in_=pp[:]), reads=[bpp], writes=[bo_])
                    S.dma(d['out'][n * 128:(n + 1) * 128, :], o_[:], reads=[bo_])

    def run(self):
        S, d = self.S, self.d
        self.setup()
        if CUT == 1:
            return
        self.load_T(d['x'], 16, self.xT, lambda h, n: [self.bx[c][n // 4] for c in range(4 * h, 4 * h + 4)])
        if CUT == 2:
            return
        with ExitStack() as es:
            mraw = self.sb("mraw", [128, NCH, 256], F32, es)
            bmr = [[Buf()] for _ in range(NCH)]
            self.load_T(d['mem'], 2, mraw, lambda h, n: [bmr[c][0] for c in range(4 * h, 4 * h + 4)])
            bm = [[self.bmem] for _ in range(NCH)]
            self.rmsnorm(mraw, bmr, 256, self.vec['mem_norm'][:, 0, :], self.memT, bm, es)
        S.barrier()
        if CUT == 3:
            return
        for layer in range(self.depth):
            i = layer // 2
            with ExitStack() as es:
                gname = 'norm_ab' if layer % 2 == 0 else 'norm_cd'
                self.rmsnorm(self.xT, self.bx, SEQ, self.vec[gname][:, i, :], self.hnT, self.bhn, es)
            S.barrier()
            if layer % 2 == 0:
                self.even_layer(i)
            else:
                self.odd_layer(i)
            S.barrier()
            if 'cross' in STAGES:
                self.cross(layer)
            S.barrier()
        self.final()


import os
_CACHE = {}
NORM_DIV = int(os.environ.get('NORM_DIV', '1'))
S5_ACT = int(os.environ.get('S5_ACT', '1'))
BARRIERS = int(os.environ.get('BARRIERS', '1'))
ATT_SKIP = set(os.environ.get('ATT_SKIP', '').split(','))
LT_MODE = 0
CUT = 0
STAGES = {'attnA', 'poolB', 'cross', 's5D', 'sguC'}


def build_nc(depth=4):
    key = (depth, tuple(sorted(STAGES)))
    if key in _CACHE:
        return _CACHE[key]
    plan = None
    for pass_ in range(2):
        nc = bass.Bass("TRN2", target_bir_lowering=False)
        with ExitStack() as es:
            S = Sched(nc, es)
            K = Kern(nc, S, es, depth, wplan=plan)
            K.run()
            if pass_ == 0:
                plan = [(k, sp) for k, sp in K.wrec]
                continue
            S.emit()
    _CACHE[key] = nc
    return nc


def kernel(**inputs):
    n = 8
    nc = build_nc(4)
    consts = host_consts()
    x = np.ascontiguousarray(np.asarray(inputs['x'], dtype=np.float32))
    mem = np.ascontiguousarray(np.asarray(inputs['mem'], dtype=np.float32))
    shared = {name: np.ascontiguousarray(np.asarray(inputs[name], dtype=np.float32)) for name, _ in PARAMS}
    shared.update(consts)
    in_maps = []
    for b in range(n):
        m = dict(shared)
        m['x'] = x[b]
        m['mem'] = mem[b]
        in_maps.append(m)
    res = run_bass_kernel_spmd(nc, in_maps, core_ids=list(range(n)))
    return np.stack([np.asarray(r['out'], dtype=np.float32) for r in res.results], axis=0)
```
